# Optimizing a Trainium2 kernel written in Bass

```python
import math
import jax, jax.numpy as jnp
from jax import lax
import numpy as np

D_MODEL = 1024
BATCH = 4
SEQ = 8192
DEPTH = 4

CHUNK = 64
N_MIXERS = 4
RMS_EPS = 1e-6

A_HEADS = 16
A_HEAD_DIM = D_MODEL // A_HEADS
A_WIDTH = A_HEADS * A_HEAD_DIM
A_LEFT_CHUNKS = 8
A_BAND = (A_LEFT_CHUNKS + 1) * CHUNK
A_MAX_REL = 256
A_REL_SIZE = (CHUNK - 1) + A_MAX_REL + 1

SG_FF = 6 * D_MODEL
SG_HALF = SG_FF // 2
SG_GROUPS = 8
SG_WINDOW = 128

GLA_HEADS = 4
GLA_KEY_WIDTH = D_MODEL // 2
GLA_VALUE_WIDTH = D_MODEL
GLA_DK = GLA_KEY_WIDTH // GLA_HEADS
GLA_DV = GLA_VALUE_WIDTH // GLA_HEADS
GLA_GATE_RANK = 16
GLA_TAU = 16.0
GLA_IN_WIDTH = 2 * GLA_KEY_WIDTH + 2 * GLA_VALUE_WIDTH + GLA_GATE_RANK

SB_HEADS = 16
SB_HEAD_DIM = D_MODEL // SB_HEADS
SB_WIDTH = SB_HEADS * SB_HEAD_DIM
SB_BLOCK = 128

FFN_HIDDEN = 4 * D_MODEL

kernel_name = "hybrid_chunk_causal_interleaved_trunk"


def rms_norm(x, gain):
    xf = x.astype(jnp.float32)
    y = xf * lax.rsqrt(jnp.mean(xf * xf, axis=-1, keepdims=True) + RMS_EPS)
    return (y * gain.astype(jnp.float32)).astype(x.dtype)


def chunked_relpos_attention(h, w_in, q_gain, k_gain, rel_bias, w_out):
    B, S, _ = h.shape
    nc = S // CHUNK
    q, k, v = jnp.split(h @ w_in, 3, axis=-1)
    q = rms_norm(q.reshape(B, S, A_HEADS, A_HEAD_DIM), q_gain) * (A_HEAD_DIM ** -0.5)
    k = rms_norm(k.reshape(B, S, A_HEADS, A_HEAD_DIM), k_gain)
    v = v.reshape(B, S, A_HEADS, A_HEAD_DIM)
    pad = A_LEFT_CHUNKS * CHUNK
    kp = jnp.pad(k, ((0, 0), (pad, 0), (0, 0), (0, 0)))
    vp = jnp.pad(v, ((0, 0), (pad, 0), (0, 0), (0, 0)))
    rel = jnp.arange(CHUNK)[:, None] + pad - jnp.arange(A_BAND)[None, :]
    bias_idx = jnp.clip(rel, -(CHUNK - 1), A_MAX_REL) + (CHUNK - 1)
    bias = rel_bias.astype(jnp.float32)[:, bias_idx]

    def one_chunk(ci):
        start = ci * CHUNK
        qc = lax.dynamic_slice_in_dim(q, start, CHUNK, axis=1)
        kb = lax.dynamic_slice_in_dim(kp, start, A_BAND, axis=1)
        vb = lax.dynamic_slice_in_dim(vp, start, A_BAND, axis=1)
        s = jnp.einsum('bihd,bjhd->bhij', qc, kb).astype(jnp.float32) + bias
        valid = (start + jnp.arange(A_BAND)) >= pad
        s = jnp.where(valid[None, None, None, :], s, -jnp.inf)
        p = jax.nn.softmax(s, axis=-1).astype(vb.dtype)
        return jnp.einsum('bhij,bjhd->bihd', p, vb)

    o = lax.map(one_chunk, jnp.arange(nc))
    o = o.transpose(1, 0, 2, 3, 4).reshape(B, S, A_WIDTH)
    return o @ w_out


def chunked_spatial_gating(h, w_in, v_gain, w_s, b_s, w_out):
    B, S, _ = h.shape
    nw = S // SG_WINDOW
    z = jax.nn.gelu(h @ w_in, approximate=False)
    u, v = jnp.split(z, 2, axis=-1)
    v = rms_norm(v, v_gain).reshape(B, nw, SG_WINDOW, SG_GROUPS, SG_HALF // SG_GROUPS)
    chunk_id = jnp.arange(SG_WINDOW) // CHUNK
    mask = chunk_id[:, None] >= chunk_id[None, :]
    w = jnp.where(mask[None], w_s, 0.0).astype(v.dtype)
    vm = jnp.einsum('gij,bnjgc->bnigc', w, v) + b_s.T.astype(v.dtype)[None, None, :, :, None]
    return (u * vm.reshape(B, S, SG_HALF)) @ w_out


def gated_linear_attention(h, w_in, w_gate_up, b_gate, o_gain, w_out):
    B, S, _ = h.shape
    nc = S // CHUNK
    splits = [GLA_KEY_WIDTH, 2 * GLA_KEY_WIDTH, 2 * GLA_KEY_WIDTH + GLA_VALUE_WIDTH,
              2 * GLA_KEY_WIDTH + 2 * GLA_VALUE_WIDTH]
    q, k, v, r, a = jnp.split(h @ w_in, splits, axis=-1)
    log_alpha = jax.nn.log_sigmoid((a @ w_gate_up + b_gate).astype(jnp.float32)) / GLA_TAU

    def to_chunks(t, d):
        return t.reshape(B, nc, CHUNK, GLA_HEADS, d).transpose(1, 0, 3, 2, 4).astype(jnp.float32)

    qc = to_chunks(q, GLA_DK) * (GLA_DK ** -0.5)
    kc = to_chunks(k, GLA_DK)
    vc = to_chunks(v, GLA_DV)
    bc = jnp.cumsum(to_chunks(log_alpha, GLA_DK), axis=3)
    causal = jnp.tril(jnp.ones((CHUNK, CHUNK), dtype=bool))

    def step(state, inp):
        qt, kt, vt, bt = inp
        o_inter = jnp.einsum('bhtk,bhkv->bhtv', qt * jnp.exp(bt), state)
        decay = bt[:, :, :, None, :] - bt[:, :, None, :, :]
        decay = jnp.exp(jnp.where(causal[None, None, :, :, None], decay, -jnp.inf))
        att = jnp.einsum('bhtk,bhsk,bhtsk->bhts', qt, kt, decay)
        o_intra = jnp.einsum('bhts,bhsv->bhtv', att, vt)
        b_last = bt[:, :, -1:, :]
        k_dec = kt * jnp.exp(b_last - bt)
        state = state * jnp.exp(b_last[:, :, 0, :])[..., None] + jnp.einsum('bhsk,bhsv->bhkv', k_dec, vt)
        return state, o_inter + o_intra

    state0 = jnp.zeros((B, GLA_HEADS, GLA_DK, GLA_DV), jnp.float32)
    _, o = lax.scan(step, state0, (qc, kc, vc, bc))
    o = o.transpose(1, 0, 3, 2, 4).reshape(B, S, GLA_HEADS, GLA_DV).astype(h.dtype)
    o = rms_norm(o, o_gain).reshape(B, S, GLA_VALUE_WIDTH)
    return (jax.nn.silu(r) * o) @ w_out


def stick_breaking_attention(h, w_in, w_out):
    B, S, _ = h.shape
    q, k, v = jnp.split(h @ w_in, 3, axis=-1)
    q = q.reshape(B, S, SB_HEADS, SB_HEAD_DIM).transpose(0, 2, 1, 3)
    k = k.reshape(B, S, SB_HEADS, SB_HEAD_DIM).transpose(0, 2, 1, 3)
    v = v.reshape(B, S, SB_HEADS, SB_HEAD_DIM).transpose(0, 2, 1, 3)
    scale = SB_HEAD_DIM ** -0.5
    outs = []
    for blk in range(S // SB_BLOCK):
        q0 = blk * SB_BLOCK
        kend = q0 + SB_BLOCK
        z = jnp.einsum('bhtd,bhsd->bhts', q[:, :, q0:kend], k[:, :, :kend]).astype(jnp.float32) * scale
        t_pos = q0 + jnp.arange(SB_BLOCK)
        strict = jnp.arange(kend)[None, :] < t_pos[:, None]
        log_not = jnp.where(strict, jax.nn.log_sigmoid(-z), 0.0)
        tail = lax.cumsum(log_not, axis=3, reverse=True)
        excl = jnp.concatenate([tail[..., 1:], jnp.zeros_like(tail[..., :1])], axis=-1)
        weights = jnp.where(strict, jnp.exp(jax.nn.log_sigmoid(z) + excl), 0.0)
        outs.append(jnp.einsum('bhts,bhsd->bhtd', weights.astype(v.dtype), v[:, :, :kend]))
    o = jnp.concatenate(outs, axis=2).transpose(0, 2, 1, 3).reshape(B, S, SB_WIDTH)
    return o @ w_out


def squared_relu_mlp(h, w1, w2):
    return jnp.square(jax.nn.relu(h @ w1)) @ w2


def _n_mixer_layers(m):
    return len(range(m, DEPTH, N_MIXERS))


def setup_inputs(seed: int = 0) -> dict:
    key = jax.random.key(seed)
    keys = iter(jax.random.split(key, 32))
    D = D_MODEL

    def nrm(shape, scale):
        return jax.random.normal(next(keys), shape, jnp.float32) * scale

    nA, nB, nC, nD = (_n_mixer_layers(m) for m in range(N_MIXERS))
    return {
        "x": nrm((BATCH, SEQ, D), 1.0),
        "c": nrm((BATCH, D), 1.0),
        "ada_w": nrm((DEPTH, D, 6 * D), 0.5 * D ** -0.5),
        "ada_b": nrm((DEPTH, 6 * D), 0.02),
        "norm_mix": 1.0 + nrm((DEPTH, D), 0.05),
        "norm_ffn": 1.0 + nrm((DEPTH, D), 0.05),
        "ffn_w1": nrm((DEPTH, D, FFN_HIDDEN), D ** -0.5),
        "ffn_w2": nrm((DEPTH, FFN_HIDDEN, D), FFN_HIDDEN ** -0.5),
        "a_w_in": nrm((nA, D, 3 * A_WIDTH), D ** -0.5),
        "a_q_gain": 1.0 + nrm((nA, A_HEAD_DIM), 0.05),
        "a_k_gain": 1.0 + nrm((nA, A_HEAD_DIM), 0.05),
        "a_rel_bias": nrm((nA, A_HEADS, A_REL_SIZE), 0.5),
        "a_w_out": nrm((nA, A_WIDTH, D), A_WIDTH ** -0.5),
        "b_w_in": nrm((nB, D, SG_FF), D ** -0.5),
        "b_v_gain": 1.0 + nrm((nB, SG_HALF), 0.05),
        "b_w_s": nrm((nB, SG_GROUPS, SG_WINDOW, SG_WINDOW), SG_WINDOW ** -0.5),
        "b_b_s": 1.0 + nrm((nB, SG_GROUPS, SG_WINDOW), 0.05),
        "b_w_out": nrm((nB, SG_HALF, D), SG_HALF ** -0.5),
        "c_w_in": nrm((nC, D, GLA_IN_WIDTH), D ** -0.5),
        "c_w_gate_up": nrm((nC, GLA_GATE_RANK, GLA_KEY_WIDTH), GLA_GATE_RANK ** -0.5),
        "c_b_gate": nrm((nC, GLA_KEY_WIDTH), 0.02),
        "c_o_gain": 1.0 + nrm((nC, GLA_DV), 0.05),
        "c_w_out": nrm((nC, GLA_VALUE_WIDTH, D), GLA_VALUE_WIDTH ** -0.5),
        "d_w_in": nrm((nD, D, 3 * SB_WIDTH), D ** -0.5),
        "d_w_out": nrm((nD, SB_WIDTH, D), SB_WIDTH ** -0.5),
    }


def reference(x, c, ada_w, ada_b, norm_mix, norm_ffn, ffn_w1, ffn_w2,
              a_w_in, a_q_gain, a_k_gain, a_rel_bias, a_w_out,
              b_w_in, b_v_gain, b_w_s, b_b_s, b_w_out,
              c_w_in, c_w_gate_up, c_b_gate, c_o_gain, c_w_out,
              d_w_in, d_w_out):
    cond = jax.nn.silu(c)
    for i in range(DEPTH):
        m, j = i % N_MIXERS, i // N_MIXERS
        mod = cond @ ada_w[i] + ada_b[i]
        sh1, sc1, g1, sh2, sc2, g2 = [t[:, None, :] for t in jnp.split(mod, 6, axis=-1)]
        h = rms_norm(x, norm_mix[i]) * (1.0 + sc1) + sh1
        if m == 0:
            y = chunked_relpos_attention(h, a_w_in[j], a_q_gain[j], a_k_gain[j], a_rel_bias[j], a_w_out[j])
        elif m == 1:
            y = chunked_spatial_gating(h, b_w_in[j], b_v_gain[j], b_w_s[j], b_b_s[j], b_w_out[j])
        elif m == 2:
            y = gated_linear_attention(h, c_w_in[j], c_w_gate_up[j], c_b_gate[j], c_o_gain[j], c_w_out[j])
        else:
            y = stick_breaking_attention(h, d_w_in[j], d_w_out[j])
        x = x + g1 * y
        h = rms_norm(x, norm_ffn[i]) * (1.0 + sc2) + sh2
        x = x + g2 * squared_relu_mlp(h, ffn_w1[i], ffn_w2[i])
    return x
```

```python
import numpy as np
import concourse.bass as bass
import concourse.mybir as mybir
from concourse.bass_utils import run_bass_kernel_spmd
from contextlib import ExitStack, contextmanager

F32 = mybir.dt.float32
BF16 = mybir.dt.bfloat16
AF = mybir.ActivationFunctionType
ALU = mybir.AluOpType
AX = mybir.AxisListType
EPS = 1e-6
DEBUG = False
SERIAL = False
ENGS = ("pe", "act", "dve", "pool", "sp")


class Buf:
    __slots__ = ("name", "w", "r")

    def __init__(self, name):
        self.name = name
        self.w = None
        self.r = []


class Ring:
    def __init__(self, items):
        self.items = items
        self.i = 0

    def next(self):
        it = self.items[self.i % len(self.items)]
        self.i += 1
        return it


class Prog:
    NDMA = 24

    def __init__(self, nc):
        self.nc = nc
        self.es = ExitStack()
        self.scopes = [self.es]
        self.streams = {e: [] for e in ENGS}
        self.esem = {e: self.es.enter_context(nc.semaphore("s_" + e)) for e in ENGS}
        self.ecount = {e: 0 for e in ENGS}
        self.dsem = [self.es.enter_context(nc.semaphore("d%d" % i)) for i in range(self.NDMA)]
        self.dcount = [0] * self.NDMA
        self.dnext = 0
        self.semobj = {}
        for e in ENGS:
            self.semobj[("e", e)] = self.esem[e]
        for i in range(self.NDMA):
            self.semobj[("d", i)] = self.dsem[i]
        self.waited = {e: {} for e in ENGS}
        self.nbuf = 0
        self.uid = 0

    @contextmanager
    def scope(self):
        st = ExitStack()
        self.scopes.append(st)
        try:
            yield
        finally:
            self.barrier()
            self.scopes.pop()
            st.close()

    def sb(self, name, shape, dt):
        self.uid += 1
        return self.scopes[-1].enter_context(self.nc.sbuf_tensor("%s_%d" % (name, self.uid), list(shape), dt))

    def ps(self, name, shape, dt=F32):
        return self.scopes[-1].enter_context(self.nc.psum_tensor(name, list(shape), dt))

    def buf(self, name=None):
        self.nbuf += 1
        return Buf(name or ("b%d" % self.nbuf))

    def bufs(self, n):
        return [self.buf() for _ in range(n)]

    def _deps(self, eng, reads, writes):
        deps = {}

        def add(d):
            if d is None:
                return
            k, v = d
            if deps.get(k, 0) < v:
                deps[k] = v
        for b in reads:
            add(b.w)
        for b in writes:
            add(b.w)
            for d in b.r:
                add(d)
        waits = []
        wd = self.waited[eng]
        for k, v in deps.items():
            if wd.get(k, 0) >= v:
                continue
            if eng == "pe" and k == ("e", "pe"):
                continue
            wd[k] = v
            waits.append((self.semobj[k], v))
        return waits

    def _mark(self, tok, reads, writes):
        for b in writes:
            b.w = tok
            b.r = []
        for b in reads:
            if b not in writes:
                b.r.append(tok)
                if len(b.r) > 64:
                    best = {}
                    for k, v in b.r:
                        if best.get(k, 0) < v:
                            best[k] = v
                    b.r = list(best.items())

    def _serial_waits(self, eng):
        waits = []
        for e2 in ENGS:
            k = ("e", e2); v = self.ecount[e2]
            if v > 0 and self.waited[eng].get(k, 0) < v:
                self.waited[eng][k] = v
                waits.append((self.esem[e2], v))
        for i in range(self.NDMA):
            k = ("d", i); v = self.dcount[i]
            if v > 0 and self.waited[eng].get(k, 0) < v:
                self.waited[eng][k] = v
                waits.append((self.dsem[i], v))
        return waits

    def op(self, eng, fn, reads=(), writes=()):
        waits = self._deps(eng, reads, writes)
        if SERIAL:
            waits = waits + self._serial_waits(eng)
        self.ecount[eng] += 1
        tok = (("e", eng), self.ecount[eng])
        self.streams[eng].append((waits, fn, self.esem[eng], 1))
        self._mark(tok, reads, writes)
        return tok

    def dma(self, out, in_, reads=(), writes=(), eng="sp"):
        i = self.dnext
        self.dnext = (self.dnext + 1) % self.NDMA
        waits = self._deps(eng, reads, writes)
        if SERIAL:
            waits = waits + self._serial_waits(eng)
        k = ("d", i)
        if self.dcount[i] > 0 and self.waited[eng].get(k, 0) < self.dcount[i]:
            self.waited[eng][k] = self.dcount[i]
            waits.append((self.dsem[i], self.dcount[i]))
        self.dcount[i] += 16
        tok = (k, self.dcount[i])
        self.streams[eng].append((waits, lambda e: e.dma_start(out=out, in_=in_), self.dsem[i], 16))
        self._mark(tok, reads, writes)
        return tok

    def collective(self, kind, ins, outs, groups, reads=(), writes=()):
        eng = "pool"
        waits = self._deps(eng, reads, writes)
        sem = self.es.enter_context(self.nc.semaphore("cc%d" % len(self.semobj)))
        k = ("c", len(self.semobj))
        self.semobj[k] = sem
        tok = (k, 1)
        self.streams[eng].append((waits, lambda e: e.collective_compute(kind, ALU.bypass, groups, [a.opt() for a in ins], [a.opt() for a in outs]), sem, 1))
        self._mark(tok, reads, writes)
        return tok

    def dump(self, name, ap, shape, dt, reads):
        if not DEBUG:
            return
        d = self.nc.dram_tensor(name, list(shape), dt, kind="ExternalOutput").ap()
        self.dma(d, ap, reads=reads, writes=[self.buf()])

    def barrier(self):
        for eng in ENGS:
            waits = []
            for k, sem in self.semobj.items():
                if k[0] == "c" and self.waited[eng].get(k, 0) < 1:
                    self.waited[eng][k] = 1
                    waits.append((sem, 1))
            for e2 in ENGS:
                k = ("e", e2)
                v = self.ecount[e2]
                if e2 != eng and v > 0 and self.waited[eng].get(k, 0) < v:
                    self.waited[eng][k] = v
                    waits.append((self.esem[e2], v))
            for i in range(self.NDMA):
                k = ("d", i)
                v = self.dcount[i]
                if v > 0 and self.waited[eng].get(k, 0) < v:
                    self.waited[eng][k] = v
                    waits.append((self.dsem[i], v))
            k = ("e", eng)
            v = self.ecount[eng]
            if v > 0 and self.waited[eng].get(k, 0) < v:
                self.waited[eng][k] = v
                waits.append((self.esem[eng], v))
            if waits:
                self.streams[eng].append((waits, None, None, 0))

    def emit(self):
        nc = self.nc
        self.barrier()
        with nc.Block() as block:
            def runner(name):
                def f(e):
                    for waits, fn, sem, inc in self.streams[name]:
                        for s, v in waits:
                            e.wait_ge(s, v)
                        if fn is not None:
                            fn(e).then_inc(sem, inc)
                return f
            block.tensor(runner("pe"))
            block.scalar(runner("act"))
            block.vector(runner("dve"))
            block.gpsimd(runner("pool"))
            block.sync(runner("sp"))
        self.es.close()


class Ctx:
    pass


def make_ctx(nc):
    P = Prog(nc)
    C = Ctx()
    C.P = P
    C.nc = nc
    C.psums = [P.ps("ps%d" % i, [128, 512]) for i in range(8)]
    C.psum_b = P.bufs(8)
    C.ones_bf = P.sb("ones_bf", [128, 128], BF16)
    C.ones_f = P.sb("ones_f", [128, 128], F32)
    C.eps = P.sb("eps_t", [128, 1], F32)
    C.cb = P.buf()

    def mk(e):
        e.memset(C.eps[:], EPS)
        e.memset(C.ones_f[:], 1.0)
        return e.memset(C.ones_bf[:], 1.0)
    P.op("pool", mk, writes=[C.cb])
    C.stage = [P.sb("stage0", [128, 4096], F32)]
    C.stage_b = P.bufs(1)
    C.sti = 0
    return C


def next_stage(C):
    i = C.sti % len(C.stage)
    C.sti += 1
    return C.stage[i], C.stage_b[i]


def load_cast(C, dst_ap, dst_buf, src_ap, shape3):
    P = C.P
    st, sb_ = next_stage(C)
    n = 1
    for s in shape3:
        n *= s
    if len(shape3) == 2:
        stv = st[:, 0:n].rearrange("p (a n) -> p a n", a=shape3[0])
    else:
        stv = st[:, 0:n]
    P.dma(stv, src_ap, writes=[sb_])
    eng = "dve" if C.sti % 2 == 0 else "pool"
    P.op(eng, lambda e: e.tensor_copy(out=dst_ap, in_=stv), reads=[sb_], writes=[dst_buf])


def emit_ada(C, cT_d, ada_w_d, ada_b_d, gains_d):
    P = C.P
    psum, psum_b = C.psums[0], C.psum_b[0]
    ct = P.sb("ct", [128, 8], F32); cond = P.sb("cond", [128, 8], F32)
    adab = P.sb("adab", [128, 48], F32); gn = P.sb("gn", [128, 16], F32)
    mod = P.sb("mod", [128, 48], F32)
    out = P.sb("AB", [128, 48], F32)
    b_ct, b_cond, b_adab, b_gn, b_mod, b_out = P.bufs(6)
    P.dma(ct[:], cT_d, writes=[b_ct])
    P.dma(adab[:], ada_b_d, writes=[b_adab])
    P.dma(gn[:], gains_d, writes=[b_gn])
    P.op("act", lambda e: e.activation(out=cond[:], in_=ct[:], func=AF.Silu), reads=[b_ct], writes=[b_cond])
    wv = ada_w_d.rearrange("(kc p) f -> p kc f", p=128)
    for g in range(12):
        st, sb_ = next_stage(C)
        stv = st[:].rearrange("p (kc f) -> p kc f", kc=8)
        P.dma(stv, wv[:, :, g * 512:(g + 1) * 512], writes=[sb_])
        for j in range(4):
            col = g * 4 + j
            for kc in range(8):
                P.op("pe", (lambda e, col=col, kc=kc, j=j, stv=stv: e.matmul(
                    psum[:, col:col + 1], lhsT=stv[:, kc, j * 128:(j + 1) * 128],
                    rhs=cond[:, kc:kc + 1], start=(kc == 0), stop=(kc == 7))),
                    reads=[sb_, b_cond], writes=[psum_b])
    P.op("dve", lambda e: e.tensor_tensor(out=mod[:], in0=psum[:, 0:48], in1=adab[:], op=ALU.add),
         reads=[psum_b, b_adab], writes=[b_mod])

    def mk(e):
        e.scalar_tensor_tensor(out=out[:, 0:8], in0=mod[:, 8:16], scalar=1.0, in1=gn[:, 0:8], op0=ALU.add, op1=ALU.mult)
        e.scalar_tensor_tensor(out=out[:, 24:32], in0=mod[:, 32:40], scalar=1.0, in1=gn[:, 8:16], op0=ALU.add, op1=ALU.mult)
        e.tensor_copy(out=out[:, 8:16], in_=mod[:, 0:8])
        e.tensor_copy(out=out[:, 16:24], in_=mod[:, 16:24])
        e.tensor_copy(out=out[:, 32:40], in_=mod[:, 24:32])
        return e.tensor_copy(out=out[:, 40:48], in_=mod[:, 40:48])
    P.op("dve", mk, reads=[b_mod, b_gn], writes=[b_out])
    C.AB, C.b_ab = out, b_out


class NormBufs:
    def __init__(self, C, T):
        P = C.P
        self.T = T
        self.sq = P.sb("sq", [128, 8, T], BF16); self.b_sq = P.buf()
        self.rstd = P.sb("rstd", [128, T], F32); self.b_rstd = P.buf()
        self.tmp = P.sb("tmp", [128, 8, T], F32); self.b_tmp = P.bufs(8)
        self.hT = P.sb("hT", [128, 8, T], BF16); self.b_hT = P.buf()


def emit_norm_mod(C, N, xt, b_xt, acol, bcol):
    P = C.P
    T = N.T
    ps, b_ps = C.psums[0], C.psum_b[0]
    A = C.AB[:, acol:acol + 8]
    Bc = C.AB[:, bcol:bcol + 8]
    P.op("act", lambda e: e.activation(out=N.sq[:], in_=xt[:], func=AF.Square), reads=[b_xt], writes=[N.b_sq])
    for kc in range(8):
        P.op("pe", lambda e, kc=kc: e.matmul(ps[:, :T], lhsT=C.ones_bf[:], rhs=N.sq[:, kc, :], start=(kc == 0), stop=(kc == 7)),
             reads=[N.b_sq, C.cb], writes=[b_ps])
    P.op("act", lambda e: e.activation(out=N.rstd[:], in_=ps[:, :T], func=AF.Sqrt, bias=C.eps[:, 0:1], scale=1.0 / 1024.0),
         reads=[b_ps, C.cb], writes=[N.b_rstd])
    P.op("dve", lambda e: e.reciprocal(out=N.rstd[:], in_=N.rstd[:]), reads=[N.b_rstd], writes=[N.b_rstd])
    for kc in range(8):
        eng = "dve" if kc % 2 == 0 else "pool"
        P.op(eng, lambda e, kc=kc: e.tensor_tensor(out=N.tmp[:, kc, :], in0=xt[:, kc, :], in1=N.rstd[:], op=ALU.mult),
             reads=[b_xt, N.b_rstd], writes=[N.b_tmp[kc]])
        P.op("act", lambda e, kc=kc: e.activation(out=N.hT[:, kc, :], in_=N.tmp[:, kc, :], func=AF.Identity,
                                                   bias=Bc[:, kc:kc + 1], scale=A[:, kc:kc + 1]),
             reads=[N.b_tmp[kc], C.b_ab], writes=[N.b_hT])


def emit_ffn(C, xT_d, yT_d, w1_d, w2_d, T_core, T=256, ydst=None):
    P = C.P
    AB = C.AB
    with P.scope():
        w1b = P.sb("w1b", [128, 8, 4096], BF16); w2b = P.sb("w2b", [128, 32, 1024], BF16)
        b_w1 = P.bufs(8); b_w2 = P.bufs(8)
        for kc in range(8):
            load_cast(C, w1b[:, kc, :], b_w1[kc], w1_d[kc * 128:(kc + 1) * 128, :], [4096])
        w2v = w2_d.rearrange("(hc p) n -> p hc n", p=128)
        for g in range(8):
            load_cast(C, w2b[:, g * 4:(g + 1) * 4, :], b_w2[g], w2v[:, g * 4:(g + 1) * 4, :], [4, 1024])
        xts = [P.sb("xt%d" % i, [128, 8, T], F32) for i in range(2)]
        b_xts = P.bufs(2)
        N = NormBufs(C, T)
        h1T = P.sb("h1T", [128, 32, T], BF16); b_h1 = P.bufs(32)
        rl = [P.sb("rl%d" % i, [128, T], F32) for i in range(2)]; b_rl = P.bufs(2)
        xv = xT_d.rearrange("(kc p) t -> p kc t", p=128)
        if ydst is None:
            yv = yT_d.rearrange("(kc p) t -> p kc t", p=128)
            ydst = lambda t0, T: yv[:, :, t0:t0 + T]
        b_out = P.buf()
        hring = Ring(list(zip(C.psums[1:5], C.psum_b[1:5])))
        yring = Ring(list(zip(C.psums[5:8], C.psum_b[5:8])))
        for ti in range(T_core // T):
            xt, b_xt = xts[ti % 2], b_xts[ti % 2]
            t0 = ti * T
            P.dma(xt[:], xv[:, :, t0:t0 + T], writes=[b_xt])
            emit_norm_mod(C, N, xt, b_xt, 24, 32)
            for hc in range(32):
                ps, bps = hring.next()
                for kc in range(8):
                    P.op("pe", lambda e, kc=kc, hc=hc, ps=ps: e.matmul(
                        ps[:, :T], lhsT=w1b[:, kc, hc * 128:(hc + 1) * 128], rhs=N.hT[:, kc, :],
                        start=(kc == 0), stop=(kc == 7)), reads=[b_w1[kc], N.b_hT], writes=[bps])
                r, br = rl[hc % 2], b_rl[hc % 2]
                P.op("act", lambda e, ps=ps, r=r: e.activation(out=r[:], in_=ps[:, :T], func=AF.Relu), reads=[bps], writes=[br])
                P.op("pool", lambda e, r=r, hc=hc: e.tensor_tensor(out=h1T[:, hc, :], in0=r[:], in1=r[:], op=ALU.mult),
                     reads=[br], writes=[b_h1[hc]])
            for oc in range(8):
                ps, bps = yring.next()
                for hc in range(32):
                    P.op("pe", lambda e, oc=oc, hc=hc, ps=ps: e.matmul(
                        ps[:, :T], lhsT=w2b[:, hc, oc * 128:(oc + 1) * 128], rhs=h1T[:, hc, :],
                        start=(hc == 0), stop=(hc == 31)), reads=[b_w2[hc // 4], b_h1[hc]], writes=[bps])
                P.op("dve", lambda e, oc=oc, ps=ps, xt=xt: e.scalar_tensor_tensor(
                    out=xt[:, oc, :], in0=ps[:, :T], scalar=AB[:, 40 + oc:41 + oc], in1=xt[:, oc, :],
                    op0=ALU.mult, op1=ALU.add), reads=[bps, C.b_ab, b_xt], writes=[b_xt])
            P.dma(ydst(t0, T), xt[:], reads=[b_xt], writes=[b_out])


def emit_mixB(C, xT_d, xo_d, w_in_d, vgain_d, wsT_d, bs_d, wout_d, vm_d, T_core, T=256):
    P = C.P
    xv = xT_d.rearrange("(kc p) t -> p kc t", p=128)
    xov = xo_d.rearrange("(kc p) t -> p kc t", p=128)
    vmv = vm_d.rearrange("fc p t -> p fc t")
    winv = w_in_d.rearrange("(kc p) f -> p kc f", p=128)
    with P.scope():
        wvb = P.sb("wvb", [128, 8, 3072], BF16); b_wv = P.bufs(8)
        for kc in range(8):
            load_cast(C, wvb[:, kc, :], b_wv[kc], winv[:, kc, 3072:6144], [3072])
        gainB = P.sb("gainB", [128, 3072], F32); b_gain = P.buf()
        P.dma(gainB[:], vgain_d, writes=[b_gain])
        wsT = P.sb("wsT", [128, 8, 128], BF16); b_ws = P.buf()
        load_cast(C, wsT[:], b_ws, wsT_d, [8, 128])
        P.op("pool", lambda e: e.memset(wsT[64:128, :, 0:64], 0.0), writes=[b_ws])
        bs = P.sb("bs", [128, 24, 128], F32); b_bs = P.buf()
        P.dma(bs[:], bs_d, writes=[b_bs])
        xts = [P.sb("xt%d" % i, [128, 8, T], F32) for i in range(2)]; b_xts = P.bufs(2)
        N = NormBufs(C, T)
        vg = P.sb("vg", [128, 3072], F32); b_vg = P.bufs(6)
        sqv = P.sb("sqv", [128, 3072], F32); b_sqv = P.buf()
        ssv = P.sb("ssv", [128, 1], F32); b_ssv = P.buf()
        vn = P.sb("vn", [128, 3072], BF16); b_vn = P.buf()
        vmT = [P.sb("vmT%d" % i, [128, 24, 128], BF16) for i in range(2)]; b_vmT = P.bufs(2)
        pring = Ring(list(zip(C.psums[1:5], C.psum_b[1:5])))
        mring = Ring(list(zip(C.psums[5:8], C.psum_b[5:8])))
        b_vmd = P.buf()
        wi = 0
        for ti in range(T_core // T):
            xt, b_xt = xts[ti % 2], b_xts[ti % 2]
            t0 = ti * T
            P.dma(xt[:], xv[:, :, t0:t0 + T], writes=[b_xt])
            emit_norm_mod(C, N, xt, b_xt, 0, 8)
            for w in range(T // 128):
                for cc in range(6):
                    ps, bps = pring.next()
                    for kc in range(8):
                        P.op("pe", lambda e, kc=kc, cc=cc, ps=ps, w=w: e.matmul(
                            ps[:, :], lhsT=N.hT[:, kc, w * 128:(w + 1) * 128], rhs=wvb[:, kc, cc * 512:(cc + 1) * 512],
                            start=(kc == 0), stop=(kc == 7)), reads=[b_wv[kc], N.b_hT], writes=[bps])
                    if DEBUG and ti == 0 and w == 0 and cc == 0:
                        dbgp = P.sb("dbgp", [128, 512], F32); b_dbgp = P.buf()
                        P.op("act", lambda e, ps=ps: e.activation(out=dbgp[:], in_=ps[:, :], func=AF.Identity), reads=[bps], writes=[b_dbgp])
                        P.dump("dbg_ps", dbgp[:], [128, 512], F32, [b_dbgp])
                        dbgp2 = P.sb("dbgp2", [128, 512], F32); b_dbgp2 = P.buf()
                        P.op("dve", lambda e, ps=ps: e.tensor_copy(out=dbgp2[:], in_=ps[:, :]), reads=[bps], writes=[b_dbgp2])
                        P.dump("dbg_ps2", dbgp2[:], [128, 512], F32, [b_dbgp2])
                        P.dump("dbg_wvb", wvb[:, :, 0:512], [128, 8, 512], BF16, b_wv)
                    P.op("act", lambda e, ps=ps, cc=cc: e.activation(out=vg[:, cc * 512:(cc + 1) * 512], in_=ps[:, :], func=AF.Gelu),
                         reads=[bps], writes=[b_vg[cc]])
                P.op("dve", lambda e: e.tensor_tensor(out=sqv[:], in0=vg[:], in1=vg[:], op=ALU.mult), reads=b_vg, writes=[b_sqv])
                P.op("dve", lambda e: e.reduce_sum(out=ssv[:], in_=sqv[:], axis=AX.X), reads=[b_sqv], writes=[b_ssv])
                P.op("act", lambda e: e.activation(out=ssv[:], in_=ssv[:], func=AF.Sqrt, bias=C.eps[:, 0:1], scale=1.0 / 3072.0),
                     reads=[b_ssv, C.cb], writes=[b_ssv])
                P.op("dve", lambda e: e.reciprocal(out=ssv[:], in_=ssv[:]), reads=[b_ssv], writes=[b_ssv])
                P.op("dve", lambda e: e.scalar_tensor_tensor(out=vn[:], in0=vg[:], scalar=ssv[:, 0:1], in1=gainB[:],
                                                               op0=ALU.mult, op1=ALU.mult),
                     reads=b_vg + [b_ssv, b_gain], writes=[b_vn])
                if ti == 0:
                    P.dump("dbg_vg%d" % w, vg[:], [128, 3072], F32, b_vg)
                    P.dump("dbg_ssv%d" % w, ssv[:], [128, 1], F32, [b_ssv])
                    P.dump("dbg_vn%d" % w, vn[:], [128, 3072], BF16, [b_vn])
                    if w == 0:
                        P.dump("dbg_hT", N.hT[:], [128, 8, T], BF16, [N.b_hT])
                vm, bvm = vmT[wi % 2], b_vmT[wi % 2]
                for q in range(6):
                    ps, bps = mring.next()
                    for j in range(4):
                        fc = q * 4 + j
                        g = fc // 3
                        P.op("pe", lambda e, ps=ps, j=j, fc=fc, g=g: e.matmul(
                            ps[:, j * 128:(j + 1) * 128], lhsT=vn[:, fc * 128:(fc + 1) * 128], rhs=wsT[:, g, :],
                            start=True, stop=True), reads=[b_vn, b_ws], writes=[bps])
                    P.op("dve", lambda e, ps=ps, q=q, vm=vm: e.tensor_tensor(
                        out=vm[:, q * 4:(q + 1) * 4, :], in0=ps[:, :].rearrange("p (a n) -> p a n", a=4),
                        in1=bs[:, q * 4:(q + 1) * 4, :], op=ALU.add), reads=[bps, b_bs], writes=[bvm])
                tw = t0 + w * 128
                P.dma(vmv[:, :, tw:tw + 128], vm[:], reads=[bvm], writes=[b_vmd])
                wi += 1
    emit_mixB2(C, xv, xov, vmv, winv, wout_d, T_core, T)


def emit_mixB2(C, xv, xov, vmv, winv, wout_d, T_core, T):
    P = C.P
    AB = C.AB
    with P.scope():
        wub = P.sb("wub", [128, 8, 3072], BF16); b_wu = P.bufs(8)
        for kc in range(8):
            load_cast(C, wub[:, kc, :], b_wu[kc], winv[:, kc, 0:3072], [3072])
        wob = P.sb("wob", [128, 24, 1024], BF16); b_wo = P.bufs(6)
        wov = wout_d.rearrange("(fc p) n -> p fc n", p=128)
        for g in range(6):
            load_cast(C, wob[:, g * 4:(g + 1) * 4, :], b_wo[g], wov[:, g * 4:(g + 1) * 4, :], [4, 1024])
        xts = [P.sb("xt%d" % i, [128, 8, T], F32) for i in range(2)]; b_xts = P.bufs(2)
        N = NormBufs(C, T)
        vmt = P.sb("vmt", [128, 24, T], BF16); b_vmt = P.buf()
        gT = P.sb("gT", [128, 24, T], BF16); b_gT = P.bufs(24)
        ut = [P.sb("ut%d" % i, [128, T], F32) for i in range(2)]; b_ut = P.bufs(2)
        uring = Ring(list(zip(C.psums[1:5], C.psum_b[1:5])))
        yring = Ring(list(zip(C.psums[5:8], C.psum_b[5:8])))
        b_xo = P.buf()
        for ti in range(T_core // T):
            xt, b_xt = xts[ti % 2], b_xts[ti % 2]
            t0 = ti * T
            P.dma(xt[:], xv[:, :, t0:t0 + T], writes=[b_xt])
            P.dma(vmt[:], vmv[:, :, t0:t0 + T], writes=[b_vmt])
            emit_norm_mod(C, N, xt, b_xt, 0, 8)
            for fc in range(24):
                ps, bps = uring.next()
                for kc in range(8):
                    P.op("pe", lambda e, kc=kc, fc=fc, ps=ps: e.matmul(
                        ps[:, :T], lhsT=wub[:, kc, fc * 128:(fc + 1) * 128], rhs=N.hT[:, kc, :],
                        start=(kc == 0), stop=(kc == 7)), reads=[b_wu[kc], N.b_hT], writes=[bps])
                u, bu = ut[fc % 2], b_ut[fc % 2]
                P.op("act", lambda e, ps=ps, u=u: e.activation(out=u[:], in_=ps[:, :T], func=AF.Gelu), reads=[bps], writes=[bu])
                eng = "pool" if fc % 2 == 0 else "dve"
                P.op(eng, lambda e, u=u, fc=fc: e.tensor_tensor(out=gT[:, fc, :], in0=u[:], in1=vmt[:, fc, :], op=ALU.mult),
                     reads=[bu, b_vmt], writes=[b_gT[fc]])
            for oc in range(8):
                ps, bps = yring.next()
                for fc in range(24):
                    P.op("pe", lambda e, oc=oc, fc=fc, ps=ps: e.matmul(
                        ps[:, :T], lhsT=wob[:, fc, oc * 128:(oc + 1) * 128], rhs=gT[:, fc, :],
                        start=(fc == 0), stop=(fc == 23)), reads=[b_wo[fc // 4], b_gT[fc]], writes=[bps])
                P.op("dve", lambda e, oc=oc, ps=ps, xt=xt: e.scalar_tensor_tensor(
                    out=xt[:, oc, :], in0=ps[:, :T], scalar=AB[:, 16 + oc:17 + oc], in1=xt[:, oc, :],
                    op0=ALU.mult, op1=ALU.add), reads=[bps, C.b_ab, b_xt], writes=[b_xt])
            P.dma(xov[:, :, t0:t0 + T], xt[:], reads=[b_xt], writes=[b_xo])


def _lay_common(d, L, b):
    return {
        "cT": np.ascontiguousarray(d["c"][b].reshape(8, 128).T),
        "ada_w": np.ascontiguousarray(d["ada_w"][L]),
        "ada_b": np.ascontiguousarray(d["ada_b"][L].reshape(48, 128).T),
        "gains": np.ascontiguousarray(np.concatenate([d["norm_mix"][L].reshape(8, 128), d["norm_ffn"][L].reshape(8, 128)], 0).T),
    }


def _decl_common(nc):
    cT_d = nc.dram_tensor("cT", [128, 8], F32, kind="ExternalInput").ap()
    adaw_d = nc.dram_tensor("ada_w", [1024, 6144], F32, kind="ExternalInput").ap()
    adab_d = nc.dram_tensor("ada_b", [128, 48], F32, kind="ExternalInput").ap()
    gains_d = nc.dram_tensor("gains", [128, 16], F32, kind="ExternalInput").ap()
    return cT_d, adaw_d, adab_d, gains_d


def build_L1(T_core=4096):
    nc = bass.Bass("TRN2", target_bir_lowering=False)
    xT_d = nc.dram_tensor("xT", [1024, T_core], F32, kind="ExternalInput").ap()
    cm = _decl_common(nc)
    w_in_d = nc.dram_tensor("b_w_in", [1024, 6144], F32, kind="ExternalInput").ap()
    vgain_d = nc.dram_tensor("b_vgain", [128, 3072], F32, kind="ExternalInput").ap()
    wsT_d = nc.dram_tensor("b_wsT", [128, 8, 128], F32, kind="ExternalInput").ap()
    bs_d = nc.dram_tensor("b_bs", [128, 24, 128], F32, kind="ExternalInput").ap()
    wout_d = nc.dram_tensor("b_w_out", [3072, 1024], F32, kind="ExternalInput").ap()
    w1_d = nc.dram_tensor("w1", [1024, 4096], F32, kind="ExternalInput").ap()
    w2_d = nc.dram_tensor("w2", [4096, 1024], F32, kind="ExternalInput").ap()
    yT_d = nc.dram_tensor("yT", [1024, T_core], F32, kind="ExternalOutput").ap()
    vm_d = nc.dram_tensor("vm_s", [24, 128, T_core], BF16, kind="ExternalOutput" if DEBUG else "Internal").ap()
    xs_d = nc.dram_tensor("xs_s", [1024, T_core], F32).ap()
    C = make_ctx(nc)
    emit_ada(C, *cm)
    emit_mixB(C, xT_d, xs_d, w_in_d, vgain_d, wsT_d, bs_d, wout_d, vm_d, T_core)
    emit_ffn(C, xs_d, yT_d, w1_d, w2_d, T_core)
    C.P.emit()
    return nc


def lay_L1(d, b):
    m = _lay_common(d, 1, b)
    m.update({
        "b_w_in": np.ascontiguousarray(d["b_w_in"][0]),
        "b_vgain": np.ascontiguousarray(np.broadcast_to(d["b_v_gain"][0][None, :], (128, 3072))),
        "b_wsT": np.ascontiguousarray(d["b_w_s"][0].transpose(2, 0, 1)),
        "b_bs": np.ascontiguousarray(np.broadcast_to(np.repeat(d["b_b_s"][0], 3, axis=0)[None], (128, 24, 128))),
        "b_w_out": np.ascontiguousarray(d["b_w_out"][0]),
        "w1": np.ascontiguousarray(d["ffn_w1"][1]), "w2": np.ascontiguousarray(d["ffn_w2"][1]),
    })
    return m


HALO = 512


def emit_mixA1(C, xT_d, w_in_d, gqk_d, qT_s, kT_s, v_s, T_core, T=256):
    P = C.P
    TT = T_core + HALO
    xv = xT_d.rearrange("(kc p) t -> p kc t", p=128)
    winv = w_in_d.rearrange("(kc p) f -> p kc f", p=128)
    qv = qT_s.rearrange("(oc p) t -> p oc t", p=128)
    kv = kT_s.rearrange("(oc p) t -> p oc t", p=128)
    with P.scope():
        wq = P.sb("wq", [128, 8, 3072], BF16); b_wq = P.bufs(8)
        for kc in range(8):
            load_cast(C, wq[:, kc, :], b_wq[kc], winv[:, kc, :], [3072])
        gqk = P.sb("gqk", [128, 2], F32); b_g = P.buf()
        P.dma(gqk[:], gqk_d, writes=[b_g])
        P.op("dve", lambda e: e.tensor_scalar_mul(out=gqk[:, 0:1], in0=gqk[:, 0:1], scalar1=0.125), reads=[b_g], writes=[b_g])
        bd = P.sb("bd", [128, 128], BF16); b_bd = P.buf()

        P.op("pool", lambda e: e.memset(bd[:], 0.0), writes=[b_bd])
        P.op("pool", lambda e: e.memset(bd[0:64, 0:64], 1.0 / 64.0), writes=[b_bd])
        P.op("pool", lambda e: e.memset(bd[64:128, 64:128], 1.0 / 64.0), writes=[b_bd])
        xts = [P.sb("xt%d" % i, [128, 8, T], F32) for i in range(2)]; b_xts = P.bufs(2)
        N = NormBufs(C, T)
        sqq = P.sb("sqq", [128, T], BF16); b_sqq = P.buf()
        rs = P.sb("rs", [128, T], F32); b_rs = P.buf()
        qk_o = [P.sb("qko%d" % i, [128, 8, T], BF16) for i in range(2)]; b_qko = P.bufs(2)
        vpad = [P.sb("vpad%d" % i, [128, 16, 128], BF16) for i in range(2)]; b_vpad = P.bufs(2)
        for i in range(2):
            P.op("pool", lambda e, i=i: e.memset(vpad[i][:], 0.0), writes=[b_vpad[i]])
        b_sc = P.buf()
        pring = Ring(list(zip(C.psums[1:4], C.psum_b[1:4])))
        sring = Ring(list(zip(C.psums[4:6], C.psum_b[4:6])))
        vring = Ring(list(zip(C.psums[6:8], C.psum_b[6:8])))
        vi = 0
        for ti in range(TT // T):
            xt, b_xt = xts[ti % 2], b_xts[ti % 2]
            t0 = ti * T
            P.dma(xt[:], xv[:, :, t0:t0 + T], writes=[b_xt])
            emit_norm_mod(C, N, xt, b_xt, 0, 8)
            for which in range(2):
                if which == 0 and t0 + T <= HALO:
                    continue
                qo, bqo = qk_o[which], b_qko[which]
                for oc in range(8):
                    ps, bps = pring.next()
                    col0 = which * 1024 + oc * 128
                    for kc in range(8):
                        P.op("pe", lambda e, kc=kc, ps=ps, col0=col0: e.matmul(
                            ps[:, :T], lhsT=wq[:, kc, col0:col0 + 128], rhs=N.hT[:, kc, :], start=(kc == 0), stop=(kc == 7)),
                            reads=[b_wq[kc], N.b_hT], writes=[bps])
                    P.op("act", lambda e, ps=ps: e.activation(out=sqq[:], in_=ps[:, :T], func=AF.Square), reads=[bps], writes=[b_sqq])
                    ps2, bps2 = sring.next()
                    P.op("pe", lambda e, ps2=ps2: e.matmul(ps2[:, :T], lhsT=bd[:], rhs=sqq[:], start=True, stop=True),
                         reads=[b_sqq, b_bd], writes=[bps2])
                    P.op("act", lambda e, ps2=ps2: e.activation(out=rs[:], in_=ps2[:, :T], func=AF.Sqrt, bias=C.eps[:, 0:1], scale=1.0),
                         reads=[bps2, C.cb], writes=[b_rs])
                    P.op("dve", lambda e: e.reciprocal(out=rs[:], in_=rs[:]), reads=[b_rs], writes=[b_rs])
                    P.op("dve", lambda e, ps=ps, oc=oc, qo=qo, which=which: e.scalar_tensor_tensor(
                        out=qo[:, oc, :], in0=ps[:, :T], scalar=gqk[:, which:which + 1], in1=rs[:], op0=ALU.mult, op1=ALU.mult),
                        reads=[bps, b_rs, b_g], writes=[bqo])
                dst = qv if which == 0 else kv
                if which == 0:
                    P.dma(dst[:, :, t0 - HALO:t0 - HALO + T], qo[:], reads=[bqo], writes=[b_sc])
                else:
                    P.dma(dst[:, :, t0:t0 + T], qo[:], reads=[bqo], writes=[b_sc])
            for w in range(T // 128):
                vp, bvp = vpad[vi % 2], b_vpad[vi % 2]
                vi += 1
                for half in range(2):
                    ps, bps = vring.next()
                    for kc in range(8):
                        P.op("pe", lambda e, kc=kc, ps=ps, w=w, half=half: e.matmul(
                            ps[:, :], lhsT=N.hT[:, kc, w * 128:(w + 1) * 128], rhs=wq[:, kc, 2048 + half * 512:2048 + (half + 1) * 512],
                            start=(kc == 0), stop=(kc == 7)), reads=[b_wq[kc], N.b_hT], writes=[bps])
                    psv = ps[:, :].rearrange("p (hp two d) -> p hp two d", hp=4, two=2)
                    vpv = vp[:, half * 8:(half + 1) * 8, :].rearrange("p (hp two) c -> p hp two c", two=2)
                    P.op("act", lambda e, psv=psv, vpv=vpv: e.activation(out=vpv[:, :, 0, 0:64], in_=psv[:, :, 0, :], func=AF.Identity),
                         reads=[bps], writes=[bvp])
                    P.op("dve", lambda e, psv=psv, vpv=vpv: e.tensor_copy(out=vpv[:, :, 1, 64:128], in_=psv[:, :, 1, :]),
                         reads=[bps], writes=[bvp])
                tw = t0 + w * 128
                P.dma(v_s[tw:tw + 128, :].rearrange("t (h c) -> t h c", h=16), vp[:], reads=[bvp], writes=[b_sc])


def emit_mixA2(C, xT_d, xo_d, qT_s, kT_s, v_s, biasT_d, valid_d, wout_d, T_core):
    P = C.P
    AB = C.AB
    xv = xT_d.rearrange("(kc p) t -> p kc t", p=128)
    xov = xo_d.rearrange("(kc p) t -> p kc t", p=128)
    qv = qT_s.rearrange("(oc p) t -> p oc t", p=128)
    kv = kT_s.rearrange("(oc p) t -> p oc t", p=128)
    with P.scope():
        wob = P.sb("wob", [128, 8, 1024], BF16); b_wo = P.bufs(2)
        wov = wout_d.rearrange("(fc p) n -> p fc n", p=128)
        for g in range(2):
            load_cast(C, wob[:, g * 4:(g + 1) * 4, :], b_wo[g], wov[:, g * 4:(g + 1) * 4, :], [4, 1024])
        ebias = P.sb("ebias", [128, 16, 640], F32); b_eb = P.buf()
        for g in range(4):
            P.dma(ebias[:, g * 4:(g + 1) * 4, :], biasT_d[:, g * 4:(g + 1) * 4, :], writes=[b_eb])
        P.op("act", lambda e: e.activation(out=ebias[:], in_=ebias[:], func=AF.Exp), reads=[b_eb], writes=[b_eb])
        nkb = (T_core + HALO) // 128
        valid = P.sb("valid", [128, nkb], F32); b_val = P.buf()
        P.dma(valid[:], valid_d, writes=[b_val])
        selA = P.sb("selA", [128, 128], BF16); selB = P.sb("selB", [128, 128], BF16); b_sel = P.buf()

        P.op("pool", lambda e: e.memset(selA[:], 0.0), writes=[b_sel])
        P.op("pool", lambda e: e.memset(selB[:], 0.0), writes=[b_sel])
        P.op("pool", lambda e: e.memset(selA[:, 0:64], 1.0), writes=[b_sel])
        P.op("pool", lambda e: e.memset(selB[:, 64:128], 1.0), writes=[b_sel])
        NB = 2
        qt = [P.sb("qt%d" % i, [128, 8, 128], BF16) for i in range(NB)]; b_qt = P.bufs(NB)
        kt = [P.sb("kt%d" % i, [128, 8, 640], BF16) for i in range(NB)]; b_kt = P.bufs(NB)
        vt = [P.sb("vt%d" % i, [128, 5, 2048], BF16) for i in range(NB)]; b_vt = P.bufs(NB)
        xts = [P.sb("xa%d" % i, [128, 8, 128], F32) for i in range(NB)]; b_xts = P.bufs(NB)
        et = [P.sb("et%d" % i, [128, 640], F32) for i in range(2)]; b_et = P.bufs(2)
        pt = [P.sb("pt%d" % i, [128, 640], BF16) for i in range(2)]; b_pt = P.bufs(2)
        oT = P.sb("oT", [128, 8, 128], BF16); b_oT = P.bufs(8)
        den = P.sb("den", [128, 128], F32); b_den = P.buf()
        b_xo = P.buf()
        sc = [(C.psums[1], C.psum_b[1], C.psums[2], C.psum_b[2]), (C.psums[3], C.psum_b[3], C.psums[4], C.psum_b[4])]
        po, b_po = C.psums[5], C.psum_b[5]
        pd, b_pd = C.psums[6], C.psum_b[6]
        py, b_py = C.psums[7], C.psum_b[7]
        hi = 0
        for m in range(T_core // 128):
            i = m % NB
            q0 = m * 128
            k0 = m * 128
            P.dma(qt[i][:], qv[:, :, q0:q0 + 128], writes=[b_qt[i]])
            P.dma(kt[i][:], kv[:, :, k0:k0 + 640], writes=[b_kt[i]])
            P.dma(vt[i][:], v_s[k0:k0 + 640, :].rearrange("(j p) c -> p j c", p=128), writes=[b_vt[i]])
            P.dma(xts[i][:], xv[:, :, HALO + q0:HALO + q0 + 128], writes=[b_xts[i]])
            kb0 = k0 // 128
            for pr in range(8):
                for ab in range(2):
                    h = pr * 2 + ab
                    sA, bsA, sB, bsB = sc[hi % 2]
                    e_, be = et[hi % 2], b_et[hi % 2]
                    p_, bp = pt[hi % 2], b_pt[hi % 2]
                    hi += 1
                    lo = ab * 64
                    for j in range(5):
                        dst, bd_ = (sA, bsA) if j < 4 else (sB, bsB)
                        c0 = (j % 4) * 128
                        P.op("pe", lambda e, j=j, dst=dst, c0=c0, pr=pr, lo=lo, i=i: e.matmul(
                            dst[:, c0:c0 + 128], lhsT=kt[i][lo:lo + 64, pr, j * 128:(j + 1) * 128], rhs=qt[i][lo:lo + 64, pr, :],
                            start=True, stop=True), reads=[b_kt[i], b_qt[i]], writes=[bd_])
                    P.op("act", lambda e, sA=sA, e_=e_: e.activation(out=e_[:, 0:512], in_=sA[:, :], func=AF.Exp), reads=[bsA], writes=[be])
                    P.op("act", lambda e, sB=sB, e_=e_: e.activation(out=e_[:, 512:640], in_=sB[:, 0:128], func=AF.Exp), reads=[bsB], writes=[be])
                    for j in range(5):
                        P.op("dve", lambda e, j=j, e_=e_, p_=p_, h=h, kb0=kb0: e.scalar_tensor_tensor(
                            out=p_[:, j * 128:(j + 1) * 128], in0=e_[:, j * 128:(j + 1) * 128], scalar=valid[:, kb0 + j:kb0 + j + 1],
                            in1=ebias[:, h, j * 128:(j + 1) * 128], op0=ALU.mult, op1=ALU.mult),
                            reads=[be, b_val, b_eb], writes=[bp])
                    sel = selA if ab == 0 else selB
                    for j in range(5):
                        first = (ab == 0 and j == 0)
                        last = (ab == 1 and j == 4)
                        P.op("pe", lambda e, j=j, p_=p_, h=h, i=i, first=first, last=last: e.matmul(
                            po[:, 0:128], lhsT=vt[i][:, j, h * 128:(h + 1) * 128], rhs=p_[:, j * 128:(j + 1) * 128],
                            start=first, stop=last), reads=[b_vt[i], bp], writes=[b_po])
                        P.op("pe", lambda e, j=j, p_=p_, sel=sel, first=first, last=last: e.matmul(
                            pd[:, 0:128], lhsT=sel[:], rhs=p_[:, j * 128:(j + 1) * 128],
                            start=first, stop=last), reads=[b_sel, bp], writes=[b_pd])
                P.op("dve", lambda e: e.reciprocal(out=den[:], in_=pd[:, 0:128]), reads=[b_pd], writes=[b_den])
                P.op("dve", lambda e, pr=pr: e.tensor_tensor(out=oT[:, pr, :], in0=po[:, 0:128], in1=den[:], op=ALU.mult),
                     reads=[b_po, b_den], writes=[b_oT[pr]])
            xt, b_xt = xts[i], b_xts[i]
            for oc in range(8):
                for pr in range(8):
                    P.op("pe", lambda e, oc=oc, pr=pr: e.matmul(
                        py[:, oc * 128 % 512:oc * 128 % 512 + 128] if False else py[:, (oc % 4) * 128:(oc % 4) * 128 + 128],
                        lhsT=wob[:, pr, oc * 128:(oc + 1) * 128], rhs=oT[:, pr, :],
                        start=(pr == 0), stop=(pr == 7)), reads=[b_wo[pr // 4], b_oT[pr]], writes=[b_py])
                P.op("dve", lambda e, oc=oc, xt=xt: e.scalar_tensor_tensor(
                    out=xt[:, oc, :], in0=py[:, (oc % 4) * 128:(oc % 4) * 128 + 128], scalar=AB[:, 16 + oc:17 + oc], in1=xt[:, oc, :],
                    op0=ALU.mult, op1=ALU.add), reads=[b_py, C.b_ab, b_xt], writes=[b_xt])
            P.dma(xov[:, :, q0:q0 + 128], xt[:], reads=[b_xt], writes=[b_xo])


def build_L0(T_core=4096):
    nc = bass.Bass("TRN2", target_bir_lowering=False)
    TT = T_core + HALO
    xT_d = nc.dram_tensor("xT", [1024, TT], F32, kind="ExternalInput").ap()
    cm = _decl_common(nc)
    w_in_d = nc.dram_tensor("a_w_in", [1024, 3072], F32, kind="ExternalInput").ap()
    gqk_d = nc.dram_tensor("a_gqk", [128, 2], F32, kind="ExternalInput").ap()
    biasT_d = nc.dram_tensor("a_biasT", [128, 16, 640], F32, kind="ExternalInput").ap()
    valid_d = nc.dram_tensor("a_valid", [128, TT // 128], F32, kind="ExternalInput").ap()
    wout_d = nc.dram_tensor("a_w_out", [1024, 1024], F32, kind="ExternalInput").ap()
    w1_d = nc.dram_tensor("w1", [1024, 4096], F32, kind="ExternalInput").ap()
    w2_d = nc.dram_tensor("w2", [4096, 1024], F32, kind="ExternalInput").ap()
    yT_d = nc.dram_tensor("yT", [1024, T_core], F32, kind="ExternalOutput").ap()
    qT_s = nc.dram_tensor("qT_s", [1024, T_core], BF16).ap()
    kT_s = nc.dram_tensor("kT_s", [1024, TT], BF16).ap()
    v_s = nc.dram_tensor("v_s", [TT, 2048], BF16).ap()
    xs_d = nc.dram_tensor("xs_s", [1024, T_core], F32).ap()
    C = make_ctx(nc)
    emit_ada(C, *cm)
    emit_mixA1(C, xT_d, w_in_d, gqk_d, qT_s, kT_s, v_s, T_core)
    emit_mixA2(C, xT_d, xs_d, qT_s, kT_s, v_s, biasT_d, valid_d, wout_d, T_core)
    emit_ffn(C, xs_d, yT_d, w1_d, w2_d, T_core)
    C.P.emit()
    return nc


def _a_bias_table(rel_bias):
    kap = np.arange(640)[:, None]
    q = np.arange(128)[None, :]
    rel = q + 512 - kap
    cq = q // 64
    inband = (kap >= cq * 64) & (kap < cq * 64 + 576)
    idx = np.clip(rel, -63, 256) + 63
    tab = rel_bias[:, idx]
    tab = np.where(inband[None], tab, np.float32(-30000.0)).astype(np.float32)
    tab = tab.reshape(16, 5, 128, 128).transpose(2, 0, 1, 3).reshape(128, 16, 640)
    return np.ascontiguousarray(tab)


def lay_L0(d, b, half, T_core=4096, x=None):
    m = _lay_common(d, 0, b)
    x = d["x"] if x is None else x
    t0 = half * T_core
    TT = T_core + HALO
    xt = np.zeros((1024, TT), np.float32)
    lo = t0 - HALO
    if lo >= 0:
        xt[:, :] = x[b, lo:lo + TT, :].T
    else:
        xt[:, HALO:] = x[b, 0:T_core, :].T
    valid = np.ones((128, TT // 128), np.float32)
    if lo < 0:
        valid[:, :HALO // 128] = 0.0
    m.update({
        "xT": xt,
        "a_w_in": np.ascontiguousarray(d["a_w_in"][0]),
        "a_gqk": np.ascontiguousarray(np.stack([np.tile(d["a_q_gain"][0], 2), np.tile(d["a_k_gain"][0], 2)], 1)),
        "a_biasT": _a_bias_table(d["a_rel_bias"][0]),
        "a_valid": valid,
        "a_w_out": np.ascontiguousarray(d["a_w_out"][0]),
        "w1": np.ascontiguousarray(d["ffn_w1"][0]), "w2": np.ascontiguousarray(d["ffn_w2"][0]),
    })
    return m


def emit_mixD1(C, xT_d, wqkv_d, qT_s, kT_s, v_s, S, T=256, xsrc=None):
    P = C.P
    if xsrc is None:
        xv0 = xT_d.rearrange("(kc p) t -> p kc t", p=128)
        xsrc = lambda t0, T: xv0[:, :, t0:t0 + T]
    wv_ = wqkv_d.rearrange("(kc p) f -> p kc f", p=128)
    qv = qT_s.rearrange("(oc p) t -> p oc t", p=128)
    kv = kT_s.rearrange("(oc p) t -> p oc t", p=128)
    with P.scope():
        wq = P.sb("wq", [128, 8, 1536], BF16); b_wq = P.bufs(8)
        for kc in range(8):
            load_cast(C, wq[:, kc, :], b_wq[kc], wv_[:, kc, :], [1536])
        xts = [P.sb("xt%d" % i, [128, 8, T], F32) for i in range(2)]; b_xts = P.bufs(2)
        N = NormBufs(C, T)
        qk_o = [P.sb("qko%d" % i, [128, 4, T], BF16) for i in range(2)]; b_qko = P.bufs(2)
        vpad = [P.sb("vpad%d" % i, [128, 8, 128], BF16) for i in range(2)]; b_vpad = P.bufs(2)
        for i in range(2):
            P.op("pool", lambda e, i=i: e.memset(vpad[i][:], 0.0), writes=[b_vpad[i]])
        b_sc = P.buf()
        pring = Ring(list(zip(C.psums[1:5], C.psum_b[1:5])))
        vring = Ring(list(zip(C.psums[5:8], C.psum_b[5:8])))
        vi = 0
        for ti in range(S // T):
            xt, b_xt = xts[ti % 2], b_xts[ti % 2]
            t0 = ti * T
            P.dma(xt[:], xsrc(t0, T), writes=[b_xt])
            emit_norm_mod(C, N, xt, b_xt, 0, 8)
            for which in range(2):
                qo, bqo = qk_o[which], b_qko[which]
                for oc in range(4):
                    ps, bps = pring.next()
                    col0 = which * 512 + oc * 128
                    for kc in range(8):
                        P.op("pe", lambda e, kc=kc, ps=ps, col0=col0: e.matmul(
                            ps[:, :T], lhsT=wq[:, kc, col0:col0 + 128], rhs=N.hT[:, kc, :], start=(kc == 0), stop=(kc == 7)),
                            reads=[b_wq[kc], N.b_hT], writes=[bps])
                    sc = 0.125 if which == 0 else 1.0
                    P.op("act", lambda e, ps=ps, oc=oc, qo=qo, sc=sc: e.activation(out=qo[:, oc, :], in_=ps[:, :T], func=AF.Identity, scale=sc),
                         reads=[bps], writes=[bqo])
                dst = qv if which == 0 else kv
                P.dma(dst[:, :, t0:t0 + T], qo[:], reads=[bqo], writes=[b_sc])
            for w in range(T // 128):
                vp, bvp = vpad[vi % 2], b_vpad[vi % 2]
                vi += 1
                ps, bps = vring.next()
                for kc in range(8):
                    P.op("pe", lambda e, kc=kc, ps=ps, w=w: e.matmul(
                        ps[:, :], lhsT=N.hT[:, kc, w * 128:(w + 1) * 128], rhs=wq[:, kc, 1024:1536],
                        start=(kc == 0), stop=(kc == 7)), reads=[b_wq[kc], N.b_hT], writes=[bps])
                psv = ps[:, :].rearrange("p (hp two d) -> p hp two d", hp=4, two=2)
                vpv = vp[:].rearrange("p (hp two) c -> p hp two c", two=2)
                P.op("act", lambda e, psv=psv, vpv=vpv: e.activation(out=vpv[:, :, 0, 0:64], in_=psv[:, :, 0, :], func=AF.Identity),
                     reads=[bps], writes=[bvp])
                P.op("dve", lambda e, psv=psv, vpv=vpv: e.tensor_copy(out=vpv[:, :, 1, 64:128], in_=psv[:, :, 1, :]),
                     reads=[bps], writes=[bvp])
                tw = t0 + w * 128
                P.dma(v_s[tw:tw + 128, :].rearrange("t (h c) -> t h c", h=8), vp[:], reads=[bvp], writes=[b_sc])


def emit_mixD2(C, qT_s, kT_s, v_s, cst_d, oT_d, S, TQ=512, odst=None):
    P = C.P
    qv = qT_s.rearrange("(oc p) t -> p oc t", p=128)
    kv = kT_s.rearrange("(oc p) t -> p oc t", p=128)
    if odst is None:
        ov = oT_d.rearrange("(oc p) t -> p oc t", p=128)
        odst = lambda t0, n: ov[:, :, t0:t0 + n]
    with P.scope():
        cst = P.sb("cst", [128, 3, 128], F32); b_cst = P.buf()
        P.dma(cst[:], cst_d, writes=[b_cst])
        cbf = P.sb("cbf", [128, 3, 128], BF16); b_cbf = P.buf()
        P.op("dve", lambda e: e.tensor_copy(out=cbf[:], in_=cst[:]), reads=[b_cst], writes=[b_cbf])
        qt = [P.sb("qt%d" % i, [128, 4, TQ], BF16) for i in range(2)]; b_qt = P.bufs(2)
        NKB = 3
        kt = [P.sb("kt%d" % i, [128, 4, 128], BF16) for i in range(NKB)]; b_kt = P.bufs(NKB)
        vt = [P.sb("vt%d" % i, [128, 8, 128], BF16) for i in range(NKB)]; b_vt = P.bufs(NKB)
        et = [P.sb("et%d" % i, [128, TQ], F32) for i in range(2)]; b_et = P.bufs(2)
        lt = [P.sb("lt%d" % i, [128, TQ], BF16) for i in range(2)]; b_lt = P.bufs(2)
        wt = [P.sb("wt%d" % i, [128, TQ], BF16) for i in range(2)]; b_wt = P.bufs(2)
        lacc = [P.sb("lacc%d" % i, [128, TQ], BF16) for i in range(8)]; b_lacc = P.bufs(8)
        osb = [P.sb("osb%d" % i, [128, 4, TQ], F32) for i in range(2)]; b_osb = P.bufs(2)
        b_od = P.buf()
        zring = Ring(list(zip(C.psums[0:4], C.psum_b[0:4])))
        poring = [(C.psums[4 + i], C.psum_b[4 + i]) for i in range(4)]
        hi = 0
        ki = 0
        nq = TQ // 128
        for qi in range(S // TQ):
            q, bq = qt[qi % 2], b_qt[qi % 2]
            q0 = qi * TQ
            P.dma(q[:], qv[:, :, q0:q0 + TQ], writes=[bq])
            os_, bos = osb[qi % 2], b_osb[qi % 2]
            kb_hi = qi * nq + nq - 1
            for kb in range(kb_hi, -1, -1):
                k_, bk = kt[ki % NKB], b_kt[ki % NKB]
                v_, bv = vt[ki % NKB], b_vt[ki % NKB]
                ki += 1
                P.dma(k_[:], kv[:, :, kb * 128:(kb + 1) * 128], writes=[bk])
                P.dma(v_[:], v_s[kb * 128:(kb + 1) * 128, :].rearrange("t (h c) -> t h c", h=8), writes=[bv])
                dq = kb - qi * nq
                c0 = max(dq, 0) * 128
                n = TQ - c0
                first = (kb == kb_hi)
                for h in range(8):
                    pr, ab = h // 2, h % 2
                    lo = ab * 64
                    zp, bzp = zring.next()
                    e_, be = et[hi % 2], b_et[hi % 2]
                    l_, bl = lt[hi % 2], b_lt[hi % 2]
                    w_, bw = wt[hi % 2], b_wt[hi % 2]
                    hi += 1
                    la, bla = lacc[h], b_lacc[h]
                    po, bpo = poring[pr]
                    P.op("pe", lambda e, zp=zp, k_=k_, q=q, pr=pr, lo=lo, c0=c0: e.matmul(
                        zp[:, c0:TQ], lhsT=k_[lo:lo + 64, pr, :], rhs=q[lo:lo + 64, pr, c0:TQ], start=True, stop=True),
                        reads=[bk, bq], writes=[bzp])
                    P.op("act", lambda e, zp=zp, e_=e_, c0=c0: e.activation(out=e_[:, c0:TQ], in_=zp[:, c0:TQ], func=AF.Exp),
                         reads=[bzp], writes=[be])
                    P.op("act", lambda e, e_=e_, l_=l_, c0=c0: e.activation(out=l_[:, c0:TQ], in_=e_[:, c0:TQ], func=AF.Ln, bias=C.ones_f[:, 0:1], scale=1.0),
                         reads=[be, C.cb], writes=[bl])
                    if dq >= 0:
                        P.op("dve", lambda e, l_=l_, c0=c0: e.tensor_tensor(out=l_[:, c0:c0 + 128], in0=l_[:, c0:c0 + 128], in1=cbf[:, 0, :], op=ALU.mult),
                             reads=[bl, b_cbf], writes=[bl])
                    zq, bzq = zring.next()
                    P.op("pe", lambda e, zq=zq, k_=k_, q=q, pr=pr, lo=lo, c0=c0: e.matmul(
                        zq[:, c0:TQ], lhsT=k_[lo:lo + 64, pr, :], rhs=q[lo:lo + 64, pr, c0:TQ], start=True, stop=False),
                        reads=[bk, bq], writes=[bzq])
                    P.op("pe", lambda e, zq=zq, l_=l_, c0=c0, first=first: e.matmul(
                        zq[:, c0:TQ], lhsT=cbf[:, 1, :], rhs=l_[:, c0:TQ], start=False, stop=first),
                        reads=[bl, b_cbf], writes=[bzq])
                    if not first:
                        P.op("pe", lambda e, zq=zq, la=la, c0=c0: e.matmul(
                            zq[:, c0:TQ], lhsT=cbf[:, 2, :], rhs=la[:, c0:TQ], start=False, stop=True),
                            reads=[bla, b_cbf], writes=[bzq])
                    if c0 > 0:
                        P.op("pool", lambda e, w_=w_, c0=c0: e.memset(w_[:, 0:c0], 0.0), writes=[bw])
                    P.op("act", lambda e, zq=zq, w_=w_, c0=c0: e.activation(out=w_[:, c0:TQ], in_=zq[:, c0:TQ], func=AF.Exp),
                         reads=[bzq], writes=[bw])
                    if dq >= 0:
                        P.op("dve", lambda e, w_=w_, c0=c0: e.tensor_tensor(out=w_[:, c0:c0 + 128], in0=w_[:, c0:c0 + 128], in1=cbf[:, 0, :], op=ALU.mult),
                             reads=[bw, b_cbf], writes=[bw])
                    if kb > 0:
                        if first:
                            _lacc_init(P, la, bla, l_, bl, c0, TQ)
                        else:
                            P.op("pool", lambda e, la=la, l_=l_, c0=c0: e.tensor_tensor(out=la[:, c0:TQ], in0=la[:, c0:TQ], in1=l_[:, c0:TQ], op=ALU.add),
                                 reads=[bl, bla], writes=[bla])
                    P.op("pe", lambda e, po=po, v_=v_, h=h, w_=w_, kb=kb, ab=ab, first=first: e.matmul(
                        po[:, 0:TQ], lhsT=v_[:, h, :], rhs=w_[:, 0:TQ], start=(first and ab == 0), stop=(kb == 0 and ab == 1)),
                        reads=[bv, bw], writes=[bpo])
            for pr in range(4):
                po, bpo = poring[pr]
                eng = "act" if pr % 2 == 0 else "dve"
                if eng == "act":
                    P.op("act", lambda e, po=po, pr=pr, os_=os_: e.activation(out=os_[:, pr, :], in_=po[:, :], func=AF.Identity), reads=[bpo], writes=[bos])
                else:
                    P.op("dve", lambda e, po=po, pr=pr, os_=os_: e.tensor_copy(out=os_[:, pr, :], in_=po[:, :]), reads=[bpo], writes=[bos])
            P.dma(odst(q0, TQ), os_[:], reads=[bos], writes=[b_od])


def _lacc_init(P, la, bla, l_, bl, c0, TQ):
    if c0 > 0:
        P.op("pool", lambda e: e.memset(la[:, 0:c0], 0.0), writes=[bla])
    P.op("pool", lambda e: e.tensor_copy(out=la[:, c0:TQ], in_=l_[:, c0:TQ]), reads=[bl], writes=[bla])


def emit_outproj(C, xT_d, oT_d, wout_d, xo_d, T_core, T=256, sel=None, xsrc=None, osrc=None):
    P = C.P
    AB = C.AB
    if xsrc is None:
        xv = xT_d.rearrange("(kc p) t -> p kc t", p=128)
        xsrc = lambda t0, T: xv[:, :, t0:t0 + T]
    if osrc is None:
        ov = oT_d.rearrange("(kc p) t -> p kc t", p=128)
        osrc = lambda h, t0, T: ov[:, :, h * T_core + t0:h * T_core + t0 + T]
    xov = xo_d.rearrange("(kc p) t -> p kc t", p=128)
    with P.scope():
        wob = P.sb("wob", [128, 8, 1024], BF16); b_wo = P.bufs(2)
        wov = wout_d.rearrange("(fc p) n -> p fc n", p=128)
        for g in range(2):
            load_cast(C, wob[:, g * 4:(g + 1) * 4, :], b_wo[g], wov[:, g * 4:(g + 1) * 4, :], [4, 1024])
        xts = [P.sb("xt%d" % i, [128, 8, T], F32) for i in range(2)]; b_xts = P.bufs(2)
        ots = [P.sb("ot%d" % i, [128, 8, T], F32) for i in range(2)]; b_ots = P.bufs(2)
        ob = P.sb("ob", [128, 8, T], BF16); b_ob = P.buf()
        if sel is not None:
            ots1 = [P.sb("ot1_%d" % i, [128, 8, T], F32) for i in range(2)]; b_ots1 = P.bufs(2)
        yring = Ring(list(zip(C.psums[1:8], C.psum_b[1:8])))
        b_xo = P.buf()
        for ti in range(T_core // T):
            xt, b_xt = xts[ti % 2], b_xts[ti % 2]
            ot, b_ot = ots[ti % 2], b_ots[ti % 2]
            t0 = ti * T
            P.dma(xt[:], xsrc(t0, T), writes=[b_xt])
            P.dma(ot[:], osrc(0, t0, T), writes=[b_ot])
            if sel is None:
                P.op("act", lambda e, ot=ot: e.activation(out=ob[:], in_=ot[:], func=AF.Identity), reads=[b_ot], writes=[b_ob])
            else:
                selt, b_sel = sel
                ot1, b_ot1 = ots1[ti % 2], b_ots1[ti % 2]
                P.dma(ot1[:], osrc(1, t0, T), writes=[b_ot1])
                P.op("act", lambda e, ot=ot: e.activation(out=ot[:], in_=ot[:], func=AF.Identity, scale=selt[:, 0:1]),
                     reads=[b_ot, b_sel], writes=[b_ot])
                P.op("dve", lambda e, ot=ot, ot1=ot1: e.scalar_tensor_tensor(out=ob[:], in0=ot1[:], scalar=selt[:, 1:2], in1=ot[:],
                                                                            op0=ALU.mult, op1=ALU.add),
                     reads=[b_ot, b_ot1, b_sel], writes=[b_ob])
            for oc in range(8):
                ps, bps = yring.next()
                for kc in range(8):
                    P.op("pe", lambda e, oc=oc, kc=kc, ps=ps: e.matmul(
                        ps[:, :T], lhsT=wob[:, kc, oc * 128:(oc + 1) * 128], rhs=ob[:, kc, :],
                        start=(kc == 0), stop=(kc == 7)), reads=[b_wo[kc // 4], b_ob], writes=[bps])
                P.op("dve", lambda e, oc=oc, ps=ps, xt=xt: e.scalar_tensor_tensor(
                    out=xt[:, oc, :], in0=ps[:, :T], scalar=AB[:, 16 + oc:17 + oc], in1=xt[:, oc, :],
                    op0=ALU.mult, op1=ALU.add), reads=[bps, C.b_ab, b_xt], writes=[b_xt])
            P.dma(xov[:, :, t0:t0 + T], xt[:], reads=[b_xt], writes=[b_xo])


def build_L3a(S=8192):
    nc = bass.Bass("TRN2", target_bir_lowering=False)
    xT_d = nc.dram_tensor("xT", [1024, S], F32, kind="ExternalInput").ap()
    cm = _decl_common(nc)
    wqkv_d = nc.dram_tensor("d_wqkv", [1024, 1536], F32, kind="ExternalInput").ap()
    cst_d = nc.dram_tensor("d_cst", [128, 3, 128], F32, kind="ExternalInput").ap()
    oT_d = nc.dram_tensor("oT", [512, S], F32, kind="ExternalOutput").ap()
    qT_s = nc.dram_tensor("qT_s", [512, S], BF16).ap()
    kT_s = nc.dram_tensor("kT_s", [512, S], BF16).ap()
    v_s = nc.dram_tensor("v_s", [S, 1024], BF16).ap()
    C = make_ctx(nc)
    emit_ada(C, *cm)
    emit_mixD1(C, xT_d, wqkv_d, qT_s, kT_s, v_s, S)
    emit_mixD2(C, qT_s, kT_s, v_s, cst_d, oT_d, S)
    C.P.emit()
    return nc


def _d_consts():
    j = np.arange(128)[:, None]
    s_ = np.arange(128)[None, :]
    mask = (s_ > j).astype(np.float32)
    ntri = -(j >= s_).astype(np.float32)
    nones = -np.ones((128, 128), np.float32)
    return np.ascontiguousarray(np.stack([mask, ntri, nones], 1))


def lay_L3a(d, b, hg, xT_full):
    m = _lay_common(d, 3, b)
    w = d["d_w_in"][0]
    m.update({
        "xT": xT_full,
        "d_wqkv": np.ascontiguousarray(np.concatenate([w[:, hg * 512:(hg + 1) * 512], w[:, 1024 + hg * 512:1024 + (hg + 1) * 512],
                                                       w[:, 2048 + hg * 512:2048 + (hg + 1) * 512]], 1)),
        "d_cst": _d_consts(),
    })
    return m


def build_Lb(T_core=4096):
    nc = bass.Bass("TRN2", target_bir_lowering=False)
    xT_d = nc.dram_tensor("xT", [1024, T_core], F32, kind="ExternalInput").ap()
    oT_d = nc.dram_tensor("oT", [1024, T_core], F32, kind="ExternalInput").ap()
    cm = _decl_common(nc)
    wout_d = nc.dram_tensor("w_out", [1024, 1024], F32, kind="ExternalInput").ap()
    w1_d = nc.dram_tensor("w1", [1024, 4096], F32, kind="ExternalInput").ap()
    w2_d = nc.dram_tensor("w2", [4096, 1024], F32, kind="ExternalInput").ap()
    yT_d = nc.dram_tensor("yT", [1024, T_core], F32, kind="ExternalOutput").ap()
    xs_d = nc.dram_tensor("xs_s", [1024, T_core], F32).ap()
    C = make_ctx(nc)
    emit_ada(C, *cm)
    emit_outproj(C, xT_d, oT_d, wout_d, xs_d, T_core)
    emit_ffn(C, xs_d, yT_d, w1_d, w2_d, T_core)
    C.P.emit()
    return nc


def lay_Lb(d, L, b, w_out):
    m = _lay_common(d, L, b)
    m.update({"w_out": np.ascontiguousarray(w_out), "w1": np.ascontiguousarray(d["ffn_w1"][L]), "w2": np.ascontiguousarray(d["ffn_w2"][L])})
    return m


def emit_mixC(C, xT_d, wc_d, wg_d, bg_d, og_d, cst_d, oT_d, S, T=256, xsrc=None, odst=None):
    P = C.P
    if xsrc is None:
        xv0 = xT_d.rearrange("(kc p) t -> p kc t", p=128)
        xsrc = lambda t0, T: xv0[:, :, t0:t0 + T]
    wv_ = wc_d.rearrange("(kc p) f -> p kc f", p=128)
    if odst is None:
        ov = oT_d.rearrange("(oc p) t -> p oc t", p=128)
        odst = lambda t0, n: ov[:, :, t0:t0 + n]
    with P.scope():
        wc = P.sb("wc", [128, 8, 1552], BF16); b_wc = P.bufs(8)
        for kc in range(8):
            load_cast(C, wc[:, kc, :], b_wc[kc], wv_[:, kc, :], [1552])
        wg = P.sb("wg", [16, 256], F32); bg = P.sb("bg", [1, 256], F32); og = P.sb("og", [128, 2], F32)
        cst = P.sb("cst", [128, 3, 128], F32)
        b_k = P.buf()
        P.dma(wg[:], wg_d, writes=[b_k]); P.dma(bg[:], bg_d, writes=[b_k]); P.dma(og[:], og_d, writes=[b_k])
        P.dma(cst[:], cst_d, writes=[b_k])
        xts = [P.sb("xt%d" % i, [128, 8, T], F32) for i in range(2)]; b_xts = P.bufs(2)
        N = NormBufs(C, T)
        q_sb = P.sb("q_sb", [128, 2, T], F32); b_q = P.bufs(2)
        k_sb = P.sb("k_sb", [128, 2, T], F32); b_kk = P.bufs(2)
        r_sb = P.sb("r_sb", [128, 4, T], F32); b_r = P.bufs(4)
        a_sb = P.sb("a_sb", [16, T], F32); b_a = P.buf()
        ktok = P.sb("ktok", [128, 256], F32); b_ktok = P.buf()
        vtok = P.sb("vtok", [128, 512], BF16); b_vtok = P.buf()
        st = [P.sb("st%d" % i, [128, 256], F32) for i in range(2)]; b_st = P.bufs(2)
        stb = [P.sb("stb%d" % i, [128, 256], BF16) for i in range(2)]; b_stb = P.bufs(2)
        for i in range(2):
            P.op("pool", lambda e, i=i: e.memset(st[i][:], 0.0), writes=[b_st[i]])
            P.op("pool", lambda e, i=i: e.memset(stb[i][:], 0.0), writes=[b_stb[i]])
        eg = P.sb("eg", [128, 128], F32); b_eg = P.buf()
        lp = P.sb("lp", [128, 128], F32); b_lp = P.buf()
        ep = P.sb("ep", [128, 128], F32); b_ep = P.buf()
        em = P.sb("em", [128, 128], F32); b_em = P.buf()
        er = P.sb("er", [128, 128], F32); b_er = P.buf()
        qd = P.sb("qd", [128, 128], BF16); b_qd = P.buf()
        kd = P.sb("kd", [128, 128], BF16); b_kd = P.buf()
        kdt = P.sb("kdt", [128, 128], BF16); b_kdt = P.buf()
        att = P.sb("att", [128, 128], BF16); b_att = P.buf()
        o_sb = P.sb("o_sb", [128, 2, 128], F32); b_o = P.bufs(2)
        osq = P.sb("osq", [128, 2, 128], BF16); b_osq = P.buf()
        ors = P.sb("ors", [128, 128], F32); b_ors = P.buf()
        ogt = [P.sb("ogt%d" % i, [128, 4, 128], F32) for i in range(2)]; b_ogt = P.bufs(2)
        b_od = P.buf()
        ringA = Ring(list(zip(C.psums[1:4], C.psum_b[1:4])))
        ringB = Ring(list(zip(C.psums[4:8], C.psum_b[4:8])))
        QS = 1.0 / (128.0 ** 0.5)
        bi = 0
        for ti in range(S // T):
            xt, b_xt = xts[ti % 2], b_xts[ti % 2]
            t0 = ti * T
            P.dma(xt[:], xsrc(t0, T), writes=[b_xt])
            emit_norm_mod(C, N, xt, b_xt, 0, 8)

            def proj(col0, m, dst_fn, tag):
                ps, bps = ringA.next()
                for kc in range(8):
                    P.op("pe", lambda e, kc=kc, ps=ps: e.matmul(ps[0:m, :T], lhsT=wc[:, kc, col0:col0 + m], rhs=N.hT[:, kc, :],
                                                              start=(kc == 0), stop=(kc == 7)), reads=[b_wc[kc], N.b_hT], writes=[bps])
                dst_fn(ps, bps)
            for h in range(2):
                proj(h * 128, 128, lambda ps, bps, h=h: P.op("act", lambda e: e.activation(out=q_sb[:, h, :], in_=ps[:, :T], func=AF.Identity, scale=QS),
                                                          reads=[bps], writes=[b_q[h]]), "q")
                proj(256 + h * 128, 128, lambda ps, bps, h=h: P.op("dve", lambda e: e.tensor_copy(out=k_sb[:, h, :], in_=ps[:, :T]),
                                                                reads=[bps], writes=[b_kk[h]]), "k")
            for j in range(4):
                proj(1024 + j * 128, 128, lambda ps, bps, j=j: P.op("act", lambda e: e.activation(out=r_sb[:, j, :], in_=ps[:, :T], func=AF.Silu),
                                                                 reads=[bps], writes=[b_r[j]]), "r")
            proj(1536, 16, lambda ps, bps: P.op("dve", lambda e: e.tensor_copy(out=a_sb[:, :], in_=ps[0:16, :T]), reads=[bps], writes=[b_a]), "a")
            for blk in range(T // 128):
                c0 = blk * 128
                og_t, b_og = ogt[bi % 2], b_ogt[bi % 2]
                bi += 1
                ps, bps = ringA.next()
                for kc in range(8):
                    P.op("pe", lambda e, kc=kc, ps=ps, c0=c0: e.matmul(ps[:, 0:256], lhsT=N.hT[:, kc, c0:c0 + 128], rhs=wc[:, kc, 256:512],
                                                                     start=(kc == 0), stop=(kc == 7)), reads=[b_wc[kc], N.b_hT], writes=[bps])
                P.op("dve", lambda e, ps=ps: e.tensor_copy(out=ktok[:], in_=ps[:, 0:256]), reads=[bps], writes=[b_ktok])
                ps, bps = ringA.next()
                for kc in range(8):
                    P.op("pe", lambda e, kc=kc, ps=ps, c0=c0: e.matmul(ps[:, 0:512], lhsT=N.hT[:, kc, c0:c0 + 128], rhs=wc[:, kc, 512:1024],
                                                                     start=(kc == 0), stop=(kc == 7)), reads=[b_wc[kc], N.b_hT], writes=[bps])
                P.op("act", lambda e, ps=ps: e.activation(out=vtok[:], in_=ps[:, 0:512], func=AF.Identity), reads=[bps], writes=[b_vtok])
                for h in range(2):
                    S_, bS = st[h], b_st[h]
                    Sb, bSb = stb[h], b_stb[h]
                    pg, bpg = ringB.next()
                    P.op("pe", lambda e, pg=pg, c0=c0, h=h: e.matmul(pg[:, 0:128], lhsT=a_sb[:, c0:c0 + 128], rhs=wg[:, h * 128:(h + 1) * 128],
                                                                   start=True, stop=False), reads=[b_a, b_k], writes=[bpg])
                    P.op("pe", lambda e, pg=pg, h=h: e.matmul(pg[:, 0:128], lhsT=C.ones_f[0:1, :], rhs=bg[0:1, h * 128:(h + 1) * 128],
                                                            start=False, stop=True), reads=[C.cb, b_k], writes=[bpg])
                    P.op("act", lambda e, pg=pg: e.activation(out=eg[:], in_=pg[:, 0:128], func=AF.Exp, scale=-1.0), reads=[bpg], writes=[b_eg])
                    P.op("act", lambda e: e.activation(out=lp[:], in_=eg[:], func=AF.Ln, bias=C.ones_f[:, 0:1], scale=1.0),
                         reads=[b_eg, C.cb], writes=[b_lp])
                    pb, bpb = ringB.next()
                    P.op("pe", lambda e, pb=pb: e.matmul(pb[:, 0:128], lhsT=lp[:], rhs=cst[:, 0, :], start=True, stop=True),
                         reads=[b_lp, b_k], writes=[bpb])
                    pr_, bpr = ringB.next()
                    P.op("pe", lambda e, pr_=pr_: e.matmul(pr_[:, 0:128], lhsT=cst[:, 1, :], rhs=lp[:], start=True, stop=True),
                         reads=[b_lp, b_k], writes=[bpr])
                    P.op("act", lambda e, pb=pb: e.activation(out=ep[:], in_=pb[:, 0:128], func=AF.Exp), reads=[bpb], writes=[b_ep])
                    P.op("act", lambda e, pb=pb: e.activation(out=em[:], in_=pb[:, 0:128], func=AF.Exp, scale=-1.0), reads=[bpb], writes=[b_em])
                    P.op("act", lambda e, pr_=pr_: e.activation(out=er[:], in_=pr_[:, 0:128], func=AF.Exp), reads=[bpr], writes=[b_er])
                    P.op("dve", lambda e, h=h, c0=c0: e.tensor_tensor(out=qd[:], in0=q_sb[:, h, c0:c0 + 128], in1=ep[:], op=ALU.mult),
                         reads=[b_q[h], b_ep], writes=[b_qd])
                    P.op("pool", lambda e, h=h, c0=c0: e.tensor_tensor(out=kd[:], in0=k_sb[:, h, c0:c0 + 128], in1=em[:], op=ALU.mult),
                         reads=[b_kk[h], b_em], writes=[b_kd])
                    P.op("dve", lambda e, h=h: e.tensor_tensor(out=kdt[:], in0=ktok[:, h * 128:(h + 1) * 128], in1=er[:], op=ALU.mult),
                         reads=[b_ktok, b_er], writes=[b_kdt])
                    pa, bpa = ringB.next()
                    P.op("pe", lambda e, pa=pa: e.matmul(pa[:, 0:128], lhsT=kd[:], rhs=qd[:], start=True, stop=True),
                         reads=[b_kd, b_qd], writes=[bpa])
                    P.op("dve", lambda e, pa=pa: e.tensor_tensor(out=att[:], in0=pa[:, 0:128], in1=cst[:, 2, :], op=ALU.mult),
                         reads=[bpa, b_k], writes=[b_att])
                    for ch in range(2):
                        r0 = ch * 64
                        po, bpo = ringB.next()
                        for vc in range(2):
                            P.op("pe", lambda e, po=po, vc=vc, Sb=Sb, r0=r0: e.matmul(
                                po[:, vc * 64:(vc + 1) * 64], lhsT=Sb[:, vc * 128:(vc + 1) * 128], rhs=qd[:, r0:r0 + 64], start=True, stop=False),
                                reads=[bSb, b_qd], writes=[bpo])
                            P.op("pe", lambda e, po=po, vc=vc, h=h, r0=r0: e.matmul(
                                po[:, vc * 64:(vc + 1) * 64], lhsT=vtok[r0:r0 + 64, h * 256 + vc * 128:h * 256 + (vc + 1) * 128],
                                rhs=att[r0:r0 + 64, r0:r0 + 64], start=False, stop=True),
                                reads=[b_vtok, b_att], writes=[bpo])
                        P.op("act", lambda e, po=po, r0=r0: e.activation(
                            out=o_sb[:, :, r0:r0 + 64], in_=po[:, 0:128].rearrange("p (v t) -> p v t", v=2), func=AF.Identity),
                            reads=[bpo], writes=[b_o[ch]])
                        pu, bpu = ringB.next()
                        P.op("pe", lambda e, pu=pu, h=h, r0=r0: e.matmul(
                            pu[:, 0:256], lhsT=kdt[r0:r0 + 64, :], rhs=vtok[r0:r0 + 64, h * 256:(h + 1) * 256], start=True, stop=True),
                            reads=[b_kdt, b_vtok], writes=[bpu])
                        P.op("dve", lambda e, pu=pu, S_=S_, r0=r0: e.scalar_tensor_tensor(
                            out=S_[:], in0=S_[:], scalar=ep[:, r0 + 63:r0 + 64], in1=pu[:, 0:256], op0=ALU.mult, op1=ALU.add),
                            reads=[bpu, bS, b_ep], writes=[bS])
                        P.op("pool", lambda e, S_=S_, Sb=Sb: e.tensor_copy(out=Sb[:], in_=S_[:]), reads=[bS], writes=[bSb])
                    P.op("act", lambda e: e.activation(out=osq[:], in_=o_sb[:], func=AF.Square), reads=b_o, writes=[b_osq])
                    pn, bpn = ringB.next()
                    for vc in range(2):
                        P.op("pe", lambda e, pn=pn, vc=vc: e.matmul(pn[:, 0:128], lhsT=C.ones_bf[:], rhs=osq[:, vc, :], start=(vc == 0), stop=(vc == 1)),
                             reads=[b_osq, C.cb], writes=[bpn])
                    P.op("act", lambda e, pn=pn: e.activation(out=ors[:], in_=pn[:, 0:128], func=AF.Sqrt, bias=C.eps[:, 0:1], scale=1.0 / 256.0),
                         reads=[bpn, C.cb], writes=[b_ors])
                    P.op("dve", lambda e: e.reciprocal(out=ors[:], in_=ors[:]), reads=[b_ors], writes=[b_ors])
                    for vc in range(2):
                        P.op("dve", lambda e, vc=vc, h=h, og_t=og_t: e.scalar_tensor_tensor(
                            out=og_t[:, h * 2 + vc, :], in0=o_sb[:, vc, :], scalar=og[:, vc:vc + 1], in1=ors[:], op0=ALU.mult, op1=ALU.mult),
                            reads=b_o + [b_ors, b_k], writes=[b_og])
                        P.op("pool", lambda e, vc=vc, h=h, og_t=og_t, c0=c0: e.tensor_tensor(
                            out=og_t[:, h * 2 + vc, :], in0=og_t[:, h * 2 + vc, :], in1=r_sb[:, h * 2 + vc, c0:c0 + 128], op=ALU.mult),
                            reads=[b_og, b_r[h * 2 + vc]], writes=[b_og])
                tb = t0 + c0
                P.dma(odst(tb, 128), og_t[:], reads=[b_og], writes=[b_od])


def build_L2a(S=8192):
    nc = bass.Bass("TRN2", target_bir_lowering=False)
    xT_d = nc.dram_tensor("xT", [1024, S], F32, kind="ExternalInput").ap()
    cm = _decl_common(nc)
    wc_d = nc.dram_tensor("c_wc", [1024, 1552], F32, kind="ExternalInput").ap()
    wg_d = nc.dram_tensor("c_wg", [16, 256], F32, kind="ExternalInput").ap()
    bg_d = nc.dram_tensor("c_bg", [1, 256], F32, kind="ExternalInput").ap()
    og_d = nc.dram_tensor("c_og", [128, 2], F32, kind="ExternalInput").ap()
    cst_d = nc.dram_tensor("c_cst", [128, 3, 128], F32, kind="ExternalInput").ap()
    oT_d = nc.dram_tensor("oT", [512, S], F32, kind="ExternalOutput").ap()
    C = make_ctx(nc)
    emit_ada(C, *cm)
    emit_mixC(C, xT_d, wc_d, wg_d, bg_d, og_d, cst_d, oT_d, S)
    C.P.emit()
    return nc


def _c_consts():
    s_ = np.arange(128)[:, None]
    t_ = np.arange(128)[None, :]
    same = (s_ // 64) == (t_ // 64)
    tric = np.where(same & (s_ <= t_), -1.0 / 16.0, 0.0).astype(np.float32)
    trir = np.where(same & (s_ > t_), -1.0 / 16.0, 0.0).astype(np.float32)
    mc = (same & (s_ <= t_)).astype(np.float32)
    return np.ascontiguousarray(np.stack([tric, trir, mc], 1))


def lay_L2a(d, b, hp, xT_full):
    m = _lay_common(d, 2, b)
    w = d["c_w_in"][0]
    h0 = hp * 2
    m.update({
        "xT": xT_full,
        "c_wc": np.ascontiguousarray(np.concatenate([
            w[:, h0 * 128:(h0 + 2) * 128], w[:, 512 + h0 * 128:512 + (h0 + 2) * 128],
            w[:, 1024 + h0 * 256:1024 + (h0 + 2) * 256], w[:, 2048 + h0 * 256:2048 + (h0 + 2) * 256], w[:, 3072:3088]], 1)),
        "c_wg": np.ascontiguousarray(d["c_w_gate_up"][0][:, h0 * 128:(h0 + 2) * 128]),
        "c_bg": np.ascontiguousarray(d["c_b_gate"][0][h0 * 128:(h0 + 2) * 128].reshape(1, 256)),
        "c_og": np.ascontiguousarray(d["c_o_gain"][0].reshape(2, 128).T),
        "c_cst": _c_consts(),
    })
    return m


PAIRS = [[0, 1], [2, 3], [4, 5], [6, 7]]


def build_fused(H=4096, PAIRS=PAIRS):
    S = 2 * H
    nc = bass.Bass("TRN2", target_bir_lowering=False)
    TT = H + HALO

    def din(name, shape):
        return nc.dram_tensor(name, list(shape), F32, kind="ExternalInput").ap()
    xT_d = din("xT", [1024, TT])
    cT_d = din("cT", [128, 8])
    adaw = [din("ada_w%d" % L, [1024, 6144]) for L in range(4)]
    adab = [din("ada_b%d" % L, [128, 48]) for L in range(4)]
    gains = [din("gains%d" % L, [128, 16]) for L in range(4)]
    w1 = [din("w1_%d" % L, [1024, 4096]) for L in range(4)]
    w2 = [din("w2_%d" % L, [4096, 1024]) for L in range(4)]
    a_w_in = din("a_w_in", [1024, 3072]); a_gqk = din("a_gqk", [128, 2]); a_biasT = din("a_biasT", [128, 16, 640])
    a_valid = din("a_valid", [128, TT // 128]); a_w_out = din("a_w_out", [1024, 1024])
    b_w_in = din("b_w_in", [1024, 6144]); b_vgain = din("b_vgain", [128, 3072]); b_wsT = din("b_wsT", [128, 8, 128])
    b_bs = din("b_bs", [128, 24, 128]); b_w_out = din("b_w_out", [3072, 1024])
    c_wc = din("c_wc", [1024, 1552]); c_wg = din("c_wg", [16, 256]); c_bg = din("c_bg", [1, 256]); c_og = din("c_og", [128, 2])
    c_cst = din("c_cst", [128, 3, 128]); c_w_out = din("c_w_out", [1024, 1024])
    d_wqkv = din("d_wqkv", [1024, 1536]); d_cst = din("d_cst", [128, 3, 128]); d_w_out = din("d_w_out", [1024, 1024])
    sel_d = din("sel", [128, 2])
    yT_d = nc.dram_tensor("yT", [1024, H], F32, kind="ExternalOutput").ap()

    def scr(name, shape, dt=F32):
        return nc.dram_tensor(name, list(shape), dt).ap()
    qT_s = scr("a_qT_s", [1024, H], BF16); kT_s = scr("a_kT_s", [1024, TT], BF16); v_s = scr("a_v_s", [TT, 2048], BF16)
    vm_s = scr("b_vm_s", [24, 128, H], BF16)
    xs = [scr("xs%d" % i, [1024, H]) for i in range(4)]
    xa = scr("xa", [1024, H])
    XC = min(512, H)
    OC = min(1024, S)
    xb_c = [scr("xb_c%d" % i, [1024, XC]) for i in range(H // XC)]; xb_g = [scr("xb_g%d" % i, [2048, XC]) for i in range(H // XC)]
    xc_c = [scr("xc_c%d" % i, [1024, XC]) for i in range(H // XC)]; xc_g = [scr("xc_g%d" % i, [2048, XC]) for i in range(H // XC)]
    oc_c = [scr("oc_c%d" % i, [512, OC]) for i in range(S // OC)]; oc_g = [scr("oc_g%d" % i, [1024, OC]) for i in range(S // OC)]
    od_c = [scr("od_c%d" % i, [512, OC]) for i in range(S // OC)]; od_g = [scr("od_g%d" % i, [1024, OC]) for i in range(S // OC)]
    dq_s = scr("d_qT_s", [512, S], BF16); dk_s = scr("d_kT_s", [512, S], BF16); dv_s = scr("d_v_s", [S, 1024], BF16)

    C = make_ctx(nc)
    P = C.P
    selt = P.sb("selt", [128, 2], F32); b_sel = P.buf()
    P.dma(selt[:], sel_d, writes=[b_sel])

    def x_own(chunks):
        def f(t0, T):
            return chunks[t0 // XC][:, t0 % XC:t0 % XC + T].rearrange("(kc p) t -> p kc t", p=128)
        return f

    def x_gath(chunks):
        def f(t0, T):
            r, tl = t0 // H, t0 % H
            return chunks[tl // XC][r * 1024:(r + 1) * 1024, tl % XC:tl % XC + T].rearrange("(kc p) t -> p kc t", p=128)
        return f

    def o_dst(chunks):
        def f(t0, n):
            return chunks[t0 // OC].rearrange("(oc p) t -> p oc t", p=128)[:, :, t0 % OC:t0 % OC + n]
        return f

    def o_gath(chunks):
        def f(h, t0, T):
            g = h * H + t0
            return chunks[g // OC].rearrange("(kc p) t -> p kc t", p=128)[:, :, g % OC:g % OC + T]
        return f

    def gather(src, dst):
        for a_, b_ in zip(src, dst):
            P.collective("AllGather", [a_], [b_], PAIRS)
        P.barrier()
    emit_ada(C, cT_d, adaw[0], adab[0], gains[0])
    emit_mixA1(C, xT_d, a_w_in, a_gqk, qT_s, kT_s, v_s, H)
    emit_mixA2(C, xT_d, xs[0], qT_s, kT_s, v_s, a_biasT, a_valid, a_w_out, H)
    emit_ffn(C, xs[0], xa, w1[0], w2[0], H)
    emit_ada(C, cT_d, adaw[1], adab[1], gains[1])
    emit_mixB(C, xa, xs[1], b_w_in, b_vgain, b_wsT, b_bs, b_w_out, vm_s, H)
    emit_ffn(C, xs[1], None, w1[1], w2[1], H, ydst=x_own(xb_c))
    gather(xb_c, xb_g)
    emit_ada(C, cT_d, adaw[2], adab[2], gains[2])
    emit_mixC(C, None, c_wc, c_wg, c_bg, c_og, c_cst, None, S, xsrc=x_gath(xb_g), odst=o_dst(oc_c))
    gather(oc_c, oc_g)
    emit_outproj(C, None, None, c_w_out, xs[2], H, sel=(selt, b_sel), xsrc=x_own(xb_c), osrc=o_gath(oc_g))
    emit_ffn(C, xs[2], None, w1[2], w2[2], H, ydst=x_own(xc_c))
    gather(xc_c, xc_g)
    emit_ada(C, cT_d, adaw[3], adab[3], gains[3])
    emit_mixD1(C, None, d_wqkv, dq_s, dk_s, dv_s, S, xsrc=x_gath(xc_g))
    emit_mixD2(C, dq_s, dk_s, dv_s, d_cst, None, S, odst=o_dst(od_c))
    gather(od_c, od_g)
    emit_outproj(C, None, None, d_w_out, xs[3], H, sel=(selt, b_sel), xsrc=x_own(xc_c), osrc=o_gath(od_g))
    emit_ffn(C, xs[3], yT_d, w1[3], w2[3], H)
    P.emit()
    return nc


def lay_fused(d, b, hf, H=4096):
    m = {}
    l0 = lay_L0(d, b, hf, H)
    for k in ("xT", "cT", "a_w_in", "a_gqk", "a_biasT", "a_valid", "a_w_out"):
        m[k] = l0[k]
    for L in range(4):
        c = _lay_common(d, L, b)
        m["ada_w%d" % L] = c["ada_w"]; m["ada_b%d" % L] = c["ada_b"]; m["gains%d" % L] = c["gains"]
        m["w1_%d" % L] = np.ascontiguousarray(d["ffn_w1"][L]); m["w2_%d" % L] = np.ascontiguousarray(d["ffn_w2"][L])
    l1 = lay_L1(d, b)
    for k in ("b_w_in", "b_vgain", "b_wsT", "b_bs", "b_w_out"):
        m[k] = l1[k]
    l2 = lay_L2a(d, b, hf, None)
    for k in ("c_wc", "c_wg", "c_bg", "c_og", "c_cst"):
        m[k] = l2[k]
    m["c_w_out"] = np.ascontiguousarray(d["c_w_out"][0])
    l3 = lay_L3a(d, b, hf, None)
    for k in ("d_wqkv", "d_cst"):
        m[k] = l3[k]
    m["d_w_out"] = np.ascontiguousarray(d["d_w_out"][0])
    sel = np.zeros((128, 2), np.float32); sel[:, hf] = 1.0
    m["sel"] = sel
    return m


_PROGS = {}


def _prog(name, builder):
    if name not in _PROGS:
        _PROGS[name] = builder()
    return _PROGS[name]


def _run(nc, in_maps):
    res = run_bass_kernel_spmd(nc, in_maps, core_ids=list(range(len(in_maps))))
    return res.results


def kernel(**inputs):
    d = {k: np.asarray(v, dtype=np.float32) for k, v in inputs.items()}
    B, S, D = d["x"].shape
    H = S // 2
    cores = [(b, hf) for b in range(B) for hf in range(2)]
    nc = _prog("fused", lambda: build_fused(H))
    r = _run(nc, [lay_fused(d, b, hf, H) for (b, hf) in cores])
    out = np.empty((B, S, D), np.float32)
    for b in range(B):
        out[b, :H] = r[2 * b]["yT"].T
        out[b, H:] = r[2 * b + 1]["yT"].T
    return out


def kernel_unfused(**inputs):
    d = {k: np.asarray(v, dtype=np.float32) for k, v in inputs.items()}
    B, S, D = d["x"].shape
    H = S // 2
    cores = [(b, hf) for b in range(B) for hf in range(2)]
    nc = _prog("L0", lambda: build_L0(H))
    r = _run(nc, [lay_L0(d, b, hf, H) for (b, hf) in cores])
    xT = [np.concatenate([r[2 * b]["yT"], r[2 * b + 1]["yT"]], axis=1) for b in range(B)]
    nc = _prog("L1", lambda: build_L1(H))
    ins = []
    for (b, hf) in cores:
        m = lay_L1(d, b)
        m["xT"] = np.ascontiguousarray(xT[b][:, hf * H:(hf + 1) * H])
        ins.append(m)
    r = _run(nc, ins)
    xT = [np.concatenate([r[2 * b]["yT"], r[2 * b + 1]["yT"]], axis=1) for b in range(B)]
    nc = _prog("L2a", lambda: build_L2a(S))
    r = _run(nc, [lay_L2a(d, b, hp, xT[b]) for (b, hp) in cores])
    oT = [np.concatenate([r[2 * b]["oT"], r[2 * b + 1]["oT"]], axis=0) for b in range(B)]
    nc = _prog("Lb", lambda: build_Lb(H))
    ins = []
    for (b, hf) in cores:
        m = lay_Lb(d, 2, b, d["c_w_out"][0])
        m["xT"] = np.ascontiguousarray(xT[b][:, hf * H:(hf + 1) * H])
        m["oT"] = np.ascontiguousarray(oT[b][:, hf * H:(hf + 1) * H])
        ins.append(m)
    r = _run(nc, ins)
    xT = [np.concatenate([r[2 * b]["yT"], r[2 * b + 1]["yT"]], axis=1) for b in range(B)]
    nc = _prog("L3a", lambda: build_L3a(S))
    r = _run(nc, [lay_L3a(d, b, hg, xT[b]) for (b, hg) in cores])
    oT = [np.concatenate([r[2 * b]["oT"], r[2 * b + 1]["oT"]], axis=0) for b in range(B)]
    nc = _prog("Lb", lambda: build_Lb(H))
    ins = []
    for (b, hf) in cores:
        m = lay_Lb(d, 3, b, d["d_w_out"][0])
        m["xT"] = np.ascontiguousarray(xT[b][:, hf * H:(hf + 1) * H])
        m["oT"] = np.ascontiguousarray(oT[b][:, hf * H:(hf + 1) * H])
        ins.append(m)
    r = _run(nc, ins)
    out = np.empty((B, S, D), np.float32)
    for b in range(B):
        out[b, :H] = r[2 * b]["yT"].T
        out[b, H:] = r[2 * b + 1]["yT"].T
    return out
```

```python
import numpy as np
import concourse.bass as bass
import concourse.mybir as mybir
from concourse.bass_utils import run_bass_kernel_spmd
from contextlib import ExitStack, contextmanager

F32 = mybir.dt.float32
BF16 = mybir.dt.bfloat16
AF = mybir.ActivationFunctionType
ALU = mybir.AluOpType
AX = mybir.AxisListType
EPS = 1e-6
DEBUG = False
SERIAL = False
NSTAGE = 2
STAGE_ELEMS = 2048
ENGS = ("pe", "act", "dve", "pool", "sp")


class Buf:
    __slots__ = ("name", "w", "r")

    def __init__(self, name):
        self.name = name
        self.w = None
        self.r = []


class Ring:
    def __init__(self, items):
        self.items = items
        self.i = 0

    def next(self):
        it = self.items[self.i % len(self.items)]
        self.i += 1
        return it


class Prog:
    NDMA = 24

    def __init__(self, nc):
        self.nc = nc
        self.es = ExitStack()
        self.scopes = [self.es]
        self.streams = {e: [] for e in ENGS}
        self.esem = {e: self.es.enter_context(nc.semaphore("s_" + e)) for e in ENGS}
        self.ecount = {e: 0 for e in ENGS}
        self.dsem = [self.es.enter_context(nc.semaphore("d%d" % i)) for i in range(self.NDMA)]
        self.dcount = [0] * self.NDMA
        self.dnext = 0
        self.semobj = {}
        for e in ENGS:
            self.semobj[("e", e)] = self.esem[e]
        for i in range(self.NDMA):
            self.semobj[("d", i)] = self.dsem[i]
        self.waited = {e: {} for e in ENGS}
        self.nbuf = 0
        self.uid = 0

    @contextmanager
    def scope(self):
        st = ExitStack()
        self.scopes.append(st)
        try:
            yield
        finally:
            self.barrier()
            self.scopes.pop()
            st.close()

    def sb(self, name, shape, dt):
        self.uid += 1
        return self.scopes[-1].enter_context(self.nc.sbuf_tensor("%s_%d" % (name, self.uid), list(shape), dt))

    def ps(self, name, shape, dt=F32):
        return self.scopes[-1].enter_context(self.nc.psum_tensor(name, list(shape), dt))

    def buf(self, name=None):
        self.nbuf += 1
        return Buf(name or ("b%d" % self.nbuf))

    def bufs(self, n):
        return [self.buf() for _ in range(n)]

    def _deps(self, eng, reads, writes):
        deps = {}

        def add(d):
            if d is None:
                return
            k, v = d
            if deps.get(k, 0) < v:
                deps[k] = v
        for b in reads:
            add(b.w)
        for b in writes:
            add(b.w)
            for d in b.r:
                add(d)
        waits = []
        wd = self.waited[eng]
        for k, v in deps.items():
            if wd.get(k, 0) >= v:
                continue
            if eng == "pe" and k == ("e", "pe"):
                continue
            wd[k] = v
            waits.append((self.semobj[k], v))
        return waits

    def _mark(self, tok, reads, writes):
        for b in writes:
            b.w = tok
            b.r = []
        for b in reads:
            if b not in writes:
                b.r.append(tok)
                if len(b.r) > 64:
                    best = {}
                    for k, v in b.r:
                        if best.get(k, 0) < v:
                            best[k] = v
                    b.r = list(best.items())

    def _serial_waits(self, eng):
        waits = []
        for e2 in ENGS:
            k = ("e", e2); v = self.ecount[e2]
            if v > 0 and self.waited[eng].get(k, 0) < v:
                self.waited[eng][k] = v
                waits.append((self.esem[e2], v))
        for i in range(self.NDMA):
            k = ("d", i); v = self.dcount[i]
            if v > 0 and self.waited[eng].get(k, 0) < v:
                self.waited[eng][k] = v
                waits.append((self.dsem[i], v))
        return waits

    def op(self, eng, fn, reads=(), writes=()):
        waits = self._deps(eng, reads, writes)
        if SERIAL:
            waits = waits + self._serial_waits(eng)
        self.ecount[eng] += 1
        tok = (("e", eng), self.ecount[eng])
        self.streams[eng].append((waits, fn, self.esem[eng], 1))
        self._mark(tok, reads, writes)
        return tok

    def dma(self, out, in_, reads=(), writes=(), eng="sp"):
        i = self.dnext
        self.dnext = (self.dnext + 1) % self.NDMA
        waits = self._deps(eng, reads, writes)
        if SERIAL:
            waits = waits + self._serial_waits(eng)
        k = ("d", i)
        if self.dcount[i] > 0 and self.waited[eng].get(k, 0) < self.dcount[i]:
            self.waited[eng][k] = self.dcount[i]
            waits.append((self.dsem[i], self.dcount[i]))
        self.dcount[i] += 16
        tok = (k, self.dcount[i])
        self.streams[eng].append((waits, lambda e: e.dma_start(out=out, in_=in_), self.dsem[i], 16))
        self._mark(tok, reads, writes)
        return tok

    def collective(self, kind, ins, outs, groups, reads=(), writes=()):
        eng = "pool"
        waits = self._deps(eng, reads, writes)
        sem = self.es.enter_context(self.nc.semaphore("cc%d" % len(self.semobj)))
        k = ("c", len(self.semobj))
        self.semobj[k] = sem
        tok = (k, 1)
        self.streams[eng].append((waits, lambda e: e.collective_compute(kind, ALU.bypass, groups, [a.opt() for a in ins], [a.opt() for a in outs]), sem, 1))
        self._mark(tok, reads, writes)
        return tok

    def dump(self, name, ap, shape, dt, reads):
        if not DEBUG:
            return
        d = self.nc.dram_tensor(name, list(shape), dt, kind="ExternalOutput").ap()
        self.dma(d, ap, reads=reads, writes=[self.buf()])

    def barrier(self):
        for eng in ENGS:
            waits = []
            for k, sem in self.semobj.items():
                if k[0] == "c" and self.waited[eng].get(k, 0) < 1:
                    self.waited[eng][k] = 1
                    waits.append((sem, 1))
            for e2 in ENGS:
                k = ("e", e2)
                v = self.ecount[e2]
                if e2 != eng and v > 0 and self.waited[eng].get(k, 0) < v:
                    self.waited[eng][k] = v
                    waits.append((self.esem[e2], v))
            for i in range(self.NDMA):
                k = ("d", i)
                v = self.dcount[i]
                if v > 0 and self.waited[eng].get(k, 0) < v:
                    self.waited[eng][k] = v
                    waits.append((self.dsem[i], v))
            k = ("e", eng)
            v = self.ecount[eng]
            if v > 0 and self.waited[eng].get(k, 0) < v:
                self.waited[eng][k] = v
                waits.append((self.esem[eng], v))
            if waits:
                self.streams[eng].append((waits, None, None, 0))

    def emit(self):
        nc = self.nc
        self.barrier()
        with nc.Block() as block:
            def runner(name):
                def f(e):
                    for waits, fn, sem, inc in self.streams[name]:
                        for s, v in waits:
                            e.wait_ge(s, v)
                        if fn is not None:
                            fn(e).then_inc(sem, inc)
                return f
            block.tensor(runner("pe"))
            block.scalar(runner("act"))
            block.vector(runner("dve"))
            block.gpsimd(runner("pool"))
            block.sync(runner("sp"))
        self.es.close()


class Ctx:
    pass


def make_ctx(nc):
    P = Prog(nc)
    C = Ctx()
    C.P = P
    C.nc = nc
    C.psums = [P.ps("ps%d" % i, [128, 512]) for i in range(8)]
    C.psum_b = P.bufs(8)
    C.ones_bf = P.sb("ones_bf", [128, 128], BF16)
    C.ones_f = P.sb("ones_f", [128, 128], F32)
    C.eps = P.sb("eps_t", [128, 1], F32)
    C.cb = P.buf()

    def mk(e):
        e.memset(C.eps[:], EPS)
        e.memset(C.ones_f[:], 1.0)
        return e.memset(C.ones_bf[:], 1.0)
    P.op("pool", mk, writes=[C.cb])
    C.stage = [P.sb("stage%d" % i, [128, STAGE_ELEMS], F32) for i in range(NSTAGE)]
    C.stage_b = P.bufs(NSTAGE)
    C.sti = 0
    return C


def next_stage(C):
    i = C.sti % len(C.stage)
    C.sti += 1
    return C.stage[i], C.stage_b[i]


def load_cast(C, dst_ap, dst_buf, src_ap, shape3):
    P = C.P
    n = 1
    for s in shape3:
        n *= s
    if n > STAGE_ELEMS:
        h = shape3[0] // 2
        if len(shape3) == 1:
            load_cast(C, dst_ap[:, 0:h], dst_buf, src_ap[:, 0:h], [h])
            load_cast(C, dst_ap[:, h:2 * h], dst_buf, src_ap[:, h:2 * h], [h])
        else:
            load_cast(C, dst_ap[:, 0:h, :], dst_buf, src_ap[:, 0:h, :], [h, shape3[1]])
            load_cast(C, dst_ap[:, h:2 * h, :], dst_buf, src_ap[:, h:2 * h, :], [h, shape3[1]])
        return
    st, sb_ = next_stage(C)
    if len(shape3) == 2:
        stv = st[:, 0:n].rearrange("p (a n) -> p a n", a=shape3[0])
    else:
        stv = st[:, 0:n]
    P.dma(stv, src_ap, writes=[sb_])
    eng = "dve" if C.sti % 2 == 0 else "pool"
    P.op(eng, lambda e: e.tensor_copy(out=dst_ap, in_=stv), reads=[sb_], writes=[dst_buf])


def emit_ada(C, cT_d, ada_w_d, ada_b_d, gains_d):
    P = C.P
    psum, psum_b = C.psums[0], C.psum_b[0]
    ct = P.sb("ct", [128, 8], F32); cond = P.sb("cond", [128, 8], F32)
    adab = P.sb("adab", [128, 48], F32); gn = P.sb("gn", [128, 16], F32)
    mod = P.sb("mod", [128, 48], F32)
    out = P.sb("AB", [128, 48], F32)
    b_ct, b_cond, b_adab, b_gn, b_mod, b_out = P.bufs(6)
    P.dma(ct[:], cT_d, writes=[b_ct])
    P.dma(adab[:], ada_b_d, writes=[b_adab])
    P.dma(gn[:], gains_d, writes=[b_gn])
    P.op("act", lambda e: e.activation(out=cond[:], in_=ct[:], func=AF.Silu), reads=[b_ct], writes=[b_cond])
    wv = ada_w_d.rearrange("(kc p) f -> p kc f", p=128)
    for g in range(24):
        st, sb_ = next_stage(C)
        stv = st[:, 0:2048].rearrange("p (kc f) -> p kc f", kc=8)
        P.dma(stv, wv[:, :, g * 256:(g + 1) * 256], writes=[sb_])
        for j in range(2):
            col = g * 2 + j
            for kc in range(8):
                P.op("pe", (lambda e, col=col, kc=kc, j=j, stv=stv: e.matmul(
                    psum[:, col:col + 1], lhsT=stv[:, kc, j * 128:(j + 1) * 128],
                    rhs=cond[:, kc:kc + 1], start=(kc == 0), stop=(kc == 7))),
                    reads=[sb_, b_cond], writes=[psum_b])
    P.op("dve", lambda e: e.tensor_tensor(out=mod[:], in0=psum[:, 0:48], in1=adab[:], op=ALU.add),
         reads=[psum_b, b_adab], writes=[b_mod])

    def mk(e):
        e.scalar_tensor_tensor(out=out[:, 0:8], in0=mod[:, 8:16], scalar=1.0, in1=gn[:, 0:8], op0=ALU.add, op1=ALU.mult)
        e.scalar_tensor_tensor(out=out[:, 24:32], in0=mod[:, 32:40], scalar=1.0, in1=gn[:, 8:16], op0=ALU.add, op1=ALU.mult)
        e.tensor_copy(out=out[:, 8:16], in_=mod[:, 0:8])
        e.tensor_copy(out=out[:, 16:24], in_=mod[:, 16:24])
        e.tensor_copy(out=out[:, 32:40], in_=mod[:, 24:32])
        return e.tensor_copy(out=out[:, 40:48], in_=mod[:, 40:48])
    P.op("dve", mk, reads=[b_mod, b_gn], writes=[b_out])
    C.AB, C.b_ab = out, b_out


class NormBufs:
    def __init__(self, C, T):
        P = C.P
        self.T = T
        self.sq = P.sb("sq", [128, 8, T], BF16); self.b_sq = P.buf()
        self.rstd = P.sb("rstd", [128, T], F32); self.b_rstd = P.buf()
        self.tmp = P.sb("tmp", [128, 4, T], F32); self.b_tmp = P.bufs(4)
        self.hT = P.sb("hT", [128, 8, T], BF16); self.b_hT = P.buf()


def emit_norm_mod(C, N, xt, b_xt, acol, bcol):
    P = C.P
    T = N.T
    ps, b_ps = C.psums[0], C.psum_b[0]
    A = C.AB[:, acol:acol + 8]
    Bc = C.AB[:, bcol:bcol + 8]
    P.op("act", lambda e: e.activation(out=N.sq[:], in_=xt[:], func=AF.Square), reads=[b_xt], writes=[N.b_sq])
    for kc in range(8):
        P.op("pe", lambda e, kc=kc: e.matmul(ps[:, :T], lhsT=C.ones_bf[:], rhs=N.sq[:, kc, :], start=(kc == 0), stop=(kc == 7)),
             reads=[N.b_sq, C.cb], writes=[b_ps])
    P.op("act", lambda e: e.activation(out=N.rstd[:], in_=ps[:, :T], func=AF.Sqrt, bias=C.eps[:, 0:1], scale=1.0 / 1024.0),
         reads=[b_ps, C.cb], writes=[N.b_rstd])
    P.op("dve", lambda e: e.reciprocal(out=N.rstd[:], in_=N.rstd[:]), reads=[N.b_rstd], writes=[N.b_rstd])
    for kc in range(8):
        eng = "dve" if kc % 2 == 0 else "pool"
        P.op(eng, lambda e, kc=kc: e.tensor_tensor(out=N.tmp[:, kc % 4, :], in0=xt[:, kc, :], in1=N.rstd[:], op=ALU.mult),
             reads=[b_xt, N.b_rstd], writes=[N.b_tmp[kc % 4]])
        P.op("act", lambda e, kc=kc: e.activation(out=N.hT[:, kc, :], in_=N.tmp[:, kc % 4, :], func=AF.Identity,
                                                   bias=Bc[:, kc:kc + 1], scale=A[:, kc:kc + 1]),
             reads=[N.b_tmp[kc % 4], C.b_ab], writes=[N.b_hT])


def emit_ffn(C, xT_d, yT_d, w1_d, w2_d, T_core, T=256, ydst=None):
    P = C.P
    AB = C.AB
    with P.scope():
        w1b = P.sb("w1b", [128, 8, 4096], BF16); w2b = P.sb("w2b", [128, 32, 1024], BF16)
        b_w1 = P.bufs(8); b_w2 = P.bufs(8)
        for kc in range(8):
            load_cast(C, w1b[:, kc, :], b_w1[kc], w1_d[kc * 128:(kc + 1) * 128, :], [4096])
        w2v = w2_d.rearrange("(hc p) n -> p hc n", p=128)
        for g in range(8):
            load_cast(C, w2b[:, g * 4:(g + 1) * 4, :], b_w2[g], w2v[:, g * 4:(g + 1) * 4, :], [4, 1024])
        xts = [P.sb("xt%d" % i, [128, 8, T], F32) for i in range(2)]
        b_xts = P.bufs(2)
        N = NormBufs(C, T)
        h1T = P.sb("h1T", [128, 32, T], BF16); b_h1 = P.bufs(32)
        rl = [P.sb("rl%d" % i, [128, T], F32) for i in range(2)]; b_rl = P.bufs(2)
        xv = xT_d.rearrange("(kc p) t -> p kc t", p=128)
        if ydst is None:
            yv = yT_d.rearrange("(kc p) t -> p kc t", p=128)
            ydst = lambda t0, T: yv[:, :, t0:t0 + T]
        b_out = P.buf()
        hring = Ring(list(zip(C.psums[1:5], C.psum_b[1:5])))
        yring = Ring(list(zip(C.psums[5:8], C.psum_b[5:8])))
        nt = T_core // T
        P.dma(xts[0][:], xv[:, :, 0:T], writes=[b_xts[0]])
        emit_norm_mod(C, N, xts[0], b_xts[0], 24, 32)
        for ti in range(nt):
            xt, b_xt = xts[ti % 2], b_xts[ti % 2]
            t0 = ti * T
            for hc in range(32):
                ps, bps = hring.next()
                for kc in range(8):
                    P.op("pe", lambda e, kc=kc, hc=hc, ps=ps: e.matmul(
                        ps[:, :T], lhsT=w1b[:, kc, hc * 128:(hc + 1) * 128], rhs=N.hT[:, kc, :],
                        start=(kc == 0), stop=(kc == 7)), reads=[b_w1[kc], N.b_hT], writes=[bps])
                r, br = rl[hc % 2], b_rl[hc % 2]
                P.op("act", lambda e, ps=ps, r=r: e.activation(out=r[:], in_=ps[:, :T], func=AF.Relu), reads=[bps], writes=[br])
                P.op("pool", lambda e, r=r, hc=hc: e.tensor_tensor(out=h1T[:, hc, :], in0=r[:], in1=r[:], op=ALU.mult),
                     reads=[br], writes=[b_h1[hc]])
            if ti + 1 < nt:
                nx, b_nx = xts[(ti + 1) % 2], b_xts[(ti + 1) % 2]
                P.dma(nx[:], xv[:, :, t0 + T:t0 + 2 * T], writes=[b_nx])
                emit_norm_mod(C, N, nx, b_nx, 24, 32)
            for oc in range(8):
                ps, bps = yring.next()
                for hc in range(32):
                    P.op("pe", lambda e, oc=oc, hc=hc, ps=ps: e.matmul(
                        ps[:, :T], lhsT=w2b[:, hc, oc * 128:(oc + 1) * 128], rhs=h1T[:, hc, :],
                        start=(hc == 0), stop=(hc == 31)), reads=[b_w2[hc // 4], b_h1[hc]], writes=[bps])
                P.op("dve", lambda e, oc=oc, ps=ps, xt=xt: e.scalar_tensor_tensor(
                    out=xt[:, oc, :], in0=ps[:, :T], scalar=AB[:, 40 + oc:41 + oc], in1=xt[:, oc, :],
                    op0=ALU.mult, op1=ALU.add), reads=[bps, C.b_ab, b_xt], writes=[b_xt])
            P.dma(ydst(t0, T), xt[:], reads=[b_xt], writes=[b_out])


def emit_mixB(C, xT_d, xo_d, w_in_d, vgain_d, wsT_d, bs_d, wout_d, vm_d, T_core, T=256):
    P = C.P
    xv = xT_d.rearrange("(kc p) t -> p kc t", p=128)
    xov = xo_d.rearrange("(kc p) t -> p kc t", p=128)
    vmv = vm_d.rearrange("fc p t -> p fc t")
    winv = w_in_d.rearrange("(kc p) f -> p kc f", p=128)
    with P.scope():
        wvb = P.sb("wvb", [128, 8, 3072], BF16); b_wv = P.bufs(8)
        for kc in range(8):
            load_cast(C, wvb[:, kc, :], b_wv[kc], winv[:, kc, 3072:6144], [3072])
        gainB = P.sb("gainB", [128, 3072], F32); b_gain = P.buf()
        P.dma(gainB[:], vgain_d, writes=[b_gain])
        wsT = P.sb("wsT", [128, 8, 128], BF16); b_ws = P.buf()
        load_cast(C, wsT[:], b_ws, wsT_d, [8, 128])
        P.op("pool", lambda e: e.memset(wsT[64:128, :, 0:64], 0.0), writes=[b_ws])
        bs = P.sb("bs", [128, 24, 128], F32); b_bs = P.buf()
        P.dma(bs[:], bs_d, writes=[b_bs])
        xts = [P.sb("xt%d" % i, [128, 8, T], F32) for i in range(2)]; b_xts = P.bufs(2)
        N = NormBufs(C, T)
        vg = P.sb("vg", [128, 3072], F32); b_vg = P.bufs(6)
        sqv = P.sb("sqv", [128, 3072], F32); b_sqv = P.buf()
        ssv = P.sb("ssv", [128, 1], F32); b_ssv = P.buf()
        vn = P.sb("vn", [128, 3072], BF16); b_vn = P.buf()
        vmT = [P.sb("vmT%d" % i, [128, 24, 128], BF16) for i in range(2)]; b_vmT = P.bufs(2)
        pring = Ring(list(zip(C.psums[1:5], C.psum_b[1:5])))
        mring = Ring(list(zip(C.psums[5:8], C.psum_b[5:8])))
        b_vmd = P.buf()
        wi = 0
        for ti in range(T_core // T):
            xt, b_xt = xts[ti % 2], b_xts[ti % 2]
            t0 = ti * T
            P.dma(xt[:], xv[:, :, t0:t0 + T], writes=[b_xt])
            emit_norm_mod(C, N, xt, b_xt, 0, 8)
            for w in range(T // 128):
                for cc in range(6):
                    ps, bps = pring.next()
                    for kc in range(8):
                        P.op("pe", lambda e, kc=kc, cc=cc, ps=ps, w=w: e.matmul(
                            ps[:, :], lhsT=N.hT[:, kc, w * 128:(w + 1) * 128], rhs=wvb[:, kc, cc * 512:(cc + 1) * 512],
                            start=(kc == 0), stop=(kc == 7)), reads=[b_wv[kc], N.b_hT], writes=[bps])
                    if DEBUG and ti == 0 and w == 0 and cc == 0:
                        dbgp = P.sb("dbgp", [128, 512], F32); b_dbgp = P.buf()
                        P.op("act", lambda e, ps=ps: e.activation(out=dbgp[:], in_=ps[:, :], func=AF.Identity), reads=[bps], writes=[b_dbgp])
                        P.dump("dbg_ps", dbgp[:], [128, 512], F32, [b_dbgp])
                        dbgp2 = P.sb("dbgp2", [128, 512], F32); b_dbgp2 = P.buf()
                        P.op("dve", lambda e, ps=ps: e.tensor_copy(out=dbgp2[:], in_=ps[:, :]), reads=[bps], writes=[b_dbgp2])
                        P.dump("dbg_ps2", dbgp2[:], [128, 512], F32, [b_dbgp2])
                        P.dump("dbg_wvb", wvb[:, :, 0:512], [128, 8, 512], BF16, b_wv)
                    P.op("act", lambda e, ps=ps, cc=cc: e.activation(out=vg[:, cc * 512:(cc + 1) * 512], in_=ps[:, :], func=AF.Gelu),
                         reads=[bps], writes=[b_vg[cc]])
                P.op("dve", lambda e: e.tensor_tensor(out=sqv[:], in0=vg[:], in1=vg[:], op=ALU.mult), reads=b_vg, writes=[b_sqv])
                P.op("dve", lambda e: e.reduce_sum(out=ssv[:], in_=sqv[:], axis=AX.X), reads=[b_sqv], writes=[b_ssv])
                P.op("act", lambda e: e.activation(out=ssv[:], in_=ssv[:], func=AF.Sqrt, bias=C.eps[:, 0:1], scale=1.0 / 3072.0),
                     reads=[b_ssv, C.cb], writes=[b_ssv])
                P.op("dve", lambda e: e.reciprocal(out=ssv[:], in_=ssv[:]), reads=[b_ssv], writes=[b_ssv])
                P.op("dve", lambda e: e.scalar_tensor_tensor(out=vn[:], in0=vg[:], scalar=ssv[:, 0:1], in1=gainB[:],
                                                               op0=ALU.mult, op1=ALU.mult),
                     reads=b_vg + [b_ssv, b_gain], writes=[b_vn])
                if ti == 0:
                    P.dump("dbg_vg%d" % w, vg[:], [128, 3072], F32, b_vg)
                    P.dump("dbg_ssv%d" % w, ssv[:], [128, 1], F32, [b_ssv])
                    P.dump("dbg_vn%d" % w, vn[:], [128, 3072], BF16, [b_vn])
                    if w == 0:
                        P.dump("dbg_hT", N.hT[:], [128, 8, T], BF16, [N.b_hT])
                vm, bvm = vmT[wi % 2], b_vmT[wi % 2]
                for q in range(6):
                    ps, bps = mring.next()
                    for j in range(4):
                        fc = q * 4 + j
                        g = fc // 3
                        P.op("pe", lambda e, ps=ps, j=j, fc=fc, g=g: e.matmul(
                            ps[:, j * 128:(j + 1) * 128], lhsT=vn[:, fc * 128:(fc + 1) * 128], rhs=wsT[:, g, :],
                            start=True, stop=True), reads=[b_vn, b_ws], writes=[bps])
                    P.op("dve", lambda e, ps=ps, q=q, vm=vm: e.tensor_tensor(
                        out=vm[:, q * 4:(q + 1) * 4, :], in0=ps[:, :].rearrange("p (a n) -> p a n", a=4),
                        in1=bs[:, q * 4:(q + 1) * 4, :], op=ALU.add), reads=[bps, b_bs], writes=[bvm])
                tw = t0 + w * 128
                P.dma(vmv[:, :, tw:tw + 128], vm[:], reads=[bvm], writes=[b_vmd])
                wi += 1
    emit_mixB2(C, xv, xov, vmv, winv, wout_d, T_core, T)


def emit_mixB2(C, xv, xov, vmv, winv, wout_d, T_core, T):
    P = C.P
    AB = C.AB
    with P.scope():
        wub = P.sb("wub", [128, 8, 3072], BF16); b_wu = P.bufs(8)
        for kc in range(8):
            load_cast(C, wub[:, kc, :], b_wu[kc], winv[:, kc, 0:3072], [3072])
        wob = P.sb("wob", [128, 24, 1024], BF16); b_wo = P.bufs(6)
        wov = wout_d.rearrange("(fc p) n -> p fc n", p=128)
        for g in range(6):
            load_cast(C, wob[:, g * 4:(g + 1) * 4, :], b_wo[g], wov[:, g * 4:(g + 1) * 4, :], [4, 1024])
        xts = [P.sb("xt%d" % i, [128, 8, T], F32) for i in range(2)]; b_xts = P.bufs(2)
        N = NormBufs(C, T)
        vmt = P.sb("vmt", [128, 24, T], BF16); b_vmt = P.buf()
        gT = P.sb("gT", [128, 24, T], BF16); b_gT = P.bufs(24)
        ut = [P.sb("ut%d" % i, [128, T], F32) for i in range(2)]; b_ut = P.bufs(2)
        uring = Ring(list(zip(C.psums[1:5], C.psum_b[1:5])))
        yring = Ring(list(zip(C.psums[5:8], C.psum_b[5:8])))
        b_xo = P.buf()
        for ti in range(T_core // T):
            xt, b_xt = xts[ti % 2], b_xts[ti % 2]
            t0 = ti * T
            P.dma(xt[:], xv[:, :, t0:t0 + T], writes=[b_xt])
            P.dma(vmt[:], vmv[:, :, t0:t0 + T], writes=[b_vmt])
            emit_norm_mod(C, N, xt, b_xt, 0, 8)
            for fc in range(24):
                ps, bps = uring.next()
                for kc in range(8):
                    P.op("pe", lambda e, kc=kc, fc=fc, ps=ps: e.matmul(
                        ps[:, :T], lhsT=wub[:, kc, fc * 128:(fc + 1) * 128], rhs=N.hT[:, kc, :],
                        start=(kc == 0), stop=(kc == 7)), reads=[b_wu[kc], N.b_hT], writes=[bps])
                u, bu = ut[fc % 2], b_ut[fc % 2]
                P.op("act", lambda e, ps=ps, u=u: e.activation(out=u[:], in_=ps[:, :T], func=AF.Gelu), reads=[bps], writes=[bu])
                eng = "pool" if fc % 2 == 0 else "dve"
                P.op(eng, lambda e, u=u, fc=fc: e.tensor_tensor(out=gT[:, fc, :], in0=u[:], in1=vmt[:, fc, :], op=ALU.mult),
                     reads=[bu, b_vmt], writes=[b_gT[fc]])
            for oc in range(8):
                ps, bps = yring.next()
                for fc in range(24):
                    P.op("pe", lambda e, oc=oc, fc=fc, ps=ps: e.matmul(
                        ps[:, :T], lhsT=wob[:, fc, oc * 128:(oc + 1) * 128], rhs=gT[:, fc, :],
                        start=(fc == 0), stop=(fc == 23)), reads=[b_wo[fc // 4], b_gT[fc]], writes=[bps])
                P.op("dve", lambda e, oc=oc, ps=ps, xt=xt: e.scalar_tensor_tensor(
                    out=xt[:, oc, :], in0=ps[:, :T], scalar=AB[:, 16 + oc:17 + oc], in1=xt[:, oc, :],
                    op0=ALU.mult, op1=ALU.add), reads=[bps, C.b_ab, b_xt], writes=[b_xt])
            P.dma(xov[:, :, t0:t0 + T], xt[:], reads=[b_xt], writes=[b_xo])


def _lay_common(d, L, b):
    return {
        "cT": np.ascontiguousarray(d["c"][b].reshape(8, 128).T),
        "ada_w": np.ascontiguousarray(d["ada_w"][L]),
        "ada_b": np.ascontiguousarray(d["ada_b"][L].reshape(48, 128).T),
        "gains": np.ascontiguousarray(np.concatenate([d["norm_mix"][L].reshape(8, 128), d["norm_ffn"][L].reshape(8, 128)], 0).T),
    }


def _decl_common(nc):
    cT_d = nc.dram_tensor("cT", [128, 8], F32, kind="ExternalInput").ap()
    adaw_d = nc.dram_tensor("ada_w", [1024, 6144], F32, kind="ExternalInput").ap()
    adab_d = nc.dram_tensor("ada_b", [128, 48], F32, kind="ExternalInput").ap()
    gains_d = nc.dram_tensor("gains", [128, 16], F32, kind="ExternalInput").ap()
    return cT_d, adaw_d, adab_d, gains_d


def build_L1(T_core=4096):
    nc = bass.Bass("TRN2", target_bir_lowering=False)
    xT_d = nc.dram_tensor("xT", [1024, T_core], F32, kind="ExternalInput").ap()
    cm = _decl_common(nc)
    w_in_d = nc.dram_tensor("b_w_in", [1024, 6144], F32, kind="ExternalInput").ap()
    vgain_d = nc.dram_tensor("b_vgain", [128, 3072], F32, kind="ExternalInput").ap()
    wsT_d = nc.dram_tensor("b_wsT", [128, 8, 128], F32, kind="ExternalInput").ap()
    bs_d = nc.dram_tensor("b_bs", [128, 24, 128], F32, kind="ExternalInput").ap()
    wout_d = nc.dram_tensor("b_w_out", [3072, 1024], F32, kind="ExternalInput").ap()
    w1_d = nc.dram_tensor("w1", [1024, 4096], F32, kind="ExternalInput").ap()
    w2_d = nc.dram_tensor("w2", [4096, 1024], F32, kind="ExternalInput").ap()
    yT_d = nc.dram_tensor("yT", [1024, T_core], F32, kind="ExternalOutput").ap()
    vm_d = nc.dram_tensor("vm_s", [24, 128, T_core], BF16, kind="ExternalOutput" if DEBUG else "Internal").ap()
    xs_d = nc.dram_tensor("xs_s", [1024, T_core], F32).ap()
    C = make_ctx(nc)
    emit_ada(C, *cm)
    emit_mixB(C, xT_d, xs_d, w_in_d, vgain_d, wsT_d, bs_d, wout_d, vm_d, T_core)
    emit_ffn(C, xs_d, yT_d, w1_d, w2_d, T_core)
    C.P.emit()
    return nc


def lay_L1(d, b):
    m = _lay_common(d, 1, b)
    m.update({
        "b_w_in": np.ascontiguousarray(d["b_w_in"][0]),
        "b_vgain": np.ascontiguousarray(np.broadcast_to(d["b_v_gain"][0][None, :], (128, 3072))),
        "b_wsT": np.ascontiguousarray(d["b_w_s"][0].transpose(2, 0, 1)),
        "b_bs": np.ascontiguousarray(np.broadcast_to(np.repeat(d["b_b_s"][0], 3, axis=0)[None], (128, 24, 128))),
        "b_w_out": np.ascontiguousarray(d["b_w_out"][0]),
        "w1": np.ascontiguousarray(d["ffn_w1"][1]), "w2": np.ascontiguousarray(d["ffn_w2"][1]),
    })
    return m


HALO = 512


def emit_mixA1(C, xT_d, w_in_d, gqk_d, qT_s, kT_s, v_s, T_core, T=256):
    P = C.P
    TT = T_core + HALO
    xv = xT_d.rearrange("(kc p) t -> p kc t", p=128)
    winv = w_in_d.rearrange("(kc p) f -> p kc f", p=128)
    qv = qT_s.rearrange("(oc p) t -> p oc t", p=128)
    kv = kT_s.rearrange("(oc p) t -> p oc t", p=128)
    with P.scope():
        wq = P.sb("wq", [128, 8, 3072], BF16); b_wq = P.bufs(8)
        for kc in range(8):
            load_cast(C, wq[:, kc, :], b_wq[kc], winv[:, kc, :], [3072])
        gqk = P.sb("gqk", [128, 2], F32); b_g = P.buf()
        P.dma(gqk[:], gqk_d, writes=[b_g])
        P.op("dve", lambda e: e.tensor_scalar_mul(out=gqk[:, 0:1], in0=gqk[:, 0:1], scalar1=0.125), reads=[b_g], writes=[b_g])
        bd = P.sb("bd", [128, 128], BF16); b_bd = P.buf()

        P.op("pool", lambda e: e.memset(bd[:], 0.0), writes=[b_bd])
        P.op("pool", lambda e: e.memset(bd[0:64, 0:64], 1.0 / 64.0), writes=[b_bd])
        P.op("pool", lambda e: e.memset(bd[64:128, 64:128], 1.0 / 64.0), writes=[b_bd])
        xts = [P.sb("xt%d" % i, [128, 8, T], F32) for i in range(2)]; b_xts = P.bufs(2)
        N = NormBufs(C, T)
        sqq = P.sb("sqq", [128, T], BF16); b_sqq = P.buf()
        rs = P.sb("rs", [128, T], F32); b_rs = P.buf()
        qk_o = [P.sb("qko%d" % i, [128, 8, T], BF16) for i in range(2)]; b_qko = P.bufs(2)
        vpad = [P.sb("vpad%d" % i, [128, 16, 128], BF16) for i in range(2)]; b_vpad = P.bufs(2)
        for i in range(2):
            P.op("pool", lambda e, i=i: e.memset(vpad[i][:], 0.0), writes=[b_vpad[i]])
        b_sc = P.buf()
        pring = Ring(list(zip(C.psums[1:4], C.psum_b[1:4])))
        sring = Ring(list(zip(C.psums[4:6], C.psum_b[4:6])))
        vring = Ring(list(zip(C.psums[6:8], C.psum_b[6:8])))
        vi = 0
        for ti in range(TT // T):
            xt, b_xt = xts[ti % 2], b_xts[ti % 2]
            t0 = ti * T
            P.dma(xt[:], xv[:, :, t0:t0 + T], writes=[b_xt])
            emit_norm_mod(C, N, xt, b_xt, 0, 8)
            for which in range(2):
                if which == 0 and t0 + T <= HALO:
                    continue
                qo, bqo = qk_o[which], b_qko[which]
                for oc in range(8):
                    ps, bps = pring.next()
                    col0 = which * 1024 + oc * 128
                    for kc in range(8):
                        P.op("pe", lambda e, kc=kc, ps=ps, col0=col0: e.matmul(
                            ps[:, :T], lhsT=wq[:, kc, col0:col0 + 128], rhs=N.hT[:, kc, :], start=(kc == 0), stop=(kc == 7)),
                            reads=[b_wq[kc], N.b_hT], writes=[bps])
                    P.op("act", lambda e, ps=ps: e.activation(out=sqq[:], in_=ps[:, :T], func=AF.Square), reads=[bps], writes=[b_sqq])
                    ps2, bps2 = sring.next()
                    P.op("pe", lambda e, ps2=ps2: e.matmul(ps2[:, :T], lhsT=bd[:], rhs=sqq[:], start=True, stop=True),
                         reads=[b_sqq, b_bd], writes=[bps2])
                    P.op("act", lambda e, ps2=ps2: e.activation(out=rs[:], in_=ps2[:, :T], func=AF.Sqrt, bias=C.eps[:, 0:1], scale=1.0),
                         reads=[bps2, C.cb], writes=[b_rs])
                    P.op("dve", lambda e: e.reciprocal(out=rs[:], in_=rs[:]), reads=[b_rs], writes=[b_rs])
                    P.op("dve", lambda e, ps=ps, oc=oc, qo=qo, which=which: e.scalar_tensor_tensor(
                        out=qo[:, oc, :], in0=ps[:, :T], scalar=gqk[:, which:which + 1], in1=rs[:], op0=ALU.mult, op1=ALU.mult),
                        reads=[bps, b_rs, b_g], writes=[bqo])
                dst = qv if which == 0 else kv
                if which == 0:
                    P.dma(dst[:, :, t0 - HALO:t0 - HALO + T], qo[:], reads=[bqo], writes=[b_sc])
                else:
                    P.dma(dst[:, :, t0:t0 + T], qo[:], reads=[bqo], writes=[b_sc])
            for w in range(T // 128):
                vp, bvp = vpad[vi % 2], b_vpad[vi % 2]
                vi += 1
                for half in range(2):
                    ps, bps = vring.next()
                    for kc in range(8):
                        P.op("pe", lambda e, kc=kc, ps=ps, w=w, half=half: e.matmul(
                            ps[:, :], lhsT=N.hT[:, kc, w * 128:(w + 1) * 128], rhs=wq[:, kc, 2048 + half * 512:2048 + (half + 1) * 512],
                            start=(kc == 0), stop=(kc == 7)), reads=[b_wq[kc], N.b_hT], writes=[bps])
                    psv = ps[:, :].rearrange("p (hp two d) -> p hp two d", hp=4, two=2)
                    vpv = vp[:, half * 8:(half + 1) * 8, :].rearrange("p (hp two) c -> p hp two c", two=2)
                    P.op("act", lambda e, psv=psv, vpv=vpv: e.activation(out=vpv[:, :, 0, 0:64], in_=psv[:, :, 0, :], func=AF.Identity),
                         reads=[bps], writes=[bvp])
                    P.op("dve", lambda e, psv=psv, vpv=vpv: e.tensor_copy(out=vpv[:, :, 1, 64:128], in_=psv[:, :, 1, :]),
                         reads=[bps], writes=[bvp])
                tw = t0 + w * 128
                P.dma(v_s[tw:tw + 128, :].rearrange("t (h c) -> t h c", h=16), vp[:], reads=[bvp], writes=[b_sc])


def emit_mixA2(C, xT_d, xo_d, qT_s, kT_s, v_s, biasT_d, valid_d, wout_d, T_core):
    P = C.P
    AB = C.AB
    xv = xT_d.rearrange("(kc p) t -> p kc t", p=128)
    xov = xo_d.rearrange("(kc p) t -> p kc t", p=128)
    qv = qT_s.rearrange("(oc p) t -> p oc t", p=128)
    kv = kT_s.rearrange("(oc p) t -> p oc t", p=128)
    with P.scope():
        wob = P.sb("wob", [128, 8, 1024], BF16); b_wo = P.bufs(2)
        wov = wout_d.rearrange("(fc p) n -> p fc n", p=128)
        for g in range(2):
            load_cast(C, wob[:, g * 4:(g + 1) * 4, :], b_wo[g], wov[:, g * 4:(g + 1) * 4, :], [4, 1024])
        ebias = P.sb("ebias", [128, 16, 640], F32); b_eb = P.buf()
        for g in range(4):
            P.dma(ebias[:, g * 4:(g + 1) * 4, :], biasT_d[:, g * 4:(g + 1) * 4, :], writes=[b_eb])
        P.op("act", lambda e: e.activation(out=ebias[:], in_=ebias[:], func=AF.Exp), reads=[b_eb], writes=[b_eb])
        nkb = (T_core + HALO) // 128
        valid = P.sb("valid", [128, nkb], F32); b_val = P.buf()
        P.dma(valid[:], valid_d, writes=[b_val])
        selA = P.sb("selA", [128, 128], BF16); selB = P.sb("selB", [128, 128], BF16); b_sel = P.buf()

        P.op("pool", lambda e: e.memset(selA[:], 0.0), writes=[b_sel])
        P.op("pool", lambda e: e.memset(selB[:], 0.0), writes=[b_sel])
        P.op("pool", lambda e: e.memset(selA[:, 0:64], 1.0), writes=[b_sel])
        P.op("pool", lambda e: e.memset(selB[:, 64:128], 1.0), writes=[b_sel])
        NB = 2
        qt = [P.sb("qt%d" % i, [128, 8, 128], BF16) for i in range(NB)]; b_qt = P.bufs(NB)
        kt = [P.sb("kt%d" % i, [128, 8, 640], BF16) for i in range(NB)]; b_kt = P.bufs(NB)
        vt = [P.sb("vt%d" % i, [128, 5, 2048], BF16) for i in range(NB)]; b_vt = P.bufs(NB)
        xts = [P.sb("xa%d" % i, [128, 8, 128], F32) for i in range(NB)]; b_xts = P.bufs(NB)
        et = [P.sb("et%d" % i, [128, 640], F32) for i in range(2)]; b_et = P.bufs(2)
        pt = [P.sb("pt%d" % i, [128, 640], BF16) for i in range(2)]; b_pt = P.bufs(2)
        oT = P.sb("oT", [128, 8, 128], BF16); b_oT = P.bufs(8)
        den = P.sb("den", [128, 128], F32); b_den = P.buf()
        b_xo = P.buf()
        sc = [(C.psums[1], C.psum_b[1], C.psums[2], C.psum_b[2]), (C.psums[3], C.psum_b[3], C.psums[4], C.psum_b[4])]
        po, b_po = C.psums[5], C.psum_b[5]
        pd, b_pd = C.psums[6], C.psum_b[6]
        py, b_py = C.psums[7], C.psum_b[7]
        def make_item(hi, i, m, pr, ab, kb0):
            h = pr * 2 + ab
            sA, bsA, sB, bsB = sc[hi % 2]
            e_, be = et[hi % 2], b_et[hi % 2]
            p_, bp = pt[hi % 2], b_pt[hi % 2]
            lo = ab * 64
            sel = selA if ab == 0 else selB

            def stage_a():
                for j in range(5):
                    dst, bd_ = (sA, bsA) if j < 4 else (sB, bsB)
                    c0 = (j % 4) * 128
                    P.op("pe", lambda e, j=j, dst=dst, c0=c0: e.matmul(
                        dst[:, c0:c0 + 128], lhsT=kt[i][lo:lo + 64, pr, j * 128:(j + 1) * 128], rhs=qt[i][lo:lo + 64, pr, :],
                        start=True, stop=True), reads=[b_kt[i], b_qt[i]], writes=[bd_])
                P.op("act", lambda e: e.activation(out=e_[:, 0:512], in_=sA[:, :], func=AF.Exp), reads=[bsA], writes=[be])
                P.op("act", lambda e: e.activation(out=e_[:, 512:640], in_=sB[:, 0:128], func=AF.Exp), reads=[bsB], writes=[be])
                for j in range(5):
                    P.op("dve", lambda e, j=j: e.scalar_tensor_tensor(
                        out=p_[:, j * 128:(j + 1) * 128], in0=e_[:, j * 128:(j + 1) * 128], scalar=valid[:, kb0 + j:kb0 + j + 1],
                        in1=ebias[:, h, j * 128:(j + 1) * 128], op0=ALU.mult, op1=ALU.mult),
                        reads=[be, b_val, b_eb], writes=[bp])

            def stage_b():
                for j in range(5):
                    first = (ab == 0 and j == 0)
                    last = (ab == 1 and j == 4)
                    P.op("pe", lambda e, j=j, first=first, last=last: e.matmul(
                        po[:, 0:128], lhsT=vt[i][:, j, h * 128:(h + 1) * 128], rhs=p_[:, j * 128:(j + 1) * 128],
                        start=first, stop=last), reads=[b_vt[i], bp], writes=[b_po])
                    P.op("pe", lambda e, j=j, first=first, last=last: e.matmul(
                        pd[:, 0:128], lhsT=sel[:], rhs=p_[:, j * 128:(j + 1) * 128],
                        start=first, stop=last), reads=[b_sel, bp], writes=[b_pd])
                if ab == 1:
                    P.op("dve", lambda e: e.reciprocal(out=den[:], in_=pd[:, 0:128]), reads=[b_pd], writes=[b_den])
                    P.op("dve", lambda e: e.tensor_tensor(out=oT[:, pr, :], in0=po[:, 0:128], in1=den[:], op=ALU.mult),
                         reads=[b_po, b_den], writes=[b_oT[pr]])
                if ab == 1 and pr == 7:
                    xt, b_xt = xts[i], b_xts[i]
                    q0 = m * 128
                    for oc in range(8):
                        for pr2 in range(8):
                            P.op("pe", lambda e, oc=oc, pr2=pr2: e.matmul(
                                py[:, (oc % 4) * 128:(oc % 4) * 128 + 128],
                                lhsT=wob[:, pr2, oc * 128:(oc + 1) * 128], rhs=oT[:, pr2, :],
                                start=(pr2 == 0), stop=(pr2 == 7)), reads=[b_wo[pr2 // 4], b_oT[pr2]], writes=[b_py])
                        P.op("dve", lambda e, oc=oc: e.scalar_tensor_tensor(
                            out=xt[:, oc, :], in0=py[:, (oc % 4) * 128:(oc % 4) * 128 + 128], scalar=AB[:, 16 + oc:17 + oc], in1=xt[:, oc, :],
                            op0=ALU.mult, op1=ALU.add), reads=[b_py, C.b_ab, b_xt], writes=[b_xt])
                    P.dma(xov[:, :, q0:q0 + 128], xt[:], reads=[b_xt], writes=[b_xo])
            return stage_a, stage_b

        pending = None
        hi = 0
        for m in range(T_core // 128):
            i = m % NB
            q0 = m * 128
            k0 = m * 128
            P.dma(qt[i][:], qv[:, :, q0:q0 + 128], writes=[b_qt[i]])
            P.dma(kt[i][:], kv[:, :, k0:k0 + 640], writes=[b_kt[i]])
            P.dma(vt[i][:], v_s[k0:k0 + 640, :].rearrange("(j p) c -> p j c", p=128), writes=[b_vt[i]])
            P.dma(xts[i][:], xv[:, :, HALO + q0:HALO + q0 + 128], writes=[b_xts[i]])
            kb0 = k0 // 128
            for pr in range(8):
                for ab in range(2):
                    sa, sb_ = make_item(hi, i, m, pr, ab, kb0)
                    hi += 1
                    sa()
                    if pending is not None:
                        pending()
                    pending = sb_
        pending()


def build_L0(T_core=4096):
    nc = bass.Bass("TRN2", target_bir_lowering=False)
    TT = T_core + HALO
    xT_d = nc.dram_tensor("xT", [1024, TT], F32, kind="ExternalInput").ap()
    cm = _decl_common(nc)
    w_in_d = nc.dram_tensor("a_w_in", [1024, 3072], F32, kind="ExternalInput").ap()
    gqk_d = nc.dram_tensor("a_gqk", [128, 2], F32, kind="ExternalInput").ap()
    biasT_d = nc.dram_tensor("a_biasT", [128, 16, 640], F32, kind="ExternalInput").ap()
    valid_d = nc.dram_tensor("a_valid", [128, TT // 128], F32, kind="ExternalInput").ap()
    wout_d = nc.dram_tensor("a_w_out", [1024, 1024], F32, kind="ExternalInput").ap()
    w1_d = nc.dram_tensor("w1", [1024, 4096], F32, kind="ExternalInput").ap()
    w2_d = nc.dram_tensor("w2", [4096, 1024], F32, kind="ExternalInput").ap()
    yT_d = nc.dram_tensor("yT", [1024, T_core], F32, kind="ExternalOutput").ap()
    qT_s = nc.dram_tensor("qT_s", [1024, T_core], BF16).ap()
    kT_s = nc.dram_tensor("kT_s", [1024, TT], BF16).ap()
    v_s = nc.dram_tensor("v_s", [TT, 2048], BF16).ap()
    xs_d = nc.dram_tensor("xs_s", [1024, T_core], F32).ap()
    C = make_ctx(nc)
    emit_ada(C, *cm)
    emit_mixA1(C, xT_d, w_in_d, gqk_d, qT_s, kT_s, v_s, T_core)
    emit_mixA2(C, xT_d, xs_d, qT_s, kT_s, v_s, biasT_d, valid_d, wout_d, T_core)
    emit_ffn(C, xs_d, yT_d, w1_d, w2_d, T_core)
    C.P.emit()
    return nc


def _a_bias_table(rel_bias):
    kap = np.arange(640)[:, None]
    q = np.arange(128)[None, :]
    rel = q + 512 - kap
    cq = q // 64
    inband = (kap >= cq * 64) & (kap < cq * 64 + 576)
    idx = np.clip(rel, -63, 256) + 63
    tab = rel_bias[:, idx]
    tab = np.where(inband[None], tab, np.float32(-30000.0)).astype(np.float32)
    tab = tab.reshape(16, 5, 128, 128).transpose(2, 0, 1, 3).reshape(128, 16, 640)
    return np.ascontiguousarray(tab)


def lay_L0(d, b, half, T_core=4096, x=None):
    m = _lay_common(d, 0, b)
    x = d["x"] if x is None else x
    t0 = half * T_core
    TT = T_core + HALO
    xt = np.zeros((1024, TT), np.float32)
    lo = t0 - HALO
    if lo >= 0:
        xt[:, :] = x[b, lo:lo + TT, :].T
    else:
        xt[:, HALO:] = x[b, 0:T_core, :].T
    valid = np.ones((128, TT // 128), np.float32)
    if lo < 0:
        valid[:, :HALO // 128] = 0.0
    m.update({
        "xT": xt,
        "a_w_in": np.ascontiguousarray(d["a_w_in"][0]),
        "a_gqk": np.ascontiguousarray(np.stack([np.tile(d["a_q_gain"][0], 2), np.tile(d["a_k_gain"][0], 2)], 1)),
        "a_biasT": _a_bias_table(d["a_rel_bias"][0]),
        "a_valid": valid,
        "a_w_out": np.ascontiguousarray(d["a_w_out"][0]),
        "w1": np.ascontiguousarray(d["ffn_w1"][0]), "w2": np.ascontiguousarray(d["ffn_w2"][0]),
    })
    return m


def emit_mixD1(C, xT_d, wqkv_d, qT_s, kT_s, v_s, S, T=256, xsrc=None):
    P = C.P
    if xsrc is None:
        xv0 = xT_d.rearrange("(kc p) t -> p kc t", p=128)
        xsrc = lambda t0, T: xv0[:, :, t0:t0 + T]
    wv_ = wqkv_d.rearrange("(kc p) f -> p kc f", p=128)
    qv = qT_s.rearrange("(oc p) t -> p oc t", p=128)
    kv = kT_s.rearrange("(oc p) t -> p oc t", p=128)
    with P.scope():
        wq = P.sb("wq", [128, 8, 1536], BF16); b_wq = P.bufs(8)
        for kc in range(8):
            load_cast(C, wq[:, kc, :], b_wq[kc], wv_[:, kc, :], [1536])
        xts = [P.sb("xt%d" % i, [128, 8, T], F32) for i in range(2)]; b_xts = P.bufs(2)
        N = NormBufs(C, T)
        qk_o = [P.sb("qko%d" % i, [128, 4, T], BF16) for i in range(2)]; b_qko = P.bufs(2)
        vpad = [P.sb("vpad%d" % i, [128, 8, 128], BF16) for i in range(2)]; b_vpad = P.bufs(2)
        for i in range(2):
            P.op("pool", lambda e, i=i: e.memset(vpad[i][:], 0.0), writes=[b_vpad[i]])
        b_sc = P.buf()
        pring = Ring(list(zip(C.psums[1:5], C.psum_b[1:5])))
        vring = Ring(list(zip(C.psums[5:8], C.psum_b[5:8])))
        vi = 0
        for ti in range(S // T):
            xt, b_xt = xts[ti % 2], b_xts[ti % 2]
            t0 = ti * T
            P.dma(xt[:], xsrc(t0, T), writes=[b_xt])
            emit_norm_mod(C, N, xt, b_xt, 0, 8)
            for which in range(2):
                qo, bqo = qk_o[which], b_qko[which]
                for oc in range(4):
                    ps, bps = pring.next()
                    col0 = which * 512 + oc * 128
                    for kc in range(8):
                        P.op("pe", lambda e, kc=kc, ps=ps, col0=col0: e.matmul(
                            ps[:, :T], lhsT=wq[:, kc, col0:col0 + 128], rhs=N.hT[:, kc, :], start=(kc == 0), stop=(kc == 7)),
                            reads=[b_wq[kc], N.b_hT], writes=[bps])
                    sc = 0.125 if which == 0 else 1.0
                    P.op("act", lambda e, ps=ps, oc=oc, qo=qo, sc=sc: e.activation(out=qo[:, oc, :], in_=ps[:, :T], func=AF.Identity, scale=sc),
                         reads=[bps], writes=[bqo])
                dst = qv if which == 0 else kv
                P.dma(dst[:, :, t0:t0 + T], qo[:], reads=[bqo], writes=[b_sc])
            for w in range(T // 128):
                vp, bvp = vpad[vi % 2], b_vpad[vi % 2]
                vi += 1
                ps, bps = vring.next()
                for kc in range(8):
                    P.op("pe", lambda e, kc=kc, ps=ps, w=w: e.matmul(
                        ps[:, :], lhsT=N.hT[:, kc, w * 128:(w + 1) * 128], rhs=wq[:, kc, 1024:1536],
                        start=(kc == 0), stop=(kc == 7)), reads=[b_wq[kc], N.b_hT], writes=[bps])
                psv = ps[:, :].rearrange("p (hp two d) -> p hp two d", hp=4, two=2)
                vpv = vp[:].rearrange("p (hp two) c -> p hp two c", two=2)
                P.op("act", lambda e, psv=psv, vpv=vpv: e.activation(out=vpv[:, :, 0, 0:64], in_=psv[:, :, 0, :], func=AF.Identity),
                     reads=[bps], writes=[bvp])
                P.op("dve", lambda e, psv=psv, vpv=vpv: e.tensor_copy(out=vpv[:, :, 1, 64:128], in_=psv[:, :, 1, :]),
                     reads=[bps], writes=[bvp])
                tw = t0 + w * 128
                P.dma(v_s[tw:tw + 128, :].rearrange("t (h c) -> t h c", h=8), vp[:], reads=[bvp], writes=[b_sc])


def emit_mixD2(C, qT_s, kT_s, v_s, cst_d, oT_d, S, TQ=512, odst=None):
    P = C.P
    qv = qT_s.rearrange("(oc p) t -> p oc t", p=128)
    kv = kT_s.rearrange("(oc p) t -> p oc t", p=128)
    if odst is None:
        ov = oT_d.rearrange("(oc p) t -> p oc t", p=128)
        odst = lambda t0, n: ov[:, :, t0:t0 + n]
    with P.scope():
        cst = P.sb("cst", [128, 3, 128], F32); b_cst = P.buf()
        P.dma(cst[:], cst_d, writes=[b_cst])
        cbf = P.sb("cbf", [128, 3, 128], BF16); b_cbf = P.buf()
        P.op("dve", lambda e: e.tensor_copy(out=cbf[:], in_=cst[:]), reads=[b_cst], writes=[b_cbf])
        qt = [P.sb("qt%d" % i, [128, 4, TQ], BF16) for i in range(2)]; b_qt = P.bufs(2)
        NKB = 3
        kt = [P.sb("kt%d" % i, [128, 4, 128], BF16) for i in range(NKB)]; b_kt = P.bufs(NKB)
        vt = [P.sb("vt%d" % i, [128, 8, 128], BF16) for i in range(NKB)]; b_vt = P.bufs(NKB)
        NS = 3
        et = [P.sb("et%d" % i, [128, TQ], F32) for i in range(NS)]; b_et = P.bufs(NS)
        lt = [P.sb("lt%d" % i, [128, TQ], BF16) for i in range(NS)]; b_lt = P.bufs(NS)
        wt = [P.sb("wt%d" % i, [128, TQ], BF16) for i in range(NS)]; b_wt = P.bufs(NS)
        lacc = [P.sb("lacc%d" % i, [128, TQ], BF16) for i in range(8)]; b_lacc = P.bufs(8)
        osb = [P.sb("osb%d" % i, [128, 4, TQ], F32) for i in range(2)]; b_osb = P.bufs(2)
        b_od = P.buf()
        z1ring = Ring(list(zip(C.psums[0:2], C.psum_b[0:2])))
        z2ring = Ring(list(zip(C.psums[2:4], C.psum_b[2:4])))
        poring = [(C.psums[4 + i], C.psum_b[4 + i]) for i in range(4)]
        nq = TQ // 128

        def make_item(hi, q, bq, k_, bk, v_, bv, h, kb, dq, c0, first):
            pr, ab = h // 2, h % 2
            lo = ab * 64
            e_, be = et[hi % NS], b_et[hi % NS]
            l_, bl = lt[hi % NS], b_lt[hi % NS]
            w_, bw = wt[hi % NS], b_wt[hi % NS]
            la, bla = lacc[h], b_lacc[h]
            po, bpo = poring[pr]

            def stage_a():
                zp, bzp = z1ring.next()
                P.op("pe", lambda e: e.matmul(zp[:, c0:TQ], lhsT=k_[lo:lo + 64, pr, :], rhs=q[lo:lo + 64, pr, c0:TQ], start=True, stop=True),
                     reads=[bk, bq], writes=[bzp])
                P.op("act", lambda e: e.activation(out=e_[:, c0:TQ], in_=zp[:, c0:TQ], func=AF.Exp), reads=[bzp], writes=[be])
                P.op("act", lambda e: e.activation(out=l_[:, c0:TQ], in_=e_[:, c0:TQ], func=AF.Ln, bias=C.ones_f[:, 0:1], scale=1.0),
                     reads=[be, C.cb], writes=[bl])
                if dq >= 0:
                    P.op("dve", lambda e: e.tensor_tensor(out=l_[:, c0:c0 + 128], in0=l_[:, c0:c0 + 128], in1=cbf[:, 0, :], op=ALU.mult),
                         reads=[bl, b_cbf], writes=[bl])

            def stage_b():
                zq, bzq = z2ring.next()
                P.op("pe", lambda e: e.matmul(zq[:, c0:TQ], lhsT=k_[lo:lo + 64, pr, :], rhs=q[lo:lo + 64, pr, c0:TQ], start=True, stop=False),
                     reads=[bk, bq], writes=[bzq])
                P.op("pe", lambda e: e.matmul(zq[:, c0:TQ], lhsT=cbf[:, 1, :], rhs=l_[:, c0:TQ], start=False, stop=first),
                     reads=[bl, b_cbf], writes=[bzq])
                if not first:
                    P.op("pe", lambda e: e.matmul(zq[:, c0:TQ], lhsT=cbf[:, 2, :], rhs=la[:, c0:TQ], start=False, stop=True),
                         reads=[bla, b_cbf], writes=[bzq])
                if c0 > 0:
                    P.op("pool", lambda e: e.memset(w_[:, 0:c0], 0.0), writes=[bw])
                P.op("act", lambda e: e.activation(out=w_[:, c0:TQ], in_=zq[:, c0:TQ], func=AF.Exp), reads=[bzq], writes=[bw])
                if dq >= 0:
                    P.op("dve", lambda e: e.tensor_tensor(out=w_[:, c0:c0 + 128], in0=w_[:, c0:c0 + 128], in1=cbf[:, 0, :], op=ALU.mult),
                         reads=[bw, b_cbf], writes=[bw])
                if kb > 0:
                    if first:
                        _lacc_init(P, la, bla, l_, bl, c0, TQ)
                    else:
                        P.op("pool", lambda e: e.tensor_tensor(out=la[:, c0:TQ], in0=la[:, c0:TQ], in1=l_[:, c0:TQ], op=ALU.add),
                             reads=[bl, bla], writes=[bla])
                P.op("pe", lambda e: e.matmul(po[:, 0:TQ], lhsT=v_[:, h, :], rhs=w_[:, 0:TQ], start=(first and ab == 0), stop=(kb == 0 and ab == 1)),
                     reads=[bv, bw], writes=[bpo])
            return stage_a, stage_b

        def make_post(os_, bos, q0):
            def post():
                for pr in range(4):
                    po, bpo = poring[pr]
                    if pr % 2 == 0:
                        P.op("act", lambda e, po=po, pr=pr: e.activation(out=os_[:, pr, :], in_=po[:, :], func=AF.Identity), reads=[bpo], writes=[bos])
                    else:
                        P.op("dve", lambda e, po=po, pr=pr: e.tensor_copy(out=os_[:, pr, :], in_=po[:, :]), reads=[bpo], writes=[bos])
                P.dma(odst(q0, TQ), os_[:], reads=[bos], writes=[b_od])
            return post

        pending = None

        def push(stage_a, stage_b, post):
            nonlocal pending
            stage_a()
            if pending is not None:
                pending[0]()
                if pending[1] is not None:
                    pending[1]()
            pending = (stage_b, post)

        hi = 0
        ki = 0
        for qi in range(S // TQ):
            q, bq = qt[qi % 2], b_qt[qi % 2]
            q0 = qi * TQ
            P.dma(q[:], qv[:, :, q0:q0 + TQ], writes=[bq])
            os_, bos = osb[qi % 2], b_osb[qi % 2]
            kb_hi = qi * nq + nq - 1
            for kb in range(kb_hi, -1, -1):
                k_, bk = kt[ki % NKB], b_kt[ki % NKB]
                v_, bv = vt[ki % NKB], b_vt[ki % NKB]
                ki += 1
                P.dma(k_[:], kv[:, :, kb * 128:(kb + 1) * 128], writes=[bk])
                P.dma(v_[:], v_s[kb * 128:(kb + 1) * 128, :].rearrange("t (h c) -> t h c", h=8), writes=[bv])
                dq = kb - qi * nq
                c0 = max(dq, 0) * 128
                first = (kb == kb_hi)
                for h in range(8):
                    sa, sb_ = make_item(hi, q, bq, k_, bk, v_, bv, h, kb, dq, c0, first)
                    hi += 1
                    post = make_post(os_, bos, q0) if (kb == 0 and h == 7) else None
                    push(sa, sb_, post)
        pending[0]()
        if pending[1] is not None:
            pending[1]()


def _lacc_init(P, la, bla, l_, bl, c0, TQ):
    if c0 > 0:
        P.op("pool", lambda e: e.memset(la[:, 0:c0], 0.0), writes=[bla])
    P.op("pool", lambda e: e.tensor_copy(out=la[:, c0:TQ], in_=l_[:, c0:TQ]), reads=[bl], writes=[bla])


def emit_outproj(C, xT_d, oT_d, wout_d, xo_d, T_core, T=256, sel=None, xsrc=None, osrc=None):
    P = C.P
    AB = C.AB
    if xsrc is None:
        xv = xT_d.rearrange("(kc p) t -> p kc t", p=128)
        xsrc = lambda t0, T: xv[:, :, t0:t0 + T]
    if osrc is None:
        ov = oT_d.rearrange("(kc p) t -> p kc t", p=128)
        osrc = lambda h, t0, T: ov[:, :, h * T_core + t0:h * T_core + t0 + T]
    xov = xo_d.rearrange("(kc p) t -> p kc t", p=128)
    with P.scope():
        wob = P.sb("wob", [128, 8, 1024], BF16); b_wo = P.bufs(2)
        wov = wout_d.rearrange("(fc p) n -> p fc n", p=128)
        for g in range(2):
            load_cast(C, wob[:, g * 4:(g + 1) * 4, :], b_wo[g], wov[:, g * 4:(g + 1) * 4, :], [4, 1024])
        xts = [P.sb("xt%d" % i, [128, 8, T], F32) for i in range(2)]; b_xts = P.bufs(2)
        ots = [P.sb("ot%d" % i, [128, 8, T], F32) for i in range(2)]; b_ots = P.bufs(2)
        ob = P.sb("ob", [128, 8, T], BF16); b_ob = P.buf()
        if sel is not None:
            ots1 = [P.sb("ot1_%d" % i, [128, 8, T], F32) for i in range(2)]; b_ots1 = P.bufs(2)
        yring = Ring(list(zip(C.psums[1:8], C.psum_b[1:8])))
        b_xo = P.buf()
        for ti in range(T_core // T):
            xt, b_xt = xts[ti % 2], b_xts[ti % 2]
            ot, b_ot = ots[ti % 2], b_ots[ti % 2]
            t0 = ti * T
            P.dma(xt[:], xsrc(t0, T), writes=[b_xt])
            P.dma(ot[:], osrc(0, t0, T), writes=[b_ot])
            if sel is None:
                P.op("act", lambda e, ot=ot: e.activation(out=ob[:], in_=ot[:], func=AF.Identity), reads=[b_ot], writes=[b_ob])
            else:
                selt, b_sel = sel
                ot1, b_ot1 = ots1[ti % 2], b_ots1[ti % 2]
                P.dma(ot1[:], osrc(1, t0, T), writes=[b_ot1])
                P.op("act", lambda e, ot=ot: e.activation(out=ot[:], in_=ot[:], func=AF.Identity, scale=selt[:, 0:1]),
                     reads=[b_ot, b_sel], writes=[b_ot])
                P.op("dve", lambda e, ot=ot, ot1=ot1: e.scalar_tensor_tensor(out=ob[:], in0=ot1[:], scalar=selt[:, 1:2], in1=ot[:],
                                                                            op0=ALU.mult, op1=ALU.add),
                     reads=[b_ot, b_ot1, b_sel], writes=[b_ob])
            for oc in range(8):
                ps, bps = yring.next()
                for kc in range(8):
                    P.op("pe", lambda e, oc=oc, kc=kc, ps=ps: e.matmul(
                        ps[:, :T], lhsT=wob[:, kc, oc * 128:(oc + 1) * 128], rhs=ob[:, kc, :],
                        start=(kc == 0), stop=(kc == 7)), reads=[b_wo[kc // 4], b_ob], writes=[bps])
                P.op("dve", lambda e, oc=oc, ps=ps, xt=xt: e.scalar_tensor_tensor(
                    out=xt[:, oc, :], in0=ps[:, :T], scalar=AB[:, 16 + oc:17 + oc], in1=xt[:, oc, :],
                    op0=ALU.mult, op1=ALU.add), reads=[bps, C.b_ab, b_xt], writes=[b_xt])
            P.dma(xov[:, :, t0:t0 + T], xt[:], reads=[b_xt], writes=[b_xo])


def build_L3a(S=8192):
    nc = bass.Bass("TRN2", target_bir_lowering=False)
    xT_d = nc.dram_tensor("xT", [1024, S], F32, kind="ExternalInput").ap()
    cm = _decl_common(nc)
    wqkv_d = nc.dram_tensor("d_wqkv", [1024, 1536], F32, kind="ExternalInput").ap()
    cst_d = nc.dram_tensor("d_cst", [128, 3, 128], F32, kind="ExternalInput").ap()
    oT_d = nc.dram_tensor("oT", [512, S], F32, kind="ExternalOutput").ap()
    qT_s = nc.dram_tensor("qT_s", [512, S], BF16).ap()
    kT_s = nc.dram_tensor("kT_s", [512, S], BF16).ap()
    v_s = nc.dram_tensor("v_s", [S, 1024], BF16).ap()
    C = make_ctx(nc)
    emit_ada(C, *cm)
    emit_mixD1(C, xT_d, wqkv_d, qT_s, kT_s, v_s, S)
    emit_mixD2(C, qT_s, kT_s, v_s, cst_d, oT_d, S)
    C.P.emit()
    return nc


def _d_consts():
    j = np.arange(128)[:, None]
    s_ = np.arange(128)[None, :]
    mask = (s_ > j).astype(np.float32)
    ntri = -(j >= s_).astype(np.float32)
    nones = -np.ones((128, 128), np.float32)
    return np.ascontiguousarray(np.stack([mask, ntri, nones], 1))


def lay_L3a(d, b, hg, xT_full):
    m = _lay_common(d, 3, b)
    w = d["d_w_in"][0]
    m.update({
        "xT": xT_full,
        "d_wqkv": np.ascontiguousarray(np.concatenate([w[:, hg * 512:(hg + 1) * 512], w[:, 1024 + hg * 512:1024 + (hg + 1) * 512],
                                                       w[:, 2048 + hg * 512:2048 + (hg + 1) * 512]], 1)),
        "d_cst": _d_consts(),
    })
    return m


def build_Lb(T_core=4096):
    nc = bass.Bass("TRN2", target_bir_lowering=False)
    xT_d = nc.dram_tensor("xT", [1024, T_core], F32, kind="ExternalInput").ap()
    oT_d = nc.dram_tensor("oT", [1024, T_core], F32, kind="ExternalInput").ap()
    cm = _decl_common(nc)
    wout_d = nc.dram_tensor("w_out", [1024, 1024], F32, kind="ExternalInput").ap()
    w1_d = nc.dram_tensor("w1", [1024, 4096], F32, kind="ExternalInput").ap()
    w2_d = nc.dram_tensor("w2", [4096, 1024], F32, kind="ExternalInput").ap()
    yT_d = nc.dram_tensor("yT", [1024, T_core], F32, kind="ExternalOutput").ap()
    xs_d = nc.dram_tensor("xs_s", [1024, T_core], F32).ap()
    C = make_ctx(nc)
    emit_ada(C, *cm)
    emit_outproj(C, xT_d, oT_d, wout_d, xs_d, T_core)
    emit_ffn(C, xs_d, yT_d, w1_d, w2_d, T_core)
    C.P.emit()
    return nc


def lay_Lb(d, L, b, w_out):
    m = _lay_common(d, L, b)
    m.update({"w_out": np.ascontiguousarray(w_out), "w1": np.ascontiguousarray(d["ffn_w1"][L]), "w2": np.ascontiguousarray(d["ffn_w2"][L])})
    return m


def emit_mixC(C, xT_d, wc_d, wg_d, bg_d, og_d, cst_d, oT_d, S, T=256, xsrc=None, odst=None):
    P = C.P
    if xsrc is None:
        xv0 = xT_d.rearrange("(kc p) t -> p kc t", p=128)
        xsrc = lambda t0, T: xv0[:, :, t0:t0 + T]
    wv_ = wc_d.rearrange("(kc p) f -> p kc f", p=128)
    if odst is None:
        ov = oT_d.rearrange("(oc p) t -> p oc t", p=128)
        odst = lambda t0, n: ov[:, :, t0:t0 + n]
    with P.scope():
        wc = P.sb("wc", [128, 8, 1552], BF16); b_wc = P.bufs(8)
        for kc in range(8):
            load_cast(C, wc[:, kc, :], b_wc[kc], wv_[:, kc, :], [1552])
        wg = P.sb("wg", [16, 256], F32); bg = P.sb("bg", [1, 256], F32); og = P.sb("og", [128, 2], F32)
        cst = P.sb("cst", [128, 3, 128], F32)
        b_k = P.buf()
        P.dma(wg[:], wg_d, writes=[b_k]); P.dma(bg[:], bg_d, writes=[b_k]); P.dma(og[:], og_d, writes=[b_k])
        P.dma(cst[:], cst_d, writes=[b_k])
        xts = [P.sb("xt%d" % i, [128, 8, T], F32) for i in range(2)]; b_xts = P.bufs(2)
        N = NormBufs(C, T)
        q_sb = P.sb("q_sb", [128, 2, T], F32); b_q = P.bufs(2)
        k_sb = P.sb("k_sb", [128, 2, T], F32); b_kk = P.bufs(2)
        r_sb = P.sb("r_sb", [128, 4, T], F32); b_r = P.bufs(4)
        a_sb = P.sb("a_sb", [16, T], F32); b_a = P.buf()
        ktok = P.sb("ktok", [128, 256], F32); b_ktok = P.buf()
        vtok = P.sb("vtok", [128, 512], BF16); b_vtok = P.buf()
        st = [P.sb("st%d" % i, [128, 256], F32) for i in range(2)]; b_st = P.bufs(2)
        stb = [P.sb("stb%d" % i, [128, 256], BF16) for i in range(2)]; b_stb = P.bufs(2)
        for i in range(2):
            P.op("pool", lambda e, i=i: e.memset(st[i][:], 0.0), writes=[b_st[i]])
            P.op("pool", lambda e, i=i: e.memset(stb[i][:], 0.0), writes=[b_stb[i]])
        eg = P.sb("eg", [128, 128], F32); b_eg = P.buf()
        lp = P.sb("lp", [128, 128], F32); b_lp = P.buf()
        ep = P.sb("ep", [128, 128], F32); b_ep = P.buf()
        em = P.sb("em", [128, 128], F32); b_em = P.buf()
        er = P.sb("er", [128, 128], F32); b_er = P.buf()
        qd = P.sb("qd", [128, 128], BF16); b_qd = P.buf()
        kd = P.sb("kd", [128, 128], BF16); b_kd = P.buf()
        kdt = P.sb("kdt", [128, 128], BF16); b_kdt = P.buf()
        att = P.sb("att", [128, 128], BF16); b_att = P.buf()
        o_sb = P.sb("o_sb", [128, 2, 128], F32); b_o = P.bufs(2)
        osq = P.sb("osq", [128, 2, 128], BF16); b_osq = P.buf()
        ors = P.sb("ors", [128, 128], F32); b_ors = P.buf()
        ogt = [P.sb("ogt%d" % i, [128, 4, 128], F32) for i in range(2)]; b_ogt = P.bufs(2)
        b_od = P.buf()
        ringA = Ring(list(zip(C.psums[1:4], C.psum_b[1:4])))
        ringB = Ring(list(zip(C.psums[4:8], C.psum_b[4:8])))
        QS = 1.0 / (128.0 ** 0.5)
        bi = 0
        for ti in range(S // T):
            xt, b_xt = xts[ti % 2], b_xts[ti % 2]
            t0 = ti * T
            P.dma(xt[:], xsrc(t0, T), writes=[b_xt])
            emit_norm_mod(C, N, xt, b_xt, 0, 8)

            def proj(col0, m, dst_fn, tag):
                ps, bps = ringA.next()
                for kc in range(8):
                    P.op("pe", lambda e, kc=kc, ps=ps: e.matmul(ps[0:m, :T], lhsT=wc[:, kc, col0:col0 + m], rhs=N.hT[:, kc, :],
                                                              start=(kc == 0), stop=(kc == 7)), reads=[b_wc[kc], N.b_hT], writes=[bps])
                dst_fn(ps, bps)
            for h in range(2):
                proj(h * 128, 128, lambda ps, bps, h=h: P.op("act", lambda e: e.activation(out=q_sb[:, h, :], in_=ps[:, :T], func=AF.Identity, scale=QS),
                                                          reads=[bps], writes=[b_q[h]]), "q")
                proj(256 + h * 128, 128, lambda ps, bps, h=h: P.op("dve", lambda e: e.tensor_copy(out=k_sb[:, h, :], in_=ps[:, :T]),
                                                                reads=[bps], writes=[b_kk[h]]), "k")
            for j in range(4):
                proj(1024 + j * 128, 128, lambda ps, bps, j=j: P.op("act", lambda e: e.activation(out=r_sb[:, j, :], in_=ps[:, :T], func=AF.Silu),
                                                                 reads=[bps], writes=[b_r[j]]), "r")
            proj(1536, 16, lambda ps, bps: P.op("dve", lambda e: e.tensor_copy(out=a_sb[:, :], in_=ps[0:16, :T]), reads=[bps], writes=[b_a]), "a")
            for blk in range(T // 128):
                c0 = blk * 128
                og_t, b_og = ogt[bi % 2], b_ogt[bi % 2]
                bi += 1
                ps, bps = ringA.next()
                for kc in range(8):
                    P.op("pe", lambda e, kc=kc, ps=ps, c0=c0: e.matmul(ps[:, 0:256], lhsT=N.hT[:, kc, c0:c0 + 128], rhs=wc[:, kc, 256:512],
                                                                     start=(kc == 0), stop=(kc == 7)), reads=[b_wc[kc], N.b_hT], writes=[bps])
                P.op("dve", lambda e, ps=ps: e.tensor_copy(out=ktok[:], in_=ps[:, 0:256]), reads=[bps], writes=[b_ktok])
                ps, bps = ringA.next()
                for kc in range(8):
                    P.op("pe", lambda e, kc=kc, ps=ps, c0=c0: e.matmul(ps[:, 0:512], lhsT=N.hT[:, kc, c0:c0 + 128], rhs=wc[:, kc, 512:1024],
                                                                     start=(kc == 0), stop=(kc == 7)), reads=[b_wc[kc], N.b_hT], writes=[bps])
                P.op("act", lambda e, ps=ps: e.activation(out=vtok[:], in_=ps[:, 0:512], func=AF.Identity), reads=[bps], writes=[b_vtok])
                for h in range(2):
                    S_, bS = st[h], b_st[h]
                    Sb, bSb = stb[h], b_stb[h]
                    pg, bpg = ringB.next()
                    P.op("pe", lambda e, pg=pg, c0=c0, h=h: e.matmul(pg[:, 0:128], lhsT=a_sb[:, c0:c0 + 128], rhs=wg[:, h * 128:(h + 1) * 128],
                                                                   start=True, stop=False), reads=[b_a, b_k], writes=[bpg])
                    P.op("pe", lambda e, pg=pg, h=h: e.matmul(pg[:, 0:128], lhsT=C.ones_f[0:1, :], rhs=bg[0:1, h * 128:(h + 1) * 128],
                                                            start=False, stop=True), reads=[C.cb, b_k], writes=[bpg])
                    P.op("act", lambda e, pg=pg: e.activation(out=eg[:], in_=pg[:, 0:128], func=AF.Exp, scale=-1.0), reads=[bpg], writes=[b_eg])
                    P.op("act", lambda e: e.activation(out=lp[:], in_=eg[:], func=AF.Ln, bias=C.ones_f[:, 0:1], scale=1.0),
                         reads=[b_eg, C.cb], writes=[b_lp])
                    pb, bpb = ringB.next()
                    P.op("pe", lambda e, pb=pb: e.matmul(pb[:, 0:128], lhsT=lp[:], rhs=cst[:, 0, :], start=True, stop=True),
                         reads=[b_lp, b_k], writes=[bpb])
                    pr_, bpr = ringB.next()
                    P.op("pe", lambda e, pr_=pr_: e.matmul(pr_[:, 0:128], lhsT=cst[:, 1, :], rhs=lp[:], start=True, stop=True),
                         reads=[b_lp, b_k], writes=[bpr])
                    P.op("act", lambda e, pb=pb: e.activation(out=ep[:], in_=pb[:, 0:128], func=AF.Exp), reads=[bpb], writes=[b_ep])
                    P.op("act", lambda e, pb=pb: e.activation(out=em[:], in_=pb[:, 0:128], func=AF.Exp, scale=-1.0), reads=[bpb], writes=[b_em])
                    P.op("act", lambda e, pr_=pr_: e.activation(out=er[:], in_=pr_[:, 0:128], func=AF.Exp), reads=[bpr], writes=[b_er])
                    P.op("dve", lambda e, h=h, c0=c0: e.tensor_tensor(out=qd[:], in0=q_sb[:, h, c0:c0 + 128], in1=ep[:], op=ALU.mult),
                         reads=[b_q[h], b_ep], writes=[b_qd])
                    P.op("pool", lambda e, h=h, c0=c0: e.tensor_tensor(out=kd[:], in0=k_sb[:, h, c0:c0 + 128], in1=em[:], op=ALU.mult),
                         reads=[b_kk[h], b_em], writes=[b_kd])
                    P.op("dve", lambda e, h=h: e.tensor_tensor(out=kdt[:], in0=ktok[:, h * 128:(h + 1) * 128], in1=er[:], op=ALU.mult),
                         reads=[b_ktok, b_er], writes=[b_kdt])
                    pa, bpa = ringB.next()
                    P.op("pe", lambda e, pa=pa: e.matmul(pa[:, 0:128], lhsT=kd[:], rhs=qd[:], start=True, stop=True),
                         reads=[b_kd, b_qd], writes=[bpa])
                    P.op("dve", lambda e, pa=pa: e.tensor_tensor(out=att[:], in0=pa[:, 0:128], in1=cst[:, 2, :], op=ALU.mult),
                         reads=[bpa, b_k], writes=[b_att])
                    for ch in range(2):
                        r0 = ch * 64
                        po, bpo = ringB.next()
                        for vc in range(2):
                            P.op("pe", lambda e, po=po, vc=vc, Sb=Sb, r0=r0: e.matmul(
                                po[:, vc * 64:(vc + 1) * 64], lhsT=Sb[:, vc * 128:(vc + 1) * 128], rhs=qd[:, r0:r0 + 64], start=True, stop=False),
                                reads=[bSb, b_qd], writes=[bpo])
                            P.op("pe", lambda e, po=po, vc=vc, h=h, r0=r0: e.matmul(
                                po[:, vc * 64:(vc + 1) * 64], lhsT=vtok[r0:r0 + 64, h * 256 + vc * 128:h * 256 + (vc + 1) * 128],
                                rhs=att[r0:r0 + 64, r0:r0 + 64], start=False, stop=True),
                                reads=[b_vtok, b_att], writes=[bpo])
                        P.op("act", lambda e, po=po, r0=r0: e.activation(
                            out=o_sb[:, :, r0:r0 + 64], in_=po[:, 0:128].rearrange("p (v t) -> p v t", v=2), func=AF.Identity),
                            reads=[bpo], writes=[b_o[ch]])
                        pu, bpu = ringB.next()
                        P.op("pe", lambda e, pu=pu, h=h, r0=r0: e.matmul(
                            pu[:, 0:256], lhsT=kdt[r0:r0 + 64, :], rhs=vtok[r0:r0 + 64, h * 256:(h + 1) * 256], start=True, stop=True),
                            reads=[b_kdt, b_vtok], writes=[bpu])
                        P.op("dve", lambda e, pu=pu, S_=S_, r0=r0: e.scalar_tensor_tensor(
                            out=S_[:], in0=S_[:], scalar=ep[:, r0 + 63:r0 + 64], in1=pu[:, 0:256], op0=ALU.mult, op1=ALU.add),
                            reads=[bpu, bS, b_ep], writes=[bS])
                        P.op("pool", lambda e, S_=S_, Sb=Sb: e.tensor_copy(out=Sb[:], in_=S_[:]), reads=[bS], writes=[bSb])
                    P.op("act", lambda e: e.activation(out=osq[:], in_=o_sb[:], func=AF.Square), reads=b_o, writes=[b_osq])
                    pn, bpn = ringB.next()
                    for vc in range(2):
                        P.op("pe", lambda e, pn=pn, vc=vc: e.matmul(pn[:, 0:128], lhsT=C.ones_bf[:], rhs=osq[:, vc, :], start=(vc == 0), stop=(vc == 1)),
                             reads=[b_osq, C.cb], writes=[bpn])
                    P.op("act", lambda e, pn=pn: e.activation(out=ors[:], in_=pn[:, 0:128], func=AF.Sqrt, bias=C.eps[:, 0:1], scale=1.0 / 256.0),
                         reads=[bpn, C.cb], writes=[b_ors])
                    P.op("dve", lambda e: e.reciprocal(out=ors[:], in_=ors[:]), reads=[b_ors], writes=[b_ors])
                    for vc in range(2):
                        P.op("dve", lambda e, vc=vc, h=h, og_t=og_t: e.scalar_tensor_tensor(
                            out=og_t[:, h * 2 + vc, :], in0=o_sb[:, vc, :], scalar=og[:, vc:vc + 1], in1=ors[:], op0=ALU.mult, op1=ALU.mult),
                            reads=b_o + [b_ors, b_k], writes=[b_og])
                        P.op("pool", lambda e, vc=vc, h=h, og_t=og_t, c0=c0: e.tensor_tensor(
                            out=og_t[:, h * 2 + vc, :], in0=og_t[:, h * 2 + vc, :], in1=r_sb[:, h * 2 + vc, c0:c0 + 128], op=ALU.mult),
                            reads=[b_og, b_r[h * 2 + vc]], writes=[b_og])
                tb = t0 + c0
                P.dma(odst(tb, 128), og_t[:], reads=[b_og], writes=[b_od])


def build_L2a(S=8192):
    nc = bass.Bass("TRN2", target_bir_lowering=False)
    xT_d = nc.dram_tensor("xT", [1024, S], F32, kind="ExternalInput").ap()
    cm = _decl_common(nc)
    wc_d = nc.dram_tensor("c_wc", [1024, 1552], F32, kind="ExternalInput").ap()
    wg_d = nc.dram_tensor("c_wg", [16, 256], F32, kind="ExternalInput").ap()
    bg_d = nc.dram_tensor("c_bg", [1, 256], F32, kind="ExternalInput").ap()
    og_d = nc.dram_tensor("c_og", [128, 2], F32, kind="ExternalInput").ap()
    cst_d = nc.dram_tensor("c_cst", [128, 3, 128], F32, kind="ExternalInput").ap()
    oT_d = nc.dram_tensor("oT", [512, S], F32, kind="ExternalOutput").ap()
    C = make_ctx(nc)
    emit_ada(C, *cm)
    emit_mixC(C, xT_d, wc_d, wg_d, bg_d, og_d, cst_d, oT_d, S)
    C.P.emit()
    return nc


def _c_consts():
    s_ = np.arange(128)[:, None]
    t_ = np.arange(128)[None, :]
    same = (s_ // 64) == (t_ // 64)
    tric = np.where(same & (s_ <= t_), -1.0 / 16.0, 0.0).astype(np.float32)
    trir = np.where(same & (s_ > t_), -1.0 / 16.0, 0.0).astype(np.float32)
    mc = (same & (s_ <= t_)).astype(np.float32)
    return np.ascontiguousarray(np.stack([tric, trir, mc], 1))


def lay_L2a(d, b, hp, xT_full):
    m = _lay_common(d, 2, b)
    w = d["c_w_in"][0]
    h0 = hp * 2
    m.update({
        "xT": xT_full,
        "c_wc": np.ascontiguousarray(np.concatenate([
            w[:, h0 * 128:(h0 + 2) * 128], w[:, 512 + h0 * 128:512 + (h0 + 2) * 128],
            w[:, 1024 + h0 * 256:1024 + (h0 + 2) * 256], w[:, 2048 + h0 * 256:2048 + (h0 + 2) * 256], w[:, 3072:3088]], 1)),
        "c_wg": np.ascontiguousarray(d["c_w_gate_up"][0][:, h0 * 128:(h0 + 2) * 128]),
        "c_bg": np.ascontiguousarray(d["c_b_gate"][0][h0 * 128:(h0 + 2) * 128].reshape(1, 256)),
        "c_og": np.ascontiguousarray(d["c_o_gain"][0].reshape(2, 128).T),
        "c_cst": _c_consts(),
    })
    return m


PAIRS = [[0, 1], [2, 3], [4, 5], [6, 7]]


def build_fused(H=4096, PAIRS=PAIRS):
    S = 2 * H
    nc = bass.Bass("TRN2", target_bir_lowering=False)
    TT = H + HALO

    def din(name, shape):
        return nc.dram_tensor(name, list(shape), F32, kind="ExternalInput").ap()
    xT_d = din("xT", [1024, TT])
    cT_d = din("cT", [128, 8])
    adaw = [din("ada_w%d" % L, [1024, 6144]) for L in range(4)]
    adab = [din("ada_b%d" % L, [128, 48]) for L in range(4)]
    gains = [din("gains%d" % L, [128, 16]) for L in range(4)]
    w1 = [din("w1_%d" % L, [1024, 4096]) for L in range(4)]
    w2 = [din("w2_%d" % L, [4096, 1024]) for L in range(4)]
    a_w_in = din("a_w_in", [1024, 3072]); a_gqk = din("a_gqk", [128, 2]); a_biasT = din("a_biasT", [128, 16, 640])
    a_valid = din("a_valid", [128, TT // 128]); a_w_out = din("a_w_out", [1024, 1024])
    b_w_in = din("b_w_in", [1024, 6144]); b_vgain = din("b_vgain", [128, 3072]); b_wsT = din("b_wsT", [128, 8, 128])
    b_bs = din("b_bs", [128, 24, 128]); b_w_out = din("b_w_out", [3072, 1024])
    c_wc = din("c_wc", [1024, 1552]); c_wg = din("c_wg", [16, 256]); c_bg = din("c_bg", [1, 256]); c_og = din("c_og", [128, 2])
    c_cst = din("c_cst", [128, 3, 128]); c_w_out = din("c_w_out", [1024, 1024])
    d_wqkv = din("d_wqkv", [1024, 1536]); d_cst = din("d_cst", [128, 3, 128]); d_w_out = din("d_w_out", [1024, 1024])
    sel_d = din("sel", [128, 2])
    yT_d = nc.dram_tensor("yT", [1024, H], F32, kind="ExternalOutput").ap()

    def scr(name, shape, dt=F32):
        return nc.dram_tensor(name, list(shape), dt).ap()
    qT_s = scr("a_qT_s", [1024, H], BF16); kT_s = scr("a_kT_s", [1024, TT], BF16); v_s = scr("a_v_s", [TT, 2048], BF16)
    vm_s = scr("b_vm_s", [24, 128, H], BF16)
    xs = [scr("xs%d" % i, [1024, H]) for i in range(4)]
    xa = scr("xa", [1024, H])
    XC = min(512, H)
    OC = min(1024, S)
    xb_c = [scr("xb_c%d" % i, [1024, XC]) for i in range(H // XC)]; xb_g = [scr("xb_g%d" % i, [2048, XC]) for i in range(H // XC)]
    xc_c = [scr("xc_c%d" % i, [1024, XC]) for i in range(H // XC)]; xc_g = [scr("xc_g%d" % i, [2048, XC]) for i in range(H // XC)]
    oc_c = [scr("oc_c%d" % i, [512, OC]) for i in range(S // OC)]; oc_g = [scr("oc_g%d" % i, [1024, OC]) for i in range(S // OC)]
    od_c = [scr("od_c%d" % i, [512, OC]) for i in range(S // OC)]; od_g = [scr("od_g%d" % i, [1024, OC]) for i in range(S // OC)]
    dq_s = scr("d_qT_s", [512, S], BF16); dk_s = scr("d_kT_s", [512, S], BF16); dv_s = scr("d_v_s", [S, 1024], BF16)

    C = make_ctx(nc)
    P = C.P
    selt = P.sb("selt", [128, 2], F32); b_sel = P.buf()
    P.dma(selt[:], sel_d, writes=[b_sel])

    def x_own(chunks):
        def f(t0, T):
            return chunks[t0 // XC][:, t0 % XC:t0 % XC + T].rearrange("(kc p) t -> p kc t", p=128)
        return f

    def x_gath(chunks):
        def f(t0, T):
            r, tl = t0 // H, t0 % H
            return chunks[tl // XC][r * 1024:(r + 1) * 1024, tl % XC:tl % XC + T].rearrange("(kc p) t -> p kc t", p=128)
        return f

    def o_dst(chunks):
        def f(t0, n):
            return chunks[t0 // OC].rearrange("(oc p) t -> p oc t", p=128)[:, :, t0 % OC:t0 % OC + n]
        return f

    def o_gath(chunks):
        def f(h, t0, T):
            g = h * H + t0
            return chunks[g // OC].rearrange("(kc p) t -> p kc t", p=128)[:, :, g % OC:g % OC + T]
        return f

    def gather(src, dst):
        for a_, b_ in zip(src, dst):
            P.collective("AllGather", [a_], [b_], PAIRS)
        P.barrier()
    emit_ada(C, cT_d, adaw[0], adab[0], gains[0])
    emit_mixA1(C, xT_d, a_w_in, a_gqk, qT_s, kT_s, v_s, H)
    emit_mixA2(C, xT_d, xs[0], qT_s, kT_s, v_s, a_biasT, a_valid, a_w_out, H)
    emit_ffn(C, xs[0], xa, w1[0], w2[0], H)
    emit_ada(C, cT_d, adaw[1], adab[1], gains[1])
    emit_mixB(C, xa, xs[1], b_w_in, b_vgain, b_wsT, b_bs, b_w_out, vm_s, H)
    emit_ffn(C, xs[1], None, w1[1], w2[1], H, ydst=x_own(xb_c))
    gather(xb_c, xb_g)
    emit_ada(C, cT_d, adaw[2], adab[2], gains[2])
    emit_mixC(C, None, c_wc, c_wg, c_bg, c_og, c_cst, None, S, xsrc=x_gath(xb_g), odst=o_dst(oc_c))
    gather(oc_c, oc_g)
    emit_outproj(C, None, None, c_w_out, xs[2], H, sel=(selt, b_sel), xsrc=x_own(xb_c), osrc=o_gath(oc_g))
    emit_ffn(C, xs[2], None, w1[2], w2[2], H, ydst=x_own(xc_c))
    gather(xc_c, xc_g)
    emit_ada(C, cT_d, adaw[3], adab[3], gains[3])
    emit_mixD1(C, None, d_wqkv, dq_s, dk_s, dv_s, S, xsrc=x_gath(xc_g))
    emit_mixD2(C, dq_s, dk_s, dv_s, d_cst, None, S, odst=o_dst(od_c))
    gather(od_c, od_g)
    emit_outproj(C, None, None, d_w_out, xs[3], H, sel=(selt, b_sel), xsrc=x_own(xc_c), osrc=o_gath(od_g))
    emit_ffn(C, xs[3], yT_d, w1[3], w2[3], H)
    P.emit()
    return nc


def lay_fused(d, b, hf, H=4096):
    m = {}
    l0 = lay_L0(d, b, hf, H)
    for k in ("xT", "cT", "a_w_in", "a_gqk", "a_biasT", "a_valid", "a_w_out"):
        m[k] = l0[k]
    for L in range(4):
        c = _lay_common(d, L, b)
        m["ada_w%d" % L] = c["ada_w"]; m["ada_b%d" % L] = c["ada_b"]; m["gains%d" % L] = c["gains"]
        m["w1_%d" % L] = np.ascontiguousarray(d["ffn_w1"][L]); m["w2_%d" % L] = np.ascontiguousarray(d["ffn_w2"][L])
    l1 = lay_L1(d, b)
    for k in ("b_w_in", "b_vgain", "b_wsT", "b_bs", "b_w_out"):
        m[k] = l1[k]
    l2 = lay_L2a(d, b, hf, None)
    for k in ("c_wc", "c_wg", "c_bg", "c_og", "c_cst"):
        m[k] = l2[k]
    m["c_w_out"] = np.ascontiguousarray(d["c_w_out"][0])
    l3 = lay_L3a(d, b, hf, None)
    for k in ("d_wqkv", "d_cst"):
        m[k] = l3[k]
    m["d_w_out"] = np.ascontiguousarray(d["d_w_out"][0])
    sel = np.zeros((128, 2), np.float32); sel[:, hf] = 1.0
    m["sel"] = sel
    return m


_PROGS = {}


def _prog(name, builder):
    if name not in _PROGS:
        _PROGS[name] = builder()
    return _PROGS[name]


def _run(nc, in_maps):
    res = run_bass_kernel_spmd(nc, in_maps, core_ids=list(range(len(in_maps))))
    return res.results


def kernel(**inputs):
    d = {k: np.asarray(v, dtype=np.float32) for k, v in inputs.items()}
    B, S, D = d["x"].shape
    H = S // 2
    cores = [(b, hf) for b in range(B) for hf in range(2)]
    nc = _prog("fused", lambda: build_fused(H))
    r = _run(nc, [lay_fused(d, b, hf, H) for (b, hf) in cores])
    out = np.empty((B, S, D), np.float32)
    for b in range(B):
        out[b, :H] = r[2 * b]["yT"].T
        out[b, H:] = r[2 * b + 1]["yT"].T
    return out


def kernel_unfused(**inputs):
    d = {k: np.asarray(v, dtype=np.float32) for k, v in inputs.items()}
    B, S, D = d["x"].shape
    H = S // 2
    cores = [(b, hf) for b in range(B) for hf in range(2)]
    nc = _prog("L0", lambda: build_L0(H))
    r = _run(nc, [lay_L0(d, b, hf, H) for (b, hf) in cores])
    xT = [np.concatenate([r[2 * b]["yT"], r[2 * b + 1]["yT"]], axis=1) for b in range(B)]
    nc = _prog("L1", lambda: build_L1(H))
    ins = []
    for (b, hf) in cores:
        m = lay_L1(d, b)
        m["xT"] = np.ascontiguousarray(xT[b][:, hf * H:(hf + 1) * H])
        ins.append(m)
    r = _run(nc, ins)
    xT = [np.concatenate([r[2 * b]["yT"], r[2 * b + 1]["yT"]], axis=1) for b in range(B)]
    nc = _prog("L2a", lambda: build_L2a(S))
    r = _run(nc, [lay_L2a(d, b, hp, xT[b]) for (b, hp) in cores])
    oT = [np.concatenate([r[2 * b]["oT"], r[2 * b + 1]["oT"]], axis=0) for b in range(B)]
    nc = _prog("Lb", lambda: build_Lb(H))
    ins = []
    for (b, hf) in cores:
        m = lay_Lb(d, 2, b, d["c_w_out"][0])
        m["xT"] = np.ascontiguousarray(xT[b][:, hf * H:(hf + 1) * H])
        m["oT"] = np.ascontiguousarray(oT[b][:, hf * H:(hf + 1) * H])
        ins.append(m)
    r = _run(nc, ins)
    xT = [np.concatenate([r[2 * b]["yT"], r[2 * b + 1]["yT"]], axis=1) for b in range(B)]
    nc = _prog("L3a", lambda: build_L3a(S))
    r = _run(nc, [lay_L3a(d, b, hg, xT[b]) for (b, hg) in cores])
    oT = [np.concatenate([r[2 * b]["oT"], r[2 * b + 1]["oT"]], axis=0) for b in range(B)]
    nc = _prog("Lb", lambda: build_Lb(H))
    ins = []
    for (b, hf) in cores:
        m = lay_Lb(d, 3, b, d["d_w_out"][0])
        m["xT"] = np.ascontiguousarray(xT[b][:, hf * H:(hf + 1) * H])
        m["oT"] = np.ascontiguousarray(oT[b][:, hf * H:(hf + 1) * H])
        ins.append(m)
    r = _run(nc, ins)
    out = np.empty((B, S, D), np.float32)
    for b in range(B):
        out[b, :H] = r[2 * b]["yT"].T
        out[b, H:] = r[2 * b + 1]["yT"].T
    return out
```

```python
import numpy as np
import concourse.bass as bass
import concourse.mybir as mybir
from concourse.bass_utils import run_bass_kernel_spmd
from contextlib import ExitStack, contextmanager

F32 = mybir.dt.float32
BF16 = mybir.dt.bfloat16
AF = mybir.ActivationFunctionType
ALU = mybir.AluOpType
AX = mybir.AxisListType
EPS = 1e-6
DEBUG = False
SERIAL = False
NSTAGE = 2
STAGE_ELEMS = 2048
ENGS = ("pe", "act", "dve", "pool", "sp")


class Buf:
    __slots__ = ("name", "w", "r")

    def __init__(self, name):
        self.name = name
        self.w = None
        self.r = []


class Ring:
    def __init__(self, items):
        self.items = items
        self.i = 0

    def next(self):
        it = self.items[self.i % len(self.items)]
        self.i += 1
        return it


class Prog:
    NDMA = 24

    def __init__(self, nc):
        self.nc = nc
        self.es = ExitStack()
        self.scopes = [self.es]
        self.streams = {e: [] for e in ENGS}
        self.esem = {e: self.es.enter_context(nc.semaphore("s_" + e)) for e in ENGS}
        self.ecount = {e: 0 for e in ENGS}
        self.dsem = [self.es.enter_context(nc.semaphore("d%d" % i)) for i in range(self.NDMA)]
        self.dcount = [0] * self.NDMA
        self.dnext = 0
        self.semobj = {}
        for e in ENGS:
            self.semobj[("e", e)] = self.esem[e]
        for i in range(self.NDMA):
            self.semobj[("d", i)] = self.dsem[i]
        self.waited = {e: {} for e in ENGS}
        self.nbuf = 0
        self.uid = 0

    @contextmanager
    def scope(self):
        st = ExitStack()
        self.scopes.append(st)
        try:
            yield
        finally:
            self.barrier()
            self.scopes.pop()
            st.close()

    def sb(self, name, shape, dt):
        self.uid += 1
        return self.scopes[-1].enter_context(self.nc.sbuf_tensor("%s_%d" % (name, self.uid), list(shape), dt))

    def ps(self, name, shape, dt=F32):
        return self.scopes[-1].enter_context(self.nc.psum_tensor(name, list(shape), dt))

    def buf(self, name=None):
        self.nbuf += 1
        return Buf(name or ("b%d" % self.nbuf))

    def bufs(self, n):
        return [self.buf() for _ in range(n)]

    def _deps(self, eng, reads, writes):
        deps = {}

        def add(d):
            if d is None:
                return
            k, v = d
            if deps.get(k, 0) < v:
                deps[k] = v
        for b in reads:
            add(b.w)
        for b in writes:
            add(b.w)
            for d in b.r:
                add(d)
        waits = []
        wd = self.waited[eng]
        for k, v in deps.items():
            if wd.get(k, 0) >= v:
                continue
            if eng == "pe" and k == ("e", "pe"):
                continue
            wd[k] = v
            waits.append((self.semobj[k], v))
        return waits

    def _mark(self, tok, reads, writes):
        for b in writes:
            b.w = tok
            b.r = []
        for b in reads:
            if b not in writes:
                b.r.append(tok)
                if len(b.r) > 64:
                    best = {}
                    for k, v in b.r:
                        if best.get(k, 0) < v:
                            best[k] = v
                    b.r = list(best.items())

    def _serial_waits(self, eng):
        waits = []
        for e2 in ENGS:
            k = ("e", e2); v = self.ecount[e2]
            if v > 0 and self.waited[eng].get(k, 0) < v:
                self.waited[eng][k] = v
                waits.append((self.esem[e2], v))
        for i in range(self.NDMA):
            k = ("d", i); v = self.dcount[i]
            if v > 0 and self.waited[eng].get(k, 0) < v:
                self.waited[eng][k] = v
                waits.append((self.dsem[i], v))
        return waits

    def op(self, eng, fn, reads=(), writes=()):
        waits = self._deps(eng, reads, writes)
        if SERIAL:
            waits = waits + self._serial_waits(eng)
        self.ecount[eng] += 1
        tok = (("e", eng), self.ecount[eng])
        self.streams[eng].append((waits, fn, self.esem[eng], 1))
        self._mark(tok, reads, writes)
        return tok

    def dma(self, out, in_, reads=(), writes=(), eng="sp"):
        i = self.dnext
        self.dnext = (self.dnext + 1) % self.NDMA
        waits = self._deps(eng, reads, writes)
        if SERIAL:
            waits = waits + self._serial_waits(eng)
        k = ("d", i)
        if self.dcount[i] > 0 and self.waited[eng].get(k, 0) < self.dcount[i]:
            self.waited[eng][k] = self.dcount[i]
            waits.append((self.dsem[i], self.dcount[i]))
        self.dcount[i] += 16
        tok = (k, self.dcount[i])
        self.streams[eng].append((waits, lambda e: e.dma_start(out=out, in_=in_), self.dsem[i], 16))
        self._mark(tok, reads, writes)
        return tok

    def collective(self, kind, ins, outs, groups, reads=(), writes=()):
        eng = "pool"
        waits = self._deps(eng, reads, writes)
        sem = self.es.enter_context(self.nc.semaphore("cc%d" % len(self.semobj)))
        k = ("c", len(self.semobj))
        self.semobj[k] = sem
        tok = (k, 1)
        self.streams[eng].append((waits, lambda e: e.collective_compute(kind, ALU.bypass, groups, [a.opt() for a in ins], [a.opt() for a in outs]), sem, 1))
        self._mark(tok, reads, writes)
        return tok

    def dump(self, name, ap, shape, dt, reads):
        if not DEBUG:
            return
        d = self.nc.dram_tensor(name, list(shape), dt, kind="ExternalOutput").ap()
        self.dma(d, ap, reads=reads, writes=[self.buf()])

    def barrier(self):
        for eng in ENGS:
            waits = []
            for k, sem in self.semobj.items():
                if k[0] == "c" and self.waited[eng].get(k, 0) < 1:
                    self.waited[eng][k] = 1
                    waits.append((sem, 1))
            for e2 in ENGS:
                k = ("e", e2)
                v = self.ecount[e2]
                if e2 != eng and v > 0 and self.waited[eng].get(k, 0) < v:
                    self.waited[eng][k] = v
                    waits.append((self.esem[e2], v))
            for i in range(self.NDMA):
                k = ("d", i)
                v = self.dcount[i]
                if v > 0 and self.waited[eng].get(k, 0) < v:
                    self.waited[eng][k] = v
                    waits.append((self.dsem[i], v))
            k = ("e", eng)
            v = self.ecount[eng]
            if v > 0 and self.waited[eng].get(k, 0) < v:
                self.waited[eng][k] = v
                waits.append((self.esem[eng], v))
            if waits:
                self.streams[eng].append((waits, None, None, 0))

    def emit(self):
        nc = self.nc
        self.barrier()
        with nc.Block() as block:
            def runner(name):
                def f(e):
                    for waits, fn, sem, inc in self.streams[name]:
                        for s, v in waits:
                            e.wait_ge(s, v)
                        if fn is not None:
                            fn(e).then_inc(sem, inc)
                return f
            block.tensor(runner("pe"))
            block.scalar(runner("act"))
            block.vector(runner("dve"))
            block.gpsimd(runner("pool"))
            block.sync(runner("sp"))
        self.es.close()


class Ctx:
    pass


def make_ctx(nc):
    P = Prog(nc)
    C = Ctx()
    C.P = P
    C.nc = nc
    C.psums = [P.ps("ps%d" % i, [128, 512]) for i in range(8)]
    C.psum_b = P.bufs(8)
    C.ones_bf = P.sb("ones_bf", [128, 128], BF16)
    C.ones_f = P.sb("ones_f", [128, 128], F32)
    C.eps = P.sb("eps_t", [128, 1], F32)
    C.cb = P.buf()

    def mk(e):
        e.memset(C.eps[:], EPS)
        e.memset(C.ones_f[:], 1.0)
        return e.memset(C.ones_bf[:], 1.0)
    P.op("pool", mk, writes=[C.cb])
    C.stage = [P.sb("stage%d" % i, [128, STAGE_ELEMS], F32) for i in range(NSTAGE)]
    C.stage_b = P.bufs(NSTAGE)
    C.sti = 0
    return C


def next_stage(C):
    i = C.sti % len(C.stage)
    C.sti += 1
    return C.stage[i], C.stage_b[i]


def load_cast(C, dst_ap, dst_buf, src_ap, shape3):
    P = C.P
    n = 1
    for s in shape3:
        n *= s
    if n > STAGE_ELEMS:
        h = shape3[0] // 2
        if len(shape3) == 1:
            load_cast(C, dst_ap[:, 0:h], dst_buf, src_ap[:, 0:h], [h])
            load_cast(C, dst_ap[:, h:2 * h], dst_buf, src_ap[:, h:2 * h], [h])
        else:
            load_cast(C, dst_ap[:, 0:h, :], dst_buf, src_ap[:, 0:h, :], [h, shape3[1]])
            load_cast(C, dst_ap[:, h:2 * h, :], dst_buf, src_ap[:, h:2 * h, :], [h, shape3[1]])
        return
    st, sb_ = next_stage(C)
    if len(shape3) == 2:
        stv = st[:, 0:n].rearrange("p (a n) -> p a n", a=shape3[0])
    else:
        stv = st[:, 0:n]
    P.dma(stv, src_ap, writes=[sb_])
    eng = "dve" if C.sti % 2 == 0 else "pool"
    P.op(eng, lambda e: e.tensor_copy(out=dst_ap, in_=stv), reads=[sb_], writes=[dst_buf])


def emit_ada(C, cT_d, ada_w_d, ada_b_d, gains_d):
    P = C.P
    psum, psum_b = C.psums[0], C.psum_b[0]
    ct = P.sb("ct", [128, 8], F32); cond = P.sb("cond", [128, 8], F32)
    adab = P.sb("adab", [128, 48], F32); gn = P.sb("gn", [128, 16], F32)
    mod = P.sb("mod", [128, 48], F32)
    out = P.sb("AB", [128, 48], F32)
    b_ct, b_cond, b_adab, b_gn, b_mod, b_out = P.bufs(6)
    P.dma(ct[:], cT_d, writes=[b_ct])
    P.dma(adab[:], ada_b_d, writes=[b_adab])
    P.dma(gn[:], gains_d, writes=[b_gn])
    P.op("act", lambda e: e.activation(out=cond[:], in_=ct[:], func=AF.Silu), reads=[b_ct], writes=[b_cond])
    wv = ada_w_d.rearrange("(kc p) f -> p kc f", p=128)
    for g in range(24):
        st, sb_ = next_stage(C)
        stv = st[:, 0:2048].rearrange("p (kc f) -> p kc f", kc=8)
        P.dma(stv, wv[:, :, g * 256:(g + 1) * 256], writes=[sb_])
        for j in range(2):
            col = g * 2 + j
            for kc in range(8):
                P.op("pe", (lambda e, col=col, kc=kc, j=j, stv=stv: e.matmul(
                    psum[:, col:col + 1], lhsT=stv[:, kc, j * 128:(j + 1) * 128],
                    rhs=cond[:, kc:kc + 1], start=(kc == 0), stop=(kc == 7))),
                    reads=[sb_, b_cond], writes=[psum_b])
    P.op("dve", lambda e: e.tensor_tensor(out=mod[:], in0=psum[:, 0:48], in1=adab[:], op=ALU.add),
         reads=[psum_b, b_adab], writes=[b_mod])

    def mk(e):
        e.scalar_tensor_tensor(out=out[:, 0:8], in0=mod[:, 8:16], scalar=1.0, in1=gn[:, 0:8], op0=ALU.add, op1=ALU.mult)
        e.scalar_tensor_tensor(out=out[:, 24:32], in0=mod[:, 32:40], scalar=1.0, in1=gn[:, 8:16], op0=ALU.add, op1=ALU.mult)
        e.tensor_copy(out=out[:, 8:16], in_=mod[:, 0:8])
        e.tensor_copy(out=out[:, 16:24], in_=mod[:, 16:24])
        e.tensor_copy(out=out[:, 32:40], in_=mod[:, 24:32])
        return e.tensor_copy(out=out[:, 40:48], in_=mod[:, 40:48])
    P.op("dve", mk, reads=[b_mod, b_gn], writes=[b_out])
    C.AB, C.b_ab = out, b_out


class NormBufs:
    def __init__(self, C, T):
        P = C.P
        self.T = T
        self.sq = P.sb("sq", [128, 8, T], BF16); self.b_sq = P.buf()
        self.rstd = P.sb("rstd", [128, T], F32); self.b_rstd = P.buf()
        self.tmp = P.sb("tmp", [128, 4, T], F32); self.b_tmp = P.bufs(4)
        self.hT = P.sb("hT", [128, 8, T], BF16); self.b_hT = P.buf()


def emit_norm_mod(C, N, xt, b_xt, acol, bcol):
    P = C.P
    T = N.T
    ps, b_ps = C.psums[0], C.psum_b[0]
    A = C.AB[:, acol:acol + 8]
    Bc = C.AB[:, bcol:bcol + 8]
    P.op("act", lambda e: e.activation(out=N.sq[:], in_=xt[:], func=AF.Square), reads=[b_xt], writes=[N.b_sq])
    for kc in range(8):
        P.op("pe", lambda e, kc=kc: e.matmul(ps[:, :T], lhsT=C.ones_bf[:], rhs=N.sq[:, kc, :], start=(kc == 0), stop=(kc == 7)),
             reads=[N.b_sq, C.cb], writes=[b_ps])
    P.op("act", lambda e: e.activation(out=N.rstd[:], in_=ps[:, :T], func=AF.Sqrt, bias=C.eps[:, 0:1], scale=1.0 / 1024.0),
         reads=[b_ps, C.cb], writes=[N.b_rstd])
    P.op("dve", lambda e: e.reciprocal(out=N.rstd[:], in_=N.rstd[:]), reads=[N.b_rstd], writes=[N.b_rstd])
    for kc in range(8):
        eng = "dve" if kc % 2 == 0 else "pool"
        P.op(eng, lambda e, kc=kc: e.tensor_tensor(out=N.tmp[:, kc % 4, :], in0=xt[:, kc, :], in1=N.rstd[:], op=ALU.mult),
             reads=[b_xt, N.b_rstd], writes=[N.b_tmp[kc % 4]])
        P.op("act", lambda e, kc=kc: e.activation(out=N.hT[:, kc, :], in_=N.tmp[:, kc % 4, :], func=AF.Identity,
                                                   bias=Bc[:, kc:kc + 1], scale=A[:, kc:kc + 1]),
             reads=[N.b_tmp[kc % 4], C.b_ab], writes=[N.b_hT])


def emit_ffn(C, xT_d, yT_d, w1_d, w2_d, T_core, T=256, ydst=None):
    P = C.P
    AB = C.AB
    with P.scope():
        w1b = P.sb("w1b", [128, 8, 4096], BF16); w2b = P.sb("w2b", [128, 32, 1024], BF16)
        b_w1 = P.bufs(8); b_w2 = P.bufs(8)
        for kc in range(8):
            load_cast(C, w1b[:, kc, :], b_w1[kc], w1_d[kc * 128:(kc + 1) * 128, :], [4096])
        w2v = w2_d.rearrange("(hc p) n -> p hc n", p=128)
        for g in range(8):
            load_cast(C, w2b[:, g * 4:(g + 1) * 4, :], b_w2[g], w2v[:, g * 4:(g + 1) * 4, :], [4, 1024])
        xts = [P.sb("xt%d" % i, [128, 8, T], F32) for i in range(2)]
        b_xts = P.bufs(2)
        N = NormBufs(C, T)
        h1T = P.sb("h1T", [128, 32, T], BF16); b_h1 = P.bufs(32)
        rl = [P.sb("rl%d" % i, [128, T], F32) for i in range(2)]; b_rl = P.bufs(2)
        xv = xT_d.rearrange("(kc p) t -> p kc t", p=128)
        if ydst is None:
            yv = yT_d.rearrange("(kc p) t -> p kc t", p=128)
            ydst = lambda t0, T: yv[:, :, t0:t0 + T]
        b_out = P.buf()
        hring = Ring(list(zip(C.psums[1:5], C.psum_b[1:5])))
        yring = Ring(list(zip(C.psums[5:8], C.psum_b[5:8])))
        nt = T_core // T
        P.dma(xts[0][:], xv[:, :, 0:T], writes=[b_xts[0]])
        emit_norm_mod(C, N, xts[0], b_xts[0], 24, 32)
        for ti in range(nt):
            xt, b_xt = xts[ti % 2], b_xts[ti % 2]
            t0 = ti * T
            for hc in range(32):
                ps, bps = hring.next()
                for kc in range(8):
                    P.op("pe", lambda e, kc=kc, hc=hc, ps=ps: e.matmul(
                        ps[:, :T], lhsT=w1b[:, kc, hc * 128:(hc + 1) * 128], rhs=N.hT[:, kc, :],
                        start=(kc == 0), stop=(kc == 7)), reads=[b_w1[kc], N.b_hT], writes=[bps])
                r, br = rl[hc % 2], b_rl[hc % 2]
                P.op("act", lambda e, ps=ps, r=r: e.activation(out=r[:], in_=ps[:, :T], func=AF.Relu), reads=[bps], writes=[br])
                P.op("pool", lambda e, r=r, hc=hc: e.tensor_tensor(out=h1T[:, hc, :], in0=r[:], in1=r[:], op=ALU.mult),
                     reads=[br], writes=[b_h1[hc]])
            if ti + 1 < nt:
                nx, b_nx = xts[(ti + 1) % 2], b_xts[(ti + 1) % 2]
                P.dma(nx[:], xv[:, :, t0 + T:t0 + 2 * T], writes=[b_nx])
                emit_norm_mod(C, N, nx, b_nx, 24, 32)
            for oc in range(8):
                ps, bps = yring.next()
                for hc in range(32):
                    P.op("pe", lambda e, oc=oc, hc=hc, ps=ps: e.matmul(
                        ps[:, :T], lhsT=w2b[:, hc, oc * 128:(oc + 1) * 128], rhs=h1T[:, hc, :],
                        start=(hc == 0), stop=(hc == 31)), reads=[b_w2[hc // 4], b_h1[hc]], writes=[bps])
                P.op("dve", lambda e, oc=oc, ps=ps, xt=xt: e.scalar_tensor_tensor(
                    out=xt[:, oc, :], in0=ps[:, :T], scalar=AB[:, 40 + oc:41 + oc], in1=xt[:, oc, :],
                    op0=ALU.mult, op1=ALU.add), reads=[bps, C.b_ab, b_xt], writes=[b_xt])
            P.dma(ydst(t0, T), xt[:], reads=[b_xt], writes=[b_out])


def emit_mixB(C, xT_d, xo_d, w_in_d, vgain_d, wsT_d, bs_d, wout_d, vm_d, T_core, T=256):
    P = C.P
    xv = xT_d.rearrange("(kc p) t -> p kc t", p=128)
    xov = xo_d.rearrange("(kc p) t -> p kc t", p=128)
    vmv = vm_d.rearrange("fc p t -> p fc t")
    winv = w_in_d.rearrange("(kc p) f -> p kc f", p=128)
    with P.scope():
        wvb = P.sb("wvb", [128, 8, 3072], BF16); b_wv = P.bufs(8)
        for kc in range(8):
            load_cast(C, wvb[:, kc, :], b_wv[kc], winv[:, kc, 3072:6144], [3072])
        gainB = P.sb("gainB", [128, 3072], F32); b_gain = P.buf()
        P.dma(gainB[:], vgain_d, writes=[b_gain])
        wsT = P.sb("wsT", [128, 8, 128], BF16); b_ws = P.buf()
        load_cast(C, wsT[:], b_ws, wsT_d, [8, 128])
        P.op("pool", lambda e: e.memset(wsT[64:128, :, 0:64], 0.0), writes=[b_ws])
        bs = P.sb("bs", [128, 24, 128], F32); b_bs = P.buf()
        P.dma(bs[:], bs_d, writes=[b_bs])
        xts = [P.sb("xt%d" % i, [128, 8, T], F32) for i in range(2)]; b_xts = P.bufs(2)
        N = NormBufs(C, T)
        vg = P.sb("vg", [128, 3072], F32); b_vg = P.bufs(6)
        sqv = P.sb("sqv", [128, 3072], F32); b_sqv = P.buf()
        ssv = P.sb("ssv", [128, 1], F32); b_ssv = P.buf()
        vn = P.sb("vn", [128, 3072], BF16); b_vn = P.buf()
        vmT = [P.sb("vmT%d" % i, [128, 24, 128], BF16) for i in range(2)]; b_vmT = P.bufs(2)
        pring = Ring(list(zip(C.psums[1:5], C.psum_b[1:5])))
        mring = Ring(list(zip(C.psums[5:8], C.psum_b[5:8])))
        b_vmd = P.buf()
        wi = 0
        for ti in range(T_core // T):
            xt, b_xt = xts[ti % 2], b_xts[ti % 2]
            t0 = ti * T
            P.dma(xt[:], xv[:, :, t0:t0 + T], writes=[b_xt])
            emit_norm_mod(C, N, xt, b_xt, 0, 8)
            for w in range(T // 128):
                for cc in range(6):
                    ps, bps = pring.next()
                    for kc in range(8):
                        P.op("pe", lambda e, kc=kc, cc=cc, ps=ps, w=w: e.matmul(
                            ps[:, :], lhsT=N.hT[:, kc, w * 128:(w + 1) * 128], rhs=wvb[:, kc, cc * 512:(cc + 1) * 512],
                            start=(kc == 0), stop=(kc == 7)), reads=[b_wv[kc], N.b_hT], writes=[bps])
                    if DEBUG and ti == 0 and w == 0 and cc == 0:
                        dbgp = P.sb("dbgp", [128, 512], F32); b_dbgp = P.buf()
                        P.op("act", lambda e, ps=ps: e.activation(out=dbgp[:], in_=ps[:, :], func=AF.Identity), reads=[bps], writes=[b_dbgp])
                        P.dump("dbg_ps", dbgp[:], [128, 512], F32, [b_dbgp])
                        dbgp2 = P.sb("dbgp2", [128, 512], F32); b_dbgp2 = P.buf()
                        P.op("dve", lambda e, ps=ps: e.tensor_copy(out=dbgp2[:], in_=ps[:, :]), reads=[bps], writes=[b_dbgp2])
                        P.dump("dbg_ps2", dbgp2[:], [128, 512], F32, [b_dbgp2])
                        P.dump("dbg_wvb", wvb[:, :, 0:512], [128, 8, 512], BF16, b_wv)
                    P.op("act", lambda e, ps=ps, cc=cc: e.activation(out=vg[:, cc * 512:(cc + 1) * 512], in_=ps[:, :], func=AF.Gelu),
                         reads=[bps], writes=[b_vg[cc]])
                P.op("dve", lambda e: e.tensor_tensor(out=sqv[:], in0=vg[:], in1=vg[:], op=ALU.mult), reads=b_vg, writes=[b_sqv])
                P.op("dve", lambda e: e.reduce_sum(out=ssv[:], in_=sqv[:], axis=AX.X), reads=[b_sqv], writes=[b_ssv])
                P.op("act", lambda e: e.activation(out=ssv[:], in_=ssv[:], func=AF.Sqrt, bias=C.eps[:, 0:1], scale=1.0 / 3072.0),
                     reads=[b_ssv, C.cb], writes=[b_ssv])
                P.op("dve", lambda e: e.reciprocal(out=ssv[:], in_=ssv[:]), reads=[b_ssv], writes=[b_ssv])
                P.op("dve", lambda e: e.scalar_tensor_tensor(out=vn[:], in0=vg[:], scalar=ssv[:, 0:1], in1=gainB[:],
                                                               op0=ALU.mult, op1=ALU.mult),
                     reads=b_vg + [b_ssv, b_gain], writes=[b_vn])
                if ti == 0:
                    P.dump("dbg_vg%d" % w, vg[:], [128, 3072], F32, b_vg)
                    P.dump("dbg_ssv%d" % w, ssv[:], [128, 1], F32, [b_ssv])
                    P.dump("dbg_vn%d" % w, vn[:], [128, 3072], BF16, [b_vn])
                    if w == 0:
                        P.dump("dbg_hT", N.hT[:], [128, 8, T], BF16, [N.b_hT])
                vm, bvm = vmT[wi % 2], b_vmT[wi % 2]
                for q in range(6):
                    ps, bps = mring.next()
                    for j in range(4):
                        fc = q * 4 + j
                        g = fc // 3
                        P.op("pe", lambda e, ps=ps, j=j, fc=fc, g=g: e.matmul(
                            ps[:, j * 128:(j + 1) * 128], lhsT=vn[:, fc * 128:(fc + 1) * 128], rhs=wsT[:, g, :],
                            start=True, stop=True), reads=[b_vn, b_ws], writes=[bps])
                    P.op("dve", lambda e, ps=ps, q=q, vm=vm: e.tensor_tensor(
                        out=vm[:, q * 4:(q + 1) * 4, :], in0=ps[:, :].rearrange("p (a n) -> p a n", a=4),
                        in1=bs[:, q * 4:(q + 1) * 4, :], op=ALU.add), reads=[bps, b_bs], writes=[bvm])
                tw = t0 + w * 128
                P.dma(vmv[:, :, tw:tw + 128], vm[:], reads=[bvm], writes=[b_vmd])
                wi += 1
    emit_mixB2(C, xv, xov, vmv, winv, wout_d, T_core, T)


def emit_mixB2(C, xv, xov, vmv, winv, wout_d, T_core, T):
    P = C.P
    AB = C.AB
    with P.scope():
        wub = P.sb("wub", [128, 8, 3072], BF16); b_wu = P.bufs(8)
        for kc in range(8):
            load_cast(C, wub[:, kc, :], b_wu[kc], winv[:, kc, 0:3072], [3072])
        wob = P.sb("wob", [128, 24, 1024], BF16); b_wo = P.bufs(6)
        wov = wout_d.rearrange("(fc p) n -> p fc n", p=128)
        for g in range(6):
            load_cast(C, wob[:, g * 4:(g + 1) * 4, :], b_wo[g], wov[:, g * 4:(g + 1) * 4, :], [4, 1024])
        xts = [P.sb("xt%d" % i, [128, 8, T], F32) for i in range(2)]; b_xts = P.bufs(2)
        N = NormBufs(C, T)
        vmt = P.sb("vmt", [128, 24, T], BF16); b_vmt = P.buf()
        gT = P.sb("gT", [128, 24, T], BF16); b_gT = P.bufs(24)
        ut = [P.sb("ut%d" % i, [128, T], F32) for i in range(2)]; b_ut = P.bufs(2)
        uring = Ring(list(zip(C.psums[1:5], C.psum_b[1:5])))
        yring = Ring(list(zip(C.psums[5:8], C.psum_b[5:8])))
        b_xo = P.buf()
        for ti in range(T_core // T):
            xt, b_xt = xts[ti % 2], b_xts[ti % 2]
            t0 = ti * T
            P.dma(xt[:], xv[:, :, t0:t0 + T], writes=[b_xt])
            P.dma(vmt[:], vmv[:, :, t0:t0 + T], writes=[b_vmt])
            emit_norm_mod(C, N, xt, b_xt, 0, 8)
            for fc in range(24):
                ps, bps = uring.next()
                for kc in range(8):
                    P.op("pe", lambda e, kc=kc, fc=fc, ps=ps: e.matmul(
                        ps[:, :T], lhsT=wub[:, kc, fc * 128:(fc + 1) * 128], rhs=N.hT[:, kc, :],
                        start=(kc == 0), stop=(kc == 7)), reads=[b_wu[kc], N.b_hT], writes=[bps])
                u, bu = ut[fc % 2], b_ut[fc % 2]
                P.op("act", lambda e, ps=ps, u=u: e.activation(out=u[:], in_=ps[:, :T], func=AF.Gelu), reads=[bps], writes=[bu])
                eng = "pool" if fc % 2 == 0 else "dve"
                P.op(eng, lambda e, u=u, fc=fc: e.tensor_tensor(out=gT[:, fc, :], in0=u[:], in1=vmt[:, fc, :], op=ALU.mult),
                     reads=[bu, b_vmt], writes=[b_gT[fc]])
            for oc in range(8):
                ps, bps = yring.next()
                for fc in range(24):
                    P.op("pe", lambda e, oc=oc, fc=fc, ps=ps: e.matmul(
                        ps[:, :T], lhsT=wob[:, fc, oc * 128:(oc + 1) * 128], rhs=gT[:, fc, :],
                        start=(fc == 0), stop=(fc == 23)), reads=[b_wo[fc // 4], b_gT[fc]], writes=[bps])
                P.op("dve", lambda e, oc=oc, ps=ps, xt=xt: e.scalar_tensor_tensor(
                    out=xt[:, oc, :], in0=ps[:, :T], scalar=AB[:, 16 + oc:17 + oc], in1=xt[:, oc, :],
                    op0=ALU.mult, op1=ALU.add), reads=[bps, C.b_ab, b_xt], writes=[b_xt])
            P.dma(xov[:, :, t0:t0 + T], xt[:], reads=[b_xt], writes=[b_xo])


def _lay_common(d, L, b):
    return {
        "cT": np.ascontiguousarray(d["c"][b].reshape(8, 128).T),
        "ada_w": np.ascontiguousarray(d["ada_w"][L]),
        "ada_b": np.ascontiguousarray(d["ada_b"][L].reshape(48, 128).T),
        "gains": np.ascontiguousarray(np.concatenate([d["norm_mix"][L].reshape(8, 128), d["norm_ffn"][L].reshape(8, 128)], 0).T),
    }


def _decl_common(nc):
    cT_d = nc.dram_tensor("cT", [128, 8], F32, kind="ExternalInput").ap()
    adaw_d = nc.dram_tensor("ada_w", [1024, 6144], F32, kind="ExternalInput").ap()
    adab_d = nc.dram_tensor("ada_b", [128, 48], F32, kind="ExternalInput").ap()
    gains_d = nc.dram_tensor("gains", [128, 16], F32, kind="ExternalInput").ap()
    return cT_d, adaw_d, adab_d, gains_d


def build_L1(T_core=4096):
    nc = bass.Bass("TRN2", target_bir_lowering=False)
    xT_d = nc.dram_tensor("xT", [1024, T_core], F32, kind="ExternalInput").ap()
    cm = _decl_common(nc)
    w_in_d = nc.dram_tensor("b_w_in", [1024, 6144], F32, kind="ExternalInput").ap()
    vgain_d = nc.dram_tensor("b_vgain", [128, 3072], F32, kind="ExternalInput").ap()
    wsT_d = nc.dram_tensor("b_wsT", [128, 8, 128], F32, kind="ExternalInput").ap()
    bs_d = nc.dram_tensor("b_bs", [128, 24, 128], F32, kind="ExternalInput").ap()
    wout_d = nc.dram_tensor("b_w_out", [3072, 1024], F32, kind="ExternalInput").ap()
    w1_d = nc.dram_tensor("w1", [1024, 4096], F32, kind="ExternalInput").ap()
    w2_d = nc.dram_tensor("w2", [4096, 1024], F32, kind="ExternalInput").ap()
    yT_d = nc.dram_tensor("yT", [1024, T_core], F32, kind="ExternalOutput").ap()
    vm_d = nc.dram_tensor("vm_s", [24, 128, T_core], BF16, kind="ExternalOutput" if DEBUG else "Internal").ap()
    xs_d = nc.dram_tensor("xs_s", [1024, T_core], F32).ap()
    C = make_ctx(nc)
    emit_ada(C, *cm)
    emit_mixB(C, xT_d, xs_d, w_in_d, vgain_d, wsT_d, bs_d, wout_d, vm_d, T_core)
    emit_ffn(C, xs_d, yT_d, w1_d, w2_d, T_core)
    C.P.emit()
    return nc


def lay_L1(d, b):
    m = _lay_common(d, 1, b)
    m.update({
        "b_w_in": np.ascontiguousarray(d["b_w_in"][0]),
        "b_vgain": np.ascontiguousarray(np.broadcast_to(d["b_v_gain"][0][None, :], (128, 3072))),
        "b_wsT": np.ascontiguousarray(d["b_w_s"][0].transpose(2, 0, 1)),
        "b_bs": np.ascontiguousarray(np.broadcast_to(np.repeat(d["b_b_s"][0], 3, axis=0)[None], (128, 24, 128))),
        "b_w_out": np.ascontiguousarray(d["b_w_out"][0]),
        "w1": np.ascontiguousarray(d["ffn_w1"][1]), "w2": np.ascontiguousarray(d["ffn_w2"][1]),
    })
    return m


HALO = 512


def emit_mixA1(C, xT_d, w_in_d, gqk_d, qT_s, kT_s, v_s, T_core, T=256):
    P = C.P
    TT = T_core + HALO
    xv = xT_d.rearrange("(kc p) t -> p kc t", p=128)
    winv = w_in_d.rearrange("(kc p) f -> p kc f", p=128)
    qv = qT_s.rearrange("(oc p) t -> p oc t", p=128)
    kv = kT_s.rearrange("(oc p) t -> p oc t", p=128)
    with P.scope():
        wq = P.sb("wq", [128, 8, 3072], BF16); b_wq = P.bufs(8)
        for kc in range(8):
            load_cast(C, wq[:, kc, :], b_wq[kc], winv[:, kc, :], [3072])
        gqk = P.sb("gqk", [128, 2], F32); b_g = P.buf()
        P.dma(gqk[:], gqk_d, writes=[b_g])
        P.op("dve", lambda e: e.tensor_scalar_mul(out=gqk[:, 0:1], in0=gqk[:, 0:1], scalar1=0.125), reads=[b_g], writes=[b_g])
        bd = P.sb("bd", [128, 128], BF16); b_bd = P.buf()

        P.op("pool", lambda e: e.memset(bd[:], 0.0), writes=[b_bd])
        P.op("pool", lambda e: e.memset(bd[0:64, 0:64], 1.0 / 64.0), writes=[b_bd])
        P.op("pool", lambda e: e.memset(bd[64:128, 64:128], 1.0 / 64.0), writes=[b_bd])
        xts = [P.sb("xt%d" % i, [128, 8, T], F32) for i in range(2)]; b_xts = P.bufs(2)
        N = NormBufs(C, T)
        sqq = P.sb("sqq", [128, T], BF16); b_sqq = P.buf()
        rs = P.sb("rs", [128, T], F32); b_rs = P.buf()
        qk_o = [P.sb("qko%d" % i, [128, 8, T], BF16) for i in range(2)]; b_qko = P.bufs(2)
        vpad = [P.sb("vpad%d" % i, [128, 16, 128], BF16) for i in range(2)]; b_vpad = P.bufs(2)
        for i in range(2):
            P.op("pool", lambda e, i=i: e.memset(vpad[i][:], 0.0), writes=[b_vpad[i]])
        b_sc = P.buf()
        pring = Ring(list(zip(C.psums[1:4], C.psum_b[1:4])))
        sring = Ring(list(zip(C.psums[4:6], C.psum_b[4:6])))
        vring = Ring(list(zip(C.psums[6:8], C.psum_b[6:8])))
        vi = 0
        for ti in range(TT // T):
            xt, b_xt = xts[ti % 2], b_xts[ti % 2]
            t0 = ti * T
            P.dma(xt[:], xv[:, :, t0:t0 + T], writes=[b_xt])
            emit_norm_mod(C, N, xt, b_xt, 0, 8)
            for which in range(2):
                if which == 0 and t0 + T <= HALO:
                    continue
                qo, bqo = qk_o[which], b_qko[which]
                for oc in range(8):
                    ps, bps = pring.next()
                    col0 = which * 1024 + oc * 128
                    for kc in range(8):
                        P.op("pe", lambda e, kc=kc, ps=ps, col0=col0: e.matmul(
                            ps[:, :T], lhsT=wq[:, kc, col0:col0 + 128], rhs=N.hT[:, kc, :], start=(kc == 0), stop=(kc == 7)),
                            reads=[b_wq[kc], N.b_hT], writes=[bps])
                    P.op("act", lambda e, ps=ps: e.activation(out=sqq[:], in_=ps[:, :T], func=AF.Square), reads=[bps], writes=[b_sqq])
                    ps2, bps2 = sring.next()
                    P.op("pe", lambda e, ps2=ps2: e.matmul(ps2[:, :T], lhsT=bd[:], rhs=sqq[:], start=True, stop=True),
                         reads=[b_sqq, b_bd], writes=[bps2])
                    P.op("act", lambda e, ps2=ps2: e.activation(out=rs[:], in_=ps2[:, :T], func=AF.Sqrt, bias=C.eps[:, 0:1], scale=1.0),
                         reads=[bps2, C.cb], writes=[b_rs])
                    P.op("dve", lambda e: e.reciprocal(out=rs[:], in_=rs[:]), reads=[b_rs], writes=[b_rs])
                    P.op("dve", lambda e, ps=ps, oc=oc, qo=qo, which=which: e.scalar_tensor_tensor(
                        out=qo[:, oc, :], in0=ps[:, :T], scalar=gqk[:, which:which + 1], in1=rs[:], op0=ALU.mult, op1=ALU.mult),
                        reads=[bps, b_rs, b_g], writes=[bqo])
                dst = qv if which == 0 else kv
                if which == 0:
                    P.dma(dst[:, :, t0 - HALO:t0 - HALO + T], qo[:], reads=[bqo], writes=[b_sc])
                else:
                    P.dma(dst[:, :, t0:t0 + T], qo[:], reads=[bqo], writes=[b_sc])
            for w in range(T // 128):
                vp, bvp = vpad[vi % 2], b_vpad[vi % 2]
                vi += 1
                for half in range(2):
                    ps, bps = vring.next()
                    for kc in range(8):
                        P.op("pe", lambda e, kc=kc, ps=ps, w=w, half=half: e.matmul(
                            ps[:, :], lhsT=N.hT[:, kc, w * 128:(w + 1) * 128], rhs=wq[:, kc, 2048 + half * 512:2048 + (half + 1) * 512],
                            start=(kc == 0), stop=(kc == 7)), reads=[b_wq[kc], N.b_hT], writes=[bps])
                    psv = ps[:, :].rearrange("p (hp two d) -> p hp two d", hp=4, two=2)
                    vpv = vp[:, half * 8:(half + 1) * 8, :].rearrange("p (hp two) c -> p hp two c", two=2)
                    P.op("act", lambda e, psv=psv, vpv=vpv: e.activation(out=vpv[:, :, 0, 0:64], in_=psv[:, :, 0, :], func=AF.Identity),
                         reads=[bps], writes=[bvp])
                    P.op("dve", lambda e, psv=psv, vpv=vpv: e.tensor_copy(out=vpv[:, :, 1, 64:128], in_=psv[:, :, 1, :]),
                         reads=[bps], writes=[bvp])
                tw = t0 + w * 128
                P.dma(v_s[tw:tw + 128, :].rearrange("t (h c) -> t h c", h=16), vp[:], reads=[bvp], writes=[b_sc])


def emit_mixA2(C, xT_d, xo_d, qT_s, kT_s, v_s, biasT_d, valid_d, wout_d, T_core):
    P = C.P
    AB = C.AB
    xv = xT_d.rearrange("(kc p) t -> p kc t", p=128)
    xov = xo_d.rearrange("(kc p) t -> p kc t", p=128)
    qv = qT_s.rearrange("(oc p) t -> p oc t", p=128)
    kv = kT_s.rearrange("(oc p) t -> p oc t", p=128)
    with P.scope():
        wob = P.sb("wob", [128, 8, 1024], BF16); b_wo = P.bufs(2)
        wov = wout_d.rearrange("(fc p) n -> p fc n", p=128)
        for g in range(2):
            load_cast(C, wob[:, g * 4:(g + 1) * 4, :], b_wo[g], wov[:, g * 4:(g + 1) * 4, :], [4, 1024])
        ebias = P.sb("ebias", [128, 16, 640], F32); b_eb = P.buf()
        for g in range(4):
            P.dma(ebias[:, g * 4:(g + 1) * 4, :], biasT_d[:, g * 4:(g + 1) * 4, :], writes=[b_eb])
        P.op("act", lambda e: e.activation(out=ebias[:], in_=ebias[:], func=AF.Exp), reads=[b_eb], writes=[b_eb])
        nkb = (T_core + HALO) // 128
        valid = P.sb("valid", [128, nkb], F32); b_val = P.buf()
        P.dma(valid[:], valid_d, writes=[b_val])
        selA = P.sb("selA", [128, 128], BF16); selB = P.sb("selB", [128, 128], BF16); b_sel = P.buf()

        P.op("pool", lambda e: e.memset(selA[:], 0.0), writes=[b_sel])
        P.op("pool", lambda e: e.memset(selB[:], 0.0), writes=[b_sel])
        P.op("pool", lambda e: e.memset(selA[:, 0:64], 1.0), writes=[b_sel])
        P.op("pool", lambda e: e.memset(selB[:, 64:128], 1.0), writes=[b_sel])
        NB = 2
        qt = [P.sb("qt%d" % i, [128, 8, 128], BF16) for i in range(NB)]; b_qt = P.bufs(NB)
        kt = [P.sb("kt%d" % i, [128, 8, 640], BF16) for i in range(NB)]; b_kt = P.bufs(NB)
        vt = [P.sb("vt%d" % i, [128, 5, 2048], BF16) for i in range(NB)]; b_vt = P.bufs(NB)
        xts = [P.sb("xa%d" % i, [128, 8, 128], F32) for i in range(NB)]; b_xts = P.bufs(NB)
        et = [P.sb("et%d" % i, [128, 640], F32) for i in range(2)]; b_et = P.bufs(2)
        pt = [P.sb("pt%d" % i, [128, 640], BF16) for i in range(2)]; b_pt = P.bufs(2)
        oT = P.sb("oT", [128, 8, 128], BF16); b_oT = P.bufs(8)
        den = P.sb("den", [128, 128], F32); b_den = P.buf()
        b_xo = P.buf()
        sc = [(C.psums[1], C.psum_b[1], C.psums[2], C.psum_b[2]), (C.psums[3], C.psum_b[3], C.psums[4], C.psum_b[4])]
        po, b_po = C.psums[5], C.psum_b[5]
        pd, b_pd = C.psums[6], C.psum_b[6]
        py, b_py = C.psums[7], C.psum_b[7]
        def make_item(hi, i, m, pr, ab, kb0):
            h = pr * 2 + ab
            sA, bsA, sB, bsB = sc[hi % 2]
            e_, be = et[hi % 2], b_et[hi % 2]
            p_, bp = pt[hi % 2], b_pt[hi % 2]
            lo = ab * 64
            sel = selA if ab == 0 else selB

            def stage_a():
                for j in range(5):
                    dst, bd_ = (sA, bsA) if j < 4 else (sB, bsB)
                    c0 = (j % 4) * 128
                    P.op("pe", lambda e, j=j, dst=dst, c0=c0: e.matmul(
                        dst[:, c0:c0 + 128], lhsT=kt[i][lo:lo + 64, pr, j * 128:(j + 1) * 128], rhs=qt[i][lo:lo + 64, pr, :],
                        start=True, stop=True), reads=[b_kt[i], b_qt[i]], writes=[bd_])
                P.op("act", lambda e: e.activation(out=e_[:, 0:512], in_=sA[:, :], func=AF.Exp), reads=[bsA], writes=[be])
                P.op("act", lambda e: e.activation(out=e_[:, 512:640], in_=sB[:, 0:128], func=AF.Exp), reads=[bsB], writes=[be])
                for j in range(5):
                    P.op("dve", lambda e, j=j: e.scalar_tensor_tensor(
                        out=p_[:, j * 128:(j + 1) * 128], in0=e_[:, j * 128:(j + 1) * 128], scalar=valid[:, kb0 + j:kb0 + j + 1],
                        in1=ebias[:, h, j * 128:(j + 1) * 128], op0=ALU.mult, op1=ALU.mult),
                        reads=[be, b_val, b_eb], writes=[bp])

            def stage_b():
                for j in range(5):
                    first = (ab == 0 and j == 0)
                    last = (ab == 1 and j == 4)
                    P.op("pe", lambda e, j=j, first=first, last=last: e.matmul(
                        po[:, 0:128], lhsT=vt[i][:, j, h * 128:(h + 1) * 128], rhs=p_[:, j * 128:(j + 1) * 128],
                        start=first, stop=last), reads=[b_vt[i], bp], writes=[b_po])
                    P.op("pe", lambda e, j=j, first=first, last=last: e.matmul(
                        pd[:, 0:128], lhsT=sel[:], rhs=p_[:, j * 128:(j + 1) * 128],
                        start=first, stop=last), reads=[b_sel, bp], writes=[b_pd])
                if ab == 1:
                    P.op("dve", lambda e: e.reciprocal(out=den[:], in_=pd[:, 0:128]), reads=[b_pd], writes=[b_den])
                    P.op("dve", lambda e: e.tensor_tensor(out=oT[:, pr, :], in0=po[:, 0:128], in1=den[:], op=ALU.mult),
                         reads=[b_po, b_den], writes=[b_oT[pr]])
                if ab == 1 and pr == 7:
                    xt, b_xt = xts[i], b_xts[i]
                    q0 = m * 128
                    for oc in range(8):
                        for pr2 in range(8):
                            P.op("pe", lambda e, oc=oc, pr2=pr2: e.matmul(
                                py[:, (oc % 4) * 128:(oc % 4) * 128 + 128],
                                lhsT=wob[:, pr2, oc * 128:(oc + 1) * 128], rhs=oT[:, pr2, :],
                                start=(pr2 == 0), stop=(pr2 == 7)), reads=[b_wo[pr2 // 4], b_oT[pr2]], writes=[b_py])
                        P.op("dve", lambda e, oc=oc: e.scalar_tensor_tensor(
                            out=xt[:, oc, :], in0=py[:, (oc % 4) * 128:(oc % 4) * 128 + 128], scalar=AB[:, 16 + oc:17 + oc], in1=xt[:, oc, :],
                            op0=ALU.mult, op1=ALU.add), reads=[b_py, C.b_ab, b_xt], writes=[b_xt])
                    P.dma(xov[:, :, q0:q0 + 128], xt[:], reads=[b_xt], writes=[b_xo])
            return stage_a, stage_b

        pending = None
        hi = 0
        for m in range(T_core // 128):
            i = m % NB
            q0 = m * 128
            k0 = m * 128
            P.dma(qt[i][:], qv[:, :, q0:q0 + 128], writes=[b_qt[i]])
            P.dma(kt[i][:], kv[:, :, k0:k0 + 640], writes=[b_kt[i]])
            P.dma(vt[i][:], v_s[k0:k0 + 640, :].rearrange("(j p) c -> p j c", p=128), writes=[b_vt[i]])
            P.dma(xts[i][:], xv[:, :, HALO + q0:HALO + q0 + 128], writes=[b_xts[i]])
            kb0 = k0 // 128
            for pr in range(8):
                for ab in range(2):
                    sa, sb_ = make_item(hi, i, m, pr, ab, kb0)
                    hi += 1
                    sa()
                    if pending is not None:
                        pending()
                    pending = sb_
        pending()


def build_L0(T_core=4096):
    nc = bass.Bass("TRN2", target_bir_lowering=False)
    TT = T_core + HALO
    xT_d = nc.dram_tensor("xT", [1024, TT], F32, kind="ExternalInput").ap()
    cm = _decl_common(nc)
    w_in_d = nc.dram_tensor("a_w_in", [1024, 3072], F32, kind="ExternalInput").ap()
    gqk_d = nc.dram_tensor("a_gqk", [128, 2], F32, kind="ExternalInput").ap()
    biasT_d = nc.dram_tensor("a_biasT", [128, 16, 640], F32, kind="ExternalInput").ap()
    valid_d = nc.dram_tensor("a_valid", [128, TT // 128], F32, kind="ExternalInput").ap()
    wout_d = nc.dram_tensor("a_w_out", [1024, 1024], F32, kind="ExternalInput").ap()
    w1_d = nc.dram_tensor("w1", [1024, 4096], F32, kind="ExternalInput").ap()
    w2_d = nc.dram_tensor("w2", [4096, 1024], F32, kind="ExternalInput").ap()
    yT_d = nc.dram_tensor("yT", [1024, T_core], F32, kind="ExternalOutput").ap()
    qT_s = nc.dram_tensor("qT_s", [1024, T_core], BF16).ap()
    kT_s = nc.dram_tensor("kT_s", [1024, TT], BF16).ap()
    v_s = nc.dram_tensor("v_s", [TT, 2048], BF16).ap()
    xs_d = nc.dram_tensor("xs_s", [1024, T_core], F32).ap()
    C = make_ctx(nc)
    emit_ada(C, *cm)
    emit_mixA1(C, xT_d, w_in_d, gqk_d, qT_s, kT_s, v_s, T_core)
    emit_mixA2(C, xT_d, xs_d, qT_s, kT_s, v_s, biasT_d, valid_d, wout_d, T_core)
    emit_ffn(C, xs_d, yT_d, w1_d, w2_d, T_core)
    C.P.emit()
    return nc


def _a_bias_table(rel_bias):
    kap = np.arange(640)[:, None]
    q = np.arange(128)[None, :]
    rel = q + 512 - kap
    cq = q // 64
    inband = (kap >= cq * 64) & (kap < cq * 64 + 576)
    idx = np.clip(rel, -63, 256) + 63
    tab = rel_bias[:, idx]
    tab = np.where(inband[None], tab, np.float32(-30000.0)).astype(np.float32)
    tab = tab.reshape(16, 5, 128, 128).transpose(2, 0, 1, 3).reshape(128, 16, 640)
    return np.ascontiguousarray(tab)


def lay_L0(d, b, half, T_core=4096, x=None):
    m = _lay_common(d, 0, b)
    x = d["x"] if x is None else x
    t0 = half * T_core
    TT = T_core + HALO
    xt = np.zeros((1024, TT), np.float32)
    lo = t0 - HALO
    if lo >= 0:
        xt[:, :] = x[b, lo:lo + TT, :].T
    else:
        xt[:, HALO:] = x[b, 0:T_core, :].T
    valid = np.ones((128, TT // 128), np.float32)
    if lo < 0:
        valid[:, :HALO // 128] = 0.0
    m.update({
        "xT": xt,
        "a_w_in": np.ascontiguousarray(d["a_w_in"][0]),
        "a_gqk": np.ascontiguousarray(np.stack([np.tile(d["a_q_gain"][0], 2), np.tile(d["a_k_gain"][0], 2)], 1)),
        "a_biasT": _a_bias_table(d["a_rel_bias"][0]),
        "a_valid": valid,
        "a_w_out": np.ascontiguousarray(d["a_w_out"][0]),
        "w1": np.ascontiguousarray(d["ffn_w1"][0]), "w2": np.ascontiguousarray(d["ffn_w2"][0]),
    })
    return m


def emit_mixD1(C, xT_d, wqkv_d, qT_s, kT_s, v_s, S, T=256, xsrc=None):
    P = C.P
    if xsrc is None:
        xv0 = xT_d.rearrange("(kc p) t -> p kc t", p=128)
        xsrc = lambda t0, T: xv0[:, :, t0:t0 + T]
    wv_ = wqkv_d.rearrange("(kc p) f -> p kc f", p=128)
    qv = qT_s.rearrange("(oc p) t -> p oc t", p=128)
    kv = kT_s.rearrange("(oc p) t -> p oc t", p=128)
    with P.scope():
        wq = P.sb("wq", [128, 8, 1536], BF16); b_wq = P.bufs(8)
        for kc in range(8):
            load_cast(C, wq[:, kc, :], b_wq[kc], wv_[:, kc, :], [1536])
        xts = [P.sb("xt%d" % i, [128, 8, T], F32) for i in range(2)]; b_xts = P.bufs(2)
        N = NormBufs(C, T)
        qk_o = [P.sb("qko%d" % i, [128, 4, T], BF16) for i in range(2)]; b_qko = P.bufs(2)
        vpad = [P.sb("vpad%d" % i, [128, 8, 128], BF16) for i in range(2)]; b_vpad = P.bufs(2)
        for i in range(2):
            P.op("pool", lambda e, i=i: e.memset(vpad[i][:], 0.0), writes=[b_vpad[i]])
        b_sc = P.buf()
        pring = Ring(list(zip(C.psums[1:5], C.psum_b[1:5])))
        vring = Ring(list(zip(C.psums[5:8], C.psum_b[5:8])))
        vi = 0
        for ti in range(S // T):
            xt, b_xt = xts[ti % 2], b_xts[ti % 2]
            t0 = ti * T
            P.dma(xt[:], xsrc(t0, T), writes=[b_xt])
            emit_norm_mod(C, N, xt, b_xt, 0, 8)
            for which in range(2):
                qo, bqo = qk_o[which], b_qko[which]
                for oc in range(4):
                    ps, bps = pring.next()
                    col0 = which * 512 + oc * 128
                    for kc in range(8):
                        P.op("pe", lambda e, kc=kc, ps=ps, col0=col0: e.matmul(
                            ps[:, :T], lhsT=wq[:, kc, col0:col0 + 128], rhs=N.hT[:, kc, :], start=(kc == 0), stop=(kc == 7)),
                            reads=[b_wq[kc], N.b_hT], writes=[bps])
                    sc = 0.125 if which == 0 else 1.0
                    P.op("act", lambda e, ps=ps, oc=oc, qo=qo, sc=sc: e.activation(out=qo[:, oc, :], in_=ps[:, :T], func=AF.Identity, scale=sc),
                         reads=[bps], writes=[bqo])
                dst = qv if which == 0 else kv
                P.dma(dst[:, :, t0:t0 + T], qo[:], reads=[bqo], writes=[b_sc])
            for w in range(T // 128):
                vp, bvp = vpad[vi % 2], b_vpad[vi % 2]
                vi += 1
                ps, bps = vring.next()
                for kc in range(8):
                    P.op("pe", lambda e, kc=kc, ps=ps, w=w: e.matmul(
                        ps[:, :], lhsT=N.hT[:, kc, w * 128:(w + 1) * 128], rhs=wq[:, kc, 1024:1536],
                        start=(kc == 0), stop=(kc == 7)), reads=[b_wq[kc], N.b_hT], writes=[bps])
                psv = ps[:, :].rearrange("p (hp two d) -> p hp two d", hp=4, two=2)
                vpv = vp[:].rearrange("p (hp two) c -> p hp two c", two=2)
                P.op("act", lambda e, psv=psv, vpv=vpv: e.activation(out=vpv[:, :, 0, 0:64], in_=psv[:, :, 0, :], func=AF.Identity),
                     reads=[bps], writes=[bvp])
                P.op("dve", lambda e, psv=psv, vpv=vpv: e.tensor_copy(out=vpv[:, :, 1, 64:128], in_=psv[:, :, 1, :]),
                     reads=[bps], writes=[bvp])
                tw = t0 + w * 128
                P.dma(v_s[tw:tw + 128, :].rearrange("t (h c) -> t h c", h=8), vp[:], reads=[bvp], writes=[b_sc])


def emit_mixD2(C, qT_s, kT_s, v_s, cst_d, oT_d, S, TQ=512, odst=None):
    P = C.P
    qv = qT_s.rearrange("(oc p) t -> p oc t", p=128)
    kv = kT_s.rearrange("(oc p) t -> p oc t", p=128)
    if odst is None:
        ov = oT_d.rearrange("(oc p) t -> p oc t", p=128)
        odst = lambda t0, n: ov[:, :, t0:t0 + n]
    with P.scope():
        cst = P.sb("cst", [128, 3, 128], F32); b_cst = P.buf()
        P.dma(cst[:], cst_d, writes=[b_cst])
        cbf = P.sb("cbf", [128, 3, 128], BF16); b_cbf = P.buf()
        P.op("dve", lambda e: e.tensor_copy(out=cbf[:], in_=cst[:]), reads=[b_cst], writes=[b_cbf])
        qt = [P.sb("qt%d" % i, [128, 4, TQ], BF16) for i in range(2)]; b_qt = P.bufs(2)
        NKB = 3
        kt = [P.sb("kt%d" % i, [128, 4, 128], BF16) for i in range(NKB)]; b_kt = P.bufs(NKB)
        vt = [P.sb("vt%d" % i, [128, 8, 128], BF16) for i in range(NKB)]; b_vt = P.bufs(NKB)
        NS = 4
        et = [P.sb("et%d" % i, [128, TQ], F32) for i in range(NS)]; b_et = P.bufs(NS)
        lt = [P.sb("lt%d" % i, [128, TQ], BF16) for i in range(NS)]; b_lt = P.bufs(NS)
        wt = [P.sb("wt%d" % i, [128, TQ], BF16) for i in range(NS)]; b_wt = P.bufs(NS)
        lacc = [P.sb("lacc%d" % i, [128, TQ], BF16) for i in range(8)]; b_lacc = P.bufs(8)
        osb = [P.sb("osb%d" % i, [128, 4, TQ], F32) for i in range(2)]; b_osb = P.bufs(2)
        b_od = P.buf()
        z1ring = Ring(list(zip(C.psums[0:2], C.psum_b[0:2])))
        z2ring = Ring(list(zip(C.psums[2:4], C.psum_b[2:4])))
        poring = [(C.psums[4 + i], C.psum_b[4 + i]) for i in range(4)]
        nq = TQ // 128

        def make_item(hi, q, bq, k_, bk, v_, bv, h, kb, dq, c0, first):
            pr, ab = h // 2, h % 2
            lo = ab * 64
            e_, be = et[hi % NS], b_et[hi % NS]
            l_, bl = lt[hi % NS], b_lt[hi % NS]
            w_, bw = wt[hi % NS], b_wt[hi % NS]
            la, bla = lacc[h], b_lacc[h]
            po, bpo = poring[pr]

            def stage_a():
                zp, bzp = z1ring.next()
                P.op("pe", lambda e: e.matmul(zp[:, c0:TQ], lhsT=k_[lo:lo + 64, pr, :], rhs=q[lo:lo + 64, pr, c0:TQ], start=True, stop=True),
                     reads=[bk, bq], writes=[bzp])
                P.op("act", lambda e: e.activation(out=e_[:, c0:TQ], in_=zp[:, c0:TQ], func=AF.Exp), reads=[bzp], writes=[be])
                P.op("act", lambda e: e.activation(out=l_[:, c0:TQ], in_=e_[:, c0:TQ], func=AF.Ln, bias=C.ones_f[:, 0:1], scale=1.0),
                     reads=[be, C.cb], writes=[bl])
                if dq >= 0:
                    P.op("dve", lambda e: e.tensor_tensor(out=l_[:, c0:c0 + 128], in0=l_[:, c0:c0 + 128], in1=cbf[:, 0, :], op=ALU.mult),
                         reads=[bl, b_cbf], writes=[bl])

            def stage_b():
                zq, bzq = z2ring.next()
                P.op("pe", lambda e: e.matmul(zq[:, c0:TQ], lhsT=k_[lo:lo + 64, pr, :], rhs=q[lo:lo + 64, pr, c0:TQ], start=True, stop=False),
                     reads=[bk, bq], writes=[bzq])
                P.op("pe", lambda e: e.matmul(zq[:, c0:TQ], lhsT=cbf[:, 1, :], rhs=l_[:, c0:TQ], start=False, stop=first),
                     reads=[bl, b_cbf], writes=[bzq])
                if not first:
                    P.op("pe", lambda e: e.matmul(zq[:, c0:TQ], lhsT=cbf[:, 2, :], rhs=la[:, c0:TQ], start=False, stop=True),
                         reads=[bla, b_cbf], writes=[bzq])
                if c0 > 0:
                    P.op("pool", lambda e: e.memset(w_[:, 0:c0], 0.0), writes=[bw])
                P.op("act", lambda e: e.activation(out=w_[:, c0:TQ], in_=zq[:, c0:TQ], func=AF.Exp), reads=[bzq], writes=[bw])
                if dq >= 0:
                    P.op("dve", lambda e: e.tensor_tensor(out=w_[:, c0:c0 + 128], in0=w_[:, c0:c0 + 128], in1=cbf[:, 0, :], op=ALU.mult),
                         reads=[bw, b_cbf], writes=[bw])

            def stage_c():
                if kb > 0:
                    if first:
                        _lacc_init(P, la, bla, l_, bl, c0, TQ)
                    else:
                        P.op("pool", lambda e: e.tensor_tensor(out=la[:, c0:TQ], in0=la[:, c0:TQ], in1=l_[:, c0:TQ], op=ALU.add),
                             reads=[bl, bla], writes=[bla])
                P.op("pe", lambda e: e.matmul(po[:, 0:TQ], lhsT=v_[:, h, :], rhs=w_[:, 0:TQ], start=(first and ab == 0), stop=(kb == 0 and ab == 1)),
                     reads=[bv, bw], writes=[bpo])
            return stage_a, stage_b, stage_c

        def make_post(os_, bos, q0):
            def post():
                for pr in range(4):
                    po, bpo = poring[pr]
                    if pr % 2 == 0:
                        P.op("act", lambda e, po=po, pr=pr: e.activation(out=os_[:, pr, :], in_=po[:, :], func=AF.Identity), reads=[bpo], writes=[bos])
                    else:
                        P.op("dve", lambda e, po=po, pr=pr: e.tensor_copy(out=os_[:, pr, :], in_=po[:, :]), reads=[bpo], writes=[bos])
                P.dma(odst(q0, TQ), os_[:], reads=[bos], writes=[b_od])
            return post

        pend_b = []
        pend_c = []

        def push(stage_a, stage_b, stage_c, post):
            stage_a()
            nb = pend_b.pop(0) if pend_b else None
            if nb is not None:
                nb[0]()
            if pend_c:
                c_, post_ = pend_c.pop(0)
                c_()
                if post_ is not None:
                    post_()
            if nb is not None:
                pend_c.append((nb[1], nb[2]))
            pend_b.append((stage_b, stage_c, post))

        hi = 0
        ki = 0
        for qi in range(S // TQ):
            q, bq = qt[qi % 2], b_qt[qi % 2]
            q0 = qi * TQ
            P.dma(q[:], qv[:, :, q0:q0 + TQ], writes=[bq])
            os_, bos = osb[qi % 2], b_osb[qi % 2]
            kb_hi = qi * nq + nq - 1
            for kb in range(kb_hi, -1, -1):
                k_, bk = kt[ki % NKB], b_kt[ki % NKB]
                v_, bv = vt[ki % NKB], b_vt[ki % NKB]
                ki += 1
                P.dma(k_[:], kv[:, :, kb * 128:(kb + 1) * 128], writes=[bk])
                P.dma(v_[:], v_s[kb * 128:(kb + 1) * 128, :].rearrange("t (h c) -> t h c", h=8), writes=[bv])
                dq = kb - qi * nq
                c0 = max(dq, 0) * 128
                first = (kb == kb_hi)
                for h in range(8):
                    sa, sb_, sc_ = make_item(hi, q, bq, k_, bk, v_, bv, h, kb, dq, c0, first)
                    hi += 1
                    post = make_post(os_, bos, q0) if (kb == 0 and h == 7) else None
                    push(sa, sb_, sc_, post)
        while pend_b or pend_c:
            nb = pend_b.pop(0) if pend_b else None
            if nb is not None:
                nb[0]()
            if pend_c:
                c_, post_ = pend_c.pop(0)
                c_()
                if post_ is not None:
                    post_()
            if nb is not None:
                pend_c.append((nb[1], nb[2]))


def _lacc_init(P, la, bla, l_, bl, c0, TQ):
    if c0 > 0:
        P.op("pool", lambda e: e.memset(la[:, 0:c0], 0.0), writes=[bla])
    P.op("pool", lambda e: e.tensor_copy(out=la[:, c0:TQ], in_=l_[:, c0:TQ]), reads=[bl], writes=[bla])


def emit_outproj(C, xT_d, oT_d, wout_d, xo_d, T_core, T=256, sel=None, xsrc=None, osrc=None):
    P = C.P
    AB = C.AB
    if xsrc is None:
        xv = xT_d.rearrange("(kc p) t -> p kc t", p=128)
        xsrc = lambda t0, T: xv[:, :, t0:t0 + T]
    if osrc is None:
        ov = oT_d.rearrange("(kc p) t -> p kc t", p=128)
        osrc = lambda h, t0, T: ov[:, :, h * T_core + t0:h * T_core + t0 + T]
    xov = xo_d.rearrange("(kc p) t -> p kc t", p=128)
    with P.scope():
        wob = P.sb("wob", [128, 8, 1024], BF16); b_wo = P.bufs(2)
        wov = wout_d.rearrange("(fc p) n -> p fc n", p=128)
        for g in range(2):
            load_cast(C, wob[:, g * 4:(g + 1) * 4, :], b_wo[g], wov[:, g * 4:(g + 1) * 4, :], [4, 1024])
        xts = [P.sb("xt%d" % i, [128, 8, T], F32) for i in range(2)]; b_xts = P.bufs(2)
        ots = [P.sb("ot%d" % i, [128, 8, T], F32) for i in range(2)]; b_ots = P.bufs(2)
        ob = P.sb("ob", [128, 8, T], BF16); b_ob = P.buf()
        if sel is not None:
            ots1 = [P.sb("ot1_%d" % i, [128, 8, T], F32) for i in range(2)]; b_ots1 = P.bufs(2)
        yring = Ring(list(zip(C.psums[1:8], C.psum_b[1:8])))
        b_xo = P.buf()
        for ti in range(T_core // T):
            xt, b_xt = xts[ti % 2], b_xts[ti % 2]
            ot, b_ot = ots[ti % 2], b_ots[ti % 2]
            t0 = ti * T
            P.dma(xt[:], xsrc(t0, T), writes=[b_xt])
            P.dma(ot[:], osrc(0, t0, T), writes=[b_ot])
            if sel is None:
                P.op("act", lambda e, ot=ot: e.activation(out=ob[:], in_=ot[:], func=AF.Identity), reads=[b_ot], writes=[b_ob])
            else:
                selt, b_sel = sel
                ot1, b_ot1 = ots1[ti % 2], b_ots1[ti % 2]
                P.dma(ot1[:], osrc(1, t0, T), writes=[b_ot1])
                P.op("act", lambda e, ot=ot: e.activation(out=ot[:], in_=ot[:], func=AF.Identity, scale=selt[:, 0:1]),
                     reads=[b_ot, b_sel], writes=[b_ot])
                P.op("dve", lambda e, ot=ot, ot1=ot1: e.scalar_tensor_tensor(out=ob[:], in0=ot1[:], scalar=selt[:, 1:2], in1=ot[:],
                                                                            op0=ALU.mult, op1=ALU.add),
                     reads=[b_ot, b_ot1, b_sel], writes=[b_ob])
            for oc in range(8):
                ps, bps = yring.next()
                for kc in range(8):
                    P.op("pe", lambda e, oc=oc, kc=kc, ps=ps: e.matmul(
                        ps[:, :T], lhsT=wob[:, kc, oc * 128:(oc + 1) * 128], rhs=ob[:, kc, :],
                        start=(kc == 0), stop=(kc == 7)), reads=[b_wo[kc // 4], b_ob], writes=[bps])
                P.op("dve", lambda e, oc=oc, ps=ps, xt=xt: e.scalar_tensor_tensor(
                    out=xt[:, oc, :], in0=ps[:, :T], scalar=AB[:, 16 + oc:17 + oc], in1=xt[:, oc, :],
                    op0=ALU.mult, op1=ALU.add), reads=[bps, C.b_ab, b_xt], writes=[b_xt])
            P.dma(xov[:, :, t0:t0 + T], xt[:], reads=[b_xt], writes=[b_xo])


def build_L3a(S=8192):
    nc = bass.Bass("TRN2", target_bir_lowering=False)
    xT_d = nc.dram_tensor("xT", [1024, S], F32, kind="ExternalInput").ap()
    cm = _decl_common(nc)
    wqkv_d = nc.dram_tensor("d_wqkv", [1024, 1536], F32, kind="ExternalInput").ap()
    cst_d = nc.dram_tensor("d_cst", [128, 3, 128], F32, kind="ExternalInput").ap()
    oT_d = nc.dram_tensor("oT", [512, S], F32, kind="ExternalOutput").ap()
    qT_s = nc.dram_tensor("qT_s", [512, S], BF16).ap()
    kT_s = nc.dram_tensor("kT_s", [512, S], BF16).ap()
    v_s = nc.dram_tensor("v_s", [S, 1024], BF16).ap()
    C = make_ctx(nc)
    emit_ada(C, *cm)
    emit_mixD1(C, xT_d, wqkv_d, qT_s, kT_s, v_s, S)
    emit_mixD2(C, qT_s, kT_s, v_s, cst_d, oT_d, S)
    C.P.emit()
    return nc


def _d_consts():
    j = np.arange(128)[:, None]
    s_ = np.arange(128)[None, :]
    mask = (s_ > j).astype(np.float32)
    ntri = -(j >= s_).astype(np.float32)
    nones = -np.ones((128, 128), np.float32)
    return np.ascontiguousarray(np.stack([mask, ntri, nones], 1))


def lay_L3a(d, b, hg, xT_full):
    m = _lay_common(d, 3, b)
    w = d["d_w_in"][0]
    m.update({
        "xT": xT_full,
        "d_wqkv": np.ascontiguousarray(np.concatenate([w[:, hg * 512:(hg + 1) * 512], w[:, 1024 + hg * 512:1024 + (hg + 1) * 512],
                                                       w[:, 2048 + hg * 512:2048 + (hg + 1) * 512]], 1)),
        "d_cst": _d_consts(),
    })
    return m


def build_Lb(T_core=4096):
    nc = bass.Bass("TRN2", target_bir_lowering=False)
    xT_d = nc.dram_tensor("xT", [1024, T_core], F32, kind="ExternalInput").ap()
    oT_d = nc.dram_tensor("oT", [1024, T_core], F32, kind="ExternalInput").ap()
    cm = _decl_common(nc)
    wout_d = nc.dram_tensor("w_out", [1024, 1024], F32, kind="ExternalInput").ap()
    w1_d = nc.dram_tensor("w1", [1024, 4096], F32, kind="ExternalInput").ap()
    w2_d = nc.dram_tensor("w2", [4096, 1024], F32, kind="ExternalInput").ap()
    yT_d = nc.dram_tensor("yT", [1024, T_core], F32, kind="ExternalOutput").ap()
    xs_d = nc.dram_tensor("xs_s", [1024, T_core], F32).ap()
    C = make_ctx(nc)
    emit_ada(C, *cm)
    emit_outproj(C, xT_d, oT_d, wout_d, xs_d, T_core)
    emit_ffn(C, xs_d, yT_d, w1_d, w2_d, T_core)
    C.P.emit()
    return nc


def lay_Lb(d, L, b, w_out):
    m = _lay_common(d, L, b)
    m.update({"w_out": np.ascontiguousarray(w_out), "w1": np.ascontiguousarray(d["ffn_w1"][L]), "w2": np.ascontiguousarray(d["ffn_w2"][L])})
    return m


def emit_mixC(C, xT_d, wc_d, wg_d, bg_d, og_d, cst_d, oT_d, S, T=256, xsrc=None, odst=None):
    P = C.P
    if xsrc is None:
        xv0 = xT_d.rearrange("(kc p) t -> p kc t", p=128)
        xsrc = lambda t0, T: xv0[:, :, t0:t0 + T]
    wv_ = wc_d.rearrange("(kc p) f -> p kc f", p=128)
    if odst is None:
        ov = oT_d.rearrange("(oc p) t -> p oc t", p=128)
        odst = lambda t0, n: ov[:, :, t0:t0 + n]
    with P.scope():
        wc = P.sb("wc", [128, 8, 1552], BF16); b_wc = P.bufs(8)
        for kc in range(8):
            load_cast(C, wc[:, kc, :], b_wc[kc], wv_[:, kc, :], [1552])
        wg = P.sb("wg", [16, 256], F32); bg = P.sb("bg", [1, 256], F32); og = P.sb("og", [128, 2], F32)
        cst = P.sb("cst", [128, 3, 128], F32)
        b_k = P.buf()
        P.dma(wg[:], wg_d, writes=[b_k]); P.dma(bg[:], bg_d, writes=[b_k]); P.dma(og[:], og_d, writes=[b_k])
        P.dma(cst[:], cst_d, writes=[b_k])
        xts = [P.sb("xt%d" % i, [128, 8, T], F32) for i in range(2)]; b_xts = P.bufs(2)
        N = NormBufs(C, T)
        q_sb = P.sb("q_sb", [128, 2, T], F32); b_q = P.bufs(2)
        k_sb = P.sb("k_sb", [128, 2, T], F32); b_kk = P.bufs(2)
        r_sb = P.sb("r_sb", [128, 4, T], F32); b_r = P.bufs(4)
        a_sb = P.sb("a_sb", [16, T], F32); b_a = P.buf()
        ktok = P.sb("ktok", [128, 256], F32); b_ktok = P.buf()
        vtok = P.sb("vtok", [128, 512], BF16); b_vtok = P.buf()
        st = [P.sb("st%d" % i, [128, 256], F32) for i in range(2)]; b_st = P.bufs(2)
        stb = [P.sb("stb%d" % i, [128, 256], BF16) for i in range(2)]; b_stb = P.bufs(2)
        for i in range(2):
            P.op("pool", lambda e, i=i: e.memset(st[i][:], 0.0), writes=[b_st[i]])
            P.op("pool", lambda e, i=i: e.memset(stb[i][:], 0.0), writes=[b_stb[i]])
        eg = P.sb("eg", [128, 128], F32); b_eg = P.buf()
        lp = P.sb("lp", [128, 128], F32); b_lp = P.buf()
        ep = P.sb("ep", [128, 128], F32); b_ep = P.buf()
        em = P.sb("em", [128, 128], F32); b_em = P.buf()
        er = P.sb("er", [128, 128], F32); b_er = P.buf()
        qd = P.sb("qd", [128, 128], BF16); b_qd = P.buf()
        kd = P.sb("kd", [128, 128], BF16); b_kd = P.buf()
        kdt = P.sb("kdt", [128, 128], BF16); b_kdt = P.buf()
        att = P.sb("att", [128, 128], BF16); b_att = P.buf()
        o_sb = P.sb("o_sb", [128, 2, 128], F32); b_o = P.bufs(2)
        osq = P.sb("osq", [128, 2, 128], BF16); b_osq = P.buf()
        ors = P.sb("ors", [128, 128], F32); b_ors = P.buf()
        ogt = [P.sb("ogt%d" % i, [128, 4, 128], F32) for i in range(2)]; b_ogt = P.bufs(2)
        b_od = P.buf()
        ringA = Ring(list(zip(C.psums[1:4], C.psum_b[1:4])))
        ringB = Ring(list(zip(C.psums[4:8], C.psum_b[4:8])))
        QS = 1.0 / (128.0 ** 0.5)
        bi = 0
        for ti in range(S // T):
            xt, b_xt = xts[ti % 2], b_xts[ti % 2]
            t0 = ti * T
            P.dma(xt[:], xsrc(t0, T), writes=[b_xt])
            emit_norm_mod(C, N, xt, b_xt, 0, 8)

            def proj(col0, m, dst_fn, tag):
                ps, bps = ringA.next()
                for kc in range(8):
                    P.op("pe", lambda e, kc=kc, ps=ps: e.matmul(ps[0:m, :T], lhsT=wc[:, kc, col0:col0 + m], rhs=N.hT[:, kc, :],
                                                              start=(kc == 0), stop=(kc == 7)), reads=[b_wc[kc], N.b_hT], writes=[bps])
                dst_fn(ps, bps)
            for h in range(2):
                proj(h * 128, 128, lambda ps, bps, h=h: P.op("act", lambda e: e.activation(out=q_sb[:, h, :], in_=ps[:, :T], func=AF.Identity, scale=QS),
                                                          reads=[bps], writes=[b_q[h]]), "q")
                proj(256 + h * 128, 128, lambda ps, bps, h=h: P.op("dve", lambda e: e.tensor_copy(out=k_sb[:, h, :], in_=ps[:, :T]),
                                                                reads=[bps], writes=[b_kk[h]]), "k")
            for j in range(4):
                proj(1024 + j * 128, 128, lambda ps, bps, j=j: P.op("act", lambda e: e.activation(out=r_sb[:, j, :], in_=ps[:, :T], func=AF.Silu),
                                                                 reads=[bps], writes=[b_r[j]]), "r")
            proj(1536, 16, lambda ps, bps: P.op("dve", lambda e: e.tensor_copy(out=a_sb[:, :], in_=ps[0:16, :T]), reads=[bps], writes=[b_a]), "a")
            for blk in range(T // 128):
                c0 = blk * 128
                og_t, b_og = ogt[bi % 2], b_ogt[bi % 2]
                bi += 1
                ps, bps = ringA.next()
                for kc in range(8):
                    P.op("pe", lambda e, kc=kc, ps=ps, c0=c0: e.matmul(ps[:, 0:256], lhsT=N.hT[:, kc, c0:c0 + 128], rhs=wc[:, kc, 256:512],
                                                                     start=(kc == 0), stop=(kc == 7)), reads=[b_wc[kc], N.b_hT], writes=[bps])
                P.op("dve", lambda e, ps=ps: e.tensor_copy(out=ktok[:], in_=ps[:, 0:256]), reads=[bps], writes=[b_ktok])
                ps, bps = ringA.next()
                for kc in range(8):
                    P.op("pe", lambda e, kc=kc, ps=ps, c0=c0: e.matmul(ps[:, 0:512], lhsT=N.hT[:, kc, c0:c0 + 128], rhs=wc[:, kc, 512:1024],
                                                                     start=(kc == 0), stop=(kc == 7)), reads=[b_wc[kc], N.b_hT], writes=[bps])
                P.op("act", lambda e, ps=ps: e.activation(out=vtok[:], in_=ps[:, 0:512], func=AF.Identity), reads=[bps], writes=[b_vtok])
                for h in range(2):
                    S_, bS = st[h], b_st[h]
                    Sb, bSb = stb[h], b_stb[h]
                    pg, bpg = ringB.next()
                    P.op("pe", lambda e, pg=pg, c0=c0, h=h: e.matmul(pg[:, 0:128], lhsT=a_sb[:, c0:c0 + 128], rhs=wg[:, h * 128:(h + 1) * 128],
                                                                   start=True, stop=False), reads=[b_a, b_k], writes=[bpg])
                    P.op("pe", lambda e, pg=pg, h=h: e.matmul(pg[:, 0:128], lhsT=C.ones_f[0:1, :], rhs=bg[0:1, h * 128:(h + 1) * 128],
                                                            start=False, stop=True), reads=[C.cb, b_k], writes=[bpg])
                    P.op("act", lambda e, pg=pg: e.activation(out=eg[:], in_=pg[:, 0:128], func=AF.Exp, scale=-1.0), reads=[bpg], writes=[b_eg])
                    P.op("act", lambda e: e.activation(out=lp[:], in_=eg[:], func=AF.Ln, bias=C.ones_f[:, 0:1], scale=1.0),
                         reads=[b_eg, C.cb], writes=[b_lp])
                    pb, bpb = ringB.next()
                    P.op("pe", lambda e, pb=pb: e.matmul(pb[:, 0:128], lhsT=lp[:], rhs=cst[:, 0, :], start=True, stop=True),
                         reads=[b_lp, b_k], writes=[bpb])
                    pr_, bpr = ringB.next()
                    P.op("pe", lambda e, pr_=pr_: e.matmul(pr_[:, 0:128], lhsT=cst[:, 1, :], rhs=lp[:], start=True, stop=True),
                         reads=[b_lp, b_k], writes=[bpr])
                    P.op("act", lambda e, pb=pb: e.activation(out=ep[:], in_=pb[:, 0:128], func=AF.Exp), reads=[bpb], writes=[b_ep])
                    P.op("act", lambda e, pb=pb: e.activation(out=em[:], in_=pb[:, 0:128], func=AF.Exp, scale=-1.0), reads=[bpb], writes=[b_em])
                    P.op("act", lambda e, pr_=pr_: e.activation(out=er[:], in_=pr_[:, 0:128], func=AF.Exp), reads=[bpr], writes=[b_er])
                    P.op("dve", lambda e, h=h, c0=c0: e.tensor_tensor(out=qd[:], in0=q_sb[:, h, c0:c0 + 128], in1=ep[:], op=ALU.mult),
                         reads=[b_q[h], b_ep], writes=[b_qd])
                    P.op("pool", lambda e, h=h, c0=c0: e.tensor_tensor(out=kd[:], in0=k_sb[:, h, c0:c0 + 128], in1=em[:], op=ALU.mult),
                         reads=[b_kk[h], b_em], writes=[b_kd])
                    P.op("dve", lambda e, h=h: e.tensor_tensor(out=kdt[:], in0=ktok[:, h * 128:(h + 1) * 128], in1=er[:], op=ALU.mult),
                         reads=[b_ktok, b_er], writes=[b_kdt])
                    pa, bpa = ringB.next()
                    P.op("pe", lambda e, pa=pa: e.matmul(pa[:, 0:128], lhsT=kd[:], rhs=qd[:], start=True, stop=True),
                         reads=[b_kd, b_qd], writes=[bpa])
                    P.op("dve", lambda e, pa=pa: e.tensor_tensor(out=att[:], in0=pa[:, 0:128], in1=cst[:, 2, :], op=ALU.mult),
                         reads=[bpa, b_k], writes=[b_att])
                    for ch in range(2):
                        r0 = ch * 64
                        po, bpo = ringB.next()
                        for vc in range(2):
                            P.op("pe", lambda e, po=po, vc=vc, Sb=Sb, r0=r0: e.matmul(
                                po[:, vc * 64:(vc + 1) * 64], lhsT=Sb[:, vc * 128:(vc + 1) * 128], rhs=qd[:, r0:r0 + 64], start=True, stop=False),
                                reads=[bSb, b_qd], writes=[bpo])
                            P.op("pe", lambda e, po=po, vc=vc, h=h, r0=r0: e.matmul(
                                po[:, vc * 64:(vc + 1) * 64], lhsT=vtok[r0:r0 + 64, h * 256 + vc * 128:h * 256 + (vc + 1) * 128],
                                rhs=att[r0:r0 + 64, r0:r0 + 64], start=False, stop=True),
                                reads=[b_vtok, b_att], writes=[bpo])
                        P.op("act", lambda e, po=po, r0=r0: e.activation(
                            out=o_sb[:, :, r0:r0 + 64], in_=po[:, 0:128].rearrange("p (v t) -> p v t", v=2), func=AF.Identity),
                            reads=[bpo], writes=[b_o[ch]])
                        pu, bpu = ringB.next()
                        P.op("pe", lambda e, pu=pu, h=h, r0=r0: e.matmul(
                            pu[:, 0:256], lhsT=kdt[r0:r0 + 64, :], rhs=vtok[r0:r0 + 64, h * 256:(h + 1) * 256], start=True, stop=True),
                            reads=[b_kdt, b_vtok], writes=[bpu])
                        P.op("dve", lambda e, pu=pu, S_=S_, r0=r0: e.scalar_tensor_tensor(
                            out=S_[:], in0=S_[:], scalar=ep[:, r0 + 63:r0 + 64], in1=pu[:, 0:256], op0=ALU.mult, op1=ALU.add),
                            reads=[bpu, bS, b_ep], writes=[bS])
                        P.op("pool", lambda e, S_=S_, Sb=Sb: e.tensor_copy(out=Sb[:], in_=S_[:]), reads=[bS], writes=[bSb])
                    P.op("act", lambda e: e.activation(out=osq[:], in_=o_sb[:], func=AF.Square), reads=b_o, writes=[b_osq])
                    pn, bpn = ringB.next()
                    for vc in range(2):
                        P.op("pe", lambda e, pn=pn, vc=vc: e.matmul(pn[:, 0:128], lhsT=C.ones_bf[:], rhs=osq[:, vc, :], start=(vc == 0), stop=(vc == 1)),
                             reads=[b_osq, C.cb], writes=[bpn])
                    P.op("act", lambda e, pn=pn: e.activation(out=ors[:], in_=pn[:, 0:128], func=AF.Sqrt, bias=C.eps[:, 0:1], scale=1.0 / 256.0),
                         reads=[bpn, C.cb], writes=[b_ors])
                    P.op("dve", lambda e: e.reciprocal(out=ors[:], in_=ors[:]), reads=[b_ors], writes=[b_ors])
                    for vc in range(2):
                        P.op("dve", lambda e, vc=vc, h=h, og_t=og_t: e.scalar_tensor_tensor(
                            out=og_t[:, h * 2 + vc, :], in0=o_sb[:, vc, :], scalar=og[:, vc:vc + 1], in1=ors[:], op0=ALU.mult, op1=ALU.mult),
                            reads=b_o + [b_ors, b_k], writes=[b_og])
                        P.op("pool", lambda e, vc=vc, h=h, og_t=og_t, c0=c0: e.tensor_tensor(
                            out=og_t[:, h * 2 + vc, :], in0=og_t[:, h * 2 + vc, :], in1=r_sb[:, h * 2 + vc, c0:c0 + 128], op=ALU.mult),
                            reads=[b_og, b_r[h * 2 + vc]], writes=[b_og])
                tb = t0 + c0
                P.dma(odst(tb, 128), og_t[:], reads=[b_og], writes=[b_od])


def build_L2a(S=8192):
    nc = bass.Bass("TRN2", target_bir_lowering=False)
    xT_d = nc.dram_tensor("xT", [1024, S], F32, kind="ExternalInput").ap()
    cm = _decl_common(nc)
    wc_d = nc.dram_tensor("c_wc", [1024, 1552], F32, kind="ExternalInput").ap()
    wg_d = nc.dram_tensor("c_wg", [16, 256], F32, kind="ExternalInput").ap()
    bg_d = nc.dram_tensor("c_bg", [1, 256], F32, kind="ExternalInput").ap()
    og_d = nc.dram_tensor("c_og", [128, 2], F32, kind="ExternalInput").ap()
    cst_d = nc.dram_tensor("c_cst", [128, 3, 128], F32, kind="ExternalInput").ap()
    oT_d = nc.dram_tensor("oT", [512, S], F32, kind="ExternalOutput").ap()
    C = make_ctx(nc)
    emit_ada(C, *cm)
    emit_mixC(C, xT_d, wc_d, wg_d, bg_d, og_d, cst_d, oT_d, S)
    C.P.emit()
    return nc


def _c_consts():
    s_ = np.arange(128)[:, None]
    t_ = np.arange(128)[None, :]
    same = (s_ // 64) == (t_ // 64)
    tric = np.where(same & (s_ <= t_), -1.0 / 16.0, 0.0).astype(np.float32)
    trir = np.where(same & (s_ > t_), -1.0 / 16.0, 0.0).astype(np.float32)
    mc = (same & (s_ <= t_)).astype(np.float32)
    return np.ascontiguousarray(np.stack([tric, trir, mc], 1))


def lay_L2a(d, b, hp, xT_full):
    m = _lay_common(d, 2, b)
    w = d["c_w_in"][0]
    h0 = hp * 2
    m.update({
        "xT": xT_full,
        "c_wc": np.ascontiguousarray(np.concatenate([
            w[:, h0 * 128:(h0 + 2) * 128], w[:, 512 + h0 * 128:512 + (h0 + 2) * 128],
            w[:, 1024 + h0 * 256:1024 + (h0 + 2) * 256], w[:, 2048 + h0 * 256:2048 + (h0 + 2) * 256], w[:, 3072:3088]], 1)),
        "c_wg": np.ascontiguousarray(d["c_w_gate_up"][0][:, h0 * 128:(h0 + 2) * 128]),
        "c_bg": np.ascontiguousarray(d["c_b_gate"][0][h0 * 128:(h0 + 2) * 128].reshape(1, 256)),
        "c_og": np.ascontiguousarray(d["c_o_gain"][0].reshape(2, 128).T),
        "c_cst": _c_consts(),
    })
    return m


PAIRS = [[0, 1], [2, 3], [4, 5], [6, 7]]


def build_fused(H=4096, PAIRS=PAIRS):
    S = 2 * H
    nc = bass.Bass("TRN2", target_bir_lowering=False)
    TT = H + HALO

    def din(name, shape):
        return nc.dram_tensor(name, list(shape), F32, kind="ExternalInput").ap()
    xT_d = din("xT", [1024, TT])
    cT_d = din("cT", [128, 8])
    adaw = [din("ada_w%d" % L, [1024, 6144]) for L in range(4)]
    adab = [din("ada_b%d" % L, [128, 48]) for L in range(4)]
    gains = [din("gains%d" % L, [128, 16]) for L in range(4)]
    w1 = [din("w1_%d" % L, [1024, 4096]) for L in range(4)]
    w2 = [din("w2_%d" % L, [4096, 1024]) for L in range(4)]
    a_w_in = din("a_w_in", [1024, 3072]); a_gqk = din("a_gqk", [128, 2]); a_biasT = din("a_biasT", [128, 16, 640])
    a_valid = din("a_valid", [128, TT // 128]); a_w_out = din("a_w_out", [1024, 1024])
    b_w_in = din("b_w_in", [1024, 6144]); b_vgain = din("b_vgain", [128, 3072]); b_wsT = din("b_wsT", [128, 8, 128])
    b_bs = din("b_bs", [128, 24, 128]); b_w_out = din("b_w_out", [3072, 1024])
    c_wc = din("c_wc", [1024, 1552]); c_wg = din("c_wg", [16, 256]); c_bg = din("c_bg", [1, 256]); c_og = din("c_og", [128, 2])
    c_cst = din("c_cst", [128, 3, 128]); c_w_out = din("c_w_out", [1024, 1024])
    d_wqkv = din("d_wqkv", [1024, 1536]); d_cst = din("d_cst", [128, 3, 128]); d_w_out = din("d_w_out", [1024, 1024])
    sel_d = din("sel", [128, 2])
    yT_d = nc.dram_tensor("yT", [1024, H], F32, kind="ExternalOutput").ap()

    def scr(name, shape, dt=F32):
        return nc.dram_tensor(name, list(shape), dt).ap()
    qT_s = scr("a_qT_s", [1024, H], BF16); kT_s = scr("a_kT_s", [1024, TT], BF16); v_s = scr("a_v_s", [TT, 2048], BF16)
    vm_s = scr("b_vm_s", [24, 128, H], BF16)
    xs = [scr("xs%d" % i, [1024, H]) for i in range(4)]
    xa = scr("xa", [1024, H])
    XC = min(512, H)
    OC = min(1024, S)
    xb_c = [scr("xb_c%d" % i, [1024, XC]) for i in range(H // XC)]; xb_g = [scr("xb_g%d" % i, [2048, XC]) for i in range(H // XC)]
    xc_c = [scr("xc_c%d" % i, [1024, XC]) for i in range(H // XC)]; xc_g = [scr("xc_g%d" % i, [2048, XC]) for i in range(H // XC)]
    oc_c = [scr("oc_c%d" % i, [512, OC]) for i in range(S // OC)]; oc_g = [scr("oc_g%d" % i, [1024, OC]) for i in range(S // OC)]
    od_c = [scr("od_c%d" % i, [512, OC]) for i in range(S // OC)]; od_g = [scr("od_g%d" % i, [1024, OC]) for i in range(S // OC)]
    dq_s = scr("d_qT_s", [512, S], BF16); dk_s = scr("d_kT_s", [512, S], BF16); dv_s = scr("d_v_s", [S, 1024], BF16)

    C = make_ctx(nc)
    P = C.P
    selt = P.sb("selt", [128, 2], F32); b_sel = P.buf()
    P.dma(selt[:], sel_d, writes=[b_sel])

    def x_own(chunks):
        def f(t0, T):
            return chunks[t0 // XC][:, t0 % XC:t0 % XC + T].rearrange("(kc p) t -> p kc t", p=128)
        return f

    def x_gath(chunks):
        def f(t0, T):
            r, tl = t0 // H, t0 % H
            return chunks[tl // XC][r * 1024:(r + 1) * 1024, tl % XC:tl % XC + T].rearrange("(kc p) t -> p kc t", p=128)
        return f

    def o_dst(chunks):
        def f(t0, n):
            return chunks[t0 // OC].rearrange("(oc p) t -> p oc t", p=128)[:, :, t0 % OC:t0 % OC + n]
        return f

    def o_gath(chunks):
        def f(h, t0, T):
            g = h * H + t0
            return chunks[g // OC].rearrange("(kc p) t -> p kc t", p=128)[:, :, g % OC:g % OC + T]
        return f

    def gather(src, dst):
        for a_, b_ in zip(src, dst):
            P.collective("AllGather", [a_], [b_], PAIRS)
        P.barrier()
    emit_ada(C, cT_d, adaw[0], adab[0], gains[0])
    emit_mixA1(C, xT_d, a_w_in, a_gqk, qT_s, kT_s, v_s, H)
    emit_mixA2(C, xT_d, xs[0], qT_s, kT_s, v_s, a_biasT, a_valid, a_w_out, H)
    emit_ffn(C, xs[0], xa, w1[0], w2[0], H)
    emit_ada(C, cT_d, adaw[1], adab[1], gains[1])
    emit_mixB(C, xa, xs[1], b_w_in, b_vgain, b_wsT, b_bs, b_w_out, vm_s, H)
    emit_ffn(C, xs[1], None, w1[1], w2[1], H, ydst=x_own(xb_c))
    gather(xb_c, xb_g)
    emit_ada(C, cT_d, adaw[2], adab[2], gains[2])
    emit_mixC(C, None, c_wc, c_wg, c_bg, c_og, c_cst, None, S, xsrc=x_gath(xb_g), odst=o_dst(oc_c))
    gather(oc_c, oc_g)
    emit_outproj(C, None, None, c_w_out, xs[2], H, sel=(selt, b_sel), xsrc=x_own(xb_c), osrc=o_gath(oc_g))
    emit_ffn(C, xs[2], None, w1[2], w2[2], H, ydst=x_own(xc_c))
    gather(xc_c, xc_g)
    emit_ada(C, cT_d, adaw[3], adab[3], gains[3])
    emit_mixD1(C, None, d_wqkv, dq_s, dk_s, dv_s, S, xsrc=x_gath(xc_g))
    emit_mixD2(C, dq_s, dk_s, dv_s, d_cst, None, S, odst=o_dst(od_c))
    gather(od_c, od_g)
    emit_outproj(C, None, None, d_w_out, xs[3], H, sel=(selt, b_sel), xsrc=x_own(xc_c), osrc=o_gath(od_g))
    emit_ffn(C, xs[3], yT_d, w1[3], w2[3], H)
    P.emit()
    return nc


def lay_fused(d, b, hf, H=4096):
    m = {}
    l0 = lay_L0(d, b, hf, H)
    for k in ("xT", "cT", "a_w_in", "a_gqk", "a_biasT", "a_valid", "a_w_out"):
        m[k] = l0[k]
    for L in range(4):
        c = _lay_common(d, L, b)
        m["ada_w%d" % L] = c["ada_w"]; m["ada_b%d" % L] = c["ada_b"]; m["gains%d" % L] = c["gains"]
        m["w1_%d" % L] = np.ascontiguousarray(d["ffn_w1"][L]); m["w2_%d" % L] = np.ascontiguousarray(d["ffn_w2"][L])
    l1 = lay_L1(d, b)
    for k in ("b_w_in", "b_vgain", "b_wsT", "b_bs", "b_w_out"):
        m[k] = l1[k]
    l2 = lay_L2a(d, b, hf, None)
    for k in ("c_wc", "c_wg", "c_bg", "c_og", "c_cst"):
        m[k] = l2[k]
    m["c_w_out"] = np.ascontiguousarray(d["c_w_out"][0])
    l3 = lay_L3a(d, b, hf, None)
    for k in ("d_wqkv", "d_cst"):
        m[k] = l3[k]
    m["d_w_out"] = np.ascontiguousarray(d["d_w_out"][0])
    sel = np.zeros((128, 2), np.float32); sel[:, hf] = 1.0
    m["sel"] = sel
    return m


_PROGS = {}


def _prog(name, builder):
    if name not in _PROGS:
        _PROGS[name] = builder()
    return _PROGS[name]


def _run(nc, in_maps):
    res = run_bass_kernel_spmd(nc, in_maps, core_ids=list(range(len(in_maps))))
    return res.results


def kernel(**inputs):
    d = {k: np.asarray(v, dtype=np.float32) for k, v in inputs.items()}
    B, S, D = d["x"].shape
    H = S // 2
    cores = [(b, hf) for b in range(B) for hf in range(2)]
    nc = _prog("fused", lambda: build_fused(H))
    r = _run(nc, [lay_fused(d, b, hf, H) for (b, hf) in cores])
    out = np.empty((B, S, D), np.float32)
    for b in range(B):
        out[b, :H] = r[2 * b]["yT"].T
        out[b, H:] = r[2 * b + 1]["yT"].T
    return out


def kernel_unfused(**inputs):
    d = {k: np.asarray(v, dtype=np.float32) for k, v in inputs.items()}
    B, S, D = d["x"].shape
    H = S // 2
    cores = [(b, hf) for b in range(B) for hf in range(2)]
    nc = _prog("L0", lambda: build_L0(H))
    r = _run(nc, [lay_L0(d, b, hf, H) for (b, hf) in cores])
    xT = [np.concatenate([r[2 * b]["yT"], r[2 * b + 1]["yT"]], axis=1) for b in range(B)]
    nc = _prog("L1", lambda: build_L1(H))
    ins = []
    for (b, hf) in cores:
        m = lay_L1(d, b)
        m["xT"] = np.ascontiguousarray(xT[b][:, hf * H:(hf + 1) * H])
        ins.append(m)
    r = _run(nc, ins)
    xT = [np.concatenate([r[2 * b]["yT"], r[2 * b + 1]["yT"]], axis=1) for b in range(B)]
    nc = _prog("L2a", lambda: build_L2a(S))
    r = _run(nc, [lay_L2a(d, b, hp, xT[b]) for (b, hp) in cores])
    oT = [np.concatenate([r[2 * b]["oT"], r[2 * b + 1]["oT"]], axis=0) for b in range(B)]
    nc = _prog("Lb", lambda: build_Lb(H))
    ins = []
    for (b, hf) in cores:
        m = lay_Lb(d, 2, b, d["c_w_out"][0])
        m["xT"] = np.ascontiguousarray(xT[b][:, hf * H:(hf + 1) * H])
        m["oT"] = np.ascontiguousarray(oT[b][:, hf * H:(hf + 1) * H])
        ins.append(m)
    r = _run(nc, ins)
    xT = [np.concatenate([r[2 * b]["yT"], r[2 * b + 1]["yT"]], axis=1) for b in range(B)]
    nc = _prog("L3a", lambda: build_L3a(S))
    r = _run(nc, [lay_L3a(d, b, hg, xT[b]) for (b, hg) in cores])
    oT = [np.concatenate([r[2 * b]["oT"], r[2 * b + 1]["oT"]], axis=0) for b in range(B)]
    nc = _prog("Lb", lambda: build_Lb(H))
    ins = []
    for (b, hf) in cores:
        m = lay_Lb(d, 3, b, d["d_w_out"][0])
        m["xT"] = np.ascontiguousarray(xT[b][:, hf * H:(hf + 1) * H])
        m["oT"] = np.ascontiguousarray(oT[b][:, hf * H:(hf + 1) * H])
        ins.append(m)
    r = _run(nc, ins)
    out = np.empty((B, S, D), np.float32)
    for b in range(B):
        out[b, :H] = r[2 * b]["yT"].T
        out[b, H:] = r[2 * b + 1]["yT"].T
    return out
```

```python
import numpy as np
import concourse.bass as bass
import concourse.mybir as mybir
from concourse.bass_utils import run_bass_kernel_spmd
from contextlib import ExitStack, contextmanager

F32 = mybir.dt.float32
BF16 = mybir.dt.bfloat16
AF = mybir.ActivationFunctionType
ALU = mybir.AluOpType
AX = mybir.AxisListType
EPS = 1e-6
DEBUG = False
SERIAL = False
NSTAGE = 2
STAGE_ELEMS = 2048
ENGS = ("pe", "act", "dve", "pool", "sp")


class Buf:
    __slots__ = ("name", "w", "r")

    def __init__(self, name):
        self.name = name
        self.w = None
        self.r = []


class Ring:
    def __init__(self, items):
        self.items = items
        self.i = 0

    def next(self):
        it = self.items[self.i % len(self.items)]
        self.i += 1
        return it


class Prog:
    NDMA = 24

    def __init__(self, nc):
        self.nc = nc
        self.es = ExitStack()
        self.scopes = [self.es]
        self.streams = {e: [] for e in ENGS}
        self.esem = {e: self.es.enter_context(nc.semaphore("s_" + e)) for e in ENGS}
        self.ecount = {e: 0 for e in ENGS}
        self.dsem = [self.es.enter_context(nc.semaphore("d%d" % i)) for i in range(self.NDMA)]
        self.dcount = [0] * self.NDMA
        self.dnext = 0
        self.semobj = {}
        for e in ENGS:
            self.semobj[("e", e)] = self.esem[e]
        for i in range(self.NDMA):
            self.semobj[("d", i)] = self.dsem[i]
        self.waited = {e: {} for e in ENGS}
        self.nbuf = 0
        self.uid = 0

    @contextmanager
    def scope(self):
        st = ExitStack()
        self.scopes.append(st)
        try:
            yield
        finally:
            self.barrier()
            self.scopes.pop()
            st.close()

    def sb(self, name, shape, dt):
        self.uid += 1
        return self.scopes[-1].enter_context(self.nc.sbuf_tensor("%s_%d" % (name, self.uid), list(shape), dt))

    def ps(self, name, shape, dt=F32):
        return self.scopes[-1].enter_context(self.nc.psum_tensor(name, list(shape), dt))

    def buf(self, name=None):
        self.nbuf += 1
        return Buf(name or ("b%d" % self.nbuf))

    def bufs(self, n):
        return [self.buf() for _ in range(n)]

    def _deps(self, eng, reads, writes):
        deps = {}

        def add(d):
            if d is None:
                return
            k, v = d
            if deps.get(k, 0) < v:
                deps[k] = v
        for b in reads:
            add(b.w)
        for b in writes:
            add(b.w)
            for d in b.r:
                add(d)
        waits = []
        wd = self.waited[eng]
        for k, v in deps.items():
            if wd.get(k, 0) >= v:
                continue
            if eng == "pe" and k == ("e", "pe"):
                continue
            wd[k] = v
            waits.append((self.semobj[k], v))
        return waits

    def _mark(self, tok, reads, writes):
        for b in writes:
            b.w = tok
            b.r = []
        for b in reads:
            if b not in writes:
                b.r.append(tok)
                if len(b.r) > 64:
                    best = {}
                    for k, v in b.r:
                        if best.get(k, 0) < v:
                            best[k] = v
                    b.r = list(best.items())

    def _serial_waits(self, eng):
        waits = []
        for e2 in ENGS:
            k = ("e", e2); v = self.ecount[e2]
            if v > 0 and self.waited[eng].get(k, 0) < v:
                self.waited[eng][k] = v
                waits.append((self.esem[e2], v))
        for i in range(self.NDMA):
            k = ("d", i); v = self.dcount[i]
            if v > 0 and self.waited[eng].get(k, 0) < v:
                self.waited[eng][k] = v
                waits.append((self.dsem[i], v))
        return waits

    def op(self, eng, fn, reads=(), writes=()):
        waits = self._deps(eng, reads, writes)
        if SERIAL:
            waits = waits + self._serial_waits(eng)
        self.ecount[eng] += 1
        tok = (("e", eng), self.ecount[eng])
        self.streams[eng].append((waits, fn, self.esem[eng], 1))
        self._mark(tok, reads, writes)
        return tok

    def dma(self, out, in_, reads=(), writes=(), eng="sp"):
        i = self.dnext
        self.dnext = (self.dnext + 1) % self.NDMA
        waits = self._deps(eng, reads, writes)
        if SERIAL:
            waits = waits + self._serial_waits(eng)
        k = ("d", i)
        if self.dcount[i] > 0 and self.waited[eng].get(k, 0) < self.dcount[i]:
            self.waited[eng][k] = self.dcount[i]
            waits.append((self.dsem[i], self.dcount[i]))
        self.dcount[i] += 16
        tok = (k, self.dcount[i])
        self.streams[eng].append((waits, lambda e: e.dma_start(out=out, in_=in_), self.dsem[i], 16))
        self._mark(tok, reads, writes)
        return tok

    def collective(self, kind, ins, outs, groups, reads=(), writes=()):
        eng = "pool"
        waits = self._deps(eng, reads, writes)
        sem = self.es.enter_context(self.nc.semaphore("cc%d" % len(self.semobj)))
        k = ("c", len(self.semobj))
        self.semobj[k] = sem
        tok = (k, 1)
        self.streams[eng].append((waits, lambda e: e.collective_compute(kind, ALU.bypass, groups, [a.opt() for a in ins], [a.opt() for a in outs]), sem, 1))
        self._mark(tok, reads, writes)
        return tok

    def dump(self, name, ap, shape, dt, reads):
        if not DEBUG:
            return
        d = self.nc.dram_tensor(name, list(shape), dt, kind="ExternalOutput").ap()
        self.dma(d, ap, reads=reads, writes=[self.buf()])

    def barrier(self):
        for eng in ENGS:
            waits = []
            for k, sem in self.semobj.items():
                if k[0] == "c" and self.waited[eng].get(k, 0) < 1:
                    self.waited[eng][k] = 1
                    waits.append((sem, 1))
            for e2 in ENGS:
                k = ("e", e2)
                v = self.ecount[e2]
                if e2 != eng and v > 0 and self.waited[eng].get(k, 0) < v:
                    self.waited[eng][k] = v
                    waits.append((self.esem[e2], v))
            for i in range(self.NDMA):
                k = ("d", i)
                v = self.dcount[i]
                if v > 0 and self.waited[eng].get(k, 0) < v:
                    self.waited[eng][k] = v
                    waits.append((self.dsem[i], v))
            k = ("e", eng)
            v = self.ecount[eng]
            if v > 0 and self.waited[eng].get(k, 0) < v:
                self.waited[eng][k] = v
                waits.append((self.esem[eng], v))
            if waits:
                self.streams[eng].append((waits, None, None, 0))

    def emit(self):
        nc = self.nc
        self.barrier()
        with nc.Block() as block:
            def runner(name):
                def f(e):
                    for waits, fn, sem, inc in self.streams[name]:
                        for s, v in waits:
                            e.wait_ge(s, v)
                        if fn is not None:
                            fn(e).then_inc(sem, inc)
                return f
            block.tensor(runner("pe"))
            block.scalar(runner("act"))
            block.vector(runner("dve"))
            block.gpsimd(runner("pool"))
            block.sync(runner("sp"))
        self.es.close()


class Ctx:
    pass


def make_ctx(nc):
    P = Prog(nc)
    C = Ctx()
    C.P = P
    C.nc = nc
    C.psums = [P.ps("ps%d" % i, [128, 512]) for i in range(8)]
    C.psum_b = P.bufs(8)
    C.ones_bf = P.sb("ones_bf", [128, 128], BF16)
    C.ones_f = P.sb("ones_f", [128, 128], F32)
    C.eps = P.sb("eps_t", [128, 1], F32)
    C.cb = P.buf()

    def mk(e):
        e.memset(C.eps[:], EPS)
        e.memset(C.ones_f[:], 1.0)
        return e.memset(C.ones_bf[:], 1.0)
    P.op("pool", mk, writes=[C.cb])
    C.stage = [P.sb("stage%d" % i, [128, STAGE_ELEMS], F32) for i in range(NSTAGE)]
    C.stage_b = P.bufs(NSTAGE)
    C.sti = 0
    return C


def next_stage(C):
    i = C.sti % len(C.stage)
    C.sti += 1
    return C.stage[i], C.stage_b[i]


def load_cast(C, dst_ap, dst_buf, src_ap, shape3):
    P = C.P
    n = 1
    for s in shape3:
        n *= s
    if n > STAGE_ELEMS:
        h = shape3[0] // 2
        if len(shape3) == 1:
            load_cast(C, dst_ap[:, 0:h], dst_buf, src_ap[:, 0:h], [h])
            load_cast(C, dst_ap[:, h:2 * h], dst_buf, src_ap[:, h:2 * h], [h])
        else:
            load_cast(C, dst_ap[:, 0:h, :], dst_buf, src_ap[:, 0:h, :], [h, shape3[1]])
            load_cast(C, dst_ap[:, h:2 * h, :], dst_buf, src_ap[:, h:2 * h, :], [h, shape3[1]])
        return
    st, sb_ = next_stage(C)
    if len(shape3) == 2:
        stv = st[:, 0:n].rearrange("p (a n) -> p a n", a=shape3[0])
    else:
        stv = st[:, 0:n]
    P.dma(stv, src_ap, writes=[sb_])
    eng = "dve" if C.sti % 2 == 0 else "pool"
    P.op(eng, lambda e: e.tensor_copy(out=dst_ap, in_=stv), reads=[sb_], writes=[dst_buf])


def emit_ada(C, cT_d, ada_w_d, ada_b_d, gains_d):
    P = C.P
    psum, psum_b = C.psums[0], C.psum_b[0]
    ct = P.sb("ct", [128, 8], F32); cond = P.sb("cond", [128, 8], F32)
    adab = P.sb("adab", [128, 48], F32); gn = P.sb("gn", [128, 16], F32)
    mod = P.sb("mod", [128, 48], F32)
    out = P.sb("AB", [128, 48], F32)
    b_ct, b_cond, b_adab, b_gn, b_mod, b_out = P.bufs(6)
    P.dma(ct[:], cT_d, writes=[b_ct])
    P.dma(adab[:], ada_b_d, writes=[b_adab])
    P.dma(gn[:], gains_d, writes=[b_gn])
    P.op("act", lambda e: e.activation(out=cond[:], in_=ct[:], func=AF.Silu), reads=[b_ct], writes=[b_cond])
    wv = ada_w_d.rearrange("(kc p) f -> p kc f", p=128)
    for g in range(24):
        st, sb_ = next_stage(C)
        stv = st[:, 0:2048].rearrange("p (kc f) -> p kc f", kc=8)
        P.dma(stv, wv[:, :, g * 256:(g + 1) * 256], writes=[sb_])
        for j in range(2):
            col = g * 2 + j
            for kc in range(8):
                P.op("pe", (lambda e, col=col, kc=kc, j=j, stv=stv: e.matmul(
                    psum[:, col:col + 1], lhsT=stv[:, kc, j * 128:(j + 1) * 128],
                    rhs=cond[:, kc:kc + 1], start=(kc == 0), stop=(kc == 7))),
                    reads=[sb_, b_cond], writes=[psum_b])
    P.op("dve", lambda e: e.tensor_tensor(out=mod[:], in0=psum[:, 0:48], in1=adab[:], op=ALU.add),
         reads=[psum_b, b_adab], writes=[b_mod])

    def mk(e):
        e.scalar_tensor_tensor(out=out[:, 0:8], in0=mod[:, 8:16], scalar=1.0, in1=gn[:, 0:8], op0=ALU.add, op1=ALU.mult)
        e.scalar_tensor_tensor(out=out[:, 24:32], in0=mod[:, 32:40], scalar=1.0, in1=gn[:, 8:16], op0=ALU.add, op1=ALU.mult)
        e.tensor_copy(out=out[:, 8:16], in_=mod[:, 0:8])
        e.tensor_copy(out=out[:, 16:24], in_=mod[:, 16:24])
        e.tensor_copy(out=out[:, 32:40], in_=mod[:, 24:32])
        return e.tensor_copy(out=out[:, 40:48], in_=mod[:, 40:48])
    P.op("dve", mk, reads=[b_mod, b_gn], writes=[b_out])
    C.AB, C.b_ab = out, b_out


class NormBufs:
    def __init__(self, C, T):
        P = C.P
        self.T = T
        self.sq = P.sb("sq", [128, 8, T], BF16); self.b_sq = P.buf()
        self.rstd = P.sb("rstd", [128, T], F32); self.b_rstd = P.buf()
        self.tmp = P.sb("tmp", [128, 4, T], F32); self.b_tmp = P.bufs(4)
        self.hT = P.sb("hT", [128, 8, T], BF16); self.b_hT = P.buf()


def emit_norm_mod(C, N, xt, b_xt, acol, bcol):
    P = C.P
    T = N.T
    ps, b_ps = C.psums[0], C.psum_b[0]
    A = C.AB[:, acol:acol + 8]
    Bc = C.AB[:, bcol:bcol + 8]
    P.op("act", lambda e: e.activation(out=N.sq[:], in_=xt[:], func=AF.Square), reads=[b_xt], writes=[N.b_sq])
    for kc in range(8):
        P.op("pe", lambda e, kc=kc: e.matmul(ps[:, :T], lhsT=C.ones_bf[:], rhs=N.sq[:, kc, :], start=(kc == 0), stop=(kc == 7)),
             reads=[N.b_sq, C.cb], writes=[b_ps])
    P.op("act", lambda e: e.activation(out=N.rstd[:], in_=ps[:, :T], func=AF.Sqrt, bias=C.eps[:, 0:1], scale=1.0 / 1024.0),
         reads=[b_ps, C.cb], writes=[N.b_rstd])
    P.op("dve", lambda e: e.reciprocal(out=N.rstd[:], in_=N.rstd[:]), reads=[N.b_rstd], writes=[N.b_rstd])
    for kc in range(8):
        eng = "dve" if kc % 2 == 0 else "pool"
        P.op(eng, lambda e, kc=kc: e.tensor_tensor(out=N.tmp[:, kc % 4, :], in0=xt[:, kc, :], in1=N.rstd[:], op=ALU.mult),
             reads=[b_xt, N.b_rstd], writes=[N.b_tmp[kc % 4]])
        P.op("act", lambda e, kc=kc: e.activation(out=N.hT[:, kc, :], in_=N.tmp[:, kc % 4, :], func=AF.Identity,
                                                   bias=Bc[:, kc:kc + 1], scale=A[:, kc:kc + 1]),
             reads=[N.b_tmp[kc % 4], C.b_ab], writes=[N.b_hT])


def emit_ffn(C, xT_d, yT_d, w1_d, w2_d, T_core, T=256, ydst=None):
    P = C.P
    AB = C.AB
    with P.scope():
        w1b = P.sb("w1b", [128, 8, 4096], BF16); w2b = P.sb("w2b", [128, 32, 1024], BF16)
        b_w1 = P.bufs(8); b_w2 = P.bufs(8)
        for kc in range(8):
            load_cast(C, w1b[:, kc, :], b_w1[kc], w1_d[kc * 128:(kc + 1) * 128, :], [4096])
        w2v = w2_d.rearrange("(hc p) n -> p hc n", p=128)
        for g in range(8):
            load_cast(C, w2b[:, g * 4:(g + 1) * 4, :], b_w2[g], w2v[:, g * 4:(g + 1) * 4, :], [4, 1024])
        xts = [P.sb("xt%d" % i, [128, 8, T], F32) for i in range(2)]
        b_xts = P.bufs(2)
        N = NormBufs(C, T)
        h1T = P.sb("h1T", [128, 32, T], BF16); b_h1 = P.bufs(32)
        rl = [P.sb("rl%d" % i, [128, T], F32) for i in range(2)]; b_rl = P.bufs(2)
        xv = xT_d.rearrange("(kc p) t -> p kc t", p=128)
        if ydst is None:
            yv = yT_d.rearrange("(kc p) t -> p kc t", p=128)
            ydst = lambda t0, T: yv[:, :, t0:t0 + T]
        b_out = P.buf()
        hring = Ring(list(zip(C.psums[1:5], C.psum_b[1:5])))
        yring = Ring(list(zip(C.psums[5:8], C.psum_b[5:8])))
        nt = T_core // T
        P.dma(xts[0][:], xv[:, :, 0:T], writes=[b_xts[0]])
        emit_norm_mod(C, N, xts[0], b_xts[0], 24, 32)
        for ti in range(nt):
            xt, b_xt = xts[ti % 2], b_xts[ti % 2]
            t0 = ti * T
            for hc in range(32):
                ps, bps = hring.next()
                for kc in range(8):
                    P.op("pe", lambda e, kc=kc, hc=hc, ps=ps: e.matmul(
                        ps[:, :T], lhsT=w1b[:, kc, hc * 128:(hc + 1) * 128], rhs=N.hT[:, kc, :],
                        start=(kc == 0), stop=(kc == 7)), reads=[b_w1[kc], N.b_hT], writes=[bps])
                r, br = rl[hc % 2], b_rl[hc % 2]
                P.op("act", lambda e, ps=ps, r=r: e.activation(out=r[:], in_=ps[:, :T], func=AF.Relu), reads=[bps], writes=[br])
                P.op("pool", lambda e, r=r, hc=hc: e.tensor_tensor(out=h1T[:, hc, :], in0=r[:], in1=r[:], op=ALU.mult),
                     reads=[br], writes=[b_h1[hc]])
            if ti + 1 < nt:
                nx, b_nx = xts[(ti + 1) % 2], b_xts[(ti + 1) % 2]
                P.dma(nx[:], xv[:, :, t0 + T:t0 + 2 * T], writes=[b_nx])
                emit_norm_mod(C, N, nx, b_nx, 24, 32)
            for oc in range(8):
                ps, bps = yring.next()
                for hc in range(32):
                    P.op("pe", lambda e, oc=oc, hc=hc, ps=ps: e.matmul(
                        ps[:, :T], lhsT=w2b[:, hc, oc * 128:(oc + 1) * 128], rhs=h1T[:, hc, :],
                        start=(hc == 0), stop=(hc == 31)), reads=[b_w2[hc // 4], b_h1[hc]], writes=[bps])
                P.op("dve", lambda e, oc=oc, ps=ps, xt=xt: e.scalar_tensor_tensor(
                    out=xt[:, oc, :], in0=ps[:, :T], scalar=AB[:, 40 + oc:41 + oc], in1=xt[:, oc, :],
                    op0=ALU.mult, op1=ALU.add), reads=[bps, C.b_ab, b_xt], writes=[b_xt])
            P.dma(ydst(t0, T), xt[:], reads=[b_xt], writes=[b_out])


def _emit_b1_window(C, P, N, w, vg, b_vg, sqv, b_sqv, ssv, b_ssv, vn, b_vn, gainB, b_gain, wvb, b_wv, pring, ti):
    for cc in range(6):
        ps, bps = pring.next()
        for kc in range(8):
            P.op("pe", lambda e, kc=kc, cc=cc, ps=ps: e.matmul(
                ps[:, :], lhsT=N.hT[:, kc, w * 128:(w + 1) * 128], rhs=wvb[:, kc, cc * 512:(cc + 1) * 512],
                start=(kc == 0), stop=(kc == 7)), reads=[b_wv[kc], N.b_hT], writes=[bps])
        P.op("act", lambda e, ps=ps, cc=cc: e.activation(out=vg[:, cc * 512:(cc + 1) * 512], in_=ps[:, :], func=AF.Gelu),
             reads=[bps], writes=[b_vg[cc]])
    P.op("dve", lambda e: e.tensor_tensor(out=sqv[:], in0=vg[:], in1=vg[:], op=ALU.mult), reads=b_vg, writes=[b_sqv])
    P.op("dve", lambda e: e.reduce_sum(out=ssv[:], in_=sqv[:], axis=AX.X), reads=[b_sqv], writes=[b_ssv])
    P.op("act", lambda e: e.activation(out=ssv[:], in_=ssv[:], func=AF.Sqrt, bias=C.eps[:, 0:1], scale=1.0 / 3072.0),
         reads=[b_ssv, C.cb], writes=[b_ssv])
    P.op("dve", lambda e: e.reciprocal(out=ssv[:], in_=ssv[:]), reads=[b_ssv], writes=[b_ssv])
    P.op("dve", lambda e: e.scalar_tensor_tensor(out=vn[:], in0=vg[:], scalar=ssv[:, 0:1], in1=gainB[:],
                                                   op0=ALU.mult, op1=ALU.mult),
         reads=b_vg + [b_ssv, b_gain], writes=[b_vn])


def emit_mixB(C, xT_d, xo_d, w_in_d, vgain_d, wsT_d, bs_d, wout_d, vm_d, T_core, T=256):
    P = C.P
    xv = xT_d.rearrange("(kc p) t -> p kc t", p=128)
    xov = xo_d.rearrange("(kc p) t -> p kc t", p=128)
    vmv = vm_d.rearrange("fc p t -> p fc t")
    winv = w_in_d.rearrange("(kc p) f -> p kc f", p=128)
    with P.scope():
        wvb = P.sb("wvb", [128, 8, 3072], BF16); b_wv = P.bufs(8)
        for kc in range(8):
            load_cast(C, wvb[:, kc, :], b_wv[kc], winv[:, kc, 3072:6144], [3072])
        gainB = P.sb("gainB", [128, 3072], F32); b_gain = P.buf()
        P.dma(gainB[:], vgain_d, writes=[b_gain])
        wsT = P.sb("wsT", [128, 8, 128], BF16); b_ws = P.buf()
        load_cast(C, wsT[:], b_ws, wsT_d, [8, 128])
        P.op("pool", lambda e: e.memset(wsT[64:128, :, 0:64], 0.0), writes=[b_ws])
        bs = P.sb("bs", [128, 24, 128], F32); b_bs = P.buf()
        P.dma(bs[:], bs_d, writes=[b_bs])
        xts = [P.sb("xt%d" % i, [128, 8, T], F32) for i in range(2)]; b_xts = P.bufs(2)
        N = NormBufs(C, T)
        vgs = [P.sb("vg%d" % i, [128, 3072], F32) for i in range(2)]; b_vgs = [P.bufs(6) for _ in range(2)]
        sqv = P.sb("sqv", [128, 3072], F32); b_sqv = P.buf()
        ssvs = [P.sb("ssv%d" % i, [128, 1], F32) for i in range(2)]; b_ssvs = P.bufs(2)
        vns = [P.sb("vn%d" % i, [128, 3072], BF16) for i in range(2)]; b_vns = P.bufs(2)
        vmT = [P.sb("vmT%d" % i, [128, 24, 128], BF16) for i in range(2)]; b_vmT = P.bufs(2)
        pring = Ring(list(zip(C.psums[1:5], C.psum_b[1:5])))
        mring = Ring(list(zip(C.psums[5:8], C.psum_b[5:8])))
        b_vmd = P.buf()
        wi = 0
        for ti in range(T_core // T):
            xt, b_xt = xts[ti % 2], b_xts[ti % 2]
            t0 = ti * T
            P.dma(xt[:], xv[:, :, t0:t0 + T], writes=[b_xt])
            emit_norm_mod(C, N, xt, b_xt, 0, 8)
            for w in range(T // 128):
                vg, b_vg = vgs[wi % 2], b_vgs[wi % 2]
                ssv, b_ssv = ssvs[wi % 2], b_ssvs[wi % 2]
                vn, b_vn = vns[wi % 2], b_vns[wi % 2]
                _emit_b1_window(C, P, N, w, vg, b_vg, sqv, b_sqv, ssv, b_ssv, vn, b_vn, gainB, b_gain, wvb, b_wv, pring, ti)
                vm, bvm = vmT[wi % 2], b_vmT[wi % 2]
                for q in range(6):
                    ps, bps = mring.next()
                    for j in range(4):
                        fc = q * 4 + j
                        g = fc // 3
                        P.op("pe", lambda e, ps=ps, j=j, fc=fc, g=g, vn=vn: e.matmul(
                            ps[:, j * 128:(j + 1) * 128], lhsT=vn[:, fc * 128:(fc + 1) * 128], rhs=wsT[:, g, :],
                            start=True, stop=True), reads=[b_vn, b_ws], writes=[bps])
                    P.op("dve", lambda e, ps=ps, q=q, vm=vm: e.tensor_tensor(
                        out=vm[:, q * 4:(q + 1) * 4, :], in0=ps[:, :].rearrange("p (a n) -> p a n", a=4),
                        in1=bs[:, q * 4:(q + 1) * 4, :], op=ALU.add), reads=[bps, b_bs], writes=[bvm])
                tw = t0 + w * 128
                P.dma(vmv[:, :, tw:tw + 128], vm[:], reads=[bvm], writes=[b_vmd])
                wi += 1
    emit_mixB2(C, xv, xov, vmv, winv, wout_d, T_core, T)


def emit_mixB2(C, xv, xov, vmv, winv, wout_d, T_core, T):
    P = C.P
    AB = C.AB
    with P.scope():
        wub = P.sb("wub", [128, 8, 3072], BF16); b_wu = P.bufs(8)
        for kc in range(8):
            load_cast(C, wub[:, kc, :], b_wu[kc], winv[:, kc, 0:3072], [3072])
        wob = P.sb("wob", [128, 24, 1024], BF16); b_wo = P.bufs(6)
        wov = wout_d.rearrange("(fc p) n -> p fc n", p=128)
        for g in range(6):
            load_cast(C, wob[:, g * 4:(g + 1) * 4, :], b_wo[g], wov[:, g * 4:(g + 1) * 4, :], [4, 1024])
        xts = [P.sb("xt%d" % i, [128, 8, T], F32) for i in range(2)]; b_xts = P.bufs(2)
        N = NormBufs(C, T)
        vmt = P.sb("vmt", [128, 24, T], BF16); b_vmt = P.buf()
        gT = P.sb("gT", [128, 24, T], BF16); b_gT = P.bufs(24)
        ut = [P.sb("ut%d" % i, [128, T], F32) for i in range(2)]; b_ut = P.bufs(2)
        uring = Ring(list(zip(C.psums[1:5], C.psum_b[1:5])))
        yring = Ring(list(zip(C.psums[5:8], C.psum_b[5:8])))
        b_xo = P.buf()
        for ti in range(T_core // T):
            xt, b_xt = xts[ti % 2], b_xts[ti % 2]
            t0 = ti * T
            P.dma(xt[:], xv[:, :, t0:t0 + T], writes=[b_xt])
            P.dma(vmt[:], vmv[:, :, t0:t0 + T], writes=[b_vmt])
            emit_norm_mod(C, N, xt, b_xt, 0, 8)
            for fc in range(24):
                ps, bps = uring.next()
                for kc in range(8):
                    P.op("pe", lambda e, kc=kc, fc=fc, ps=ps: e.matmul(
                        ps[:, :T], lhsT=wub[:, kc, fc * 128:(fc + 1) * 128], rhs=N.hT[:, kc, :],
                        start=(kc == 0), stop=(kc == 7)), reads=[b_wu[kc], N.b_hT], writes=[bps])
                u, bu = ut[fc % 2], b_ut[fc % 2]
                P.op("act", lambda e, ps=ps, u=u: e.activation(out=u[:], in_=ps[:, :T], func=AF.Gelu), reads=[bps], writes=[bu])
                eng = "pool" if fc % 2 == 0 else "dve"
                P.op(eng, lambda e, u=u, fc=fc: e.tensor_tensor(out=gT[:, fc, :], in0=u[:], in1=vmt[:, fc, :], op=ALU.mult),
                     reads=[bu, b_vmt], writes=[b_gT[fc]])
            for oc in range(8):
                ps, bps = yring.next()
                for fc in range(24):
                    P.op("pe", lambda e, oc=oc, fc=fc, ps=ps: e.matmul(
                        ps[:, :T], lhsT=wob[:, fc, oc * 128:(oc + 1) * 128], rhs=gT[:, fc, :],
                        start=(fc == 0), stop=(fc == 23)), reads=[b_wo[fc // 4], b_gT[fc]], writes=[bps])
                P.op("dve", lambda e, oc=oc, ps=ps, xt=xt: e.scalar_tensor_tensor(
                    out=xt[:, oc, :], in0=ps[:, :T], scalar=AB[:, 16 + oc:17 + oc], in1=xt[:, oc, :],
                    op0=ALU.mult, op1=ALU.add), reads=[bps, C.b_ab, b_xt], writes=[b_xt])
            P.dma(xov[:, :, t0:t0 + T], xt[:], reads=[b_xt], writes=[b_xo])


def _lay_common(d, L, b):
    return {
        "cT": np.ascontiguousarray(d["c"][b].reshape(8, 128).T),
        "ada_w": np.ascontiguousarray(d["ada_w"][L]),
        "ada_b": np.ascontiguousarray(d["ada_b"][L].reshape(48, 128).T),
        "gains": np.ascontiguousarray(np.concatenate([d["norm_mix"][L].reshape(8, 128), d["norm_ffn"][L].reshape(8, 128)], 0).T),
    }


def _decl_common(nc):
    cT_d = nc.dram_tensor("cT", [128, 8], F32, kind="ExternalInput").ap()
    adaw_d = nc.dram_tensor("ada_w", [1024, 6144], F32, kind="ExternalInput").ap()
    adab_d = nc.dram_tensor("ada_b", [128, 48], F32, kind="ExternalInput").ap()
    gains_d = nc.dram_tensor("gains", [128, 16], F32, kind="ExternalInput").ap()
    return cT_d, adaw_d, adab_d, gains_d


def build_L1(T_core=4096):
    nc = bass.Bass("TRN2", target_bir_lowering=False)
    xT_d = nc.dram_tensor("xT", [1024, T_core], F32, kind="ExternalInput").ap()
    cm = _decl_common(nc)
    w_in_d = nc.dram_tensor("b_w_in", [1024, 6144], F32, kind="ExternalInput").ap()
    vgain_d = nc.dram_tensor("b_vgain", [128, 3072], F32, kind="ExternalInput").ap()
    wsT_d = nc.dram_tensor("b_wsT", [128, 8, 128], F32, kind="ExternalInput").ap()
    bs_d = nc.dram_tensor("b_bs", [128, 24, 128], F32, kind="ExternalInput").ap()
    wout_d = nc.dram_tensor("b_w_out", [3072, 1024], F32, kind="ExternalInput").ap()
    w1_d = nc.dram_tensor("w1", [1024, 4096], F32, kind="ExternalInput").ap()
    w2_d = nc.dram_tensor("w2", [4096, 1024], F32, kind="ExternalInput").ap()
    yT_d = nc.dram_tensor("yT", [1024, T_core], F32, kind="ExternalOutput").ap()
    vm_d = nc.dram_tensor("vm_s", [24, 128, T_core], BF16, kind="ExternalOutput" if DEBUG else "Internal").ap()
    xs_d = nc.dram_tensor("xs_s", [1024, T_core], F32).ap()
    C = make_ctx(nc)
    emit_ada(C, *cm)
    emit_mixB(C, xT_d, xs_d, w_in_d, vgain_d, wsT_d, bs_d, wout_d, vm_d, T_core)
    emit_ffn(C, xs_d, yT_d, w1_d, w2_d, T_core)
    C.P.emit()
    return nc


def lay_L1(d, b):
    m = _lay_common(d, 1, b)
    m.update({
        "b_w_in": np.ascontiguousarray(d["b_w_in"][0]),
        "b_vgain": np.ascontiguousarray(np.broadcast_to(d["b_v_gain"][0][None, :], (128, 3072))),
        "b_wsT": np.ascontiguousarray(d["b_w_s"][0].transpose(2, 0, 1)),
        "b_bs": np.ascontiguousarray(np.broadcast_to(np.repeat(d["b_b_s"][0], 3, axis=0)[None], (128, 24, 128))),
        "b_w_out": np.ascontiguousarray(d["b_w_out"][0]),
        "w1": np.ascontiguousarray(d["ffn_w1"][1]), "w2": np.ascontiguousarray(d["ffn_w2"][1]),
    })
    return m


HALO = 512


def emit_mixA1(C, xT_d, w_in_d, gqk_d, qT_s, kT_s, v_s, T_core, T=256):
    P = C.P
    TT = T_core + HALO
    xv = xT_d.rearrange("(kc p) t -> p kc t", p=128)
    winv = w_in_d.rearrange("(kc p) f -> p kc f", p=128)
    qv = qT_s.rearrange("(oc p) t -> p oc t", p=128)
    kv = kT_s.rearrange("(oc p) t -> p oc t", p=128)
    with P.scope():
        wq = P.sb("wq", [128, 8, 3072], BF16); b_wq = P.bufs(8)
        for kc in range(8):
            load_cast(C, wq[:, kc, :], b_wq[kc], winv[:, kc, :], [3072])
        gqk = P.sb("gqk", [128, 2], F32); b_g = P.buf()
        P.dma(gqk[:], gqk_d, writes=[b_g])
        P.op("dve", lambda e: e.tensor_scalar_mul(out=gqk[:, 0:1], in0=gqk[:, 0:1], scalar1=0.125), reads=[b_g], writes=[b_g])
        bd = P.sb("bd", [128, 128], BF16); b_bd = P.buf()

        P.op("pool", lambda e: e.memset(bd[:], 0.0), writes=[b_bd])
        P.op("pool", lambda e: e.memset(bd[0:64, 0:64], 1.0 / 64.0), writes=[b_bd])
        P.op("pool", lambda e: e.memset(bd[64:128, 64:128], 1.0 / 64.0), writes=[b_bd])
        xts = [P.sb("xt%d" % i, [128, 8, T], F32) for i in range(2)]; b_xts = P.bufs(2)
        N = NormBufs(C, T)
        sqq = P.sb("sqq", [128, T], BF16); b_sqq = P.buf()
        rs = P.sb("rs", [128, T], F32); b_rs = P.buf()
        qk_o = [P.sb("qko%d" % i, [128, 8, T], BF16) for i in range(2)]; b_qko = P.bufs(2)
        vpad = [P.sb("vpad%d" % i, [128, 16, 128], BF16) for i in range(2)]; b_vpad = P.bufs(2)
        for i in range(2):
            P.op("pool", lambda e, i=i: e.memset(vpad[i][:], 0.0), writes=[b_vpad[i]])
        b_sc = P.buf()
        pring = Ring(list(zip(C.psums[1:4], C.psum_b[1:4])))
        sring = Ring(list(zip(C.psums[4:6], C.psum_b[4:6])))
        vring = Ring(list(zip(C.psums[6:8], C.psum_b[6:8])))
        vi = 0
        for ti in range(TT // T):
            xt, b_xt = xts[ti % 2], b_xts[ti % 2]
            t0 = ti * T
            P.dma(xt[:], xv[:, :, t0:t0 + T], writes=[b_xt])
            emit_norm_mod(C, N, xt, b_xt, 0, 8)
            for which in range(2):
                if which == 0 and t0 + T <= HALO:
                    continue
                qo, bqo = qk_o[which], b_qko[which]
                for oc in range(8):
                    ps, bps = pring.next()
                    col0 = which * 1024 + oc * 128
                    for kc in range(8):
                        P.op("pe", lambda e, kc=kc, ps=ps, col0=col0: e.matmul(
                            ps[:, :T], lhsT=wq[:, kc, col0:col0 + 128], rhs=N.hT[:, kc, :], start=(kc == 0), stop=(kc == 7)),
                            reads=[b_wq[kc], N.b_hT], writes=[bps])
                    P.op("act", lambda e, ps=ps: e.activation(out=sqq[:], in_=ps[:, :T], func=AF.Square), reads=[bps], writes=[b_sqq])
                    ps2, bps2 = sring.next()
                    P.op("pe", lambda e, ps2=ps2: e.matmul(ps2[:, :T], lhsT=bd[:], rhs=sqq[:], start=True, stop=True),
                         reads=[b_sqq, b_bd], writes=[bps2])
                    P.op("act", lambda e, ps2=ps2: e.activation(out=rs[:], in_=ps2[:, :T], func=AF.Sqrt, bias=C.eps[:, 0:1], scale=1.0),
                         reads=[bps2, C.cb], writes=[b_rs])
                    P.op("dve", lambda e: e.reciprocal(out=rs[:], in_=rs[:]), reads=[b_rs], writes=[b_rs])
                    P.op("dve", lambda e, ps=ps, oc=oc, qo=qo, which=which: e.scalar_tensor_tensor(
                        out=qo[:, oc, :], in0=ps[:, :T], scalar=gqk[:, which:which + 1], in1=rs[:], op0=ALU.mult, op1=ALU.mult),
                        reads=[bps, b_rs, b_g], writes=[bqo])
                dst = qv if which == 0 else kv
                if which == 0:
                    P.dma(dst[:, :, t0 - HALO:t0 - HALO + T], qo[:], reads=[bqo], writes=[b_sc])
                else:
                    P.dma(dst[:, :, t0:t0 + T], qo[:], reads=[bqo], writes=[b_sc])
            for w in range(T // 128):
                vp, bvp = vpad[vi % 2], b_vpad[vi % 2]
                vi += 1
                for half in range(2):
                    ps, bps = vring.next()
                    for kc in range(8):
                        P.op("pe", lambda e, kc=kc, ps=ps, w=w, half=half: e.matmul(
                            ps[:, :], lhsT=N.hT[:, kc, w * 128:(w + 1) * 128], rhs=wq[:, kc, 2048 + half * 512:2048 + (half + 1) * 512],
                            start=(kc == 0), stop=(kc == 7)), reads=[b_wq[kc], N.b_hT], writes=[bps])
                    psv = ps[:, :].rearrange("p (hp two d) -> p hp two d", hp=4, two=2)
                    vpv = vp[:, half * 8:(half + 1) * 8, :].rearrange("p (hp two) c -> p hp two c", two=2)
                    P.op("act", lambda e, psv=psv, vpv=vpv: e.activation(out=vpv[:, :, 0, 0:64], in_=psv[:, :, 0, :], func=AF.Identity),
                         reads=[bps], writes=[bvp])
                    P.op("dve", lambda e, psv=psv, vpv=vpv: e.tensor_copy(out=vpv[:, :, 1, 64:128], in_=psv[:, :, 1, :]),
                         reads=[bps], writes=[bvp])
                tw = t0 + w * 128
                P.dma(v_s[tw:tw + 128, :].rearrange("t (h c) -> t h c", h=16), vp[:], reads=[bvp], writes=[b_sc])


def emit_mixA2(C, xT_d, xo_d, qT_s, kT_s, v_s, biasT_d, valid_d, wout_d, T_core):
    P = C.P
    AB = C.AB
    xv = xT_d.rearrange("(kc p) t -> p kc t", p=128)
    xov = xo_d.rearrange("(kc p) t -> p kc t", p=128)
    qv = qT_s.rearrange("(oc p) t -> p oc t", p=128)
    kv = kT_s.rearrange("(oc p) t -> p oc t", p=128)
    with P.scope():
        wob = P.sb("wob", [128, 8, 1024], BF16); b_wo = P.bufs(2)
        wov = wout_d.rearrange("(fc p) n -> p fc n", p=128)
        for g in range(2):
            load_cast(C, wob[:, g * 4:(g + 1) * 4, :], b_wo[g], wov[:, g * 4:(g + 1) * 4, :], [4, 1024])
        ebias = P.sb("ebias", [128, 16, 640], F32); b_eb = P.buf()
        for g in range(4):
            P.dma(ebias[:, g * 4:(g + 1) * 4, :], biasT_d[:, g * 4:(g + 1) * 4, :], writes=[b_eb])
        P.op("act", lambda e: e.activation(out=ebias[:], in_=ebias[:], func=AF.Exp), reads=[b_eb], writes=[b_eb])
        nkb = (T_core + HALO) // 128
        valid = P.sb("valid", [128, nkb], F32); b_val = P.buf()
        P.dma(valid[:], valid_d, writes=[b_val])
        selA = P.sb("selA", [128, 128], BF16); selB = P.sb("selB", [128, 128], BF16); b_sel = P.buf()

        P.op("pool", lambda e: e.memset(selA[:], 0.0), writes=[b_sel])
        P.op("pool", lambda e: e.memset(selB[:], 0.0), writes=[b_sel])
        P.op("pool", lambda e: e.memset(selA[:, 0:64], 1.0), writes=[b_sel])
        P.op("pool", lambda e: e.memset(selB[:, 64:128], 1.0), writes=[b_sel])
        NB = 2
        qt = [P.sb("qt%d" % i, [128, 8, 128], BF16) for i in range(NB)]; b_qt = P.bufs(NB)
        kt = [P.sb("kt%d" % i, [128, 8, 640], BF16) for i in range(NB)]; b_kt = P.bufs(NB)
        vt = [P.sb("vt%d" % i, [128, 5, 2048], BF16) for i in range(NB)]; b_vt = P.bufs(NB)
        xts = [P.sb("xa%d" % i, [128, 8, 128], F32) for i in range(NB)]; b_xts = P.bufs(NB)
        et = [P.sb("et%d" % i, [128, 640], F32) for i in range(2)]; b_et = P.bufs(2)
        pt = [P.sb("pt%d" % i, [128, 640], BF16) for i in range(2)]; b_pt = P.bufs(2)
        oT = P.sb("oT", [128, 8, 128], BF16); b_oT = P.bufs(8)
        den = P.sb("den", [128, 128], F32); b_den = P.buf()
        b_xo = P.buf()
        sc = [(C.psums[1], C.psum_b[1], C.psums[2], C.psum_b[2]), (C.psums[3], C.psum_b[3], C.psums[4], C.psum_b[4])]
        po, b_po = C.psums[5], C.psum_b[5]
        pd, b_pd = C.psums[6], C.psum_b[6]
        py, b_py = C.psums[7], C.psum_b[7]
        def make_item(hi, i, m, pr, ab, kb0):
            h = pr * 2 + ab
            sA, bsA, sB, bsB = sc[hi % 2]
            e_, be = et[hi % 2], b_et[hi % 2]
            p_, bp = pt[hi % 2], b_pt[hi % 2]
            lo = ab * 64
            sel = selA if ab == 0 else selB

            def stage_a():
                for j in range(5):
                    dst, bd_ = (sA, bsA) if j < 4 else (sB, bsB)
                    c0 = (j % 4) * 128
                    P.op("pe", lambda e, j=j, dst=dst, c0=c0: e.matmul(
                        dst[:, c0:c0 + 128], lhsT=kt[i][lo:lo + 64, pr, j * 128:(j + 1) * 128], rhs=qt[i][lo:lo + 64, pr, :],
                        start=True, stop=True), reads=[b_kt[i], b_qt[i]], writes=[bd_])
                P.op("act", lambda e: e.activation(out=e_[:, 0:512], in_=sA[:, :], func=AF.Exp), reads=[bsA], writes=[be])
                P.op("act", lambda e: e.activation(out=e_[:, 512:640], in_=sB[:, 0:128], func=AF.Exp), reads=[bsB], writes=[be])
                for j in range(5):
                    P.op("dve", lambda e, j=j: e.scalar_tensor_tensor(
                        out=p_[:, j * 128:(j + 1) * 128], in0=e_[:, j * 128:(j + 1) * 128], scalar=valid[:, kb0 + j:kb0 + j + 1],
                        in1=ebias[:, h, j * 128:(j + 1) * 128], op0=ALU.mult, op1=ALU.mult),
                        reads=[be, b_val, b_eb], writes=[bp])

            def stage_b():
                for j in range(5):
                    first = (ab == 0 and j == 0)
                    last = (ab == 1 and j == 4)
                    P.op("pe", lambda e, j=j, first=first, last=last: e.matmul(
                        po[:, 0:128], lhsT=vt[i][:, j, h * 128:(h + 1) * 128], rhs=p_[:, j * 128:(j + 1) * 128],
                        start=first, stop=last), reads=[b_vt[i], bp], writes=[b_po])
                    P.op("pe", lambda e, j=j, first=first, last=last: e.matmul(
                        pd[:, 0:128], lhsT=sel[:], rhs=p_[:, j * 128:(j + 1) * 128],
                        start=first, stop=last), reads=[b_sel, bp], writes=[b_pd])
                if ab == 1:
                    P.op("dve", lambda e: e.reciprocal(out=den[:], in_=pd[:, 0:128]), reads=[b_pd], writes=[b_den])
                    P.op("dve", lambda e: e.tensor_tensor(out=oT[:, pr, :], in0=po[:, 0:128], in1=den[:], op=ALU.mult),
                         reads=[b_po, b_den], writes=[b_oT[pr]])
                if ab == 1 and pr == 7:
                    xt, b_xt = xts[i], b_xts[i]
                    q0 = m * 128
                    for oc in range(8):
                        for pr2 in range(8):
                            P.op("pe", lambda e, oc=oc, pr2=pr2: e.matmul(
                                py[:, (oc % 4) * 128:(oc % 4) * 128 + 128],
                                lhsT=wob[:, pr2, oc * 128:(oc + 1) * 128], rhs=oT[:, pr2, :],
                                start=(pr2 == 0), stop=(pr2 == 7)), reads=[b_wo[pr2 // 4], b_oT[pr2]], writes=[b_py])
                        P.op("dve", lambda e, oc=oc: e.scalar_tensor_tensor(
                            out=xt[:, oc, :], in0=py[:, (oc % 4) * 128:(oc % 4) * 128 + 128], scalar=AB[:, 16 + oc:17 + oc], in1=xt[:, oc, :],
                            op0=ALU.mult, op1=ALU.add), reads=[b_py, C.b_ab, b_xt], writes=[b_xt])
                    P.dma(xov[:, :, q0:q0 + 128], xt[:], reads=[b_xt], writes=[b_xo])
            return stage_a, stage_b

        pending = None
        hi = 0
        for m in range(T_core // 128):
            i = m % NB
            q0 = m * 128
            k0 = m * 128
            P.dma(qt[i][:], qv[:, :, q0:q0 + 128], writes=[b_qt[i]])
            P.dma(kt[i][:], kv[:, :, k0:k0 + 640], writes=[b_kt[i]])
            P.dma(vt[i][:], v_s[k0:k0 + 640, :].rearrange("(j p) c -> p j c", p=128), writes=[b_vt[i]])
            P.dma(xts[i][:], xv[:, :, HALO + q0:HALO + q0 + 128], writes=[b_xts[i]])
            kb0 = k0 // 128
            for pr in range(8):
                for ab in range(2):
                    sa, sb_ = make_item(hi, i, m, pr, ab, kb0)
                    hi += 1
                    sa()
                    if pending is not None:
                        pending()
                    pending = sb_
        pending()


def build_L0(T_core=4096):
    nc = bass.Bass("TRN2", target_bir_lowering=False)
    TT = T_core + HALO
    xT_d = nc.dram_tensor("xT", [1024, TT], F32, kind="ExternalInput").ap()
    cm = _decl_common(nc)
    w_in_d = nc.dram_tensor("a_w_in", [1024, 3072], F32, kind="ExternalInput").ap()
    gqk_d = nc.dram_tensor("a_gqk", [128, 2], F32, kind="ExternalInput").ap()
    biasT_d = nc.dram_tensor("a_biasT", [128, 16, 640], F32, kind="ExternalInput").ap()
    valid_d = nc.dram_tensor("a_valid", [128, TT // 128], F32, kind="ExternalInput").ap()
    wout_d = nc.dram_tensor("a_w_out", [1024, 1024], F32, kind="ExternalInput").ap()
    w1_d = nc.dram_tensor("w1", [1024, 4096], F32, kind="ExternalInput").ap()
    w2_d = nc.dram_tensor("w2", [4096, 1024], F32, kind="ExternalInput").ap()
    yT_d = nc.dram_tensor("yT", [1024, T_core], F32, kind="ExternalOutput").ap()
    qT_s = nc.dram_tensor("qT_s", [1024, T_core], BF16).ap()
    kT_s = nc.dram_tensor("kT_s", [1024, TT], BF16).ap()
    v_s = nc.dram_tensor("v_s", [TT, 2048], BF16).ap()
    xs_d = nc.dram_tensor("xs_s", [1024, T_core], F32).ap()
    C = make_ctx(nc)
    emit_ada(C, *cm)
    emit_mixA1(C, xT_d, w_in_d, gqk_d, qT_s, kT_s, v_s, T_core)
    emit_mixA2(C, xT_d, xs_d, qT_s, kT_s, v_s, biasT_d, valid_d, wout_d, T_core)
    emit_ffn(C, xs_d, yT_d, w1_d, w2_d, T_core)
    C.P.emit()
    return nc


def _a_bias_table(rel_bias):
    kap = np.arange(640)[:, None]
    q = np.arange(128)[None, :]
    rel = q + 512 - kap
    cq = q // 64
    inband = (kap >= cq * 64) & (kap < cq * 64 + 576)
    idx = np.clip(rel, -63, 256) + 63
    tab = rel_bias[:, idx]
    tab = np.where(inband[None], tab, np.float32(-30000.0)).astype(np.float32)
    tab = tab.reshape(16, 5, 128, 128).transpose(2, 0, 1, 3).reshape(128, 16, 640)
    return np.ascontiguousarray(tab)


def lay_L0(d, b, half, T_core=4096, x=None):
    m = _lay_common(d, 0, b)
    x = d["x"] if x is None else x
    t0 = half * T_core
    TT = T_core + HALO
    xt = np.zeros((1024, TT), np.float32)
    lo = t0 - HALO
    if lo >= 0:
        xt[:, :] = x[b, lo:lo + TT, :].T
    else:
        xt[:, HALO:] = x[b, 0:T_core, :].T
    valid = np.ones((128, TT // 128), np.float32)
    if lo < 0:
        valid[:, :HALO // 128] = 0.0
    m.update({
        "xT": xt,
        "a_w_in": np.ascontiguousarray(d["a_w_in"][0]),
        "a_gqk": np.ascontiguousarray(np.stack([np.tile(d["a_q_gain"][0], 2), np.tile(d["a_k_gain"][0], 2)], 1)),
        "a_biasT": _a_bias_table(d["a_rel_bias"][0]),
        "a_valid": valid,
        "a_w_out": np.ascontiguousarray(d["a_w_out"][0]),
        "w1": np.ascontiguousarray(d["ffn_w1"][0]), "w2": np.ascontiguousarray(d["ffn_w2"][0]),
    })
    return m


def emit_mixD1(C, xT_d, wqkv_d, qT_s, kT_s, v_s, S, T=256, xsrc=None):
    P = C.P
    if xsrc is None:
        xv0 = xT_d.rearrange("(kc p) t -> p kc t", p=128)
        xsrc = lambda t0, T: xv0[:, :, t0:t0 + T]
    wv_ = wqkv_d.rearrange("(kc p) f -> p kc f", p=128)
    qv = qT_s.rearrange("(oc p) t -> p oc t", p=128)
    kv = kT_s.rearrange("(oc p) t -> p oc t", p=128)
    with P.scope():
        wq = P.sb("wq", [128, 8, 1536], BF16); b_wq = P.bufs(8)
        for kc in range(8):
            load_cast(C, wq[:, kc, :], b_wq[kc], wv_[:, kc, :], [1536])
        xts = [P.sb("xt%d" % i, [128, 8, T], F32) for i in range(2)]; b_xts = P.bufs(2)
        N = NormBufs(C, T)
        qk_o = [P.sb("qko%d" % i, [128, 4, T], BF16) for i in range(2)]; b_qko = P.bufs(2)
        vpad = [P.sb("vpad%d" % i, [128, 8, 128], BF16) for i in range(2)]; b_vpad = P.bufs(2)
        for i in range(2):
            P.op("pool", lambda e, i=i: e.memset(vpad[i][:], 0.0), writes=[b_vpad[i]])
        b_sc = P.buf()
        pring = Ring(list(zip(C.psums[1:5], C.psum_b[1:5])))
        vring = Ring(list(zip(C.psums[5:8], C.psum_b[5:8])))
        vi = 0
        for ti in range(S // T):
            xt, b_xt = xts[ti % 2], b_xts[ti % 2]
            t0 = ti * T
            P.dma(xt[:], xsrc(t0, T), reads=(xsrc.deps(t0) if hasattr(xsrc, 'deps') else []), writes=[b_xt])
            emit_norm_mod(C, N, xt, b_xt, 0, 8)
            for which in range(2):
                qo, bqo = qk_o[which], b_qko[which]
                for oc in range(4):
                    ps, bps = pring.next()
                    col0 = which * 512 + oc * 128
                    for kc in range(8):
                        P.op("pe", lambda e, kc=kc, ps=ps, col0=col0: e.matmul(
                            ps[:, :T], lhsT=wq[:, kc, col0:col0 + 128], rhs=N.hT[:, kc, :], start=(kc == 0), stop=(kc == 7)),
                            reads=[b_wq[kc], N.b_hT], writes=[bps])
                    sc = 0.125 if which == 0 else 1.0
                    P.op("act", lambda e, ps=ps, oc=oc, qo=qo, sc=sc: e.activation(out=qo[:, oc, :], in_=ps[:, :T], func=AF.Identity, scale=sc),
                         reads=[bps], writes=[bqo])
                dst = qv if which == 0 else kv
                P.dma(dst[:, :, t0:t0 + T], qo[:], reads=[bqo], writes=[b_sc])
            for w in range(T // 128):
                vp, bvp = vpad[vi % 2], b_vpad[vi % 2]
                vi += 1
                ps, bps = vring.next()
                for kc in range(8):
                    P.op("pe", lambda e, kc=kc, ps=ps, w=w: e.matmul(
                        ps[:, :], lhsT=N.hT[:, kc, w * 128:(w + 1) * 128], rhs=wq[:, kc, 1024:1536],
                        start=(kc == 0), stop=(kc == 7)), reads=[b_wq[kc], N.b_hT], writes=[bps])
                psv = ps[:, :].rearrange("p (hp two d) -> p hp two d", hp=4, two=2)
                vpv = vp[:].rearrange("p (hp two) c -> p hp two c", two=2)
                P.op("act", lambda e, psv=psv, vpv=vpv: e.activation(out=vpv[:, :, 0, 0:64], in_=psv[:, :, 0, :], func=AF.Identity),
                     reads=[bps], writes=[bvp])
                P.op("dve", lambda e, psv=psv, vpv=vpv: e.tensor_copy(out=vpv[:, :, 1, 64:128], in_=psv[:, :, 1, :]),
                     reads=[bps], writes=[bvp])
                tw = t0 + w * 128
                P.dma(v_s[tw:tw + 128, :].rearrange("t (h c) -> t h c", h=8), vp[:], reads=[bvp], writes=[b_sc])


def emit_mixD2(C, qT_s, kT_s, v_s, cst_d, oT_d, S, TQ=512, odst=None):
    P = C.P
    qv = qT_s.rearrange("(oc p) t -> p oc t", p=128)
    kv = kT_s.rearrange("(oc p) t -> p oc t", p=128)
    if odst is None:
        ov = oT_d.rearrange("(oc p) t -> p oc t", p=128)
        odst = lambda t0, n: ov[:, :, t0:t0 + n]
    with P.scope():
        cst = P.sb("cst", [128, 3, 128], F32); b_cst = P.buf()
        P.dma(cst[:], cst_d, writes=[b_cst])
        cbf = P.sb("cbf", [128, 3, 128], BF16); b_cbf = P.buf()
        P.op("dve", lambda e: e.tensor_copy(out=cbf[:], in_=cst[:]), reads=[b_cst], writes=[b_cbf])
        qt = [P.sb("qt%d" % i, [128, 4, TQ], BF16) for i in range(2)]; b_qt = P.bufs(2)
        NKB = 3
        kt = [P.sb("kt%d" % i, [128, 4, 128], BF16) for i in range(NKB)]; b_kt = P.bufs(NKB)
        vt = [P.sb("vt%d" % i, [128, 8, 128], BF16) for i in range(NKB)]; b_vt = P.bufs(NKB)
        NS = 4
        et = [P.sb("et%d" % i, [128, TQ], F32) for i in range(NS)]; b_et = P.bufs(NS)
        lt = [P.sb("lt%d" % i, [128, TQ], BF16) for i in range(NS)]; b_lt = P.bufs(NS)
        wt = [P.sb("wt%d" % i, [128, TQ], BF16) for i in range(NS)]; b_wt = P.bufs(NS)
        lacc = [P.sb("lacc%d" % i, [128, TQ], BF16) for i in range(8)]; b_lacc = P.bufs(8)
        osb = [P.sb("osb%d" % i, [128, 4, TQ], F32) for i in range(2)]; b_osb = P.bufs(2)
        b_od = P.buf()
        z1ring = Ring(list(zip(C.psums[0:2], C.psum_b[0:2])))
        z2ring = Ring(list(zip(C.psums[2:4], C.psum_b[2:4])))
        poring = [(C.psums[4 + i], C.psum_b[4 + i]) for i in range(4)]
        nq = TQ // 128

        def make_item(hi, q, bq, k_, bk, v_, bv, h, kb, dq, c0, first):
            pr, ab = h // 2, h % 2
            lo = ab * 64
            e_, be = et[hi % NS], b_et[hi % NS]
            l_, bl = lt[hi % NS], b_lt[hi % NS]
            w_, bw = wt[hi % NS], b_wt[hi % NS]
            la, bla = lacc[h], b_lacc[h]
            po, bpo = poring[pr]

            def stage_a():
                zp, bzp = z1ring.next()
                P.op("pe", lambda e: e.matmul(zp[:, c0:TQ], lhsT=k_[lo:lo + 64, pr, :], rhs=q[lo:lo + 64, pr, c0:TQ], start=True, stop=True),
                     reads=[bk, bq], writes=[bzp])
                P.op("act", lambda e: e.activation(out=e_[:, c0:TQ], in_=zp[:, c0:TQ], func=AF.Exp), reads=[bzp], writes=[be])
                P.op("act", lambda e: e.activation(out=l_[:, c0:TQ], in_=e_[:, c0:TQ], func=AF.Ln, bias=C.ones_f[:, 0:1], scale=1.0),
                     reads=[be, C.cb], writes=[bl])
                if dq >= 0:
                    P.op("dve", lambda e: e.tensor_tensor(out=l_[:, c0:c0 + 128], in0=l_[:, c0:c0 + 128], in1=cbf[:, 0, :], op=ALU.mult),
                         reads=[bl, b_cbf], writes=[bl])

            def stage_b():
                zq, bzq = z2ring.next()
                P.op("pe", lambda e: e.matmul(zq[:, c0:TQ], lhsT=k_[lo:lo + 64, pr, :], rhs=q[lo:lo + 64, pr, c0:TQ], start=True, stop=False),
                     reads=[bk, bq], writes=[bzq])
                P.op("pe", lambda e: e.matmul(zq[:, c0:TQ], lhsT=cbf[:, 1, :], rhs=l_[:, c0:TQ], start=False, stop=first),
                     reads=[bl, b_cbf], writes=[bzq])
                if not first:
                    P.op("pe", lambda e: e.matmul(zq[:, c0:TQ], lhsT=cbf[:, 2, :], rhs=la[:, c0:TQ], start=False, stop=True),
                         reads=[bla, b_cbf], writes=[bzq])
                if c0 > 0:
                    P.op("pool", lambda e: e.memset(w_[:, 0:c0], 0.0), writes=[bw])
                P.op("act", lambda e: e.activation(out=w_[:, c0:TQ], in_=zq[:, c0:TQ], func=AF.Exp), reads=[bzq], writes=[bw])
                if dq >= 0:
                    P.op("dve", lambda e: e.tensor_tensor(out=w_[:, c0:c0 + 128], in0=w_[:, c0:c0 + 128], in1=cbf[:, 0, :], op=ALU.mult),
                         reads=[bw, b_cbf], writes=[bw])

            def stage_c():
                if kb > 0:
                    if first:
                        _lacc_init(P, la, bla, l_, bl, c0, TQ)
                    else:
                        P.op("pool", lambda e: e.tensor_tensor(out=la[:, c0:TQ], in0=la[:, c0:TQ], in1=l_[:, c0:TQ], op=ALU.add),
                             reads=[bl, bla], writes=[bla])
                P.op("pe", lambda e: e.matmul(po[:, 0:TQ], lhsT=v_[:, h, :], rhs=w_[:, 0:TQ], start=(first and ab == 0), stop=(kb == 0 and ab == 1)),
                     reads=[bv, bw], writes=[bpo])
            return stage_a, stage_b, stage_c

        def make_post(os_, bos, q0):
            def post():
                for pr in range(4):
                    po, bpo = poring[pr]
                    if pr % 2 == 0:
                        P.op("act", lambda e, po=po, pr=pr: e.activation(out=os_[:, pr, :], in_=po[:, :], func=AF.Identity), reads=[bpo], writes=[bos])
                    else:
                        P.op("dve", lambda e, po=po, pr=pr: e.tensor_copy(out=os_[:, pr, :], in_=po[:, :]), reads=[bpo], writes=[bos])
                P.dma(odst(q0, TQ), os_[:], reads=[bos], writes=[b_od])
            return post

        pend_b = []
        pend_c = []

        def push(stage_a, stage_b, stage_c, post):
            stage_a()
            nb = pend_b.pop(0) if pend_b else None
            if nb is not None:
                nb[0]()
            if pend_c:
                c_, post_ = pend_c.pop(0)
                c_()
                if post_ is not None:
                    post_()
            if nb is not None:
                pend_c.append((nb[1], nb[2]))
            pend_b.append((stage_b, stage_c, post))

        hi = 0
        ki = 0
        for qi in range(S // TQ):
            q, bq = qt[qi % 2], b_qt[qi % 2]
            q0 = qi * TQ
            P.dma(q[:], qv[:, :, q0:q0 + TQ], writes=[bq])
            os_, bos = osb[qi % 2], b_osb[qi % 2]
            kb_hi = qi * nq + nq - 1
            for kb in range(kb_hi, -1, -1):
                k_, bk = kt[ki % NKB], b_kt[ki % NKB]
                v_, bv = vt[ki % NKB], b_vt[ki % NKB]
                ki += 1
                P.dma(k_[:], kv[:, :, kb * 128:(kb + 1) * 128], writes=[bk])
                P.dma(v_[:], v_s[kb * 128:(kb + 1) * 128, :].rearrange("t (h c) -> t h c", h=8), writes=[bv])
                dq = kb - qi * nq
                c0 = max(dq, 0) * 128
                first = (kb == kb_hi)
                for h in range(8):
                    sa, sb_, sc_ = make_item(hi, q, bq, k_, bk, v_, bv, h, kb, dq, c0, first)
                    hi += 1
                    post = make_post(os_, bos, q0) if (kb == 0 and h == 7) else None
                    push(sa, sb_, sc_, post)
        while pend_b or pend_c:
            nb = pend_b.pop(0) if pend_b else None
            if nb is not None:
                nb[0]()
            if pend_c:
                c_, post_ = pend_c.pop(0)
                c_()
                if post_ is not None:
                    post_()
            if nb is not None:
                pend_c.append((nb[1], nb[2]))


def _lacc_init(P, la, bla, l_, bl, c0, TQ):
    if c0 > 0:
        P.op("pool", lambda e: e.memset(la[:, 0:c0], 0.0), writes=[bla])
    P.op("pool", lambda e: e.tensor_copy(out=la[:, c0:TQ], in_=l_[:, c0:TQ]), reads=[bl], writes=[bla])


def emit_outproj(C, xT_d, oT_d, wout_d, xo_d, T_core, T=256, sel=None, xsrc=None, osrc=None):
    P = C.P
    AB = C.AB
    if xsrc is None:
        xv = xT_d.rearrange("(kc p) t -> p kc t", p=128)
        xsrc = lambda t0, T: xv[:, :, t0:t0 + T]
    if osrc is None:
        ov = oT_d.rearrange("(kc p) t -> p kc t", p=128)
        osrc = lambda h, t0, T: ov[:, :, h * T_core + t0:h * T_core + t0 + T]
    xov = xo_d.rearrange("(kc p) t -> p kc t", p=128)
    with P.scope():
        wob = P.sb("wob", [128, 8, 1024], BF16); b_wo = P.bufs(2)
        wov = wout_d.rearrange("(fc p) n -> p fc n", p=128)
        for g in range(2):
            load_cast(C, wob[:, g * 4:(g + 1) * 4, :], b_wo[g], wov[:, g * 4:(g + 1) * 4, :], [4, 1024])
        xts = [P.sb("xt%d" % i, [128, 8, T], F32) for i in range(2)]; b_xts = P.bufs(2)
        ots = [P.sb("ot%d" % i, [128, 8, T], F32) for i in range(2)]; b_ots = P.bufs(2)
        ob = P.sb("ob", [128, 8, T], BF16); b_ob = P.buf()
        if sel is not None:
            ots1 = [P.sb("ot1_%d" % i, [128, 8, T], F32) for i in range(2)]; b_ots1 = P.bufs(2)
        yring = Ring(list(zip(C.psums[1:8], C.psum_b[1:8])))
        b_xo = P.buf()
        for ti in range(T_core // T):
            xt, b_xt = xts[ti % 2], b_xts[ti % 2]
            ot, b_ot = ots[ti % 2], b_ots[ti % 2]
            t0 = ti * T
            P.dma(xt[:], xsrc(t0, T), writes=[b_xt])
            P.dma(ot[:], osrc(0, t0, T), reads=(osrc.deps(0, t0) if hasattr(osrc, 'deps') else []), writes=[b_ot])
            if sel is None:
                P.op("act", lambda e, ot=ot: e.activation(out=ob[:], in_=ot[:], func=AF.Identity), reads=[b_ot], writes=[b_ob])
            else:
                selt, b_sel = sel
                ot1, b_ot1 = ots1[ti % 2], b_ots1[ti % 2]
                P.dma(ot1[:], osrc(1, t0, T), reads=(osrc.deps(1, t0) if hasattr(osrc, 'deps') else []), writes=[b_ot1])
                P.op("act", lambda e, ot=ot: e.activation(out=ot[:], in_=ot[:], func=AF.Identity, scale=selt[:, 0:1]),
                     reads=[b_ot, b_sel], writes=[b_ot])
                P.op("dve", lambda e, ot=ot, ot1=ot1: e.scalar_tensor_tensor(out=ob[:], in0=ot1[:], scalar=selt[:, 1:2], in1=ot[:],
                                                                            op0=ALU.mult, op1=ALU.add),
                     reads=[b_ot, b_ot1, b_sel], writes=[b_ob])
            for oc in range(8):
                ps, bps = yring.next()
                for kc in range(8):
                    P.op("pe", lambda e, oc=oc, kc=kc, ps=ps: e.matmul(
                        ps[:, :T], lhsT=wob[:, kc, oc * 128:(oc + 1) * 128], rhs=ob[:, kc, :],
                        start=(kc == 0), stop=(kc == 7)), reads=[b_wo[kc // 4], b_ob], writes=[bps])
                P.op("dve", lambda e, oc=oc, ps=ps, xt=xt: e.scalar_tensor_tensor(
                    out=xt[:, oc, :], in0=ps[:, :T], scalar=AB[:, 16 + oc:17 + oc], in1=xt[:, oc, :],
                    op0=ALU.mult, op1=ALU.add), reads=[bps, C.b_ab, b_xt], writes=[b_xt])
            P.dma(xov[:, :, t0:t0 + T], xt[:], reads=[b_xt], writes=[b_xo])


def build_L3a(S=8192):
    nc = bass.Bass("TRN2", target_bir_lowering=False)
    xT_d = nc.dram_tensor("xT", [1024, S], F32, kind="ExternalInput").ap()
    cm = _decl_common(nc)
    wqkv_d = nc.dram_tensor("d_wqkv", [1024, 1536], F32, kind="ExternalInput").ap()
    cst_d = nc.dram_tensor("d_cst", [128, 3, 128], F32, kind="ExternalInput").ap()
    oT_d = nc.dram_tensor("oT", [512, S], F32, kind="ExternalOutput").ap()
    qT_s = nc.dram_tensor("qT_s", [512, S], BF16).ap()
    kT_s = nc.dram_tensor("kT_s", [512, S], BF16).ap()
    v_s = nc.dram_tensor("v_s", [S, 1024], BF16).ap()
    C = make_ctx(nc)
    emit_ada(C, *cm)
    emit_mixD1(C, xT_d, wqkv_d, qT_s, kT_s, v_s, S)
    emit_mixD2(C, qT_s, kT_s, v_s, cst_d, oT_d, S)
    C.P.emit()
    return nc


def _d_consts():
    j = np.arange(128)[:, None]
    s_ = np.arange(128)[None, :]
    mask = (s_ > j).astype(np.float32)
    ntri = -(j >= s_).astype(np.float32)
    nones = -np.ones((128, 128), np.float32)
    return np.ascontiguousarray(np.stack([mask, ntri, nones], 1))


def lay_L3a(d, b, hg, xT_full):
    m = _lay_common(d, 3, b)
    w = d["d_w_in"][0]
    m.update({
        "xT": xT_full,
        "d_wqkv": np.ascontiguousarray(np.concatenate([w[:, hg * 512:(hg + 1) * 512], w[:, 1024 + hg * 512:1024 + (hg + 1) * 512],
                                                       w[:, 2048 + hg * 512:2048 + (hg + 1) * 512]], 1)),
        "d_cst": _d_consts(),
    })
    return m


def build_Lb(T_core=4096):
    nc = bass.Bass("TRN2", target_bir_lowering=False)
    xT_d = nc.dram_tensor("xT", [1024, T_core], F32, kind="ExternalInput").ap()
    oT_d = nc.dram_tensor("oT", [1024, T_core], F32, kind="ExternalInput").ap()
    cm = _decl_common(nc)
    wout_d = nc.dram_tensor("w_out", [1024, 1024], F32, kind="ExternalInput").ap()
    w1_d = nc.dram_tensor("w1", [1024, 4096], F32, kind="ExternalInput").ap()
    w2_d = nc.dram_tensor("w2", [4096, 1024], F32, kind="ExternalInput").ap()
    yT_d = nc.dram_tensor("yT", [1024, T_core], F32, kind="ExternalOutput").ap()
    xs_d = nc.dram_tensor("xs_s", [1024, T_core], F32).ap()
    C = make_ctx(nc)
    emit_ada(C, *cm)
    emit_outproj(C, xT_d, oT_d, wout_d, xs_d, T_core)
    emit_ffn(C, xs_d, yT_d, w1_d, w2_d, T_core)
    C.P.emit()
    return nc


def lay_Lb(d, L, b, w_out):
    m = _lay_common(d, L, b)
    m.update({"w_out": np.ascontiguousarray(w_out), "w1": np.ascontiguousarray(d["ffn_w1"][L]), "w2": np.ascontiguousarray(d["ffn_w2"][L])})
    return m


def emit_mixC(C, xT_d, wc_d, wg_d, bg_d, og_d, cst_d, oT_d, S, T=256, xsrc=None, odst=None):
    P = C.P
    if xsrc is None:
        xv0 = xT_d.rearrange("(kc p) t -> p kc t", p=128)
        xsrc = lambda t0, T: xv0[:, :, t0:t0 + T]
    wv_ = wc_d.rearrange("(kc p) f -> p kc f", p=128)
    if odst is None:
        ov = oT_d.rearrange("(oc p) t -> p oc t", p=128)
        odst = lambda t0, n: ov[:, :, t0:t0 + n]
    with P.scope():
        wc = P.sb("wc", [128, 8, 1552], BF16); b_wc = P.bufs(8)
        for kc in range(8):
            load_cast(C, wc[:, kc, :], b_wc[kc], wv_[:, kc, :], [1552])
        wg = P.sb("wg", [16, 256], F32); bg = P.sb("bg", [1, 256], F32); og = P.sb("og", [128, 2], F32)
        cst = P.sb("cst", [128, 3, 128], F32)
        b_k = P.buf()
        P.dma(wg[:], wg_d, writes=[b_k]); P.dma(bg[:], bg_d, writes=[b_k]); P.dma(og[:], og_d, writes=[b_k])
        P.dma(cst[:], cst_d, writes=[b_k])
        xts = [P.sb("xt%d" % i, [128, 8, T], F32) for i in range(2)]; b_xts = P.bufs(2)
        N = NormBufs(C, T)
        q_sb = P.sb("q_sb", [128, 2, T], F32); b_q = P.bufs(2)
        k_sb = P.sb("k_sb", [128, 2, T], F32); b_kk = P.bufs(2)
        r_sb = P.sb("r_sb", [128, 4, T], F32); b_r = P.bufs(4)
        a_sb = P.sb("a_sb", [16, T], F32); b_a = P.buf()
        ktok = P.sb("ktok", [128, 256], F32); b_ktok = P.buf()
        vtok = P.sb("vtok", [128, 512], BF16); b_vtok = P.buf()
        st = [P.sb("st%d" % i, [128, 256], F32) for i in range(2)]; b_st = P.bufs(2)
        stb = [P.sb("stb%d" % i, [128, 256], BF16) for i in range(2)]; b_stb = P.bufs(2)
        for i in range(2):
            P.op("pool", lambda e, i=i: e.memset(st[i][:], 0.0), writes=[b_st[i]])
            P.op("pool", lambda e, i=i: e.memset(stb[i][:], 0.0), writes=[b_stb[i]])
        HB = []
        for hh in range(2):
            d_ = {}
            for nm, shp, dt_ in (("eg", [128, 128], F32), ("lp", [128, 128], F32), ("ep", [128, 128], F32), ("em", [128, 128], F32),
                                 ("er", [128, 128], F32), ("qd", [128, 128], BF16), ("kd", [128, 128], BF16), ("kdt", [128, 128], BF16),
                                 ("att", [128, 128], BF16), ("osq", [128, 2, 128], BF16), ("ors", [128, 128], F32)):
                d_[nm] = P.sb("%s%d" % (nm, hh), shp, dt_); d_["b_" + nm] = P.buf()
            d_["o_sb"] = P.sb("o_sb%d" % hh, [128, 2, 128], F32); d_["b_o"] = P.bufs(2)
            HB.append(d_)
        ogt = [P.sb("ogt%d" % i, [128, 4, 128], F32) for i in range(2)]; b_ogt = P.bufs(2)
        b_od = P.buf()
        ringA = Ring(list(zip(C.psums[1:3], C.psum_b[1:3])))
        ringB = Ring(list(zip(C.psums[3:8], C.psum_b[3:8])))
        QS = 1.0 / (128.0 ** 0.5)
        bi = 0
        for ti in range(S // T):
            xt, b_xt = xts[ti % 2], b_xts[ti % 2]
            t0 = ti * T
            P.dma(xt[:], xsrc(t0, T), reads=(xsrc.deps(t0) if hasattr(xsrc, 'deps') else []), writes=[b_xt])
            emit_norm_mod(C, N, xt, b_xt, 0, 8)

            def proj(col0, m, dst_fn, tag):
                ps, bps = ringA.next()
                for kc in range(8):
                    P.op("pe", lambda e, kc=kc, ps=ps: e.matmul(ps[0:m, :T], lhsT=wc[:, kc, col0:col0 + m], rhs=N.hT[:, kc, :],
                                                              start=(kc == 0), stop=(kc == 7)), reads=[b_wc[kc], N.b_hT], writes=[bps])
                dst_fn(ps, bps)
            for h in range(2):
                proj(h * 128, 128, lambda ps, bps, h=h: P.op("act", lambda e: e.activation(out=q_sb[:, h, :], in_=ps[:, :T], func=AF.Identity, scale=QS),
                                                          reads=[bps], writes=[b_q[h]]), "q")
                proj(256 + h * 128, 128, lambda ps, bps, h=h: P.op("dve", lambda e: e.tensor_copy(out=k_sb[:, h, :], in_=ps[:, :T]),
                                                                reads=[bps], writes=[b_kk[h]]), "k")
            for j in range(4):
                proj(1024 + j * 128, 128, lambda ps, bps, j=j: P.op("act", lambda e: e.activation(out=r_sb[:, j, :], in_=ps[:, :T], func=AF.Silu),
                                                                 reads=[bps], writes=[b_r[j]]), "r")
            proj(1536, 16, lambda ps, bps: P.op("dve", lambda e: e.tensor_copy(out=a_sb[:, :], in_=ps[0:16, :T]), reads=[bps], writes=[b_a]), "a")
            for blk in range(T // 128):
                c0 = blk * 128
                og_t, b_og = ogt[bi % 2], b_ogt[bi % 2]
                bi += 1
                ps, bps = ringA.next()
                for kc in range(8):
                    P.op("pe", lambda e, kc=kc, ps=ps, c0=c0: e.matmul(ps[:, 0:256], lhsT=N.hT[:, kc, c0:c0 + 128], rhs=wc[:, kc, 256:512],
                                                                     start=(kc == 0), stop=(kc == 7)), reads=[b_wc[kc], N.b_hT], writes=[bps])
                P.op("dve", lambda e, ps=ps: e.tensor_copy(out=ktok[:], in_=ps[:, 0:256]), reads=[bps], writes=[b_ktok])
                ps, bps = ringA.next()
                for kc in range(8):
                    P.op("pe", lambda e, kc=kc, ps=ps, c0=c0: e.matmul(ps[:, 0:512], lhsT=N.hT[:, kc, c0:c0 + 128], rhs=wc[:, kc, 512:1024],
                                                                     start=(kc == 0), stop=(kc == 7)), reads=[b_wc[kc], N.b_hT], writes=[bps])
                P.op("act", lambda e, ps=ps: e.activation(out=vtok[:], in_=ps[:, 0:512], func=AF.Identity), reads=[bps], writes=[b_vtok])
                def head_gen(h, c0=c0, og_t=og_t, b_og=b_og):
                    hb = HB[h]
                    eg, lp, ep, em, er, qd, kd, kdt, att, osq, ors, o_sb = (hb[k] for k in ('eg','lp','ep','em','er','qd','kd','kdt','att','osq','ors','o_sb'))
                    b_eg, b_lp, b_ep, b_em, b_er, b_qd, b_kd, b_kdt, b_att, b_osq, b_ors, b_o = (hb['b_' + k] for k in ('eg','lp','ep','em','er','qd','kd','kdt','att','osq','ors','o'))
                    S_, bS = st[h], b_st[h]
                    Sb, bSb = stb[h], b_stb[h]
                    pg, bpg = ringB.next()
                    P.op("pe", lambda e, pg=pg, c0=c0, h=h: e.matmul(pg[:, 0:128], lhsT=a_sb[:, c0:c0 + 128], rhs=wg[:, h * 128:(h + 1) * 128],
                                                                   start=True, stop=False), reads=[b_a, b_k], writes=[bpg])
                    yield
                    P.op("pe", lambda e, pg=pg, h=h: e.matmul(pg[:, 0:128], lhsT=C.ones_f[0:1, :], rhs=bg[0:1, h * 128:(h + 1) * 128],
                                                            start=False, stop=True), reads=[C.cb, b_k], writes=[bpg])
                    yield
                    P.op("act", lambda e, pg=pg: e.activation(out=eg[:], in_=pg[:, 0:128], func=AF.Exp, scale=-1.0), reads=[bpg], writes=[b_eg])
                    yield
                    P.op("act", lambda e: e.activation(out=lp[:], in_=eg[:], func=AF.Ln, bias=C.ones_f[:, 0:1], scale=1.0),
                         reads=[b_eg, C.cb], writes=[b_lp])
                    yield
                    pb, bpb = ringB.next()
                    P.op("pe", lambda e, pb=pb: e.matmul(pb[:, 0:128], lhsT=lp[:], rhs=cst[:, 0, :], start=True, stop=True),
                         reads=[b_lp, b_k], writes=[bpb])
                    yield
                    pr_, bpr = ringB.next()
                    P.op("pe", lambda e, pr_=pr_: e.matmul(pr_[:, 0:128], lhsT=cst[:, 1, :], rhs=lp[:], start=True, stop=True),
                         reads=[b_lp, b_k], writes=[bpr])
                    yield
                    P.op("act", lambda e, pb=pb: e.activation(out=ep[:], in_=pb[:, 0:128], func=AF.Exp), reads=[bpb], writes=[b_ep])
                    yield
                    P.op("act", lambda e, pb=pb: e.activation(out=em[:], in_=pb[:, 0:128], func=AF.Exp, scale=-1.0), reads=[bpb], writes=[b_em])
                    yield
                    P.op("act", lambda e, pr_=pr_: e.activation(out=er[:], in_=pr_[:, 0:128], func=AF.Exp), reads=[bpr], writes=[b_er])
                    yield
                    P.op("dve", lambda e, h=h, c0=c0: e.tensor_tensor(out=qd[:], in0=q_sb[:, h, c0:c0 + 128], in1=ep[:], op=ALU.mult),
                         reads=[b_q[h], b_ep], writes=[b_qd])
                    yield
                    P.op("pool", lambda e, h=h, c0=c0: e.tensor_tensor(out=kd[:], in0=k_sb[:, h, c0:c0 + 128], in1=em[:], op=ALU.mult),
                         reads=[b_kk[h], b_em], writes=[b_kd])
                    yield
                    P.op("dve", lambda e, h=h: e.tensor_tensor(out=kdt[:], in0=ktok[:, h * 128:(h + 1) * 128], in1=er[:], op=ALU.mult),
                         reads=[b_ktok, b_er], writes=[b_kdt])
                    yield
                    pa, bpa = ringB.next()
                    P.op("pe", lambda e, pa=pa: e.matmul(pa[:, 0:128], lhsT=kd[:], rhs=qd[:], start=True, stop=True),
                         reads=[b_kd, b_qd], writes=[bpa])
                    yield
                    P.op("dve", lambda e, pa=pa: e.tensor_tensor(out=att[:], in0=pa[:, 0:128], in1=cst[:, 2, :], op=ALU.mult),
                         reads=[bpa, b_k], writes=[b_att])
                    yield
                    for ch in range(2):
                        r0 = ch * 64
                        po, bpo = ringB.next()
                        for vc in range(2):
                            P.op("pe", lambda e, po=po, vc=vc, Sb=Sb, r0=r0: e.matmul(
                                po[:, vc * 64:(vc + 1) * 64], lhsT=Sb[:, vc * 128:(vc + 1) * 128], rhs=qd[:, r0:r0 + 64], start=True, stop=False),
                                reads=[bSb, b_qd], writes=[bpo])
                            yield
                            P.op("pe", lambda e, po=po, vc=vc, h=h, r0=r0: e.matmul(
                                po[:, vc * 64:(vc + 1) * 64], lhsT=vtok[r0:r0 + 64, h * 256 + vc * 128:h * 256 + (vc + 1) * 128],
                                rhs=att[r0:r0 + 64, r0:r0 + 64], start=False, stop=True),
                                reads=[b_vtok, b_att], writes=[bpo])
                            yield
                        P.op("act", lambda e, po=po, r0=r0: e.activation(
                            out=o_sb[:, :, r0:r0 + 64], in_=po[:, 0:128].rearrange("p (v t) -> p v t", v=2), func=AF.Identity),
                            reads=[bpo], writes=[b_o[ch]])
                        yield
                        pu, bpu = ringB.next()
                        P.op("pe", lambda e, pu=pu, h=h, r0=r0: e.matmul(
                            pu[:, 0:256], lhsT=kdt[r0:r0 + 64, :], rhs=vtok[r0:r0 + 64, h * 256:(h + 1) * 256], start=True, stop=True),
                            reads=[b_kdt, b_vtok], writes=[bpu])
                        yield
                        P.op("dve", lambda e, pu=pu, S_=S_, r0=r0: e.scalar_tensor_tensor(
                            out=S_[:], in0=S_[:], scalar=ep[:, r0 + 63:r0 + 64], in1=pu[:, 0:256], op0=ALU.mult, op1=ALU.add),
                            reads=[bpu, bS, b_ep], writes=[bS])
                        yield
                        P.op("pool", lambda e, S_=S_, Sb=Sb: e.tensor_copy(out=Sb[:], in_=S_[:]), reads=[bS], writes=[bSb])
                        yield
                    P.op("act", lambda e: e.activation(out=osq[:], in_=o_sb[:], func=AF.Square), reads=b_o, writes=[b_osq])
                    yield
                    pn, bpn = ringB.next()
                    for vc in range(2):
                        P.op("pe", lambda e, pn=pn, vc=vc: e.matmul(pn[:, 0:128], lhsT=C.ones_bf[:], rhs=osq[:, vc, :], start=(vc == 0), stop=(vc == 1)),
                             reads=[b_osq, C.cb], writes=[bpn])
                        yield
                    P.op("act", lambda e, pn=pn: e.activation(out=ors[:], in_=pn[:, 0:128], func=AF.Sqrt, bias=C.eps[:, 0:1], scale=1.0 / 256.0),
                         reads=[bpn, C.cb], writes=[b_ors])
                    yield
                    P.op("dve", lambda e: e.reciprocal(out=ors[:], in_=ors[:]), reads=[b_ors], writes=[b_ors])
                    yield
                    for vc in range(2):
                        P.op("dve", lambda e, vc=vc, h=h, og_t=og_t: e.scalar_tensor_tensor(
                            out=og_t[:, h * 2 + vc, :], in0=o_sb[:, vc, :], scalar=og[:, vc:vc + 1], in1=ors[:], op0=ALU.mult, op1=ALU.mult),
                            reads=b_o + [b_ors, b_k], writes=[b_og])
                        yield
                        P.op("pool", lambda e, vc=vc, h=h, og_t=og_t, c0=c0: e.tensor_tensor(
                            out=og_t[:, h * 2 + vc, :], in0=og_t[:, h * 2 + vc, :], in1=r_sb[:, h * 2 + vc, c0:c0 + 128], op=ALU.mult),
                            reads=[b_og, b_r[h * 2 + vc]], writes=[b_og])
                        yield

                gens = [head_gen(0), head_gen(1)]
                while gens:
                    for g_ in list(gens):
                        try:
                            next(g_)
                        except StopIteration:
                            gens.remove(g_)
                tb = t0 + c0
                P.dma(odst(tb, 128), og_t[:], reads=[b_og], writes=[b_od])


def build_L2a(S=8192):
    nc = bass.Bass("TRN2", target_bir_lowering=False)
    xT_d = nc.dram_tensor("xT", [1024, S], F32, kind="ExternalInput").ap()
    cm = _decl_common(nc)
    wc_d = nc.dram_tensor("c_wc", [1024, 1552], F32, kind="ExternalInput").ap()
    wg_d = nc.dram_tensor("c_wg", [16, 256], F32, kind="ExternalInput").ap()
    bg_d = nc.dram_tensor("c_bg", [1, 256], F32, kind="ExternalInput").ap()
    og_d = nc.dram_tensor("c_og", [128, 2], F32, kind="ExternalInput").ap()
    cst_d = nc.dram_tensor("c_cst", [128, 3, 128], F32, kind="ExternalInput").ap()
    oT_d = nc.dram_tensor("oT", [512, S], F32, kind="ExternalOutput").ap()
    C = make_ctx(nc)
    emit_ada(C, *cm)
    emit_mixC(C, xT_d, wc_d, wg_d, bg_d, og_d, cst_d, oT_d, S)
    C.P.emit()
    return nc


def _c_consts():
    s_ = np.arange(128)[:, None]
    t_ = np.arange(128)[None, :]
    same = (s_ // 64) == (t_ // 64)
    tric = np.where(same & (s_ <= t_), -1.0 / 16.0, 0.0).astype(np.float32)
    trir = np.where(same & (s_ > t_), -1.0 / 16.0, 0.0).astype(np.float32)
    mc = (same & (s_ <= t_)).astype(np.float32)
    return np.ascontiguousarray(np.stack([tric, trir, mc], 1))


def lay_L2a(d, b, hp, xT_full):
    m = _lay_common(d, 2, b)
    w = d["c_w_in"][0]
    h0 = hp * 2
    m.update({
        "xT": xT_full,
        "c_wc": np.ascontiguousarray(np.concatenate([
            w[:, h0 * 128:(h0 + 2) * 128], w[:, 512 + h0 * 128:512 + (h0 + 2) * 128],
            w[:, 1024 + h0 * 256:1024 + (h0 + 2) * 256], w[:, 2048 + h0 * 256:2048 + (h0 + 2) * 256], w[:, 3072:3088]], 1)),
        "c_wg": np.ascontiguousarray(d["c_w_gate_up"][0][:, h0 * 128:(h0 + 2) * 128]),
        "c_bg": np.ascontiguousarray(d["c_b_gate"][0][h0 * 128:(h0 + 2) * 128].reshape(1, 256)),
        "c_og": np.ascontiguousarray(d["c_o_gain"][0].reshape(2, 128).T),
        "c_cst": _c_consts(),
    })
    return m


PAIRS = [[0, 1], [2, 3], [4, 5], [6, 7]]


def build_fused(H=4096, PAIRS=PAIRS):
    S = 2 * H
    nc = bass.Bass("TRN2", target_bir_lowering=False)
    TT = H + HALO

    def din(name, shape):
        return nc.dram_tensor(name, list(shape), F32, kind="ExternalInput").ap()
    xT_d = din("xT", [1024, TT])
    cT_d = din("cT", [128, 8])
    adaw = [din("ada_w%d" % L, [1024, 6144]) for L in range(4)]
    adab = [din("ada_b%d" % L, [128, 48]) for L in range(4)]
    gains = [din("gains%d" % L, [128, 16]) for L in range(4)]
    w1 = [din("w1_%d" % L, [1024, 4096]) for L in range(4)]
    w2 = [din("w2_%d" % L, [4096, 1024]) for L in range(4)]
    a_w_in = din("a_w_in", [1024, 3072]); a_gqk = din("a_gqk", [128, 2]); a_biasT = din("a_biasT", [128, 16, 640])
    a_valid = din("a_valid", [128, TT // 128]); a_w_out = din("a_w_out", [1024, 1024])
    b_w_in = din("b_w_in", [1024, 6144]); b_vgain = din("b_vgain", [128, 3072]); b_wsT = din("b_wsT", [128, 8, 128])
    b_bs = din("b_bs", [128, 24, 128]); b_w_out = din("b_w_out", [3072, 1024])
    c_wc = din("c_wc", [1024, 1552]); c_wg = din("c_wg", [16, 256]); c_bg = din("c_bg", [1, 256]); c_og = din("c_og", [128, 2])
    c_cst = din("c_cst", [128, 3, 128]); c_w_out = din("c_w_out", [1024, 1024])
    d_wqkv = din("d_wqkv", [1024, 1536]); d_cst = din("d_cst", [128, 3, 128]); d_w_out = din("d_w_out", [1024, 1024])
    sel_d = din("sel", [128, 2])
    yT_d = nc.dram_tensor("yT", [1024, H], F32, kind="ExternalOutput").ap()

    def scr(name, shape, dt=F32):
        return nc.dram_tensor(name, list(shape), dt).ap()
    qT_s = scr("a_qT_s", [1024, H], BF16); kT_s = scr("a_kT_s", [1024, TT], BF16); v_s = scr("a_v_s", [TT, 2048], BF16)
    vm_s = scr("b_vm_s", [24, 128, H], BF16)
    xs = [scr("xs%d" % i, [1024, H]) for i in range(4)]
    xa = scr("xa", [1024, H])
    XC = min(512, H)
    OC = min(1024, S)
    xb_c = [scr("xb_c%d" % i, [1024, XC]) for i in range(H // XC)]; xb_g = [scr("xb_g%d" % i, [2048, XC]) for i in range(H // XC)]
    xc_c = [scr("xc_c%d" % i, [1024, XC]) for i in range(H // XC)]; xc_g = [scr("xc_g%d" % i, [2048, XC]) for i in range(H // XC)]
    oc_c = [scr("oc_c%d" % i, [512, OC]) for i in range(S // OC)]; oc_g = [scr("oc_g%d" % i, [1024, OC]) for i in range(S // OC)]
    od_c = [scr("od_c%d" % i, [512, OC]) for i in range(S // OC)]; od_g = [scr("od_g%d" % i, [1024, OC]) for i in range(S // OC)]
    dq_s = scr("d_qT_s", [512, S], BF16); dk_s = scr("d_kT_s", [512, S], BF16); dv_s = scr("d_v_s", [S, 1024], BF16)

    C = make_ctx(nc)
    P = C.P
    selt = P.sb("selt", [128, 2], F32); b_sel = P.buf()
    P.dma(selt[:], sel_d, writes=[b_sel])

    def x_own(chunks):
        def f(t0, T):
            return chunks[t0 // XC][:, t0 % XC:t0 % XC + T].rearrange("(kc p) t -> p kc t", p=128)
        return f

    GB = {}

    def x_gath(chunks):
        def f(t0, T):
            r, tl = t0 // H, t0 % H
            return chunks[tl // XC][r * 1024:(r + 1) * 1024, tl % XC:tl % XC + T].rearrange("(kc p) t -> p kc t", p=128)
        f.deps = lambda t0: [GB[id(chunks)][(t0 % H) // XC]]
        return f

    def o_dst(chunks):
        def f(t0, n):
            return chunks[t0 // OC].rearrange("(oc p) t -> p oc t", p=128)[:, :, t0 % OC:t0 % OC + n]
        return f

    def o_gath(chunks):
        def f(h, t0, T):
            g = h * H + t0
            return chunks[g // OC].rearrange("(kc p) t -> p kc t", p=128)[:, :, g % OC:g % OC + T]
        f.deps = lambda h, t0: [GB[id(chunks)][(h * H + t0) // OC]]
        return f

    def gather(src, dst):
        bufs = P.bufs(len(src))
        GB[id(dst)] = bufs
        for a_, b_, gb in zip(src, dst, bufs):
            P.collective("AllGather", [a_], [b_], PAIRS, writes=[gb])
    emit_ada(C, cT_d, adaw[0], adab[0], gains[0])
    emit_mixA1(C, xT_d, a_w_in, a_gqk, qT_s, kT_s, v_s, H)
    emit_mixA2(C, xT_d, xs[0], qT_s, kT_s, v_s, a_biasT, a_valid, a_w_out, H)
    emit_ffn(C, xs[0], xa, w1[0], w2[0], H)
    emit_ada(C, cT_d, adaw[1], adab[1], gains[1])
    emit_mixB(C, xa, xs[1], b_w_in, b_vgain, b_wsT, b_bs, b_w_out, vm_s, H)
    emit_ffn(C, xs[1], None, w1[1], w2[1], H, ydst=x_own(xb_c))
    gather(xb_c, xb_g)
    emit_ada(C, cT_d, adaw[2], adab[2], gains[2])
    emit_mixC(C, None, c_wc, c_wg, c_bg, c_og, c_cst, None, S, xsrc=x_gath(xb_g), odst=o_dst(oc_c))
    gather(oc_c, oc_g)
    emit_outproj(C, None, None, c_w_out, xs[2], H, sel=(selt, b_sel), xsrc=x_own(xb_c), osrc=o_gath(oc_g))
    emit_ffn(C, xs[2], None, w1[2], w2[2], H, ydst=x_own(xc_c))
    gather(xc_c, xc_g)
    emit_ada(C, cT_d, adaw[3], adab[3], gains[3])
    emit_mixD1(C, None, d_wqkv, dq_s, dk_s, dv_s, S, xsrc=x_gath(xc_g))
    emit_mixD2(C, dq_s, dk_s, dv_s, d_cst, None, S, odst=o_dst(od_c))
    gather(od_c, od_g)
    emit_outproj(C, None, None, d_w_out, xs[3], H, sel=(selt, b_sel), xsrc=x_own(xc_c), osrc=o_gath(od_g))
    emit_ffn(C, xs[3], yT_d, w1[3], w2[3], H)
    P.emit()
    return nc


def lay_fused(d, b, hf, H=4096):
    m = {}
    l0 = lay_L0(d, b, hf, H)
    for k in ("xT", "cT", "a_w_in", "a_gqk", "a_biasT", "a_valid", "a_w_out"):
        m[k] = l0[k]
    for L in range(4):
        c = _lay_common(d, L, b)
        m["ada_w%d" % L] = c["ada_w"]; m["ada_b%d" % L] = c["ada_b"]; m["gains%d" % L] = c["gains"]
        m["w1_%d" % L] = np.ascontiguousarray(d["ffn_w1"][L]); m["w2_%d" % L] = np.ascontiguousarray(d["ffn_w2"][L])
    l1 = lay_L1(d, b)
    for k in ("b_w_in", "b_vgain", "b_wsT", "b_bs", "b_w_out"):
        m[k] = l1[k]
    l2 = lay_L2a(d, b, hf, None)
    for k in ("c_wc", "c_wg", "c_bg", "c_og", "c_cst"):
        m[k] = l2[k]
    m["c_w_out"] = np.ascontiguousarray(d["c_w_out"][0])
    l3 = lay_L3a(d, b, hf, None)
    for k in ("d_wqkv", "d_cst"):
        m[k] = l3[k]
    m["d_w_out"] = np.ascontiguousarray(d["d_w_out"][0])
    sel = np.zeros((128, 2), np.float32); sel[:, hf] = 1.0
    m["sel"] = sel
    return m


_PROGS = {}


def _prog(name, builder):
    if name not in _PROGS:
        _PROGS[name] = builder()
    return _PROGS[name]


def _run(nc, in_maps):
    res = run_bass_kernel_spmd(nc, in_maps, core_ids=list(range(len(in_maps))))
    return res.results


def kernel(**inputs):
    d = {k: np.asarray(v, dtype=np.float32) for k, v in inputs.items()}
    B, S, D = d["x"].shape
    H = S // 2
    cores = [(b, hf) for b in range(B) for hf in range(2)]
    nc = _prog("fused", lambda: build_fused(H))
    r = _run(nc, [lay_fused(d, b, hf, H) for (b, hf) in cores])
    out = np.empty((B, S, D), np.float32)
    for b in range(B):
        out[b, :H] = r[2 * b]["yT"].T
        out[b, H:] = r[2 * b + 1]["yT"].T
    return out


def kernel_unfused(**inputs):
    d = {k: np.asarray(v, dtype=np.float32) for k, v in inputs.items()}
    B, S, D = d["x"].shape
    H = S // 2
    cores = [(b, hf) for b in range(B) for hf in range(2)]
    nc = _prog("L0", lambda: build_L0(H))
    r = _run(nc, [lay_L0(d, b, hf, H) for (b, hf) in cores])
    xT = [np.concatenate([r[2 * b]["yT"], r[2 * b + 1]["yT"]], axis=1) for b in range(B)]
    nc = _prog("L1", lambda: build_L1(H))
    ins = []
    for (b, hf) in cores:
        m = lay_L1(d, b)
        m["xT"] = np.ascontiguousarray(xT[b][:, hf * H:(hf + 1) * H])
        ins.append(m)
    r = _run(nc, ins)
    xT = [np.concatenate([r[2 * b]["yT"], r[2 * b + 1]["yT"]], axis=1) for b in range(B)]
    nc = _prog("L2a", lambda: build_L2a(S))
    r = _run(nc, [lay_L2a(d, b, hp, xT[b]) for (b, hp) in cores])
    oT = [np.concatenate([r[2 * b]["oT"], r[2 * b + 1]["oT"]], axis=0) for b in range(B)]
    nc = _prog("Lb", lambda: build_Lb(H))
    ins = []
    for (b, hf) in cores:
        m = lay_Lb(d, 2, b, d["c_w_out"][0])
        m["xT"] = np.ascontiguousarray(xT[b][:, hf * H:(hf + 1) * H])
        m["oT"] = np.ascontiguousarray(oT[b][:, hf * H:(hf + 1) * H])
        ins.append(m)
    r = _run(nc, ins)
    xT = [np.concatenate([r[2 * b]["yT"], r[2 * b + 1]["yT"]], axis=1) for b in range(B)]
    nc = _prog("L3a", lambda: build_L3a(S))
    r = _run(nc, [lay_L3a(d, b, hg, xT[b]) for (b, hg) in cores])
    oT = [np.concatenate([r[2 * b]["oT"], r[2 * b + 1]["oT"]], axis=0) for b in range(B)]
    nc = _prog("Lb", lambda: build_Lb(H))
    ins = []
    for (b, hf) in cores:
        m = lay_Lb(d, 3, b, d["d_w_out"][0])
        m["xT"] = np.ascontiguousarray(xT[b][:, hf * H:(hf + 1) * H])
        m["oT"] = np.ascontiguousarray(oT[b][:, hf * H:(hf + 1) * H])
        ins.append(m)
    r = _run(nc, ins)
    out = np.empty((B, S, D), np.float32)
    for b in range(B):
        out[b, :H] = r[2 * b]["yT"].T
        out[b, H:] = r[2 * b + 1]["yT"].T
    return out
```

```python
import numpy as np
import concourse.bass as bass
import concourse.mybir as mybir
from concourse.bass_utils import run_bass_kernel_spmd
from contextlib import ExitStack, contextmanager

F32 = mybir.dt.float32
BF16 = mybir.dt.bfloat16
AF = mybir.ActivationFunctionType
ALU = mybir.AluOpType
AX = mybir.AxisListType
EPS = 1e-6
DEBUG = False
SERIAL = False
NSTAGE = 2
STAGE_ELEMS = 2048
ENGS = ("pe", "act", "dve", "pool", "sp")


class Buf:
    __slots__ = ("name", "w", "r")

    def __init__(self, name):
        self.name = name
        self.w = None
        self.r = []


class Ring:
    def __init__(self, items):
        self.items = items
        self.i = 0

    def next(self):
        it = self.items[self.i % len(self.items)]
        self.i += 1
        return it


class Prog:
    NDMA = 24

    def __init__(self, nc):
        self.nc = nc
        self.es = ExitStack()
        self.scopes = [self.es]
        self.streams = {e: [] for e in ENGS}
        self.esem = {e: self.es.enter_context(nc.semaphore("s_" + e)) for e in ENGS}
        self.ecount = {e: 0 for e in ENGS}
        self.dsem = [self.es.enter_context(nc.semaphore("d%d" % i)) for i in range(self.NDMA)]
        self.dcount = [0] * self.NDMA
        self.dnext = 0
        self.semobj = {}
        for e in ENGS:
            self.semobj[("e", e)] = self.esem[e]
        for i in range(self.NDMA):
            self.semobj[("d", i)] = self.dsem[i]
        self.waited = {e: {} for e in ENGS}
        self.nbuf = 0
        self.uid = 0

    @contextmanager
    def scope(self):
        st = ExitStack()
        self.scopes.append(st)
        try:
            yield
        finally:
            self.barrier()
            self.scopes.pop()
            st.close()

    def sb(self, name, shape, dt):
        self.uid += 1
        return self.scopes[-1].enter_context(self.nc.sbuf_tensor("%s_%d" % (name, self.uid), list(shape), dt))

    def ps(self, name, shape, dt=F32):
        return self.scopes[-1].enter_context(self.nc.psum_tensor(name, list(shape), dt))

    def buf(self, name=None):
        self.nbuf += 1
        return Buf(name or ("b%d" % self.nbuf))

    def bufs(self, n):
        return [self.buf() for _ in range(n)]

    def _deps(self, eng, reads, writes):
        deps = {}

        def add(d):
            if d is None:
                return
            k, v = d
            if deps.get(k, 0) < v:
                deps[k] = v
        for b in reads:
            add(b.w)
        for b in writes:
            add(b.w)
            for d in b.r:
                add(d)
        waits = []
        wd = self.waited[eng]
        for k, v in deps.items():
            if wd.get(k, 0) >= v:
                continue
            if eng == "pe" and k == ("e", "pe"):
                continue
            wd[k] = v
            waits.append((self.semobj[k], v))
        return waits

    def _mark(self, tok, reads, writes):
        for b in writes:
            b.w = tok
            b.r = []
        for b in reads:
            if b not in writes:
                b.r.append(tok)
                if len(b.r) > 64:
                    best = {}
                    for k, v in b.r:
                        if best.get(k, 0) < v:
                            best[k] = v
                    b.r = list(best.items())

    def _serial_waits(self, eng):
        waits = []
        for e2 in ENGS:
            k = ("e", e2); v = self.ecount[e2]
            if v > 0 and self.waited[eng].get(k, 0) < v:
                self.waited[eng][k] = v
                waits.append((self.esem[e2], v))
        for i in range(self.NDMA):
            k = ("d", i); v = self.dcount[i]
            if v > 0 and self.waited[eng].get(k, 0) < v:
                self.waited[eng][k] = v
                waits.append((self.dsem[i], v))
        return waits

    def op(self, eng, fn, reads=(), writes=()):
        waits = self._deps(eng, reads, writes)
        if SERIAL:
            waits = waits + self._serial_waits(eng)
        self.ecount[eng] += 1
        tok = (("e", eng), self.ecount[eng])
        self.streams[eng].append((waits, fn, self.esem[eng], 1))
        self._mark(tok, reads, writes)
        return tok

    def dma(self, out, in_, reads=(), writes=(), eng="sp"):
        i = self.dnext
        self.dnext = (self.dnext + 1) % self.NDMA
        waits = self._deps(eng, reads, writes)
        if SERIAL:
            waits = waits + self._serial_waits(eng)
        k = ("d", i)
        if self.dcount[i] > 0 and self.waited[eng].get(k, 0) < self.dcount[i]:
            self.waited[eng][k] = self.dcount[i]
            waits.append((self.dsem[i], self.dcount[i]))
        self.dcount[i] += 16
        tok = (k, self.dcount[i])
        self.streams[eng].append((waits, lambda e: e.dma_start(out=out, in_=in_), self.dsem[i], 16))
        self._mark(tok, reads, writes)
        return tok

    def collective(self, kind, ins, outs, groups, reads=(), writes=()):
        eng = "pool"
        waits = self._deps(eng, reads, writes)
        sem = self.es.enter_context(self.nc.semaphore("cc%d" % len(self.semobj)))
        k = ("c", len(self.semobj))
        self.semobj[k] = sem
        tok = (k, 1)
        self.streams[eng].append((waits, lambda e: e.collective_compute(kind, ALU.bypass, groups, [a.opt() for a in ins], [a.opt() for a in outs]), sem, 1))
        self._mark(tok, reads, writes)
        return tok

    def dump(self, name, ap, shape, dt, reads):
        if not DEBUG:
            return
        d = self.nc.dram_tensor(name, list(shape), dt, kind="ExternalOutput").ap()
        self.dma(d, ap, reads=reads, writes=[self.buf()])

    def barrier(self):
        for eng in ENGS:
            waits = []
            for k, sem in self.semobj.items():
                if k[0] == "c" and self.waited[eng].get(k, 0) < 1:
                    self.waited[eng][k] = 1
                    waits.append((sem, 1))
            for e2 in ENGS:
                k = ("e", e2)
                v = self.ecount[e2]
                if e2 != eng and v > 0 and self.waited[eng].get(k, 0) < v:
                    self.waited[eng][k] = v
                    waits.append((self.esem[e2], v))
            for i in range(self.NDMA):
                k = ("d", i)
                v = self.dcount[i]
                if v > 0 and self.waited[eng].get(k, 0) < v:
                    self.waited[eng][k] = v
                    waits.append((self.dsem[i], v))
            k = ("e", eng)
            v = self.ecount[eng]
            if v > 0 and self.waited[eng].get(k, 0) < v:
                self.waited[eng][k] = v
                waits.append((self.esem[eng], v))
            if waits:
                self.streams[eng].append((waits, None, None, 0))

    def emit(self):
        nc = self.nc
        self.barrier()
        with nc.Block() as block:
            def runner(name):
                def f(e):
                    for waits, fn, sem, inc in self.streams[name]:
                        for s, v in waits:
                            e.wait_ge(s, v)
                        if fn is not None:
                            fn(e).then_inc(sem, inc)
                return f
            block.tensor(runner("pe"))
            block.scalar(runner("act"))
            block.vector(runner("dve"))
            block.gpsimd(runner("pool"))
            block.sync(runner("sp"))
        self.es.close()


class Ctx:
    pass


def make_ctx(nc):
    P = Prog(nc)
    C = Ctx()
    C.P = P
    C.nc = nc
    C.psbig = P.ps("psbig", [128, 4096])
    C.psums = [C.psbig[:, i * 512:(i + 1) * 512] for i in range(8)]
    C.psum_b = P.bufs(8)
    C.ones_bf = P.sb("ones_bf", [128, 128], BF16)
    C.ones_f = P.sb("ones_f", [128, 128], F32)
    C.eps = P.sb("eps_t", [128, 1], F32)
    C.cb = P.buf()

    def mk(e):
        e.memset(C.eps[:], EPS)
        e.memset(C.ones_f[:], 1.0)
        return e.memset(C.ones_bf[:], 1.0)
    P.op("pool", mk, writes=[C.cb])
    C.stage = [P.sb("stage%d" % i, [128, STAGE_ELEMS], F32) for i in range(NSTAGE)]
    C.stage_b = P.bufs(NSTAGE)
    C.sti = 0
    return C


def next_stage(C):
    i = C.sti % len(C.stage)
    C.sti += 1
    return C.stage[i], C.stage_b[i]


def load_cast(C, dst_ap, dst_buf, src_ap, shape3):
    P = C.P
    n = 1
    for s in shape3:
        n *= s
    if n > STAGE_ELEMS:
        h = shape3[0] // 2
        if len(shape3) == 1:
            load_cast(C, dst_ap[:, 0:h], dst_buf, src_ap[:, 0:h], [h])
            load_cast(C, dst_ap[:, h:2 * h], dst_buf, src_ap[:, h:2 * h], [h])
        else:
            load_cast(C, dst_ap[:, 0:h, :], dst_buf, src_ap[:, 0:h, :], [h, shape3[1]])
            load_cast(C, dst_ap[:, h:2 * h, :], dst_buf, src_ap[:, h:2 * h, :], [h, shape3[1]])
        return
    st, sb_ = next_stage(C)
    if len(shape3) == 2:
        stv = st[:, 0:n].rearrange("p (a n) -> p a n", a=shape3[0])
    else:
        stv = st[:, 0:n]
    P.dma(stv, src_ap, writes=[sb_])
    eng = "dve" if C.sti % 2 == 0 else "pool"
    P.op(eng, lambda e: e.tensor_copy(out=dst_ap, in_=stv), reads=[sb_], writes=[dst_buf])


def emit_ada(C, cT_d, ada_w_d, ada_b_d, gains_d):
    P = C.P
    psum, psum_b = C.psums[0], C.psum_b[0]
    ct = P.sb("ct", [128, 8], F32); cond = P.sb("cond", [128, 8], F32)
    adab = P.sb("adab", [128, 48], F32); gn = P.sb("gn", [128, 16], F32)
    mod = P.sb("mod", [128, 48], F32)
    out = P.sb("AB", [128, 48], F32)
    b_ct, b_cond, b_adab, b_gn, b_mod, b_out = P.bufs(6)
    P.dma(ct[:], cT_d, writes=[b_ct])
    P.dma(adab[:], ada_b_d, writes=[b_adab])
    P.dma(gn[:], gains_d, writes=[b_gn])
    P.op("act", lambda e: e.activation(out=cond[:], in_=ct[:], func=AF.Silu), reads=[b_ct], writes=[b_cond])
    wv = ada_w_d.rearrange("(kc p) f -> p kc f", p=128)
    for g in range(24):
        st, sb_ = next_stage(C)
        stv = st[:, 0:2048].rearrange("p (kc f) -> p kc f", kc=8)
        P.dma(stv, wv[:, :, g * 256:(g + 1) * 256], writes=[sb_])
        for j in range(2):
            col = g * 2 + j
            for kc in range(8):
                P.op("pe", (lambda e, col=col, kc=kc, j=j, stv=stv: e.matmul(
                    psum[:, col:col + 1], lhsT=stv[:, kc, j * 128:(j + 1) * 128],
                    rhs=cond[:, kc:kc + 1], start=(kc == 0), stop=(kc == 7))),
                    reads=[sb_, b_cond], writes=[psum_b])
    P.op("dve", lambda e: e.tensor_tensor(out=mod[:], in0=psum[:, 0:48], in1=adab[:], op=ALU.add),
         reads=[psum_b, b_adab], writes=[b_mod])

    def mk(e):
        e.scalar_tensor_tensor(out=out[:, 0:8], in0=mod[:, 8:16], scalar=1.0, in1=gn[:, 0:8], op0=ALU.add, op1=ALU.mult)
        e.scalar_tensor_tensor(out=out[:, 24:32], in0=mod[:, 32:40], scalar=1.0, in1=gn[:, 8:16], op0=ALU.add, op1=ALU.mult)
        e.tensor_copy(out=out[:, 8:16], in_=mod[:, 0:8])
        e.tensor_copy(out=out[:, 16:24], in_=mod[:, 16:24])
        e.tensor_copy(out=out[:, 32:40], in_=mod[:, 24:32])
        return e.tensor_copy(out=out[:, 40:48], in_=mod[:, 40:48])
    P.op("dve", mk, reads=[b_mod, b_gn], writes=[b_out])
    C.AB, C.b_ab = out, b_out


class NormBufs:
    def __init__(self, C, T):
        P = C.P
        self.T = T
        self.sq = P.sb("sq", [128, 8, T], BF16); self.b_sq = P.buf()
        self.rstd = P.sb("rstd", [128, T], F32); self.b_rstd = P.buf()
        self.tmp = P.sb("tmp", [128, 4, T], F32); self.b_tmp = P.bufs(4)
        self.hT = P.sb("hT", [128, 8, T], BF16); self.b_hT = P.buf()


def emit_norm_mod(C, N, xt, b_xt, acol, bcol):
    P = C.P
    T = N.T
    ps, b_ps = C.psums[0], C.psum_b[0]
    A = C.AB[:, acol:acol + 8]
    Bc = C.AB[:, bcol:bcol + 8]
    P.op("act", lambda e: e.activation(out=N.sq[:], in_=xt[:], func=AF.Square), reads=[b_xt], writes=[N.b_sq])
    for kc in range(8):
        P.op("pe", lambda e, kc=kc: e.matmul(ps[:, :T], lhsT=C.ones_bf[:], rhs=N.sq[:, kc, :], start=(kc == 0), stop=(kc == 7)),
             reads=[N.b_sq, C.cb], writes=[b_ps])
    P.op("act", lambda e: e.activation(out=N.rstd[:], in_=ps[:, :T], func=AF.Sqrt, bias=C.eps[:, 0:1], scale=1.0 / 1024.0),
         reads=[b_ps, C.cb], writes=[N.b_rstd])
    P.op("dve", lambda e: e.reciprocal(out=N.rstd[:], in_=N.rstd[:]), reads=[N.b_rstd], writes=[N.b_rstd])
    for kc in range(8):
        eng = "dve" if kc % 2 == 0 else "pool"
        P.op(eng, lambda e, kc=kc: e.tensor_tensor(out=N.tmp[:, kc % 4, :], in0=xt[:, kc, :], in1=N.rstd[:], op=ALU.mult),
             reads=[b_xt, N.b_rstd], writes=[N.b_tmp[kc % 4]])
        P.op("act", lambda e, kc=kc: e.activation(out=N.hT[:, kc, :], in_=N.tmp[:, kc % 4, :], func=AF.Identity,
                                                   bias=Bc[:, kc:kc + 1], scale=A[:, kc:kc + 1]),
             reads=[N.b_tmp[kc % 4], C.b_ab], writes=[N.b_hT])


def emit_ffn(C, xT_d, yT_d, w1_d, w2_d, T_core, T=256, ydst=None):
    P = C.P
    AB = C.AB
    with P.scope():
        w1b = P.sb("w1b", [128, 8, 4096], BF16); w2b = P.sb("w2b", [128, 32, 1024], BF16)
        b_w1 = P.bufs(8); b_w2 = P.bufs(8)
        for kc in range(8):
            load_cast(C, w1b[:, kc, :], b_w1[kc], w1_d[kc * 128:(kc + 1) * 128, :], [4096])
        w2v = w2_d.rearrange("(hc p) n -> p hc n", p=128)
        for g in range(8):
            load_cast(C, w2b[:, g * 4:(g + 1) * 4, :], b_w2[g], w2v[:, g * 4:(g + 1) * 4, :], [4, 1024])
        xts = [P.sb("xt%d" % i, [128, 8, T], F32) for i in range(2)]
        b_xts = P.bufs(2)
        N = NormBufs(C, T)
        h1T = P.sb("h1T", [128, 32, T], BF16); b_h1 = P.bufs(32)
        rl = [P.sb("rl%d" % i, [128, T], F32) for i in range(2)]; b_rl = P.bufs(2)
        xv = xT_d.rearrange("(kc p) t -> p kc t", p=128)
        if ydst is None:
            yv = yT_d.rearrange("(kc p) t -> p kc t", p=128)
            ydst = lambda t0, T: yv[:, :, t0:t0 + T]
        b_out = P.buf()
        hring = Ring(list(zip(C.psums[1:5], C.psum_b[1:5])))
        yring = Ring(list(zip(C.psums[5:8], C.psum_b[5:8])))
        nt = T_core // T
        P.dma(xts[0][:], xv[:, :, 0:T], writes=[b_xts[0]])
        emit_norm_mod(C, N, xts[0], b_xts[0], 24, 32)
        for ti in range(nt):
            xt, b_xt = xts[ti % 2], b_xts[ti % 2]
            t0 = ti * T
            for hc in range(32):
                ps, bps = hring.next()
                for kc in range(8):
                    P.op("pe", lambda e, kc=kc, hc=hc, ps=ps: e.matmul(
                        ps[:, :T], lhsT=w1b[:, kc, hc * 128:(hc + 1) * 128], rhs=N.hT[:, kc, :],
                        start=(kc == 0), stop=(kc == 7)), reads=[b_w1[kc], N.b_hT], writes=[bps])
                r, br = rl[hc % 2], b_rl[hc % 2]
                P.op("act", lambda e, ps=ps, r=r: e.activation(out=r[:], in_=ps[:, :T], func=AF.Relu), reads=[bps], writes=[br])
                P.op("pool", lambda e, r=r, hc=hc: e.tensor_tensor(out=h1T[:, hc, :], in0=r[:], in1=r[:], op=ALU.mult),
                     reads=[br], writes=[b_h1[hc]])
            if ti + 1 < nt:
                nx, b_nx = xts[(ti + 1) % 2], b_xts[(ti + 1) % 2]
                P.dma(nx[:], xv[:, :, t0 + T:t0 + 2 * T], writes=[b_nx])
                emit_norm_mod(C, N, nx, b_nx, 24, 32)
            for oc in range(8):
                ps, bps = yring.next()
                for hc in range(32):
                    P.op("pe", lambda e, oc=oc, hc=hc, ps=ps: e.matmul(
                        ps[:, :T], lhsT=w2b[:, hc, oc * 128:(oc + 1) * 128], rhs=h1T[:, hc, :],
                        start=(hc == 0), stop=(hc == 31)), reads=[b_w2[hc // 4], b_h1[hc]], writes=[bps])
                P.op("dve", lambda e, oc=oc, ps=ps, xt=xt: e.scalar_tensor_tensor(
                    out=xt[:, oc, :], in0=ps[:, :T], scalar=AB[:, 40 + oc:41 + oc], in1=xt[:, oc, :],
                    op0=ALU.mult, op1=ALU.add), reads=[bps, C.b_ab, b_xt], writes=[b_xt])
            P.dma(ydst(t0, T), xt[:], reads=[b_xt], writes=[b_out])


def _emit_b1_window(C, P, N, w, vg, b_vg, sqv, b_sqv, ssv, b_ssv, vn, b_vn, gainB, b_gain, wvb, b_wv, pring, ti):
    for cc in range(6):
        ps, bps = pring.next()
        for kc in range(8):
            P.op("pe", lambda e, kc=kc, cc=cc, ps=ps: e.matmul(
                ps[:, :], lhsT=N.hT[:, kc, w * 128:(w + 1) * 128], rhs=wvb[:, kc, cc * 512:(cc + 1) * 512],
                start=(kc == 0), stop=(kc == 7)), reads=[b_wv[kc], N.b_hT], writes=[bps])
        P.op("act", lambda e, ps=ps, cc=cc: e.activation(out=vg[:, cc * 512:(cc + 1) * 512], in_=ps[:, :], func=AF.Gelu),
             reads=[bps], writes=[b_vg[cc]])
    P.op("dve", lambda e: e.tensor_tensor(out=sqv[:], in0=vg[:], in1=vg[:], op=ALU.mult), reads=b_vg, writes=[b_sqv])
    P.op("dve", lambda e: e.reduce_sum(out=ssv[:], in_=sqv[:], axis=AX.X), reads=[b_sqv], writes=[b_ssv])
    P.op("act", lambda e: e.activation(out=ssv[:], in_=ssv[:], func=AF.Sqrt, bias=C.eps[:, 0:1], scale=1.0 / 3072.0),
         reads=[b_ssv, C.cb], writes=[b_ssv])
    P.op("dve", lambda e: e.reciprocal(out=ssv[:], in_=ssv[:]), reads=[b_ssv], writes=[b_ssv])
    P.op("dve", lambda e: e.scalar_tensor_tensor(out=vn[:], in0=vg[:], scalar=ssv[:, 0:1], in1=gainB[:],
                                                   op0=ALU.mult, op1=ALU.mult),
         reads=b_vg + [b_ssv, b_gain], writes=[b_vn])


def emit_mixB(C, xT_d, xo_d, w_in_d, vgain_d, wsT_d, bs_d, wout_d, vm_d, T_core, T=256):
    P = C.P
    xv = xT_d.rearrange("(kc p) t -> p kc t", p=128)
    xov = xo_d.rearrange("(kc p) t -> p kc t", p=128)
    vmv = vm_d.rearrange("fc p t -> p fc t")
    winv = w_in_d.rearrange("(kc p) f -> p kc f", p=128)
    with P.scope():
        wvb = P.sb("wvb", [128, 8, 3072], BF16); b_wv = P.bufs(8)
        for kc in range(8):
            load_cast(C, wvb[:, kc, :], b_wv[kc], winv[:, kc, 3072:6144], [3072])
        gainB = P.sb("gainB", [128, 3072], F32); b_gain = P.buf()
        P.dma(gainB[:], vgain_d, writes=[b_gain])
        wsT = P.sb("wsT", [128, 8, 128], BF16); b_ws = P.buf()
        load_cast(C, wsT[:], b_ws, wsT_d, [8, 128])
        P.op("pool", lambda e: e.memset(wsT[64:128, :, 0:64], 0.0), writes=[b_ws])
        bs = P.sb("bs", [128, 24, 128], F32); b_bs = P.buf()
        P.dma(bs[:], bs_d, writes=[b_bs])
        xts = [P.sb("xt%d" % i, [128, 8, T], F32) for i in range(2)]; b_xts = P.bufs(2)
        N = NormBufs(C, T)
        vgs = [P.sb("vg%d" % i, [128, 3072], F32) for i in range(2)]; b_vgs = [P.bufs(6) for _ in range(2)]
        sqv = P.sb("sqv", [128, 3072], F32); b_sqv = P.buf()
        ssvs = [P.sb("ssv%d" % i, [128, 1], F32) for i in range(2)]; b_ssvs = P.bufs(2)
        vns = [P.sb("vn%d" % i, [128, 3072], BF16) for i in range(2)]; b_vns = P.bufs(2)
        vmT = [P.sb("vmT%d" % i, [128, 24, 128], BF16) for i in range(2)]; b_vmT = P.bufs(2)
        pring = Ring(list(zip(C.psums[1:5], C.psum_b[1:5])))
        mring = Ring(list(zip(C.psums[5:8], C.psum_b[5:8])))
        b_vmd = P.buf()
        wi = 0
        for ti in range(T_core // T):
            xt, b_xt = xts[ti % 2], b_xts[ti % 2]
            t0 = ti * T
            P.dma(xt[:], xv[:, :, t0:t0 + T], writes=[b_xt])
            emit_norm_mod(C, N, xt, b_xt, 0, 8)
            for w in range(T // 128):
                vg, b_vg = vgs[wi % 2], b_vgs[wi % 2]
                ssv, b_ssv = ssvs[wi % 2], b_ssvs[wi % 2]
                vn, b_vn = vns[wi % 2], b_vns[wi % 2]
                _emit_b1_window(C, P, N, w, vg, b_vg, sqv, b_sqv, ssv, b_ssv, vn, b_vn, gainB, b_gain, wvb, b_wv, pring, ti)
                vm, bvm = vmT[wi % 2], b_vmT[wi % 2]
                for q in range(6):
                    ps, bps = mring.next()
                    for j in range(4):
                        fc = q * 4 + j
                        g = fc // 3
                        P.op("pe", lambda e, ps=ps, j=j, fc=fc, g=g, vn=vn: e.matmul(
                            ps[:, j * 128:(j + 1) * 128], lhsT=vn[:, fc * 128:(fc + 1) * 128], rhs=wsT[:, g, :],
                            start=True, stop=True), reads=[b_vn, b_ws], writes=[bps])
                    P.op("dve", lambda e, ps=ps, q=q, vm=vm: e.tensor_tensor(
                        out=vm[:, q * 4:(q + 1) * 4, :], in0=ps[:, :].rearrange("p (a n) -> p a n", a=4),
                        in1=bs[:, q * 4:(q + 1) * 4, :], op=ALU.add), reads=[bps, b_bs], writes=[bvm])
                tw = t0 + w * 128
                P.dma(vmv[:, :, tw:tw + 128], vm[:], reads=[bvm], writes=[b_vmd])
                wi += 1
    emit_mixB2(C, xv, xov, vmv, winv, wout_d, T_core, T)


def emit_mixB2(C, xv, xov, vmv, winv, wout_d, T_core, T):
    P = C.P
    AB = C.AB
    with P.scope():
        wub = P.sb("wub", [128, 8, 3072], BF16); b_wu = P.bufs(8)
        for kc in range(8):
            load_cast(C, wub[:, kc, :], b_wu[kc], winv[:, kc, 0:3072], [3072])
        wob = P.sb("wob", [128, 24, 1024], BF16); b_wo = P.bufs(6)
        wov = wout_d.rearrange("(fc p) n -> p fc n", p=128)
        for g in range(6):
            load_cast(C, wob[:, g * 4:(g + 1) * 4, :], b_wo[g], wov[:, g * 4:(g + 1) * 4, :], [4, 1024])
        xts = [P.sb("xt%d" % i, [128, 8, T], F32) for i in range(2)]; b_xts = P.bufs(2)
        N = NormBufs(C, T)
        vmt = P.sb("vmt", [128, 24, T], BF16); b_vmt = P.buf()
        gT = P.sb("gT", [128, 24, T], BF16); b_gT = P.bufs(24)
        ut = [P.sb("ut%d" % i, [128, T], F32) for i in range(2)]; b_ut = P.bufs(2)
        uring = Ring(list(zip(C.psums[1:5], C.psum_b[1:5])))
        yring = Ring(list(zip(C.psums[5:8], C.psum_b[5:8])))
        b_xo = P.buf()
        for ti in range(T_core // T):
            xt, b_xt = xts[ti % 2], b_xts[ti % 2]
            t0 = ti * T
            P.dma(xt[:], xv[:, :, t0:t0 + T], writes=[b_xt])
            P.dma(vmt[:], vmv[:, :, t0:t0 + T], writes=[b_vmt])
            emit_norm_mod(C, N, xt, b_xt, 0, 8)
            for fc in range(24):
                ps, bps = uring.next()
                for kc in range(8):
                    P.op("pe", lambda e, kc=kc, fc=fc, ps=ps: e.matmul(
                        ps[:, :T], lhsT=wub[:, kc, fc * 128:(fc + 1) * 128], rhs=N.hT[:, kc, :],
                        start=(kc == 0), stop=(kc == 7)), reads=[b_wu[kc], N.b_hT], writes=[bps])
                u, bu = ut[fc % 2], b_ut[fc % 2]
                P.op("act", lambda e, ps=ps, u=u: e.activation(out=u[:], in_=ps[:, :T], func=AF.Gelu), reads=[bps], writes=[bu])
                eng = "pool" if fc % 2 == 0 else "dve"
                P.op(eng, lambda e, u=u, fc=fc: e.tensor_tensor(out=gT[:, fc, :], in0=u[:], in1=vmt[:, fc, :], op=ALU.mult),
                     reads=[bu, b_vmt], writes=[b_gT[fc]])
            for oc in range(8):
                ps, bps = yring.next()
                for fc in range(24):
                    P.op("pe", lambda e, oc=oc, fc=fc, ps=ps: e.matmul(
                        ps[:, :T], lhsT=wob[:, fc, oc * 128:(oc + 1) * 128], rhs=gT[:, fc, :],
                        start=(fc == 0), stop=(fc == 23)), reads=[b_wo[fc // 4], b_gT[fc]], writes=[bps])
                P.op("dve", lambda e, oc=oc, ps=ps, xt=xt: e.scalar_tensor_tensor(
                    out=xt[:, oc, :], in0=ps[:, :T], scalar=AB[:, 16 + oc:17 + oc], in1=xt[:, oc, :],
                    op0=ALU.mult, op1=ALU.add), reads=[bps, C.b_ab, b_xt], writes=[b_xt])
            P.dma(xov[:, :, t0:t0 + T], xt[:], reads=[b_xt], writes=[b_xo])


def _lay_common(d, L, b):
    return {
        "cT": np.ascontiguousarray(d["c"][b].reshape(8, 128).T),
        "ada_w": np.ascontiguousarray(d["ada_w"][L]),
        "ada_b": np.ascontiguousarray(d["ada_b"][L].reshape(48, 128).T),
        "gains": np.ascontiguousarray(np.concatenate([d["norm_mix"][L].reshape(8, 128), d["norm_ffn"][L].reshape(8, 128)], 0).T),
    }


def _decl_common(nc):
    cT_d = nc.dram_tensor("cT", [128, 8], F32, kind="ExternalInput").ap()
    adaw_d = nc.dram_tensor("ada_w", [1024, 6144], F32, kind="ExternalInput").ap()
    adab_d = nc.dram_tensor("ada_b", [128, 48], F32, kind="ExternalInput").ap()
    gains_d = nc.dram_tensor("gains", [128, 16], F32, kind="ExternalInput").ap()
    return cT_d, adaw_d, adab_d, gains_d


def build_L1(T_core=4096):
    nc = bass.Bass("TRN2", target_bir_lowering=False)
    xT_d = nc.dram_tensor("xT", [1024, T_core], F32, kind="ExternalInput").ap()
    cm = _decl_common(nc)
    w_in_d = nc.dram_tensor("b_w_in", [1024, 6144], F32, kind="ExternalInput").ap()
    vgain_d = nc.dram_tensor("b_vgain", [128, 3072], F32, kind="ExternalInput").ap()
    wsT_d = nc.dram_tensor("b_wsT", [128, 8, 128], F32, kind="ExternalInput").ap()
    bs_d = nc.dram_tensor("b_bs", [128, 24, 128], F32, kind="ExternalInput").ap()
    wout_d = nc.dram_tensor("b_w_out", [3072, 1024], F32, kind="ExternalInput").ap()
    w1_d = nc.dram_tensor("w1", [1024, 4096], F32, kind="ExternalInput").ap()
    w2_d = nc.dram_tensor("w2", [4096, 1024], F32, kind="ExternalInput").ap()
    yT_d = nc.dram_tensor("yT", [1024, T_core], F32, kind="ExternalOutput").ap()
    vm_d = nc.dram_tensor("vm_s", [24, 128, T_core], BF16, kind="ExternalOutput" if DEBUG else "Internal").ap()
    xs_d = nc.dram_tensor("xs_s", [1024, T_core], F32).ap()
    C = make_ctx(nc)
    emit_ada(C, *cm)
    emit_mixB(C, xT_d, xs_d, w_in_d, vgain_d, wsT_d, bs_d, wout_d, vm_d, T_core)
    emit_ffn(C, xs_d, yT_d, w1_d, w2_d, T_core)
    C.P.emit()
    return nc


def lay_L1(d, b):
    m = _lay_common(d, 1, b)
    m.update({
        "b_w_in": np.ascontiguousarray(d["b_w_in"][0]),
        "b_vgain": np.ascontiguousarray(np.broadcast_to(d["b_v_gain"][0][None, :], (128, 3072))),
        "b_wsT": np.ascontiguousarray(d["b_w_s"][0].transpose(2, 0, 1)),
        "b_bs": np.ascontiguousarray(np.broadcast_to(np.repeat(d["b_b_s"][0], 3, axis=0)[None], (128, 24, 128))),
        "b_w_out": np.ascontiguousarray(d["b_w_out"][0]),
        "w1": np.ascontiguousarray(d["ffn_w1"][1]), "w2": np.ascontiguousarray(d["ffn_w2"][1]),
    })
    return m


HALO = 512


def emit_mixA1(C, xT_d, w_in_d, gqk_d, qT_s, kT_s, v_s, T_core, T=256):
    P = C.P
    TT = T_core + HALO
    xv = xT_d.rearrange("(kc p) t -> p kc t", p=128)
    winv = w_in_d.rearrange("(kc p) f -> p kc f", p=128)
    qv = qT_s.rearrange("(oc p) t -> p oc t", p=128)
    kv = kT_s.rearrange("(oc p) t -> p oc t", p=128)
    with P.scope():
        wq = P.sb("wq", [128, 8, 3072], BF16); b_wq = P.bufs(8)
        for kc in range(8):
            load_cast(C, wq[:, kc, :], b_wq[kc], winv[:, kc, :], [3072])
        gqk = P.sb("gqk", [128, 2], F32); b_g = P.buf()
        P.dma(gqk[:], gqk_d, writes=[b_g])
        P.op("dve", lambda e: e.tensor_scalar_mul(out=gqk[:, 0:1], in0=gqk[:, 0:1], scalar1=0.125), reads=[b_g], writes=[b_g])
        bd = P.sb("bd", [128, 128], BF16); b_bd = P.buf()

        P.op("pool", lambda e: e.memset(bd[:], 0.0), writes=[b_bd])
        P.op("pool", lambda e: e.memset(bd[0:64, 0:64], 1.0 / 64.0), writes=[b_bd])
        P.op("pool", lambda e: e.memset(bd[64:128, 64:128], 1.0 / 64.0), writes=[b_bd])
        xts = [P.sb("xt%d" % i, [128, 8, T], F32) for i in range(2)]; b_xts = P.bufs(2)
        N = NormBufs(C, T)
        sqq = P.sb("sqq", [128, T], BF16); b_sqq = P.buf()
        rs = P.sb("rs", [128, T], F32); b_rs = P.buf()
        qk_o = [P.sb("qko%d" % i, [128, 8, T], BF16) for i in range(2)]; b_qko = P.bufs(2)
        vpad = [P.sb("vpad%d" % i, [128, 16, 128], BF16) for i in range(2)]; b_vpad = P.bufs(2)
        for i in range(2):
            P.op("pool", lambda e, i=i: e.memset(vpad[i][:], 0.0), writes=[b_vpad[i]])
        b_sc = P.buf()
        pring = Ring(list(zip(C.psums[1:4], C.psum_b[1:4])))
        sring = Ring(list(zip(C.psums[4:6], C.psum_b[4:6])))
        vring = Ring(list(zip(C.psums[6:8], C.psum_b[6:8])))
        vi = 0
        for ti in range(TT // T):
            xt, b_xt = xts[ti % 2], b_xts[ti % 2]
            t0 = ti * T
            P.dma(xt[:], xv[:, :, t0:t0 + T], writes=[b_xt])
            emit_norm_mod(C, N, xt, b_xt, 0, 8)
            for which in range(2):
                if which == 0 and t0 + T <= HALO:
                    continue
                qo, bqo = qk_o[which], b_qko[which]
                for oc in range(8):
                    ps, bps = pring.next()
                    col0 = which * 1024 + oc * 128
                    for kc in range(8):
                        P.op("pe", lambda e, kc=kc, ps=ps, col0=col0: e.matmul(
                            ps[:, :T], lhsT=wq[:, kc, col0:col0 + 128], rhs=N.hT[:, kc, :], start=(kc == 0), stop=(kc == 7)),
                            reads=[b_wq[kc], N.b_hT], writes=[bps])
                    P.op("act", lambda e, ps=ps: e.activation(out=sqq[:], in_=ps[:, :T], func=AF.Square), reads=[bps], writes=[b_sqq])
                    ps2, bps2 = sring.next()
                    P.op("pe", lambda e, ps2=ps2: e.matmul(ps2[:, :T], lhsT=bd[:], rhs=sqq[:], start=True, stop=True),
                         reads=[b_sqq, b_bd], writes=[bps2])
                    P.op("act", lambda e, ps2=ps2: e.activation(out=rs[:], in_=ps2[:, :T], func=AF.Sqrt, bias=C.eps[:, 0:1], scale=1.0),
                         reads=[bps2, C.cb], writes=[b_rs])
                    P.op("dve", lambda e: e.reciprocal(out=rs[:], in_=rs[:]), reads=[b_rs], writes=[b_rs])
                    P.op("dve", lambda e, ps=ps, oc=oc, qo=qo, which=which: e.scalar_tensor_tensor(
                        out=qo[:, oc, :], in0=ps[:, :T], scalar=gqk[:, which:which + 1], in1=rs[:], op0=ALU.mult, op1=ALU.mult),
                        reads=[bps, b_rs, b_g], writes=[bqo])
                dst = qv if which == 0 else kv
                if which == 0:
                    P.dma(dst[:, :, t0 - HALO:t0 - HALO + T], qo[:], reads=[bqo], writes=[b_sc])
                else:
                    P.dma(dst[:, :, t0:t0 + T], qo[:], reads=[bqo], writes=[b_sc])
            for w in range(T // 128):
                vp, bvp = vpad[vi % 2], b_vpad[vi % 2]
                vi += 1
                for half in range(2):
                    ps, bps = vring.next()
                    for kc in range(8):
                        P.op("pe", lambda e, kc=kc, ps=ps, w=w, half=half: e.matmul(
                            ps[:, :], lhsT=N.hT[:, kc, w * 128:(w + 1) * 128], rhs=wq[:, kc, 2048 + half * 512:2048 + (half + 1) * 512],
                            start=(kc == 0), stop=(kc == 7)), reads=[b_wq[kc], N.b_hT], writes=[bps])
                    psv = ps[:, :].rearrange("p (hp two d) -> p hp two d", hp=4, two=2)
                    vpv = vp[:, half * 8:(half + 1) * 8, :].rearrange("p (hp two) c -> p hp two c", two=2)
                    P.op("act", lambda e, psv=psv, vpv=vpv: e.activation(out=vpv[:, :, 0, 0:64], in_=psv[:, :, 0, :], func=AF.Identity),
                         reads=[bps], writes=[bvp])
                    P.op("dve", lambda e, psv=psv, vpv=vpv: e.tensor_copy(out=vpv[:, :, 1, 64:128], in_=psv[:, :, 1, :]),
                         reads=[bps], writes=[bvp])
                tw = t0 + w * 128
                P.dma(v_s[tw:tw + 128, :].rearrange("t (h c) -> t h c", h=16), vp[:], reads=[bvp], writes=[b_sc])


def emit_mixA2(C, xT_d, xo_d, qT_s, kT_s, v_s, biasT_d, valid_d, wout_d, T_core):
    P = C.P
    AB = C.AB
    xv = xT_d.rearrange("(kc p) t -> p kc t", p=128)
    xov = xo_d.rearrange("(kc p) t -> p kc t", p=128)
    qv = qT_s.rearrange("(oc p) t -> p oc t", p=128)
    kv = kT_s.rearrange("(oc p) t -> p oc t", p=128)
    with P.scope():
        wob = P.sb("wob", [128, 8, 1024], BF16); b_wo = P.bufs(2)
        wov = wout_d.rearrange("(fc p) n -> p fc n", p=128)
        for g in range(2):
            load_cast(C, wob[:, g * 4:(g + 1) * 4, :], b_wo[g], wov[:, g * 4:(g + 1) * 4, :], [4, 1024])
        ebias = P.sb("ebias", [128, 16, 640], F32); b_eb = P.buf()
        for g in range(4):
            P.dma(ebias[:, g * 4:(g + 1) * 4, :], biasT_d[:, g * 4:(g + 1) * 4, :], writes=[b_eb])
        P.op("act", lambda e: e.activation(out=ebias[:], in_=ebias[:], func=AF.Exp), reads=[b_eb], writes=[b_eb])
        nkb = (T_core + HALO) // 128
        valid = P.sb("valid", [128, nkb], F32); b_val = P.buf()
        P.dma(valid[:], valid_d, writes=[b_val])
        selA = P.sb("selA", [128, 128], BF16); selB = P.sb("selB", [128, 128], BF16); b_sel = P.buf()

        P.op("pool", lambda e: e.memset(selA[:], 0.0), writes=[b_sel])
        P.op("pool", lambda e: e.memset(selB[:], 0.0), writes=[b_sel])
        P.op("pool", lambda e: e.memset(selA[:, 0:64], 1.0), writes=[b_sel])
        P.op("pool", lambda e: e.memset(selB[:, 64:128], 1.0), writes=[b_sel])
        NB = 2
        qt = [P.sb("qt%d" % i, [128, 8, 128], BF16) for i in range(NB)]; b_qt = P.bufs(NB)
        kt = [P.sb("kt%d" % i, [128, 8, 640], BF16) for i in range(NB)]; b_kt = P.bufs(NB)
        vt = [P.sb("vt%d" % i, [128, 5, 2048], BF16) for i in range(NB)]; b_vt = P.bufs(NB)
        xts = [P.sb("xa%d" % i, [128, 8, 128], F32) for i in range(NB)]; b_xts = P.bufs(NB)
        et = [P.sb("et%d" % i, [128, 640], F32) for i in range(2)]; b_et = P.bufs(2)
        pt = [P.sb("pt%d" % i, [128, 640], BF16) for i in range(2)]; b_pt = P.bufs(2)
        oT = P.sb("oT", [128, 8, 128], BF16); b_oT = P.bufs(8)
        den = P.sb("den", [128, 128], F32); b_den = P.buf()
        b_xo = P.buf()
        sc = [(C.psums[1], C.psum_b[1], C.psums[2], C.psum_b[2]), (C.psums[3], C.psum_b[3], C.psums[4], C.psum_b[4])]
        po, b_po = C.psums[5], C.psum_b[5]
        pd, b_pd = C.psums[6], C.psum_b[6]
        py, b_py = C.psums[7], C.psum_b[7]
        def make_item(hi, i, m, pr, ab, kb0):
            h = pr * 2 + ab
            sA, bsA, sB, bsB = sc[hi % 2]
            e_, be = et[hi % 2], b_et[hi % 2]
            p_, bp = pt[hi % 2], b_pt[hi % 2]
            lo = ab * 64
            sel = selA if ab == 0 else selB

            def stage_a():
                for j in range(5):
                    dst, bd_ = (sA, bsA) if j < 4 else (sB, bsB)
                    c0 = (j % 4) * 128
                    P.op("pe", lambda e, j=j, dst=dst, c0=c0: e.matmul(
                        dst[:, c0:c0 + 128], lhsT=kt[i][lo:lo + 64, pr, j * 128:(j + 1) * 128], rhs=qt[i][lo:lo + 64, pr, :],
                        start=True, stop=True), reads=[b_kt[i], b_qt[i]], writes=[bd_])
                P.op("act", lambda e: e.activation(out=e_[:, 0:512], in_=sA[:, :], func=AF.Exp), reads=[bsA], writes=[be])
                P.op("act", lambda e: e.activation(out=e_[:, 512:640], in_=sB[:, 0:128], func=AF.Exp), reads=[bsB], writes=[be])
                for j in range(5):
                    P.op("dve", lambda e, j=j: e.scalar_tensor_tensor(
                        out=p_[:, j * 128:(j + 1) * 128], in0=e_[:, j * 128:(j + 1) * 128], scalar=valid[:, kb0 + j:kb0 + j + 1],
                        in1=ebias[:, h, j * 128:(j + 1) * 128], op0=ALU.mult, op1=ALU.mult),
                        reads=[be, b_val, b_eb], writes=[bp])

            def stage_b():
                for j in range(5):
                    first = (ab == 0 and j == 0)
                    last = (ab == 1 and j == 4)
                    P.op("pe", lambda e, j=j, first=first, last=last: e.matmul(
                        po[:, 0:128], lhsT=vt[i][:, j, h * 128:(h + 1) * 128], rhs=p_[:, j * 128:(j + 1) * 128],
                        start=first, stop=last), reads=[b_vt[i], bp], writes=[b_po])
                    P.op("pe", lambda e, j=j, first=first, last=last: e.matmul(
                        pd[:, 0:128], lhsT=sel[:], rhs=p_[:, j * 128:(j + 1) * 128],
                        start=first, stop=last), reads=[b_sel, bp], writes=[b_pd])
                if ab == 1:
                    P.op("dve", lambda e: e.reciprocal(out=den[:], in_=pd[:, 0:128]), reads=[b_pd], writes=[b_den])
                    P.op("dve", lambda e: e.tensor_tensor(out=oT[:, pr, :], in0=po[:, 0:128], in1=den[:], op=ALU.mult),
                         reads=[b_po, b_den], writes=[b_oT[pr]])
                if ab == 1 and pr == 7:
                    xt, b_xt = xts[i], b_xts[i]
                    q0 = m * 128
                    for oc in range(8):
                        for pr2 in range(8):
                            P.op("pe", lambda e, oc=oc, pr2=pr2: e.matmul(
                                py[:, (oc % 4) * 128:(oc % 4) * 128 + 128],
                                lhsT=wob[:, pr2, oc * 128:(oc + 1) * 128], rhs=oT[:, pr2, :],
                                start=(pr2 == 0), stop=(pr2 == 7)), reads=[b_wo[pr2 // 4], b_oT[pr2]], writes=[b_py])
                        P.op("dve", lambda e, oc=oc: e.scalar_tensor_tensor(
                            out=xt[:, oc, :], in0=py[:, (oc % 4) * 128:(oc % 4) * 128 + 128], scalar=AB[:, 16 + oc:17 + oc], in1=xt[:, oc, :],
                            op0=ALU.mult, op1=ALU.add), reads=[b_py, C.b_ab, b_xt], writes=[b_xt])
                    P.dma(xov[:, :, q0:q0 + 128], xt[:], reads=[b_xt], writes=[b_xo])
            return stage_a, stage_b

        pending = None
        hi = 0
        for m in range(T_core // 128):
            i = m % NB
            q0 = m * 128
            k0 = m * 128
            P.dma(qt[i][:], qv[:, :, q0:q0 + 128], writes=[b_qt[i]])
            P.dma(kt[i][:], kv[:, :, k0:k0 + 640], writes=[b_kt[i]])
            P.dma(vt[i][:], v_s[k0:k0 + 640, :].rearrange("(j p) c -> p j c", p=128), writes=[b_vt[i]])
            P.dma(xts[i][:], xv[:, :, HALO + q0:HALO + q0 + 128], writes=[b_xts[i]])
            kb0 = k0 // 128
            for pr in range(8):
                for ab in range(2):
                    sa, sb_ = make_item(hi, i, m, pr, ab, kb0)
                    hi += 1
                    sa()
                    if pending is not None:
                        pending()
                    pending = sb_
        pending()


def build_L0(T_core=4096):
    nc = bass.Bass("TRN2", target_bir_lowering=False)
    TT = T_core + HALO
    xT_d = nc.dram_tensor("xT", [1024, TT], F32, kind="ExternalInput").ap()
    cm = _decl_common(nc)
    w_in_d = nc.dram_tensor("a_w_in", [1024, 3072], F32, kind="ExternalInput").ap()
    gqk_d = nc.dram_tensor("a_gqk", [128, 2], F32, kind="ExternalInput").ap()
    biasT_d = nc.dram_tensor("a_biasT", [128, 16, 640], F32, kind="ExternalInput").ap()
    valid_d = nc.dram_tensor("a_valid", [128, TT // 128], F32, kind="ExternalInput").ap()
    wout_d = nc.dram_tensor("a_w_out", [1024, 1024], F32, kind="ExternalInput").ap()
    w1_d = nc.dram_tensor("w1", [1024, 4096], F32, kind="ExternalInput").ap()
    w2_d = nc.dram_tensor("w2", [4096, 1024], F32, kind="ExternalInput").ap()
    yT_d = nc.dram_tensor("yT", [1024, T_core], F32, kind="ExternalOutput").ap()
    qT_s = nc.dram_tensor("qT_s", [1024, T_core], BF16).ap()
    kT_s = nc.dram_tensor("kT_s", [1024, TT], BF16).ap()
    v_s = nc.dram_tensor("v_s", [TT, 2048], BF16).ap()
    xs_d = nc.dram_tensor("xs_s", [1024, T_core], F32).ap()
    C = make_ctx(nc)
    emit_ada(C, *cm)
    emit_mixA1(C, xT_d, w_in_d, gqk_d, qT_s, kT_s, v_s, T_core)
    emit_mixA2(C, xT_d, xs_d, qT_s, kT_s, v_s, biasT_d, valid_d, wout_d, T_core)
    emit_ffn(C, xs_d, yT_d, w1_d, w2_d, T_core)
    C.P.emit()
    return nc


def _a_bias_table(rel_bias):
    kap = np.arange(640)[:, None]
    q = np.arange(128)[None, :]
    rel = q + 512 - kap
    cq = q // 64
    inband = (kap >= cq * 64) & (kap < cq * 64 + 576)
    idx = np.clip(rel, -63, 256) + 63
    tab = rel_bias[:, idx]
    tab = np.where(inband[None], tab, np.float32(-30000.0)).astype(np.float32)
    tab = tab.reshape(16, 5, 128, 128).transpose(2, 0, 1, 3).reshape(128, 16, 640)
    return np.ascontiguousarray(tab)


def lay_L0(d, b, half, T_core=4096, x=None):
    m = _lay_common(d, 0, b)
    x = d["x"] if x is None else x
    t0 = half * T_core
    TT = T_core + HALO
    xt = np.zeros((1024, TT), np.float32)
    lo = t0 - HALO
    if lo >= 0:
        xt[:, :] = x[b, lo:lo + TT, :].T
    else:
        xt[:, HALO:] = x[b, 0:T_core, :].T
    valid = np.ones((128, TT // 128), np.float32)
    if lo < 0:
        valid[:, :HALO // 128] = 0.0
    m.update({
        "xT": xt,
        "a_w_in": np.ascontiguousarray(d["a_w_in"][0]),
        "a_gqk": np.ascontiguousarray(np.stack([np.tile(d["a_q_gain"][0], 2), np.tile(d["a_k_gain"][0], 2)], 1)),
        "a_biasT": _a_bias_table(d["a_rel_bias"][0]),
        "a_valid": valid,
        "a_w_out": np.ascontiguousarray(d["a_w_out"][0]),
        "w1": np.ascontiguousarray(d["ffn_w1"][0]), "w2": np.ascontiguousarray(d["ffn_w2"][0]),
    })
    return m


def emit_mixD1(C, xT_d, wqkv_d, qT_s, kT_s, v_s, S, T=256, xsrc=None):
    P = C.P
    if xsrc is None:
        xv0 = xT_d.rearrange("(kc p) t -> p kc t", p=128)
        xsrc = lambda t0, T: xv0[:, :, t0:t0 + T]
    wv_ = wqkv_d.rearrange("(kc p) f -> p kc f", p=128)
    qv = qT_s.rearrange("(oc p) t -> p oc t", p=128)
    kv = kT_s.rearrange("(oc p) t -> p oc t", p=128)
    with P.scope():
        wq = P.sb("wq", [128, 8, 1536], BF16); b_wq = P.bufs(8)
        for kc in range(8):
            load_cast(C, wq[:, kc, :], b_wq[kc], wv_[:, kc, :], [1536])
        xts = [P.sb("xt%d" % i, [128, 8, T], F32) for i in range(2)]; b_xts = P.bufs(2)
        N = NormBufs(C, T)
        qk_o = [P.sb("qko%d" % i, [128, 4, T], BF16) for i in range(2)]; b_qko = P.bufs(2)
        vpad = [P.sb("vpad%d" % i, [128, 8, 128], BF16) for i in range(2)]; b_vpad = P.bufs(2)
        for i in range(2):
            P.op("pool", lambda e, i=i: e.memset(vpad[i][:], 0.0), writes=[b_vpad[i]])
        b_sc = P.buf()
        pring = Ring(list(zip(C.psums[1:5], C.psum_b[1:5])))
        vring = Ring(list(zip(C.psums[5:8], C.psum_b[5:8])))
        vi = 0
        for ti in range(S // T):
            xt, b_xt = xts[ti % 2], b_xts[ti % 2]
            t0 = ti * T
            P.dma(xt[:], xsrc(t0, T), reads=(xsrc.deps(t0) if hasattr(xsrc, 'deps') else []), writes=[b_xt])
            emit_norm_mod(C, N, xt, b_xt, 0, 8)
            for which in range(2):
                qo, bqo = qk_o[which], b_qko[which]
                for oc in range(4):
                    ps, bps = pring.next()
                    col0 = which * 512 + oc * 128
                    for kc in range(8):
                        P.op("pe", lambda e, kc=kc, ps=ps, col0=col0: e.matmul(
                            ps[:, :T], lhsT=wq[:, kc, col0:col0 + 128], rhs=N.hT[:, kc, :], start=(kc == 0), stop=(kc == 7)),
                            reads=[b_wq[kc], N.b_hT], writes=[bps])
                    sc = 0.125 if which == 0 else 1.0
                    P.op("act", lambda e, ps=ps, oc=oc, qo=qo, sc=sc: e.activation(out=qo[:, oc, :], in_=ps[:, :T], func=AF.Identity, scale=sc),
                         reads=[bps], writes=[bqo])
                dst = qv if which == 0 else kv
                P.dma(dst[:, :, t0:t0 + T], qo[:], reads=[bqo], writes=[b_sc])
            for w in range(T // 128):
                vp, bvp = vpad[vi % 2], b_vpad[vi % 2]
                vi += 1
                ps, bps = vring.next()
                for kc in range(8):
                    P.op("pe", lambda e, kc=kc, ps=ps, w=w: e.matmul(
                        ps[:, :], lhsT=N.hT[:, kc, w * 128:(w + 1) * 128], rhs=wq[:, kc, 1024:1536],
                        start=(kc == 0), stop=(kc == 7)), reads=[b_wq[kc], N.b_hT], writes=[bps])
                psv = ps[:, :].rearrange("p (hp two d) -> p hp two d", hp=4, two=2)
                vpv = vp[:].rearrange("p (hp two) c -> p hp two c", two=2)
                P.op("act", lambda e, psv=psv, vpv=vpv: e.activation(out=vpv[:, :, 0, 0:64], in_=psv[:, :, 0, :], func=AF.Identity),
                     reads=[bps], writes=[bvp])
                P.op("dve", lambda e, psv=psv, vpv=vpv: e.tensor_copy(out=vpv[:, :, 1, 64:128], in_=psv[:, :, 1, :]),
                     reads=[bps], writes=[bvp])
                tw = t0 + w * 128
                P.dma(v_s[tw:tw + 128, :].rearrange("t (h c) -> t h c", h=8), vp[:], reads=[bvp], writes=[b_sc])


def emit_mixD2(C, qT_s, kT_s, v_s, cst_d, oT_d, S, TQ=512, odst=None):
    P = C.P
    qv = qT_s.rearrange("(oc p) t -> p oc t", p=128)
    kv = kT_s.rearrange("(oc p) t -> p oc t", p=128)
    if odst is None:
        ov = oT_d.rearrange("(oc p) t -> p oc t", p=128)
        odst = lambda t0, n: ov[:, :, t0:t0 + n]
    with P.scope():
        cst = P.sb("cst", [128, 3, 128], F32); b_cst = P.buf()
        P.dma(cst[:], cst_d, writes=[b_cst])
        cbf = P.sb("cbf", [128, 3, 128], BF16); b_cbf = P.buf()
        P.op("dve", lambda e: e.tensor_copy(out=cbf[:], in_=cst[:]), reads=[b_cst], writes=[b_cbf])
        qt = [P.sb("qt%d" % i, [128, 4, TQ], BF16) for i in range(2)]; b_qt = P.bufs(2)
        NKB = 3
        kt = [P.sb("kt%d" % i, [128, 4, 128], BF16) for i in range(NKB)]; b_kt = P.bufs(NKB)
        vt = [P.sb("vt%d" % i, [128, 8, 128], BF16) for i in range(NKB)]; b_vt = P.bufs(NKB)
        NS = 4
        et = [P.sb("et%d" % i, [128, 2, TQ], F32) for i in range(NS)]; b_et = P.bufs(NS)
        lt = [P.sb("lt%d" % i, [128, 2, TQ], BF16) for i in range(NS)]; b_lt = P.bufs(NS)
        wt = [P.sb("wt%d" % i, [128, 2, TQ], BF16) for i in range(NS)]; b_wt = P.bufs(NS)
        lacc = [P.sb("lacc%d" % i, [128, 2, TQ], BF16) for i in range(4)]; b_lacc = P.bufs(4)
        osb = [P.sb("osb%d" % i, [128, 4, TQ], F32) for i in range(2)]; b_osb = P.bufs(2)
        b_od = P.buf()
        z1 = C.psbig[:, 0:1024].rearrange("p (a n) -> p a n", a=2); b_z1 = P.buf()
        z2 = C.psbig[:, 1024:2048].rearrange("p (a n) -> p a n", a=2); b_z2 = P.buf()
        poring = [(C.psums[4 + i], C.psum_b[4 + i]) for i in range(4)]
        nq = TQ // 128

        def make_item(hi, q, bq, k_, bk, v_, bv, pr, kb, dq, c0, first):
            e_, be = et[hi % NS], b_et[hi % NS]
            l_, bl = lt[hi % NS], b_lt[hi % NS]
            w_, bw = wt[hi % NS], b_wt[hi % NS]
            la, bla = lacc[pr], b_lacc[pr]
            po, bpo = poring[pr]

            def stage_a():
                for a in range(2):
                    lo = a * 64
                    P.op("pe", lambda e, a=a, lo=lo: e.matmul(z1[:, a, c0:TQ], lhsT=k_[lo:lo + 64, pr, :], rhs=q[lo:lo + 64, pr, c0:TQ],
                                                             start=True, stop=True), reads=[bk, bq], writes=[b_z1])
                P.op("act", lambda e: e.activation(out=e_[:, :, c0:TQ], in_=z1[:, :, c0:TQ], func=AF.Exp), reads=[b_z1], writes=[be])
                P.op("act", lambda e: e.activation(out=l_[:, :, c0:TQ], in_=e_[:, :, c0:TQ], func=AF.Ln, bias=C.ones_f[:, 0:1], scale=1.0),
                     reads=[be, C.cb], writes=[bl])
                if dq >= 0:
                    for a in range(2):
                        P.op("dve", lambda e, a=a: e.tensor_tensor(out=l_[:, a, c0:c0 + 128], in0=l_[:, a, c0:c0 + 128], in1=cbf[:, 0, :], op=ALU.mult),
                             reads=[bl, b_cbf], writes=[bl])

            def stage_b():
                for a in range(2):
                    lo = a * 64
                    P.op("pe", lambda e, a=a, lo=lo: e.matmul(z2[:, a, c0:TQ], lhsT=k_[lo:lo + 64, pr, :], rhs=q[lo:lo + 64, pr, c0:TQ],
                                                             start=True, stop=False), reads=[bk, bq], writes=[b_z2])
                    P.op("pe", lambda e, a=a: e.matmul(z2[:, a, c0:TQ], lhsT=cbf[:, 1, :], rhs=l_[:, a, c0:TQ], start=False, stop=first),
                         reads=[bl, b_cbf], writes=[b_z2])
                    if not first:
                        P.op("pe", lambda e, a=a: e.matmul(z2[:, a, c0:TQ], lhsT=cbf[:, 2, :], rhs=la[:, a, c0:TQ], start=False, stop=True),
                             reads=[bla, b_cbf], writes=[b_z2])
                if c0 > 0:
                    P.op("pool", lambda e: e.memset(w_[:, :, 0:c0], 0.0), writes=[bw])
                P.op("act", lambda e: e.activation(out=w_[:, :, c0:TQ], in_=z2[:, :, c0:TQ], func=AF.Exp), reads=[b_z2], writes=[bw])
                if dq >= 0:
                    for a in range(2):
                        P.op("dve", lambda e, a=a: e.tensor_tensor(out=w_[:, a, c0:c0 + 128], in0=w_[:, a, c0:c0 + 128], in1=cbf[:, 0, :], op=ALU.mult),
                             reads=[bw, b_cbf], writes=[bw])

            def stage_c():
                if kb > 0:
                    if first:
                        if c0 > 0:
                            P.op("pool", lambda e: e.memset(la[:, :, 0:c0], 0.0), writes=[bla])
                        P.op("pool", lambda e: e.tensor_copy(out=la[:, :, c0:TQ], in_=l_[:, :, c0:TQ]), reads=[bl], writes=[bla])
                    else:
                        P.op("pool", lambda e: e.tensor_tensor(out=la[:, :, c0:TQ], in0=la[:, :, c0:TQ], in1=l_[:, :, c0:TQ], op=ALU.add),
                             reads=[bl, bla], writes=[bla])
                for a in range(2):
                    P.op("pe", lambda e, a=a: e.matmul(po[:, 0:TQ], lhsT=v_[:, pr * 2 + a, :], rhs=w_[:, a, 0:TQ],
                                                       start=(first and a == 0), stop=(kb == 0 and a == 1)),
                         reads=[bv, bw], writes=[bpo])
            return stage_a, stage_b, stage_c

        def make_post(os_, bos, q0):
            def post():
                for pr in range(4):
                    po, bpo = poring[pr]
                    if pr % 2 == 0:
                        P.op("act", lambda e, po=po, pr=pr: e.activation(out=os_[:, pr, :], in_=po[:, :], func=AF.Identity), reads=[bpo], writes=[bos])
                    else:
                        P.op("dve", lambda e, po=po, pr=pr: e.tensor_copy(out=os_[:, pr, :], in_=po[:, :]), reads=[bpo], writes=[bos])
                P.dma(odst(q0, TQ), os_[:], reads=[bos], writes=[b_od])
            return post

        pend_b = []
        pend_c = []

        def step(new):
            nb = pend_b.pop(0) if pend_b else None
            if nb is not None:
                nb[0]()
            if pend_c:
                c_, post_ = pend_c.pop(0)
                c_()
                if post_ is not None:
                    post_()
            if nb is not None:
                pend_c.append((nb[1], nb[2]))
            if new is not None:
                pend_b.append(new)

        hi = 0
        ki = 0
        for qi in range(S // TQ):
            q, bq = qt[qi % 2], b_qt[qi % 2]
            q0 = qi * TQ
            P.dma(q[:], qv[:, :, q0:q0 + TQ], writes=[bq])
            os_, bos = osb[qi % 2], b_osb[qi % 2]
            kb_hi = qi * nq + nq - 1
            for kb in range(kb_hi, -1, -1):
                k_, bk = kt[ki % NKB], b_kt[ki % NKB]
                v_, bv = vt[ki % NKB], b_vt[ki % NKB]
                ki += 1
                P.dma(k_[:], kv[:, :, kb * 128:(kb + 1) * 128], writes=[bk])
                P.dma(v_[:], v_s[kb * 128:(kb + 1) * 128, :].rearrange("t (h c) -> t h c", h=8), writes=[bv])
                dq = kb - qi * nq
                c0 = max(dq, 0) * 128
                first = (kb == kb_hi)
                for pr in range(4):
                    sa, sb_, sc_ = make_item(hi, q, bq, k_, bk, v_, bv, pr, kb, dq, c0, first)
                    hi += 1
                    post = make_post(os_, bos, q0) if (kb == 0 and pr == 3) else None
                    sa()
                    step((sb_, sc_, post))
        while pend_b or pend_c:
            step(None)


def _lacc_init(P, la, bla, l_, bl, c0, TQ):
    if c0 > 0:
        P.op("pool", lambda e: e.memset(la[:, 0:c0], 0.0), writes=[bla])
    P.op("pool", lambda e: e.tensor_copy(out=la[:, c0:TQ], in_=l_[:, c0:TQ]), reads=[bl], writes=[bla])


def emit_outproj(C, xT_d, oT_d, wout_d, xo_d, T_core, T=256, sel=None, xsrc=None, osrc=None):
    P = C.P
    AB = C.AB
    if xsrc is None:
        xv = xT_d.rearrange("(kc p) t -> p kc t", p=128)
        xsrc = lambda t0, T: xv[:, :, t0:t0 + T]
    if osrc is None:
        ov = oT_d.rearrange("(kc p) t -> p kc t", p=128)
        osrc = lambda h, t0, T: ov[:, :, h * T_core + t0:h * T_core + t0 + T]
    xov = xo_d.rearrange("(kc p) t -> p kc t", p=128)
    with P.scope():
        wob = P.sb("wob", [128, 8, 1024], BF16); b_wo = P.bufs(2)
        wov = wout_d.rearrange("(fc p) n -> p fc n", p=128)
        for g in range(2):
            load_cast(C, wob[:, g * 4:(g + 1) * 4, :], b_wo[g], wov[:, g * 4:(g + 1) * 4, :], [4, 1024])
        xts = [P.sb("xt%d" % i, [128, 8, T], F32) for i in range(2)]; b_xts = P.bufs(2)
        ots = [P.sb("ot%d" % i, [128, 8, T], F32) for i in range(2)]; b_ots = P.bufs(2)
        ob = P.sb("ob", [128, 8, T], BF16); b_ob = P.buf()
        if sel is not None:
            ots1 = [P.sb("ot1_%d" % i, [128, 8, T], F32) for i in range(2)]; b_ots1 = P.bufs(2)
        yring = Ring(list(zip(C.psums[1:8], C.psum_b[1:8])))
        b_xo = P.buf()
        for ti in range(T_core // T):
            xt, b_xt = xts[ti % 2], b_xts[ti % 2]
            ot, b_ot = ots[ti % 2], b_ots[ti % 2]
            t0 = ti * T
            P.dma(xt[:], xsrc(t0, T), writes=[b_xt])
            P.dma(ot[:], osrc(0, t0, T), reads=(osrc.deps(0, t0) if hasattr(osrc, 'deps') else []), writes=[b_ot])
            if sel is None:
                P.op("act", lambda e, ot=ot: e.activation(out=ob[:], in_=ot[:], func=AF.Identity), reads=[b_ot], writes=[b_ob])
            else:
                selt, b_sel = sel
                ot1, b_ot1 = ots1[ti % 2], b_ots1[ti % 2]
                P.dma(ot1[:], osrc(1, t0, T), reads=(osrc.deps(1, t0) if hasattr(osrc, 'deps') else []), writes=[b_ot1])
                P.op("act", lambda e, ot=ot: e.activation(out=ot[:], in_=ot[:], func=AF.Identity, scale=selt[:, 0:1]),
                     reads=[b_ot, b_sel], writes=[b_ot])
                P.op("dve", lambda e, ot=ot, ot1=ot1: e.scalar_tensor_tensor(out=ob[:], in0=ot1[:], scalar=selt[:, 1:2], in1=ot[:],
                                                                            op0=ALU.mult, op1=ALU.add),
                     reads=[b_ot, b_ot1, b_sel], writes=[b_ob])
            for oc in range(8):
                ps, bps = yring.next()
                for kc in range(8):
                    P.op("pe", lambda e, oc=oc, kc=kc, ps=ps: e.matmul(
                        ps[:, :T], lhsT=wob[:, kc, oc * 128:(oc + 1) * 128], rhs=ob[:, kc, :],
                        start=(kc == 0), stop=(kc == 7)), reads=[b_wo[kc // 4], b_ob], writes=[bps])
                P.op("dve", lambda e, oc=oc, ps=ps, xt=xt: e.scalar_tensor_tensor(
                    out=xt[:, oc, :], in0=ps[:, :T], scalar=AB[:, 16 + oc:17 + oc], in1=xt[:, oc, :],
                    op0=ALU.mult, op1=ALU.add), reads=[bps, C.b_ab, b_xt], writes=[b_xt])
            P.dma(xov[:, :, t0:t0 + T], xt[:], reads=[b_xt], writes=[b_xo])


def build_L3a(S=8192):
    nc = bass.Bass("TRN2", target_bir_lowering=False)
    xT_d = nc.dram_tensor("xT", [1024, S], F32, kind="ExternalInput").ap()
    cm = _decl_common(nc)
    wqkv_d = nc.dram_tensor("d_wqkv", [1024, 1536], F32, kind="ExternalInput").ap()
    cst_d = nc.dram_tensor("d_cst", [128, 3, 128], F32, kind="ExternalInput").ap()
    oT_d = nc.dram_tensor("oT", [512, S], F32, kind="ExternalOutput").ap()
    qT_s = nc.dram_tensor("qT_s", [512, S], BF16).ap()
    kT_s = nc.dram_tensor("kT_s", [512, S], BF16).ap()
    v_s = nc.dram_tensor("v_s", [S, 1024], BF16).ap()
    C = make_ctx(nc)
    emit_ada(C, *cm)
    emit_mixD1(C, xT_d, wqkv_d, qT_s, kT_s, v_s, S)
    emit_mixD2(C, qT_s, kT_s, v_s, cst_d, oT_d, S)
    C.P.emit()
    return nc


def _d_consts():
    j = np.arange(128)[:, None]
    s_ = np.arange(128)[None, :]
    mask = (s_ > j).astype(np.float32)
    ntri = -(j >= s_).astype(np.float32)
    nones = -np.ones((128, 128), np.float32)
    return np.ascontiguousarray(np.stack([mask, ntri, nones], 1))


def lay_L3a(d, b, hg, xT_full):
    m = _lay_common(d, 3, b)
    w = d["d_w_in"][0]
    m.update({
        "xT": xT_full,
        "d_wqkv": np.ascontiguousarray(np.concatenate([w[:, hg * 512:(hg + 1) * 512], w[:, 1024 + hg * 512:1024 + (hg + 1) * 512],
                                                       w[:, 2048 + hg * 512:2048 + (hg + 1) * 512]], 1)),
        "d_cst": _d_consts(),
    })
    return m


def build_Lb(T_core=4096):
    nc = bass.Bass("TRN2", target_bir_lowering=False)
    xT_d = nc.dram_tensor("xT", [1024, T_core], F32, kind="ExternalInput").ap()
    oT_d = nc.dram_tensor("oT", [1024, T_core], F32, kind="ExternalInput").ap()
    cm = _decl_common(nc)
    wout_d = nc.dram_tensor("w_out", [1024, 1024], F32, kind="ExternalInput").ap()
    w1_d = nc.dram_tensor("w1", [1024, 4096], F32, kind="ExternalInput").ap()
    w2_d = nc.dram_tensor("w2", [4096, 1024], F32, kind="ExternalInput").ap()
    yT_d = nc.dram_tensor("yT", [1024, T_core], F32, kind="ExternalOutput").ap()
    xs_d = nc.dram_tensor("xs_s", [1024, T_core], F32).ap()
    C = make_ctx(nc)
    emit_ada(C, *cm)
    emit_outproj(C, xT_d, oT_d, wout_d, xs_d, T_core)
    emit_ffn(C, xs_d, yT_d, w1_d, w2_d, T_core)
    C.P.emit()
    return nc


def lay_Lb(d, L, b, w_out):
    m = _lay_common(d, L, b)
    m.update({"w_out": np.ascontiguousarray(w_out), "w1": np.ascontiguousarray(d["ffn_w1"][L]), "w2": np.ascontiguousarray(d["ffn_w2"][L])})
    return m


def emit_mixC(C, xT_d, wc_d, wg_d, bg_d, og_d, cst_d, oT_d, S, T=256, xsrc=None, odst=None):
    P = C.P
    if xsrc is None:
        xv0 = xT_d.rearrange("(kc p) t -> p kc t", p=128)
        xsrc = lambda t0, T: xv0[:, :, t0:t0 + T]
    wv_ = wc_d.rearrange("(kc p) f -> p kc f", p=128)
    if odst is None:
        ov = oT_d.rearrange("(oc p) t -> p oc t", p=128)
        odst = lambda t0, n: ov[:, :, t0:t0 + n]
    with P.scope():
        wc = P.sb("wc", [128, 8, 1552], BF16); b_wc = P.bufs(8)
        for kc in range(8):
            load_cast(C, wc[:, kc, :], b_wc[kc], wv_[:, kc, :], [1552])
        wg = P.sb("wg", [16, 256], F32); bg = P.sb("bg", [1, 256], F32); og = P.sb("og", [128, 2], F32)
        cst = P.sb("cst", [128, 3, 128], F32)
        b_k = P.buf()
        P.dma(wg[:], wg_d, writes=[b_k]); P.dma(bg[:], bg_d, writes=[b_k]); P.dma(og[:], og_d, writes=[b_k])
        P.dma(cst[:], cst_d, writes=[b_k])
        xts = [P.sb("xt%d" % i, [128, 8, T], F32) for i in range(2)]; b_xts = P.bufs(2)
        N = NormBufs(C, T)
        q_sb = P.sb("q_sb", [128, 2, T], F32); b_q = P.bufs(2)
        k_sb = P.sb("k_sb", [128, 2, T], F32); b_kk = P.bufs(2)
        r_sb = P.sb("r_sb", [128, 4, T], F32); b_r = P.bufs(4)
        a_sb = P.sb("a_sb", [16, T], F32); b_a = P.buf()
        ktok = P.sb("ktok", [128, 256], F32); b_ktok = P.buf()
        vtok = P.sb("vtok", [128, 512], BF16); b_vtok = P.buf()
        st = [P.sb("st%d" % i, [128, 256], F32) for i in range(2)]; b_st = P.bufs(2)
        stb = [P.sb("stb%d" % i, [128, 256], BF16) for i in range(2)]; b_stb = P.bufs(2)
        for i in range(2):
            P.op("pool", lambda e, i=i: e.memset(st[i][:], 0.0), writes=[b_st[i]])
            P.op("pool", lambda e, i=i: e.memset(stb[i][:], 0.0), writes=[b_stb[i]])
        HB = []
        for hh in range(2):
            d_ = {}
            for nm, shp, dt_ in (("eg", [128, 128], F32), ("lp", [128, 128], F32), ("ep", [128, 128], F32), ("em", [128, 128], F32),
                                 ("er", [128, 128], F32), ("qd", [128, 128], BF16), ("kd", [128, 128], BF16), ("kdt", [128, 128], BF16),
                                 ("att", [128, 128], BF16), ("osq", [128, 2, 128], BF16), ("ors", [128, 128], F32)):
                d_[nm] = P.sb("%s%d" % (nm, hh), shp, dt_); d_["b_" + nm] = P.buf()
            d_["o_sb"] = P.sb("o_sb%d" % hh, [128, 2, 128], F32); d_["b_o"] = P.bufs(2)
            HB.append(d_)
        ogt = [P.sb("ogt%d" % i, [128, 4, 128], F32) for i in range(2)]; b_ogt = P.bufs(2)
        b_od = P.buf()
        ringA = Ring(list(zip(C.psums[1:3], C.psum_b[1:3])))
        ringB = Ring(list(zip(C.psums[3:8], C.psum_b[3:8])))
        QS = 1.0 / (128.0 ** 0.5)
        bi = 0
        for ti in range(S // T):
            xt, b_xt = xts[ti % 2], b_xts[ti % 2]
            t0 = ti * T
            P.dma(xt[:], xsrc(t0, T), reads=(xsrc.deps(t0) if hasattr(xsrc, 'deps') else []), writes=[b_xt])
            emit_norm_mod(C, N, xt, b_xt, 0, 8)

            def proj(col0, m, dst_fn, tag):
                ps, bps = ringA.next()
                for kc in range(8):
                    P.op("pe", lambda e, kc=kc, ps=ps: e.matmul(ps[0:m, :T], lhsT=wc[:, kc, col0:col0 + m], rhs=N.hT[:, kc, :],
                                                              start=(kc == 0), stop=(kc == 7)), reads=[b_wc[kc], N.b_hT], writes=[bps])
                dst_fn(ps, bps)
            for h in range(2):
                proj(h * 128, 128, lambda ps, bps, h=h: P.op("act", lambda e: e.activation(out=q_sb[:, h, :], in_=ps[:, :T], func=AF.Identity, scale=QS),
                                                          reads=[bps], writes=[b_q[h]]), "q")
                proj(256 + h * 128, 128, lambda ps, bps, h=h: P.op("dve", lambda e: e.tensor_copy(out=k_sb[:, h, :], in_=ps[:, :T]),
                                                                reads=[bps], writes=[b_kk[h]]), "k")
            for j in range(4):
                proj(1024 + j * 128, 128, lambda ps, bps, j=j: P.op("act", lambda e: e.activation(out=r_sb[:, j, :], in_=ps[:, :T], func=AF.Silu),
                                                                 reads=[bps], writes=[b_r[j]]), "r")
            proj(1536, 16, lambda ps, bps: P.op("dve", lambda e: e.tensor_copy(out=a_sb[:, :], in_=ps[0:16, :T]), reads=[bps], writes=[b_a]), "a")
            for blk in range(T // 128):
                c0 = blk * 128
                og_t, b_og = ogt[bi % 2], b_ogt[bi % 2]
                bi += 1
                ps, bps = ringA.next()
                for kc in range(8):
                    P.op("pe", lambda e, kc=kc, ps=ps, c0=c0: e.matmul(ps[:, 0:256], lhsT=N.hT[:, kc, c0:c0 + 128], rhs=wc[:, kc, 256:512],
                                                                     start=(kc == 0), stop=(kc == 7)), reads=[b_wc[kc], N.b_hT], writes=[bps])
                P.op("dve", lambda e, ps=ps: e.tensor_copy(out=ktok[:], in_=ps[:, 0:256]), reads=[bps], writes=[b_ktok])
                ps, bps = ringA.next()
                for kc in range(8):
                    P.op("pe", lambda e, kc=kc, ps=ps, c0=c0: e.matmul(ps[:, 0:512], lhsT=N.hT[:, kc, c0:c0 + 128], rhs=wc[:, kc, 512:1024],
                                                                     start=(kc == 0), stop=(kc == 7)), reads=[b_wc[kc], N.b_hT], writes=[bps])
                P.op("act", lambda e, ps=ps: e.activation(out=vtok[:], in_=ps[:, 0:512], func=AF.Identity), reads=[bps], writes=[b_vtok])
                def head_gen(h, c0=c0, og_t=og_t, b_og=b_og):
                    hb = HB[h]
                    eg, lp, ep, em, er, qd, kd, kdt, att, osq, ors, o_sb = (hb[k] for k in ('eg','lp','ep','em','er','qd','kd','kdt','att','osq','ors','o_sb'))
                    b_eg, b_lp, b_ep, b_em, b_er, b_qd, b_kd, b_kdt, b_att, b_osq, b_ors, b_o = (hb['b_' + k] for k in ('eg','lp','ep','em','er','qd','kd','kdt','att','osq','ors','o'))
                    S_, bS = st[h], b_st[h]
                    Sb, bSb = stb[h], b_stb[h]
                    pg, bpg = ringB.next()
                    P.op("pe", lambda e, pg=pg, c0=c0, h=h: e.matmul(pg[:, 0:128], lhsT=a_sb[:, c0:c0 + 128], rhs=wg[:, h * 128:(h + 1) * 128],
                                                                   start=True, stop=False), reads=[b_a, b_k], writes=[bpg])
                    yield
                    P.op("pe", lambda e, pg=pg, h=h: e.matmul(pg[:, 0:128], lhsT=C.ones_f[0:1, :], rhs=bg[0:1, h * 128:(h + 1) * 128],
                                                            start=False, stop=True), reads=[C.cb, b_k], writes=[bpg])
                    yield
                    P.op("act", lambda e, pg=pg: e.activation(out=eg[:], in_=pg[:, 0:128], func=AF.Exp, scale=-1.0), reads=[bpg], writes=[b_eg])
                    yield
                    P.op("act", lambda e: e.activation(out=lp[:], in_=eg[:], func=AF.Ln, bias=C.ones_f[:, 0:1], scale=1.0),
                         reads=[b_eg, C.cb], writes=[b_lp])
                    yield
                    pb, bpb = ringB.next()
                    P.op("pe", lambda e, pb=pb: e.matmul(pb[:, 0:128], lhsT=lp[:], rhs=cst[:, 0, :], start=True, stop=True),
                         reads=[b_lp, b_k], writes=[bpb])
                    yield
                    pr_, bpr = ringB.next()
                    P.op("pe", lambda e, pr_=pr_: e.matmul(pr_[:, 0:128], lhsT=cst[:, 1, :], rhs=lp[:], start=True, stop=True),
                         reads=[b_lp, b_k], writes=[bpr])
                    yield
                    P.op("act", lambda e, pb=pb: e.activation(out=ep[:], in_=pb[:, 0:128], func=AF.Exp), reads=[bpb], writes=[b_ep])
                    yield
                    P.op("act", lambda e, pb=pb: e.activation(out=em[:], in_=pb[:, 0:128], func=AF.Exp, scale=-1.0), reads=[bpb], writes=[b_em])
                    yield
                    P.op("act", lambda e, pr_=pr_: e.activation(out=er[:], in_=pr_[:, 0:128], func=AF.Exp), reads=[bpr], writes=[b_er])
                    yield
                    P.op("dve", lambda e, h=h, c0=c0: e.tensor_tensor(out=qd[:], in0=q_sb[:, h, c0:c0 + 128], in1=ep[:], op=ALU.mult),
                         reads=[b_q[h], b_ep], writes=[b_qd])
                    yield
                    P.op("pool", lambda e, h=h, c0=c0: e.tensor_tensor(out=kd[:], in0=k_sb[:, h, c0:c0 + 128], in1=em[:], op=ALU.mult),
                         reads=[b_kk[h], b_em], writes=[b_kd])
                    yield
                    P.op("dve", lambda e, h=h: e.tensor_tensor(out=kdt[:], in0=ktok[:, h * 128:(h + 1) * 128], in1=er[:], op=ALU.mult),
                         reads=[b_ktok, b_er], writes=[b_kdt])
                    yield
                    pa, bpa = ringB.next()
                    P.op("pe", lambda e, pa=pa: e.matmul(pa[:, 0:128], lhsT=kd[:], rhs=qd[:], start=True, stop=True),
                         reads=[b_kd, b_qd], writes=[bpa])
                    yield
                    P.op("dve", lambda e, pa=pa: e.tensor_tensor(out=att[:], in0=pa[:, 0:128], in1=cst[:, 2, :], op=ALU.mult),
                         reads=[bpa, b_k], writes=[b_att])
                    yield
                    for ch in range(2):
                        r0 = ch * 64
                        po, bpo = ringB.next()
                        for vc in range(2):
                            P.op("pe", lambda e, po=po, vc=vc, Sb=Sb, r0=r0: e.matmul(
                                po[:, vc * 64:(vc + 1) * 64], lhsT=Sb[:, vc * 128:(vc + 1) * 128], rhs=qd[:, r0:r0 + 64], start=True, stop=False),
                                reads=[bSb, b_qd], writes=[bpo])
                            yield
                            P.op("pe", lambda e, po=po, vc=vc, h=h, r0=r0: e.matmul(
                                po[:, vc * 64:(vc + 1) * 64], lhsT=vtok[r0:r0 + 64, h * 256 + vc * 128:h * 256 + (vc + 1) * 128],
                                rhs=att[r0:r0 + 64, r0:r0 + 64], start=False, stop=True),
                                reads=[b_vtok, b_att], writes=[bpo])
                            yield
                        P.op("act", lambda e, po=po, r0=r0: e.activation(
                            out=o_sb[:, :, r0:r0 + 64], in_=po[:, 0:128].rearrange("p (v t) -> p v t", v=2), func=AF.Identity),
                            reads=[bpo], writes=[b_o[ch]])
                        yield
                        pu, bpu = ringB.next()
                        P.op("pe", lambda e, pu=pu, h=h, r0=r0: e.matmul(
                            pu[:, 0:256], lhsT=kdt[r0:r0 + 64, :], rhs=vtok[r0:r0 + 64, h * 256:(h + 1) * 256], start=True, stop=True),
                            reads=[b_kdt, b_vtok], writes=[bpu])
                        yield
                        P.op("dve", lambda e, pu=pu, S_=S_, r0=r0: e.scalar_tensor_tensor(
                            out=S_[:], in0=S_[:], scalar=ep[:, r0 + 63:r0 + 64], in1=pu[:, 0:256], op0=ALU.mult, op1=ALU.add),
                            reads=[bpu, bS, b_ep], writes=[bS])
                        yield
                        P.op("pool", lambda e, S_=S_, Sb=Sb: e.tensor_copy(out=Sb[:], in_=S_[:]), reads=[bS], writes=[bSb])
                        yield
                    P.op("act", lambda e: e.activation(out=osq[:], in_=o_sb[:], func=AF.Square), reads=b_o, writes=[b_osq])
                    yield
                    pn, bpn = ringB.next()
                    for vc in range(2):
                        P.op("pe", lambda e, pn=pn, vc=vc: e.matmul(pn[:, 0:128], lhsT=C.ones_bf[:], rhs=osq[:, vc, :], start=(vc == 0), stop=(vc == 1)),
                             reads=[b_osq, C.cb], writes=[bpn])
                        yield
                    P.op("act", lambda e, pn=pn: e.activation(out=ors[:], in_=pn[:, 0:128], func=AF.Sqrt, bias=C.eps[:, 0:1], scale=1.0 / 256.0),
                         reads=[bpn, C.cb], writes=[b_ors])
                    yield
                    P.op("dve", lambda e: e.reciprocal(out=ors[:], in_=ors[:]), reads=[b_ors], writes=[b_ors])
                    yield
                    for vc in range(2):
                        P.op("dve", lambda e, vc=vc, h=h, og_t=og_t: e.scalar_tensor_tensor(
                            out=og_t[:, h * 2 + vc, :], in0=o_sb[:, vc, :], scalar=og[:, vc:vc + 1], in1=ors[:], op0=ALU.mult, op1=ALU.mult),
                            reads=b_o + [b_ors, b_k], writes=[b_og])
                        yield
                        P.op("pool", lambda e, vc=vc, h=h, og_t=og_t, c0=c0: e.tensor_tensor(
                            out=og_t[:, h * 2 + vc, :], in0=og_t[:, h * 2 + vc, :], in1=r_sb[:, h * 2 + vc, c0:c0 + 128], op=ALU.mult),
                            reads=[b_og, b_r[h * 2 + vc]], writes=[b_og])
                        yield

                gens = [head_gen(0), head_gen(1)]
                while gens:
                    for g_ in list(gens):
                        try:
                            next(g_)
                        except StopIteration:
                            gens.remove(g_)
                tb = t0 + c0
                P.dma(odst(tb, 128), og_t[:], reads=[b_og], writes=[b_od])


def build_L2a(S=8192):
    nc = bass.Bass("TRN2", target_bir_lowering=False)
    xT_d = nc.dram_tensor("xT", [1024, S], F32, kind="ExternalInput").ap()
    cm = _decl_common(nc)
    wc_d = nc.dram_tensor("c_wc", [1024, 1552], F32, kind="ExternalInput").ap()
    wg_d = nc.dram_tensor("c_wg", [16, 256], F32, kind="ExternalInput").ap()
    bg_d = nc.dram_tensor("c_bg", [1, 256], F32, kind="ExternalInput").ap()
    og_d = nc.dram_tensor("c_og", [128, 2], F32, kind="ExternalInput").ap()
    cst_d = nc.dram_tensor("c_cst", [128, 3, 128], F32, kind="ExternalInput").ap()
    oT_d = nc.dram_tensor("oT", [512, S], F32, kind="ExternalOutput").ap()
    C = make_ctx(nc)
    emit_ada(C, *cm)
    emit_mixC(C, xT_d, wc_d, wg_d, bg_d, og_d, cst_d, oT_d, S)
    C.P.emit()
    return nc


def _c_consts():
    s_ = np.arange(128)[:, None]
    t_ = np.arange(128)[None, :]
    same = (s_ // 64) == (t_ // 64)
    tric = np.where(same & (s_ <= t_), -1.0 / 16.0, 0.0).astype(np.float32)
    trir = np.where(same & (s_ > t_), -1.0 / 16.0, 0.0).astype(np.float32)
    mc = (same & (s_ <= t_)).astype(np.float32)
    return np.ascontiguousarray(np.stack([tric, trir, mc], 1))


def lay_L2a(d, b, hp, xT_full):
    m = _lay_common(d, 2, b)
    w = d["c_w_in"][0]
    h0 = hp * 2
    m.update({
        "xT": xT_full,
        "c_wc": np.ascontiguousarray(np.concatenate([
            w[:, h0 * 128:(h0 + 2) * 128], w[:, 512 + h0 * 128:512 + (h0 + 2) * 128],
            w[:, 1024 + h0 * 256:1024 + (h0 + 2) * 256], w[:, 2048 + h0 * 256:2048 + (h0 + 2) * 256], w[:, 3072:3088]], 1)),
        "c_wg": np.ascontiguousarray(d["c_w_gate_up"][0][:, h0 * 128:(h0 + 2) * 128]),
        "c_bg": np.ascontiguousarray(d["c_b_gate"][0][h0 * 128:(h0 + 2) * 128].reshape(1, 256)),
        "c_og": np.ascontiguousarray(d["c_o_gain"][0].reshape(2, 128).T),
        "c_cst": _c_consts(),
    })
    return m


PAIRS = [[0, 1], [2, 3], [4, 5], [6, 7]]


def build_fused(H=4096, PAIRS=PAIRS):
    S = 2 * H
    nc = bass.Bass("TRN2", target_bir_lowering=False)
    TT = H + HALO

    def din(name, shape):
        return nc.dram_tensor(name, list(shape), F32, kind="ExternalInput").ap()
    xT_d = din("xT", [1024, TT])
    cT_d = din("cT", [128, 8])
    adaw = [din("ada_w%d" % L, [1024, 6144]) for L in range(4)]
    adab = [din("ada_b%d" % L, [128, 48]) for L in range(4)]
    gains = [din("gains%d" % L, [128, 16]) for L in range(4)]
    w1 = [din("w1_%d" % L, [1024, 4096]) for L in range(4)]
    w2 = [din("w2_%d" % L, [4096, 1024]) for L in range(4)]
    a_w_in = din("a_w_in", [1024, 3072]); a_gqk = din("a_gqk", [128, 2]); a_biasT = din("a_biasT", [128, 16, 640])
    a_valid = din("a_valid", [128, TT // 128]); a_w_out = din("a_w_out", [1024, 1024])
    b_w_in = din("b_w_in", [1024, 6144]); b_vgain = din("b_vgain", [128, 3072]); b_wsT = din("b_wsT", [128, 8, 128])
    b_bs = din("b_bs", [128, 24, 128]); b_w_out = din("b_w_out", [3072, 1024])
    c_wc = din("c_wc", [1024, 1552]); c_wg = din("c_wg", [16, 256]); c_bg = din("c_bg", [1, 256]); c_og = din("c_og", [128, 2])
    c_cst = din("c_cst", [128, 3, 128]); c_w_out = din("c_w_out", [1024, 1024])
    d_wqkv = din("d_wqkv", [1024, 1536]); d_cst = din("d_cst", [128, 3, 128]); d_w_out = din("d_w_out", [1024, 1024])
    sel_d = din("sel", [128, 2])
    yT_d = nc.dram_tensor("yT", [1024, H], F32, kind="ExternalOutput").ap()

    def scr(name, shape, dt=F32):
        return nc.dram_tensor(name, list(shape), dt).ap()
    qT_s = scr("a_qT_s", [1024, H], BF16); kT_s = scr("a_kT_s", [1024, TT], BF16); v_s = scr("a_v_s", [TT, 2048], BF16)
    vm_s = scr("b_vm_s", [24, 128, H], BF16)
    xs = [scr("xs%d" % i, [1024, H]) for i in range(4)]
    xa = scr("xa", [1024, H])
    XC = min(512, H)
    OC = min(1024, S)
    xb_c = [scr("xb_c%d" % i, [1024, XC]) for i in range(H // XC)]; xb_g = [scr("xb_g%d" % i, [2048, XC]) for i in range(H // XC)]
    xc_c = [scr("xc_c%d" % i, [1024, XC]) for i in range(H // XC)]; xc_g = [scr("xc_g%d" % i, [2048, XC]) for i in range(H // XC)]
    oc_c = [scr("oc_c%d" % i, [512, OC]) for i in range(S // OC)]; oc_g = [scr("oc_g%d" % i, [1024, OC]) for i in range(S // OC)]
    od_c = [scr("od_c%d" % i, [512, OC]) for i in range(S // OC)]; od_g = [scr("od_g%d" % i, [1024, OC]) for i in range(S // OC)]
    dq_s = scr("d_qT_s", [512, S], BF16); dk_s = scr("d_kT_s", [512, S], BF16); dv_s = scr("d_v_s", [S, 1024], BF16)

    C = make_ctx(nc)
    P = C.P
    selt = P.sb("selt", [128, 2], F32); b_sel = P.buf()
    P.dma(selt[:], sel_d, writes=[b_sel])

    def x_own(chunks):
        def f(t0, T):
            return chunks[t0 // XC][:, t0 % XC:t0 % XC + T].rearrange("(kc p) t -> p kc t", p=128)
        return f

    GB = {}

    def x_gath(chunks):
        def f(t0, T):
            r, tl = t0 // H, t0 % H
            return chunks[tl // XC][r * 1024:(r + 1) * 1024, tl % XC:tl % XC + T].rearrange("(kc p) t -> p kc t", p=128)
        f.deps = lambda t0: [GB[id(chunks)][(t0 % H) // XC]]
        return f

    def o_dst(chunks):
        def f(t0, n):
            return chunks[t0 // OC].rearrange("(oc p) t -> p oc t", p=128)[:, :, t0 % OC:t0 % OC + n]
        return f

    def o_gath(chunks):
        def f(h, t0, T):
            g = h * H + t0
            return chunks[g // OC].rearrange("(kc p) t -> p kc t", p=128)[:, :, g % OC:g % OC + T]
        f.deps = lambda h, t0: [GB[id(chunks)][(h * H + t0) // OC]]
        return f

    def gather(src, dst):
        bufs = P.bufs(len(src))
        GB[id(dst)] = bufs
        for a_, b_, gb in zip(src, dst, bufs):
            P.collective("AllGather", [a_], [b_], PAIRS, writes=[gb])
    emit_ada(C, cT_d, adaw[0], adab[0], gains[0])
    emit_mixA1(C, xT_d, a_w_in, a_gqk, qT_s, kT_s, v_s, H)
    emit_mixA2(C, xT_d, xs[0], qT_s, kT_s, v_s, a_biasT, a_valid, a_w_out, H)
    emit_ffn(C, xs[0], xa, w1[0], w2[0], H)
    emit_ada(C, cT_d, adaw[1], adab[1], gains[1])
    emit_mixB(C, xa, xs[1], b_w_in, b_vgain, b_wsT, b_bs, b_w_out, vm_s, H)
    emit_ffn(C, xs[1], None, w1[1], w2[1], H, ydst=x_own(xb_c))
    gather(xb_c, xb_g)
    emit_ada(C, cT_d, adaw[2], adab[2], gains[2])
    emit_mixC(C, None, c_wc, c_wg, c_bg, c_og, c_cst, None, S, xsrc=x_gath(xb_g), odst=o_dst(oc_c))
    gather(oc_c, oc_g)
    emit_outproj(C, None, None, c_w_out, xs[2], H, sel=(selt, b_sel), xsrc=x_own(xb_c), osrc=o_gath(oc_g))
    emit_ffn(C, xs[2], None, w1[2], w2[2], H, ydst=x_own(xc_c))
    gather(xc_c, xc_g)
    emit_ada(C, cT_d, adaw[3], adab[3], gains[3])
    emit_mixD1(C, None, d_wqkv, dq_s, dk_s, dv_s, S, xsrc=x_gath(xc_g))
    emit_mixD2(C, dq_s, dk_s, dv_s, d_cst, None, S, odst=o_dst(od_c))
    gather(od_c, od_g)
    emit_outproj(C, None, None, d_w_out, xs[3], H, sel=(selt, b_sel), xsrc=x_own(xc_c), osrc=o_gath(od_g))
    emit_ffn(C, xs[3], yT_d, w1[3], w2[3], H)
    P.emit()
    return nc


def lay_fused(d, b, hf, H=4096):
    m = {}
    l0 = lay_L0(d, b, hf, H)
    for k in ("xT", "cT", "a_w_in", "a_gqk", "a_biasT", "a_valid", "a_w_out"):
        m[k] = l0[k]
    for L in range(4):
        c = _lay_common(d, L, b)
        m["ada_w%d" % L] = c["ada_w"]; m["ada_b%d" % L] = c["ada_b"]; m["gains%d" % L] = c["gains"]
        m["w1_%d" % L] = np.ascontiguousarray(d["ffn_w1"][L]); m["w2_%d" % L] = np.ascontiguousarray(d["ffn_w2"][L])
    l1 = lay_L1(d, b)
    for k in ("b_w_in", "b_vgain", "b_wsT", "b_bs", "b_w_out"):
        m[k] = l1[k]
    l2 = lay_L2a(d, b, hf, None)
    for k in ("c_wc", "c_wg", "c_bg", "c_og", "c_cst"):
        m[k] = l2[k]
    m["c_w_out"] = np.ascontiguousarray(d["c_w_out"][0])
    l3 = lay_L3a(d, b, hf, None)
    for k in ("d_wqkv", "d_cst"):
        m[k] = l3[k]
    m["d_w_out"] = np.ascontiguousarray(d["d_w_out"][0])
    sel = np.zeros((128, 2), np.float32); sel[:, hf] = 1.0
    m["sel"] = sel
    return m


_PROGS = {}


def _prog(name, builder):
    if name not in _PROGS:
        _PROGS[name] = builder()
    return _PROGS[name]


def _run(nc, in_maps):
    res = run_bass_kernel_spmd(nc, in_maps, core_ids=list(range(len(in_maps))))
    return res.results


def kernel(**inputs):
    d = {k: np.asarray(v, dtype=np.float32) for k, v in inputs.items()}
    B, S, D = d["x"].shape
    H = S // 2
    cores = [(b, hf) for b in range(B) for hf in range(2)]
    nc = _prog("fused", lambda: build_fused(H))
    r = _run(nc, [lay_fused(d, b, hf, H) for (b, hf) in cores])
    out = np.empty((B, S, D), np.float32)
    for b in range(B):
        out[b, :H] = r[2 * b]["yT"].T
        out[b, H:] = r[2 * b + 1]["yT"].T
    return out


def kernel_unfused(**inputs):
    d = {k: np.asarray(v, dtype=np.float32) for k, v in inputs.items()}
    B, S, D = d["x"].shape
    H = S // 2
    cores = [(b, hf) for b in range(B) for hf in range(2)]
    nc = _prog("L0", lambda: build_L0(H))
    r = _run(nc, [lay_L0(d, b, hf, H) for (b, hf) in cores])
    xT = [np.concatenate([r[2 * b]["yT"], r[2 * b + 1]["yT"]], axis=1) for b in range(B)]
    nc = _prog("L1", lambda: build_L1(H))
    ins = []
    for (b, hf) in cores:
        m = lay_L1(d, b)
        m["xT"] = np.ascontiguousarray(xT[b][:, hf * H:(hf + 1) * H])
        ins.append(m)
    r = _run(nc, ins)
    xT = [np.concatenate([r[2 * b]["yT"], r[2 * b + 1]["yT"]], axis=1) for b in range(B)]
    nc = _prog("L2a", lambda: build_L2a(S))
    r = _run(nc, [lay_L2a(d, b, hp, xT[b]) for (b, hp) in cores])
    oT = [np.concatenate([r[2 * b]["oT"], r[2 * b + 1]["oT"]], axis=0) for b in range(B)]
    nc = _prog("Lb", lambda: build_Lb(H))
    ins = []
    for (b, hf) in cores:
        m = lay_Lb(d, 2, b, d["c_w_out"][0])
        m["xT"] = np.ascontiguousarray(xT[b][:, hf * H:(hf + 1) * H])
        m["oT"] = np.ascontiguousarray(oT[b][:, hf * H:(hf + 1) * H])
        ins.append(m)
    r = _run(nc, ins)
    xT = [np.concatenate([r[2 * b]["yT"], r[2 * b + 1]["yT"]], axis=1) for b in range(B)]
    nc = _prog("L3a", lambda: build_L3a(S))
    r = _run(nc, [lay_L3a(d, b, hg, xT[b]) for (b, hg) in cores])
    oT = [np.concatenate([r[2 * b]["oT"], r[2 * b + 1]["oT"]], axis=0) for b in range(B)]
    nc = _prog("Lb", lambda: build_Lb(H))
    ins = []
    for (b, hf) in cores:
        m = lay_Lb(d, 3, b, d["d_w_out"][0])
        m["xT"] = np.ascontiguousarray(xT[b][:, hf * H:(hf + 1) * H])
        m["oT"] = np.ascontiguousarray(oT[b][:, hf * H:(hf + 1) * H])
        ins.append(m)
    r = _run(nc, ins)
    out = np.empty((B, S, D), np.float32)
    for b in range(B):
        out[b, :H] = r[2 * b]["yT"].T
        out[b, H:] = r[2 * b + 1]["yT"].T
    return out
```
